# Optimizing a Trainium2 kernel written in Bass

```python
import math
import jax, jax.numpy as jnp
from jax import lax
import numpy as np

D_MODEL = 1024
BATCH = 32
SEQ = 2048
DEPTH = 1
DEC_BATCH = 8
DEC_SEQ = 4096
PAST_LEN = 128

A_HEADS = D_MODEL // 128
A_DK = 64
A_DV = 2 * A_DK
AQK = A_HEADS * 2 * A_DK
AW = A_HEADS * A_DV
ROT_DIM = A_DK // 4
ROPE_THETA = 500000.0
Q_BLOCK = 128
R_N = 64
R_HEADS = D_MODEL // R_N
RW = R_HEADS * R_N
W_LORA = 64
A_LORA = 64
CONV_W = 3
GN_EPS = 64e-5
P_SIZES = (AQK, AQK, AW, AW, 3 * RW, RW, 2 * W_LORA, 2 * A_LORA, D_MODEL, D_MODEL)
P_IN = AQK + AQK + AW + AW + 3 * RW + RW + 2 * W_LORA + 2 * A_LORA + D_MODEL + D_MODEL

kernel_name = "hybrid_diffattn_rwkv7_bidir_encoder"


def _rms(x, g, eps=1e-6):
    xf = x.astype(jnp.float32)
    y = xf * lax.rsqrt(jnp.mean(xf * xf, axis=-1, keepdims=True) + eps)
    return (y * g.astype(jnp.float32)).astype(x.dtype)


def _rope_partial(x, pos):
    half = ROT_DIM // 2
    inv = ROPE_THETA ** (-jnp.arange(0, ROT_DIM, 2, dtype=jnp.float32) / ROT_DIM)
    ang = pos.astype(jnp.float32)[:, None] * inv[None, :]
    cos = jnp.cos(ang)[None, :, None, None, :]
    sin = jnp.sin(ang)[None, :, None, None, :]
    xr = x[..., :ROT_DIM].astype(jnp.float32)
    x1, x2 = xr[..., :half], xr[..., half:]
    rot = jnp.concatenate([x1 * cos - x2 * sin, x2 * cos + x1 * sin], axis=-1).astype(x.dtype)
    return jnp.concatenate([rot, x[..., ROT_DIM:]], axis=-1)


def _diff_attention(q, k, v, lam):
    B, S = q.shape[0], q.shape[1]
    nb = S // Q_BLOCK
    qb = q.reshape(B, nb, Q_BLOCK, A_HEADS, 2, A_DK).swapaxes(0, 1)
    scale = A_DK ** -0.5

    def block(qi):
        s = jnp.einsum('bqhmd,bkhmd->bhmqk', qi, k).astype(jnp.float32) * scale
        p = jax.nn.softmax(s, axis=-1)
        diff = p[:, :, 0] - lam * p[:, :, 1]
        return jnp.einsum('bhqk,bkhe->bqhe', diff.astype(v.dtype), v)

    o = lax.map(block, qb)
    return o.swapaxes(0, 1).reshape(B, S, A_HEADS, A_DV)


def _rwkv7_scan(r, w, k, v, kk, a):
    B, S, H, N = r.shape

    def step(state, inp):
        r_t, w_t, k_t, v_t, kk_t, a_t = inp
        sa = jnp.einsum('bhvk,bhk->bhv', state, -kk_t)
        state = (state * w_t[:, :, None, :]
                 + sa[..., None] * (kk_t * a_t)[:, :, None, :]
                 + v_t[..., None] * k_t[:, :, None, :])
        return state, jnp.einsum('bhvk,bhk->bhv', state, r_t)

    xs = tuple(jnp.moveaxis(t, 1, 0) for t in (r, w, k, v, kk, a))
    s0 = jnp.zeros((B, H, N, N), jnp.float32)
    _, o = lax.scan(step, s0, xs)
    return jnp.moveaxis(o, 0, 1)


def _centred_dwconv(x, w):
    pad = CONV_W // 2
    S = x.shape[1]
    xp = jnp.pad(x, ((0, 0), (pad, pad), (0, 0)))
    out = xp[:, 0:S] * w[0]
    for i in range(1, CONV_W):
        out = out + xp[:, i:i + S] * w[i]
    return out


def _layer(x, l, norm_g, w_in, conv_rkv, lam_q1, lam_k1, lam_q2, lam_k2, attn_subln_g,
           w_lora_up, w0, a_lora_up, a0, k_k, k_a, r_k, ln_x_g, ln_x_b,
           w_o_attn, w_o_rwkv, w_out):
    B, S, _ = x.shape
    f32 = jnp.float32
    xn = _rms(x, norm_g)
    h = jnp.einsum('bsd,dp->bsp', xn, w_in)
    pts = []
    acc = 0
    for sz in P_SIZES[:-1]:
        acc += sz
        pts.append(acc)
    q, k, v, g_a, rkv, g_r, w_low, a_low, gm_a, gm_b = jnp.split(h, pts, axis=-1)

    pos = jnp.arange(S)
    q = _rope_partial(q.reshape(B, S, A_HEADS, 2, A_DK), pos)
    k = _rope_partial(k.reshape(B, S, A_HEADS, 2, A_DK), pos)
    v = v.reshape(B, S, A_HEADS, A_DV)
    lam_init = 0.8 - 0.6 * math.exp(-0.3 * l)
    lam = (jnp.exp(jnp.sum(lam_q1.astype(f32) * lam_k1.astype(f32)))
           - jnp.exp(jnp.sum(lam_q2.astype(f32) * lam_k2.astype(f32))) + lam_init)
    o_a = _diff_attention(q, k, v, lam)
    o_a = _rms(o_a, attn_subln_g, eps=1e-5) * (1.0 - lam_init)
    y_a = jnp.einsum('bsc,cd->bsd', o_a.reshape(B, S, AW) * jax.nn.silu(g_a), w_o_attn)

    rkv = _centred_dwconv(rkv, conv_rkv)
    r, kr, vr = jnp.split(rkv, 3, axis=-1)
    r_h = r.reshape(B, S, R_HEADS, R_N).astype(f32)
    k_h = kr.reshape(B, S, R_HEADS, R_N).astype(f32)
    v_h = vr.reshape(B, S, R_HEADS, R_N).astype(f32)
    kk = k_h * k_k.reshape(R_HEADS, R_N).astype(f32)
    kk = kk * lax.rsqrt(jnp.sum(kk * kk, axis=-1, keepdims=True) + 1e-12)
    w_low = w_low.reshape(B, S, 2, W_LORA)
    a_low = a_low.reshape(B, S, 2, A_LORA)
    w_log = -jax.nn.softplus(-(w0 + jnp.einsum('bsdr,drc->bsdc', jnp.tanh(w_low), w_lora_up))) - 0.5
    decay = jnp.exp(-jnp.exp(w_log.astype(f32))).reshape(B, S, 2, R_HEADS, R_N)
    a = jax.nn.sigmoid((a0 + jnp.einsum('bsdr,drc->bsdc', a_low, a_lora_up)).astype(f32))
    a = a.reshape(B, S, 2, R_HEADS, R_N)
    k_dir = k_h[:, :, None] * (1.0 + (a - 1.0) * k_a.reshape(R_HEADS, R_N).astype(f32))
    o_f = _rwkv7_scan(r_h, decay[:, :, 0], k_dir[:, :, 0], v_h, kk, a[:, :, 0])
    fl = lambda t: jnp.flip(t, axis=1)
    o_b = fl(_rwkv7_scan(fl(r_h), fl(decay[:, :, 1]), fl(k_dir[:, :, 1]), fl(v_h), fl(kk), fl(a[:, :, 1])))
    o_r = o_f + o_b
    mu = jnp.mean(o_r, axis=-1, keepdims=True)
    var = jnp.mean(jnp.square(o_r - mu), axis=-1, keepdims=True)
    o_n = ((o_r - mu) * lax.rsqrt(var + GN_EPS) * ln_x_g.reshape(R_HEADS, R_N).astype(f32)
           + ln_x_b.reshape(R_HEADS, R_N).astype(f32))
    bonus = jnp.sum(r_h * k_h * r_k.reshape(R_HEADS, R_N).astype(f32), axis=-1, keepdims=True) * v_h
    o_r = (o_n + bonus).reshape(B, S, RW).astype(x.dtype)
    y_r = jnp.einsum('bsc,cd->bsd', o_r * jax.nn.silu(g_r), w_o_rwkv)

    m = jax.nn.sigmoid(gm_a) * y_a + jax.nn.sigmoid(gm_b) * y_r
    return jnp.einsum('bsd,de->bse', m, w_out)


def _encode(x, layer_params, final_g):
    for l in range(DEPTH):
        x = x + _layer(x, l, *[p[l] for p in layer_params])
    return _rms(x, final_g)


def setup_inputs(seed: int = 0) -> dict:
    key = jax.random.key(seed)
    ks = jax.random.split(key, 26)
    f32 = jnp.float32
    nrm = lambda kk, shape, scale: scale * jax.random.normal(kk, shape, f32)
    return {
        "x_prompt": nrm(ks[0], (BATCH, SEQ, D_MODEL), 1.0),
        "x_sample": nrm(ks[1], (DEC_BATCH, DEC_SEQ, D_MODEL), 1.0),
        "norm_g": 1.0 + nrm(ks[2], (DEPTH, D_MODEL), 0.02),
        "w_in": nrm(ks[3], (DEPTH, D_MODEL, P_IN), D_MODEL ** -0.5),
        "conv_rkv": nrm(ks[4], (DEPTH, CONV_W, 3 * RW), CONV_W ** -0.5),
        "lam_q1": nrm(ks[5], (DEPTH, A_DK), 0.1),
        "lam_k1": nrm(ks[6], (DEPTH, A_DK), 0.1),
        "lam_q2": nrm(ks[7], (DEPTH, A_DK), 0.1),
        "lam_k2": nrm(ks[8], (DEPTH, A_DK), 0.1),
        "attn_subln_g": 1.0 + nrm(ks[9], (DEPTH, A_DV), 0.02),
        "w_lora_up": nrm(ks[10], (DEPTH, 2, W_LORA, RW), 0.1 * W_LORA ** -0.5),
        "w0": jax.random.uniform(ks[11], (DEPTH, 2, RW), f32, -3.0, 0.0),
        "a_lora_up": nrm(ks[12], (DEPTH, 2, A_LORA, RW), 0.1 * A_LORA ** -0.5),
        "a0": nrm(ks[13], (DEPTH, 2, RW), 0.1),
        "k_k": 0.85 + nrm(ks[14], (DEPTH, RW), 0.02),
        "k_a": 1.0 + nrm(ks[15], (DEPTH, RW), 0.02),
        "r_k": nrm(ks[16], (DEPTH, RW), 0.1),
        "ln_x_g": 1.0 + nrm(ks[17], (DEPTH, RW), 0.02),
        "ln_x_b": nrm(ks[18], (DEPTH, RW), 0.02),
        "w_o_attn": nrm(ks[19], (DEPTH, AW, D_MODEL), AW ** -0.5),
        "w_o_rwkv": nrm(ks[20], (DEPTH, RW, D_MODEL), RW ** -0.5),
        "w_out": nrm(ks[21], (DEPTH, D_MODEL, D_MODEL), D_MODEL ** -0.5),
        "final_g": 1.0 + nrm(ks[22], (D_MODEL,), 0.02),
    }


def reference(x_prompt, x_sample, norm_g, w_in, conv_rkv, lam_q1, lam_k1, lam_q2, lam_k2,
              attn_subln_g, w_lora_up, w0, a_lora_up, a0, k_k, k_a, r_k, ln_x_g, ln_x_b,
              w_o_attn, w_o_rwkv, w_out, final_g):
    layer_params = (norm_g, w_in, conv_rkv, lam_q1, lam_k1, lam_q2, lam_k2, attn_subln_g,
                    w_lora_up, w0, a_lora_up, a0, k_k, k_a, r_k, ln_x_g, ln_x_b,
                    w_o_attn, w_o_rwkv, w_out)
    y_prompt = _encode(x_prompt, layer_params, final_g)
    y_sample = _encode(x_sample, layer_params, final_g)
    return (y_prompt, y_sample)
```

```python
import contextlib
import re
import numpy as np
import concourse.bass as bass
import concourse.mybir as mybir

F32 = mybir.dt.float32
BF16 = mybir.dt.bfloat16
I32 = mybir.dt.int32
AF = mybir.ActivationFunctionType
ALU = mybir.AluOpType
AX = mybir.AxisListType

SEM_ROT = 30000


class Buf:
    __slots__ = ("t", "name", "last_w", "readers", "ld_sem", "st_sem")

    def __init__(self, t, name):
        self.t = t
        self.name = name
        self.last_w = None
        self.readers = {}
        self.ld_sem = None
        self.st_sem = None

    def __getitem__(self, k):
        return self.t[k]


class FW:
    def __init__(self, nc):
        self.nc = nc
        self.stack = contextlib.ExitStack()
        self.eng = {"pe": nc.tensor, "act": nc.scalar, "dve": nc.vector,
                    "pool": nc.gpsimd, "sp": nc.sync}
        self.sems = {}
        self.cur = {}
        self.cnt = {}
        self.waited = {e: {} for e in self.eng}
        self.dma_total = {}
        self.dma_roles = {}
        self.nsem = 0
        self.n_ops = 0
        self.n_waits = 0
        for e in self.eng:
            self._new_eng_sem(e)

    def _alloc_sem(self, name):
        h = self.stack.enter_context(self.nc.semaphore(name))
        key = name
        self.sems[key] = h
        self.nsem += 1
        return key

    def _new_eng_sem(self, e):
        key = self._alloc_sem("s_%s_%d" % (e, self.nsem))
        self.cur[e] = key
        self.cnt[key] = 0

    def dma_sem(self, name):
        role = re.sub(r"@\d+", "", name)
        if role in self.dma_roles:
            return self.dma_roles[role]
        key = self._alloc_sem("d_%s_%d" % (role, self.nsem))
        self.dma_total[key] = 0
        self.dma_roles[role] = key
        return key

    def snapshot(self):
        snap = {}
        for k in self.sems:
            v = self.dma_total[k] if k in self.dma_total else self.cnt.get(k, 0)
            if v > 0:
                snap[k] = v
        return snap

    def sb(self, name, shape, dtype, stack=None):
        self.n_alloc = getattr(self, "n_alloc", 0) + 1
        t = (stack or self.stack).enter_context(self.nc.sbuf_tensor("%s_u%d" % (name.replace("@", "_"), self.n_alloc), list(shape), dtype))
        b = Buf(t, name)
        b.readers = self.snapshot()
        return b

    def ps(self, name, shape, dtype, stack=None):
        t = (stack or self.stack).enter_context(self.nc.psum_tensor(name, list(shape), dtype))
        return Buf(t, name)

    def view(self, buf_or_ap, name):
        return Buf(buf_or_ap, name)

    def _collect(self, e, reads, writes):
        deps = {}

        def add(tok):
            if tok is None:
                return
            k, v = tok
            if k in self.dma_total:
                v = self.dma_total[k]
            if deps.get(k, 0) < v:
                deps[k] = v

        for b in reads:
            add(b.last_w)
        for b in writes:
            add(b.last_w)
            for k, v in b.readers.items():
                add((k, v))
        return deps

    def _emit_waits(self, e, deps):
        eng = self.eng[e]
        w = self.waited[e]
        for k, v in deps.items():
            if e == "pe" and k.startswith("s_pe_"):
                continue
            if w.get(k, 0) >= v:
                continue
            eng.wait_ge(self.sems[k], v)
            w[k] = v
            self.n_waits += 1

    def _mark(self, tok, reads, writes):
        k, v = tok
        for b in writes:
            b.last_w = tok
            b.readers = {}
        for b in reads:
            if b.readers.get(k, 0) < v:
                b.readers[k] = v

    def op(self, e, fn, reads=(), writes=()):
        deps = self._collect(e, reads, writes)
        self._emit_waits(e, deps)
        ins = fn(self.eng[e])
        key = self.cur[e]
        if self.cnt[key] >= SEM_ROT:
            self._new_eng_sem(e)
            key = self.cur[e]
        self.cnt[key] += 1
        ins.then_inc(self.sems[key], 1)
        self._mark((key, self.cnt[key]), reads, writes)
        self.n_ops += 1
        return ins

    def dma(self, q, out, in_, sem, reads=(), writes=(), **kw):
        deps = self._collect(q, reads, writes)
        self._emit_waits(q, deps)
        ins = self.eng[q].dma_start(out=out, in_=in_, **kw)
        self.dma_total[sem] += 16
        ins.then_inc(self.sems[sem], 16)
        self._mark((sem, self.dma_total[sem]), reads, writes)
        self.n_ops += 1
        return ins

    def load(self, q, buf, out_ap, in_ap, **kw):
        if buf.ld_sem is None:
            buf.ld_sem = self.dma_sem("l" + buf.name)
        return self.dma(q, out_ap, in_ap, buf.ld_sem, reads=(), writes=(buf,), **kw)

    def store(self, q, buf, out_ap, in_ap, **kw):
        if buf.st_sem is None:
            buf.st_sem = self.dma_sem("s" + buf.name)
        return self.dma(q, out_ap, in_ap, buf.st_sem, reads=(buf,), writes=(), **kw)

    def final_wait(self, e="sp"):
        eng = self.eng[e]
        for k, h in self.sems.items():
            v = self.dma_total[k] if k in self.dma_total else self.cnt.get(k, 0)
            if v > 0 and self.waited[e].get(k, 0) < v:
                eng.wait_ge(h, v)
                self.waited[e][k] = v


import math
from types import SimpleNamespace
import contextlib
import numpy as np
import concourse.bass as bass
import concourse.mybir as mybir
from concourse.bass_utils import run_bass_kernel_spmd

D = 1024
PIN = 10496
O_Q, O_K, O_V, O_GA, O_R, O_RK, O_RV, O_GR, O_WL, O_AL, O_GMA, O_GMB = (
    0, 1024, 2048, 3072, 4096, 5120, 6144, 7168, 8192, 8320, 8448, 9472)
ROPE_THETA = 500000.0
TWO_PI = 2.0 * math.pi
CW1 = 6.28125
CW2 = TWO_PI - CW1

PARAMS = [("norm_g", [1, 1024]), ("w_in", [1024, PIN]), ("conv_rkv", [3, 3072]),
          ("lam_q1", [1, 64]), ("lam_k1", [1, 64]), ("lam_q2", [1, 64]), ("lam_k2", [1, 64]),
          ("attn_subln_g", [1, 128]), ("w_lora_up", [128, 1024]), ("w0", [2, 1024]),
          ("a_lora_up", [128, 1024]), ("a0", [2, 1024]), ("k_k", [1, 1024]), ("k_a", [1, 1024]),
          ("r_k", [1, 1024]), ("ln_x_g", [1, 1024]), ("ln_x_b", [1, 1024]),
          ("w_o_attn", [1024, 1024]), ("w_o_rwkv", [1024, 1024]), ("w_out", [1024, 1024]),
          ("final_g", [1, 1024])]


def build(seq_lens, do_attn=True, do_rwkv=True):
    nc = bass.Bass("TRN2", target_bir_lowering=False)
    TOT = sum(seq_lens)
    SMAX = max(seq_lens)
    x = nc.dram_tensor("x", [TOT, D], F32, kind="ExternalInput").ap()
    y = nc.dram_tensor("y", [TOT, D], F32, kind="ExternalOutput").ap()
    P = {n: nc.dram_tensor(n, s, F32, kind="ExternalInput").ap() for n, s in PARAMS}
    w_in = P["w_in"]
    scrA = nc.dram_tensor("scrA", [128, 8, SMAX], BF16, kind="Internal").ap()
    scrM = nc.dram_tensor("scrM", [128, 8, SMAX], BF16, kind="Internal").ap()
    scrR = nc.dram_tensor("scrR", [128, 8, SMAX], BF16, kind="Internal").ap()
    fw = FW(nc)
    op = fw.op
    ncd = nc.allow_non_contiguous_dma(reason="small param layouts")
    ncd.__enter__()

    def wview(off, ncols):
        return w_in[:, off:off + ncols].rearrange("(c p) n -> p c n", p=128)

    with fw.stack:
        dA, dM, dR = fw.view(scrA, "scrA"), fw.view(scrM, "scrM"), fw.view(scrR, "scrR")
        DBt = [fw.stack.enter_context(nc.psum_tensor("db%d" % i, [128, 1024], F32)) for i in range(4)]
        DBf = [t[:] for t in DBt]
        DBh = [t[:].bitcast(BF16) for t in DBt]
        PB = [Buf(None, "bank%d" % j) for j in range(8)]

        def bk(i, both=True, half=0):
            return (PB[2 * i], PB[2 * i + 1]) if both else (PB[2 * i + half],)

        psem = fw.dma_sem("params")
        psem2 = fw.dma_sem("paramsq")

        def pload(name, shape, src, q="sp", dtype=F32):
            b = fw.sb(name, shape, dtype)
            b.ld_sem = psem if q == "sp" else psem2
            fw.load(q, b, b[:], src)
            return b

        def colparam(name, nm):
            return pload(name, [128, 8], P[nm].rearrange("o (c p) -> p (o c)", p=128))

        gcol = colparam("gcol", "norm_g")
        kk_c = colparam("kk_c", "k_k")
        ka_c = colparam("ka_c", "k_a")
        rk_c = colparam("rk_c", "r_k")
        lg_c = colparam("lg_c", "ln_x_g")
        lb_c = colparam("lb_c", "ln_x_b")
        w0_c = pload("w0_c", [128, 2, 8], P["w0"].rearrange("d (c p) -> p d c", p=128))
        a0_c = pload("a0_c", [128, 2, 8], P["a0"].rearrange("d (c p) -> p d c", p=128))
        cv_c = pload("cv_c", [128, 3, 24], P["conv_rkv"].rearrange("i (c p) -> p i c", p=128))
        fgb = pload("fgb", [128, 1024], P["final_g"][0, :].partition_broadcast(128))
        sgb = pload("sgb", [128, 128], P["attn_subln_g"][0, :].partition_broadcast(128))
        lq = [pload("lam%d" % i, [128, 64], P[n][0, :].partition_broadcast(128))
              for i, n in enumerate(["lam_q1", "lam_k1", "lam_q2", "lam_k2"])]
        wlw = pload("wlw", [128, 1024], P["w_lora_up"], q="pool", dtype=BF16)
        wla = pload("wla", [128, 1024], P["a_lora_up"], q="pool", dtype=BF16)

        ident = fw.sb("ident", [128, 128], BF16)
        op("pool", lambda e: e.memset(ident[:], 1.0), writes=(ident,))
        op("pool", lambda e: e.affine_select(out=ident[:], in_=ident[:], pattern=[[-1, 128]], compare_op=ALU.is_equal,
                                             fill=0.0, base=0, channel_multiplier=1), reads=(ident,), writes=(ident,))
        grep = fw.sb("grep", [128, 8, 128], F32)
        op("dve", lambda e: e.memset(grep[:], 1.0), writes=(grep,))
        for c in range(8):
            op("dve", lambda e: e.tensor_scalar(out=grep[:, c, :], in0=grep[:, c, :], scalar1=gcol[:, c:c + 1],
                                                scalar2=None, op0=ALU.mult), reads=(grep, gcol), writes=(grep,))
        op("dve", lambda e: e.tensor_scalar(out=sgb[:], in0=sgb[:], scalar1=0.8, scalar2=None, op0=ALU.mult),
           reads=(sgb,), writes=(sgb,))
        hw0_c = fw.sb("hw0_c", [128, 2, 8], F32)
        ha0_c = fw.sb("ha0_c", [128, 2, 8], F32)
        op("dve", lambda e: e.tensor_scalar(out=hw0_c[:], in0=w0_c[:], scalar1=0.5, scalar2=None, op0=ALU.mult), reads=(w0_c,), writes=(hw0_c,))
        op("dve", lambda e: e.tensor_scalar(out=ha0_c[:], in0=a0_c[:], scalar1=0.5, scalar2=None, op0=ALU.mult), reads=(a0_c,), writes=(ha0_c,))
        tmka = fw.sb("tmka", [128, 8], F32)
        op("dve", lambda e: e.tensor_scalar(out=tmka[:], in0=ka_c[:], scalar1=-1.0, scalar2=2.0, op0=ALU.mult, op1=ALU.add),
           reads=(ka_c,), writes=(tmka,))
        eps5 = fw.sb("eps5", [128, 1], F32)
        op("dve", lambda e: e.memset(eps5[:], 1e-5), writes=(eps5,))
        lnhalf = fw.sb("lnhalf", [128, 1], F32)
        op("dve", lambda e: e.memset(lnhalf[:], math.log(0.5)), writes=(lnhalf,))
        omka = fw.sb("omka", [128, 8], F32)
        op("dve", lambda e: e.tensor_scalar(out=omka[:], in0=ka_c[:], scalar1=-1.0, scalar2=1.0, op0=ALU.mult, op1=ALU.add),
           reads=(ka_c,), writes=(omka,))
        lt = fw.sb("lt", [128, 64], F32)
        ls = fw.sb("ls", [128, 2], F32)
        nlam = fw.sb("nlam", [128, 1], F32)
        for i in range(2):
            op("dve", lambda e: e.tensor_tensor(out=lt[:], in0=lq[2 * i][:], in1=lq[2 * i + 1][:], op=ALU.mult),
               reads=(lq[2 * i], lq[2 * i + 1]), writes=(lt,))
            op("dve", lambda e: e.reduce_sum(out=ls[:, i:i + 1], in_=lt[:], axis=AX.X), reads=(lt,), writes=(ls,))
        op("act", lambda e: e.activation(out=ls[:], in_=ls[:], func=AF.Exp), reads=(ls,), writes=(ls,))
        op("dve", lambda e: e.tensor_tensor(out=nlam[:], in0=ls[:, 1:2], in1=ls[:, 0:1], op=ALU.subtract),
           reads=(ls,), writes=(nlam,))
        op("dve", lambda e: e.tensor_scalar(out=nlam[:], in0=nlam[:], scalar1=-0.2, scalar2=None, op0=ALU.add),
           reads=(nlam,), writes=(nlam,))

        def mkmask(name, mult_f, mult_p, base, cmp):
            m = fw.sb(name, [128, 128], BF16)
            op("pool", lambda e: e.memset(m[:], 1.0), writes=(m,))
            op("pool", lambda e: e.affine_select(out=m[:], in_=m[:], pattern=[[mult_f, 128]], compare_op=cmp,
                                                 fill=0.0, base=base, channel_multiplier=mult_p), reads=(m,), writes=(m,))
            return m
        Us = mkmask("Us", 1, -1, 0, ALU.is_gt)
        Ui = mkmask("Ui", 1, -1, 0, ALU.is_ge)
        Ls = mkmask("Ls", -1, 1, 0, ALU.is_gt)
        Li = mkmask("Li", -1, 1, 0, ALU.is_ge)
        MASK4 = []
        MASK8 = []
        MASKL = []
        for d_ in range(2):
            m4 = fw.sb("m4_%d" % d_, [128, 4, 128], BF16)
            s_, i_ = (Us, Ui) if d_ == 0 else (Ls, Li)
            for j, src in enumerate([s_, i_, s_, i_]):
                op("pool", lambda e: e.tensor_copy(out=m4[:, j, :], in_=src[:]), reads=(src,), writes=(m4,))
            MASK4.append(m4)
            m8 = fw.sb("m8_%d" % d_, [128, 2, 4, 128], BF16)
            for h_ in range(2):
                for j, src in enumerate([s_, i_, s_, i_]):
                    op("pool", lambda e: e.tensor_copy(out=m8[:, h_, j, :], in_=src[:]), reads=(src,), writes=(m8,))
            MASK8.append(m8)
            ml = fw.sb("ml_%d" % d_, [128, 2, 128], BF16)
            src = Ls if d_ == 0 else Us
            for j in range(2):
                op("pool", lambda e: e.tensor_copy(out=ml[:, j, :], in_=src[:]), reads=(src,), writes=(ml,))
            MASKL.append(ml)
        BOh = fw.sb("BOh", [128, 128], BF16)
        BOf = fw.sb("BOf", [128, 128], F32)
        for t_ in (BOh, BOf):
            op("pool", lambda e: e.memset(t_[:], 0.0), writes=(t_,))
            op("pool", lambda e: e.memset(t_[0:64, 0:64], 1.0), reads=(t_,), writes=(t_,))
            op("pool", lambda e: e.memset(t_[64:128, 64:128], 1.0), reads=(t_,), writes=(t_,))
        I2 = fw.sb("I2", [128, 64], F32)
        op("pool", lambda e: e.memset(I2[:], 1.0), writes=(I2,))
        op("pool", lambda e: e.affine_select(out=I2[0:64, :], in_=I2[0:64, :], pattern=[[-1, 64]], compare_op=ALU.is_equal,
                                             fill=0.0, base=0, channel_multiplier=1), reads=(I2,), writes=(I2,))
        op("pool", lambda e: e.affine_select(out=I2[64:128, :], in_=I2[64:128, :], pattern=[[-1, 64]], compare_op=ALU.is_equal,
                                             fill=0.0, base=0, channel_multiplier=1), reads=(I2,), writes=(I2,))
        onesS = fw.sb("onesS", [128, 128], F32)
        op("pool", lambda e: e.memset(onesS[:], 1.0), writes=(onesS,))

        def proj_fm(dst_fn, W, wslot_fn, xnT, S, evac):
            pass

        tok0 = 0
        for si, S in enumerate(seq_lens):
            NT = S // 128
            NB = S // 512 if S >= 512 else 1
            BW = min(512, S)
            with contextlib.ExitStack() as sq:
                xnT = fw.sb("xnT@%d" % si, [128, 8, S + 2], BF16, sq)
                op("pool", lambda e: e.memset(xnT[:, :, 0:1], 0.0), writes=(xnT,))
                op("pool", lambda e: e.memset(xnT[:, :, S + 1:S + 2], 0.0), writes=(xnT,))
                with contextlib.ExitStack() as sa:
                    xt = [fw.sb("xt%d@%d" % (i, si), [128, 1024], F32, sa) for i in range(2)]
                    junk = fw.sb("junkA@%d" % si, [128, 1024], F32, sa)
                    xs = [fw.sb("xs%d@%d" % (i, si), [128, 1024], BF16, sa) for i in range(2)]
                    ssA = [fw.sb("ssA%d@%d" % (i, si), [128, 1], F32, sa) for i in range(2)]
                    for tt in range(NT):
                        b = tt % 2
                        fw.load("sp", xt[b], xt[b][:], x[tok0 + tt * 128: tok0 + (tt + 1) * 128, :])
                        op("act", lambda e: e.activation(out=junk[:], in_=xt[b][:], func=AF.Square, accum_out=ssA[b][:]),
                           reads=(xt[b],), writes=(junk, ssA[b]))
                        op("dve", lambda e: e.tensor_scalar(out=ssA[b][:], in0=ssA[b][:], scalar1=1.0 / 1024, scalar2=1e-6,
                                                            op0=ALU.mult, op1=ALU.add), reads=(ssA[b],), writes=(ssA[b],))
                        op("act", lambda e: e.activation(out=ssA[b][:], in_=ssA[b][:], func=AF.Sqrt), reads=(ssA[b],), writes=(ssA[b],))
                        op("dve", lambda e: e.reciprocal(out=ssA[b][:], in_=ssA[b][:]), reads=(ssA[b],), writes=(ssA[b],))
                        op("act", lambda e: e.activation(out=xs[b][:], in_=xt[b][:], func=AF.Copy, scale=ssA[b][:, 0:1]),
                           reads=(xt[b], ssA[b]), writes=(xs[b],))
                        db = tt % 2
                        for c in range(8):
                            op("pe", lambda e: e.transpose(out=DBh[db][:, c * 128:(c + 1) * 128], in_=xs[b][:, c * 128:(c + 1) * 128],
                                                           identity=ident[:]), reads=(xs[b], ident), writes=bk(db, False, 0))
                        op("dve", lambda e: e.tensor_tensor(out=xnT[:, :, 1 + tt * 128: 1 + (tt + 1) * 128],
                                                            in0=DBh[db][:, 0:1024].rearrange("p (c t) -> p c t", c=8),
                                                            in1=grep[:], op=ALU.mult),
                           reads=bk(db, False, 0) + (grep,), writes=(xnT,))

                def xblk(kc, tb):
                    return xnT[:, kc, 1 + tb * BW: 1 + (tb + 1) * BW]

                C = SimpleNamespace(**locals())
                if do_attn:
                    stage_attn(C)
                if do_rwkv:
                    stage_rwkv(C)
                stage_out(C)
            tok0 += S
        fw.final_wait("sp")
    ncd.__exit__(None, None, None)
    return nc


def load_w(C, st, name, src_ap, shape):
    b = C.fw.sb(name, shape, BF16, st)
    C.fw.load("pool", b, b[:], src_ap)
    return b


def stage_out(C):
    fw, op, S, BW, NB, si = C.fw, C.fw.op, C.S, C.BW, C.NB, C.si
    DBf, bk = C.DBf, C.bk
    with contextlib.ExitStack() as st:
        Wout = load_w(C, st, "Wout@%d" % si, C.P["w_out"].rearrange("(c p) n -> p c n", p=128), [128, 8, 1024])
        ma = [fw.sb("ma%d@%d" % (i, si), [128, 8, BW], BF16, st) for i in range(2)]
        mr = [fw.sb("mr%d@%d" % (i, si), [128, 8, BW], BF16, st) for i in range(2)]
        xt = [fw.sb("xo%d@%d" % (i, si), [128, 1024], F32, st) for i in range(2)]
        zt = [fw.sb("zt%d@%d" % (i, si), [128, 1024], F32, st) for i in range(2)]
        junk = fw.sb("junkO@%d" % si, [128, 1024], F32, st)
        ss = [fw.sb("ssO%d@%d" % (i, si), [128, 1], F32, st) for i in range(2)]
        cnt = 0
        for tb in range(NB):
            b = tb % 2
            m = None
            if C.do_attn:
                fw.dma("sp", ma[b][:], C.scrM[:, :, tb * BW:(tb + 1) * BW], _ldsem(fw, ma[b]), reads=(C.dM,), writes=(ma[b],))
                m = ma[b]
            if C.do_rwkv:
                fw.dma("sp", mr[b][:], C.scrR[:, :, tb * BW:(tb + 1) * BW], _ldsem(fw, mr[b]), reads=(C.dR,), writes=(mr[b],))
                if m is None:
                    m = mr[b]
                else:
                    op("pool", lambda e: e.tensor_tensor(out=ma[b][:], in0=ma[b][:], in1=mr[b][:], op=ALU.add),
                       reads=(ma[b], mr[b]), writes=(ma[b],))
            for t4 in range(BW // 128):
                tt = tb * (BW // 128) + t4
                xb = cnt % 2
                db = 2 + cnt % 2
                cnt += 1
                fw.load("sp", xt[xb], xt[xb][:], C.x[C.tok0 + tt * 128: C.tok0 + (tt + 1) * 128, :])
                if m is not None:
                    for half in range(2):
                        for dc in range(8):
                            op("pe", lambda e: e.matmul(DBf[db][:, half * 512:(half + 1) * 512], lhsT=m[:, dc, t4 * 128:(t4 + 1) * 128],
                                                        rhs=Wout[:, dc, half * 512:(half + 1) * 512], start=(dc == 0), stop=(dc == 7)),
                               reads=(m, Wout), writes=bk(db, False, half))
                    op("dve", lambda e: e.tensor_tensor(out=zt[xb][:], in0=DBf[db][:], in1=xt[xb][:], op=ALU.add),
                       reads=bk(db) + (xt[xb],), writes=(zt[xb],))
                else:
                    op("dve", lambda e: e.tensor_copy(out=zt[xb][:], in_=xt[xb][:]), reads=(xt[xb],), writes=(zt[xb],))
                op("act", lambda e: e.activation(out=junk[:], in_=zt[xb][:], func=AF.Square, accum_out=ss[xb][:]),
                   reads=(zt[xb],), writes=(junk, ss[xb]))
                op("dve", lambda e: e.tensor_scalar(out=ss[xb][:], in0=ss[xb][:], scalar1=1.0 / 1024, scalar2=1e-6,
                                                    op0=ALU.mult, op1=ALU.add), reads=(ss[xb],), writes=(ss[xb],))
                op("act", lambda e: e.activation(out=ss[xb][:], in_=ss[xb][:], func=AF.Sqrt), reads=(ss[xb],), writes=(ss[xb],))
                op("dve", lambda e: e.reciprocal(out=ss[xb][:], in_=ss[xb][:]), reads=(ss[xb],), writes=(ss[xb],))
                op("dve", lambda e: e.scalar_tensor_tensor(out=zt[xb][:], in0=zt[xb][:], scalar=ss[xb][:, 0:1], in1=C.fgb[:],
                                                           op0=ALU.mult, op1=ALU.mult), reads=(zt[xb], ss[xb], C.fgb), writes=(zt[xb],))
                fw.store("sp", zt[xb], C.y[C.tok0 + tt * 128: C.tok0 + (tt + 1) * 128, :], zt[xb][:])


def _ldsem(fw, b):
    if b.ld_sem is None:
        b.ld_sem = fw.dma_sem("l" + b.name)
    return b.ld_sem


def _stsem(fw, b):
    if b.st_sem is None:
        b.st_sem = fw.dma_sem("s" + b.name)
    return b.st_sem


def stage_attn(C):
    fw, op, S, BW, NB, NT, si = C.fw, C.fw.op, C.S, C.BW, C.NB, C.NT, C.si
    DBf, DBh, bk, xnT, xblk = C.DBf, C.DBh, C.bk, C.xnT, C.xblk
    ident = C.ident
    nq = BW // 128
    with contextlib.ExitStack() as st:
        cosT, sinT = make_rope(C, st)
        Wh = fw.sb("Wh@%d" % si, [128, 5, 8, 128], BF16, st)
        op("pool", lambda e: e.memset(Wh[:, 3:5, :, :], 0.0), writes=(Wh,))
        qT = fw.sb("qT@%d" % si, [128, S], BF16, st)
        kT = fw.sb("kT@%d" % si, [128, S], BF16, st)
        Vh = fw.sb("Vh@%d" % si, [128, NT, 129], BF16, st)
        op("pool", lambda e: e.memset(Vh[:, :, 128:129], 1.0), writes=(Vh,))
        E = [fw.sb("E%d@%d" % (i, si), [128, 2, BW], BF16, st) for i in range(2)]
        t1 = fw.sb("t1@%d" % si, [128, BW], F32, st)
        t2 = fw.sb("t2@%d" % si, [128, BW], F32, st)
        rsq = [fw.sb("rs%d@%d" % (i, si), [128, 2], F32, st) for i in range(4)]
        o1q = [fw.sb("o1%d@%d" % (i, si), [128, 128], F32, st) for i in range(4)]
        ssqq = [fw.sb("ssq%d@%d" % (i, si), [128, 1], F32, st) for i in range(4)]
        onq = [fw.sb("on%d@%d" % (i, si), [128, 128], BF16, st) for i in range(4)]
        junk = fw.sb("junkB@%d" % si, [128, 128], F32, st)
        ssq = fw.sb("ssq@%d" % si, [128, 1], F32, st)
        on = fw.sb("on@%d" % si, [128, 128], BF16, st)
        ogT = [fw.sb("ogT%d@%d" % (i, si), [128, BW], BF16, st) for i in range(2)]
        for h in range(8):
            for s_, off in enumerate((O_Q, O_K, O_V)):
                fw.load("pool", Wh, Wh[:, s_, :, :], C.wview(off + h * 128, 128))
            for s_ in (0, 1):
                for m in (0, 1):
                    b0 = m * 64
                    op("pool", lambda e: e.tensor_scalar(out=Wh[:, 3 + s_, :, b0:b0 + 8], in0=Wh[:, s_, :, b0 + 8:b0 + 16],
                                                         scalar1=-1.0, scalar2=None, op0=ALU.mult), reads=(Wh,), writes=(Wh,))
                    op("pool", lambda e: e.tensor_copy(out=Wh[:, 3 + s_, :, b0 + 8:b0 + 16], in_=Wh[:, s_, :, b0:b0 + 8]),
                       reads=(Wh,), writes=(Wh,))
            for tb in range(NB):
                for s_, dst in ((0, qT), (1, kT)):
                    db = s_
                    for j, slot in enumerate((s_, 3 + s_)):
                        for kc in range(8):
                            op("pe", lambda e: e.matmul(DBf[db][:, j * 512:j * 512 + BW], lhsT=Wh[:, slot, kc, :], rhs=xblk(kc, tb),
                                                        start=(kc == 0), stop=(kc == 7)), reads=(Wh, xnT), writes=bk(db, False, j))
                    op("dve", lambda e: e.tensor_tensor(out=t1[:], in0=DBf[db][:, 0:BW], in1=cosT[:, tb * BW:(tb + 1) * BW], op=ALU.mult),
                       reads=bk(db, False, 0) + (cosT,), writes=(t1,))
                    op("dve", lambda e: e.tensor_tensor(out=t2[:], in0=DBf[db][:, 512:512 + BW], in1=sinT[:, tb * BW:(tb + 1) * BW], op=ALU.mult),
                       reads=bk(db, False, 1) + (sinT,), writes=(t2,))
                    op("pool", lambda e: e.tensor_tensor(out=dst[:, tb * BW:(tb + 1) * BW], in0=t1[:], in1=t2[:], op=ALU.add),
                       reads=(t1, t2), writes=(dst,))
                for t4 in range(nq):
                    tt = tb * nq + t4
                    for kc in range(8):
                        op("pe", lambda e: e.matmul(DBf[2][:, t4 * 128:(t4 + 1) * 128], lhsT=xnT[:, kc, 1 + tt * 128:1 + (tt + 1) * 128],
                                                    rhs=Wh[:, 2, kc, :], start=(kc == 0), stop=(kc == 7)),
                           reads=(Wh, xnT), writes=bk(2, False, 0))
                op("act", lambda e: e.activation(out=Vh[:, tb * nq:(tb + 1) * nq, 0:128],
                                                 in_=DBf[2][:, 0:BW].rearrange("p (a b) -> p a b", b=128), func=AF.Copy),
                   reads=bk(2, False, 0), writes=(Vh,))
            for qc in range(NB):
                def scores(kb):
                    sb_ = kb % 2
                    for m in (0, 1):
                        op("pe", lambda e: e.matmul(DBf[sb_][:, m * 512:m * 512 + BW], lhsT=kT[m * 64:(m + 1) * 64, kb * 128:(kb + 1) * 128],
                                                    rhs=qT[m * 64:(m + 1) * 64, qc * BW:(qc + 1) * BW], start=True, stop=True),
                           reads=(kT, qT), writes=bk(sb_, False, m))
                scores(0)
                for kb in range(NT):
                    sb_ = kb % 2
                    if kb + 1 < NT:
                        scores(kb + 1)
                    op("act", lambda e: e.activation(out=E[sb_][:], in_=DBf[sb_][:, :].rearrange("p (m q) -> p m q", m=2)[:, :, 0:BW],
                                                     func=AF.Exp, scale=0.125), reads=bk(sb_), writes=(E[sb_],))
                    for qs in range(nq):
                        dba, hf = 2 + qs // 2, qs % 2
                        for m in (0, 1):
                            off = hf * 512 + m * 129
                            op("pe", lambda e: e.matmul(DBf[dba][:, off:off + 129], lhsT=E[sb_][:, m, qs * 128:(qs + 1) * 128],
                                                        rhs=Vh[:, kb, :], start=(kb == 0 and m == 0), stop=(kb == NT - 1),
                                                        skip_group_check=True),
                               reads=(E[sb_], Vh), writes=bk(dba, False, hf))
                accs = []
                for qs in range(nq):
                    dba, hf = 2 + qs // 2, qs % 2
                    accs.append((DBf[dba][:, hf * 512:hf * 512 + 258].rearrange("p (m c) -> p m c", m=2), bk(dba, False, hf)))
                for qs in range(nq):
                    acc, pbk = accs[qs]
                    op("dve", lambda e: e.reciprocal(out=rsq[qs][:], in_=acc[:, :, 128]), reads=pbk, writes=(rsq[qs],))
                for qs in range(nq):
                    op("dve", lambda e: e.tensor_tensor(out=rsq[qs][:, 1:2], in0=rsq[qs][:, 1:2], in1=C.nlam[:, 0:1], op=ALU.mult),
                       reads=(rsq[qs], C.nlam), writes=(rsq[qs],))
                for qs in range(nq):
                    acc, pbk = accs[qs]
                    op("dve", lambda e: e.tensor_scalar(out=o1q[qs][:], in0=acc[:, 0, 0:128], scalar1=rsq[qs][:, 0:1], scalar2=None, op0=ALU.mult),
                       reads=pbk + (rsq[qs],), writes=(o1q[qs],))
                for qs in range(nq):
                    acc, pbk = accs[qs]
                    op("dve", lambda e: e.scalar_tensor_tensor(out=o1q[qs][:], in0=acc[:, 1, 0:128], scalar=rsq[qs][:, 1:2], in1=o1q[qs][:],
                                                               op0=ALU.mult, op1=ALU.add), reads=pbk + (rsq[qs], o1q[qs]), writes=(o1q[qs],))
                for qs in range(nq):
                    op("act", lambda e: e.activation(out=junk[:], in_=o1q[qs][:], func=AF.Square, accum_out=ssqq[qs][:]),
                       reads=(o1q[qs],), writes=(junk, ssqq[qs]))
                for qs in range(nq):
                    op("act", lambda e: e.activation(out=ssqq[qs][:], in_=ssqq[qs][:], func=AF.Ln, scale=1.0 / 128, bias=C.eps5[:, 0:1]),
                       reads=(ssqq[qs], C.eps5), writes=(ssqq[qs],))
                for qs in range(nq):
                    op("act", lambda e: e.activation(out=ssqq[qs][:], in_=ssqq[qs][:], func=AF.Exp, scale=-0.5), reads=(ssqq[qs],), writes=(ssqq[qs],))
                for qs in range(nq):
                    op("dve", lambda e: e.scalar_tensor_tensor(out=onq[qs][:], in0=o1q[qs][:], scalar=ssqq[qs][:, 0:1], in1=C.sgb[:],
                                                               op0=ALU.mult, op1=ALU.mult), reads=(o1q[qs], ssqq[qs], C.sgb), writes=(onq[qs],))
                for qs in range(nq):
                    op("pe", lambda e: e.transpose(out=DBh[0][:, qs * 128:(qs + 1) * 128], in_=onq[qs][:], identity=ident[:]),
                       reads=(onq[qs], ident), writes=bk(0, False, 0))
                og = ogT[qc % 2]
                op("act", lambda e: e.activation(out=og[:], in_=DBh[0][:, 0:BW], func=AF.Copy), reads=bk(0, False, 0), writes=(og,))
                fw.dma("sp", C.scrA[:, h, qc * BW:(qc + 1) * BW], og[:], _stsem(fw, og), reads=(og,), writes=(C.dA,))
    with contextlib.ExitStack() as st:
        Wg = load_w(C, st, "Wg@%d" % si, C.wview(O_GA, 1024), [128, 8, 1024])
        Woa = load_w(C, st, "Woa@%d" % si, C.P["w_o_attn"].rearrange("(c p) n -> p c n", p=128), [128, 8, 1024])
        Wgm = load_w(C, st, "Wgm@%d" % si, C.wview(O_GMA, 1024), [128, 8, 1024])
        branch_tail(C, st, "a", Wg, Woa, Wgm, C.scrA, C.dA, C.scrM, C.dM, AF.Silu)


def branch_tail(C, st, tag, Wg, Wo, Wgm, src, dsrc, dst, ddst, gate_func):
    fw, op, S, BW, NB, si = C.fw, C.fw.op, C.S, C.BW, C.NB, C.si
    DBf, bk, xnT, xblk = C.DBf, C.bk, C.xnT, C.xblk
    og = [fw.sb("og%s%d@%d" % (tag, i, si), [128, 8, BW], BF16, st) for i in range(2)]
    OG = fw.sb("OG%s@%d" % (tag, si), [128, 8, BW], BF16, st)
    sg = [fw.sb("sg%s%d@%d" % (tag, i, si), [128, BW], F32, st) for i in range(2)]
    mab = [fw.sb("mab%s%d@%d" % (tag, i, si), [128, 8, BW], BF16, st) for i in range(2)]
    for tb in range(NB):
        b = tb % 2
        fw.dma("sp", og[b][:], src[:, :, tb * BW:(tb + 1) * BW], _ldsem(fw, og[b]), reads=(dsrc,), writes=(og[b],))
        if Wg is not None:
            for dc in range(8):
                db = dc % 2
                for kc in range(8):
                    op("pe", lambda e: e.matmul(DBf[db][:, 0:BW], lhsT=Wg[:, kc, dc * 128:(dc + 1) * 128], rhs=xblk(kc, tb),
                                                start=(kc == 0), stop=(kc == 7)), reads=(Wg, xnT), writes=bk(db, False, 0))
                op("act", lambda e: e.activation(out=sg[db][:], in_=DBf[db][:, 0:BW], func=gate_func), reads=bk(db, False, 0), writes=(sg[db],))
                op("dve", lambda e: e.tensor_tensor(out=OG[:, dc, :], in0=og[b][:, dc, :], in1=sg[db][:], op=ALU.mult),
                   reads=(og[b], sg[db]), writes=(OG,))
            G = OG
        else:
            G = og[b]
        for dc in range(8):
            db = 2 + dc % 2
            for hh in range(8):
                op("pe", lambda e: e.matmul(DBf[db][:, 0:BW], lhsT=Wo[:, hh, dc * 128:(dc + 1) * 128], rhs=G[:, hh, :],
                                            start=(hh == 0), stop=(hh == 7)), reads=(Wo, G), writes=bk(db, False, 0))
            for kc in range(8):
                op("pe", lambda e: e.matmul(DBf[db][:, 512:512 + BW], lhsT=Wgm[:, kc, dc * 128:(dc + 1) * 128], rhs=xblk(kc, tb),
                                            start=(kc == 0), stop=(kc == 7)), reads=(Wgm, xnT), writes=bk(db, False, 1))
            sgi = dc % 2
            op("act", lambda e: e.activation(out=sg[sgi][:], in_=DBf[db][:, 512:512 + BW], func=AF.Sigmoid),
               reads=bk(db, False, 1), writes=(sg[sgi],))
            op("dve", lambda e: e.tensor_tensor(out=mab[b][:, dc, :], in0=DBf[db][:, 0:BW], in1=sg[sgi][:], op=ALU.mult),
               reads=bk(db, False, 0) + (sg[sgi],), writes=(mab[b],))
        fw.dma("sp", dst[:, :, tb * BW:(tb + 1) * BW], mab[b][:], _stsem(fw, mab[b]), reads=(mab[b],), writes=(ddst,))


def stage_rwkv(C):
    fw, op, S, BW, NB, NT, si = C.fw, C.fw.op, C.S, C.BW, C.NB, C.NT, C.si
    DBf, DBh, bk, xnT, xblk, PB = C.DBf, C.DBh, C.bk, C.xnT, C.xblk, C.PB
    ident, BOh, BOf, I2, onesS = C.ident, C.BOh, C.BOf, C.I2, C.onesS
    wlw, wla = C.wlw, C.wla
    GN_EPS = 64e-5
    CD = math.exp(-0.5)
    import os
    G = int(os.environ.get('RW_G', 4 if S <= 2048 else 3))
    G = min(G, NT)

    def pbank(j):
        return DBf[j // 2][:, (j % 2) * 512:(j % 2) * 512 + 512]

    def pbankh(j):
        return DBh[j // 2][:, (j % 2) * 1024:(j % 2) * 1024 + 1024]

    with contextlib.ExitStack() as st:
        TWT = fw.sb("TWT@%d" % si, [128, S], BF16, st)
        ALT = fw.sb("ALT@%d" % si, [128, S], BF16, st)
        with contextlib.ExitStack() as st0:
            Wlow = load_w(C, st0, "Wlow@%d" % si, C.wview(O_WL, 256), [128, 8, 256])
            for tb in range(NB):
                for j, (dst, func) in enumerate(((TWT, AF.Tanh), (ALT, AF.Copy))):
                    for kc in range(8):
                        op("pe", lambda e: e.matmul(pbank(j)[:, 0:BW], lhsT=Wlow[:, kc, j * 128:(j + 1) * 128], rhs=xblk(kc, tb),
                                                    start=(kc == 0), stop=(kc == 7)), reads=(Wlow, xnT), writes=(PB[j],))
                    op("act", lambda e: e.activation(out=dst[:, tb * BW:(tb + 1) * BW], in_=pbank(j)[:, 0:BW], func=func),
                       reads=(PB[j],), writes=(dst,))
        Whp = fw.sb("Whp@%d" % si, [128, 4, 8, 128], BF16, st)
        RT_ = fw.sb("RT_@%d" % si, [128, S], BF16, st)
        KT_ = fw.sb("KT_@%d" % si, [128, S], BF16, st)
        VT_ = fw.sb("VT_@%d" % si, [128, S], BF16, st)
        KKT = fw.sb("KKT@%d" % si, [128, S], BF16, st)
        OFT = fw.sb("OFT@%d" % si, [128, S], BF16, st)
        OBT = fw.sb("OBT@%d" % si, [128, S], BF16, st)
        CB = min(256, S)

        for hp in range(8):
            hpc = slice(hp, hp + 1)
            for s_, off in enumerate((O_R, O_RK, O_RV, O_GR)):
                fw.load("pool", Whp, Whp[:, s_, :, :], C.wview(off + hp * 128, 128))
            if True:
                if hp == 0:
                    ctmp = fw.sb("ctmp@%d" % si, [128, CB], F32, st)
                    kkr = fw.sb("kkrw@%d" % si, [128, CB], F32, st)
                    nrm = fw.sb("nrmw@%d" % si, [128, CB], F32, st)
                    sqw = fw.sb("sqw@%d" % si, [128, CB], BF16, st)
                for cb in range(S // CB):
                    cs = slice(cb * CB, (cb + 1) * CB)
                    for s_, dst in enumerate((RT_, KT_, VT_)):
                        j = (3 * cb + s_) % 4
                        for kc in range(8):
                            op("pe", lambda e: e.matmul(pbank(j)[:, 0:CB + 2], lhsT=Whp[:, s_, kc, :], rhs=xnT[:, kc, cb * CB:cb * CB + CB + 2],
                                                        start=(kc == 0), stop=(kc == 7)), reads=(Whp, xnT), writes=(PB[j],))
                        cw = [C.cv_c[:, i, s_ * 8 + hp:s_ * 8 + hp + 1] for i in range(3)]
                        op("dve", lambda e: e.tensor_scalar(out=ctmp[:], in0=pbank(j)[:, 0:CB], scalar1=cw[0], scalar2=None, op0=ALU.mult),
                           reads=(PB[j], C.cv_c), writes=(ctmp,))
                        op("dve", lambda e: e.scalar_tensor_tensor(out=ctmp[:], in0=pbank(j)[:, 1:CB + 1], scalar=cw[1], in1=ctmp[:],
                                                                   op0=ALU.mult, op1=ALU.add), reads=(PB[j], C.cv_c, ctmp), writes=(ctmp,))
                        op("dve", lambda e: e.scalar_tensor_tensor(out=dst[:, cs], in0=pbank(j)[:, 2:CB + 2], scalar=cw[2],
                                                                   in1=ctmp[:], op0=ALU.mult, op1=ALU.add),
                           reads=(PB[j], C.cv_c, ctmp), writes=(dst,))
                    j = 4 + cb % 2
                    if os.environ.get('RW_STOP') == 'c1nokk':
                        continue
                    op("dve", lambda e: e.tensor_scalar(out=kkr[:], in0=KT_[:, cs], scalar1=C.kk_c[:, hpc], scalar2=None, op0=ALU.mult),
                       reads=(KT_, C.kk_c), writes=(kkr,))
                    op("pool", lambda e: e.tensor_tensor(out=sqw[:], in0=kkr[:], in1=kkr[:], op=ALU.mult), reads=(kkr,), writes=(sqw,))
                    op("pe", lambda e: e.matmul(pbank(j)[:, 0:CB], lhsT=BOh[:], rhs=sqw[:], start=True, stop=True), reads=(BOh, sqw), writes=(PB[j],))
                    op("dve", lambda e: e.tensor_scalar(out=nrm[:], in0=pbank(j)[:, 0:CB], scalar1=1e-12, scalar2=None, op0=ALU.add),
                       reads=(PB[j],), writes=(nrm,))
                    op("act", lambda e: e.activation(out=nrm[:], in_=nrm[:], func=AF.Sqrt), reads=(nrm,), writes=(nrm,))
                    op("dve", lambda e: e.reciprocal(out=nrm[:], in_=nrm[:]), reads=(nrm,), writes=(nrm,))
                    op("dve", lambda e: e.tensor_tensor(out=KKT[:, cs], in0=kkr[:], in1=nrm[:], op=ALU.mult), reads=(kkr, nrm), writes=(KKT,))
            if True:
                def f32t(n, g):
                    return fw.sb("%s%d@%d" % (n, g, si), [128, 128], F32, st)

                def bft(n, g, shape):
                    return fw.sb("%s%d@%d" % (n, g, si), shape, BF16, st)
                if hp == 0:
                    W = []
                for g in (range(G) if hp == 0 else ()):
                    w = SimpleNamespace()
                    for n in ("sgw", "a_", "cum", "c2", "cex", "e_in", "e_ex", "e_ng", "kdir", "tmpb"):
                        setattr(w, n, f32t(n, g))
                    w.AR = bft("AR", g, [128, 2, 128])
                    w.BK = bft("BK", g, [128, 2, 128])
                    w.TM = bft("TM", g, [128, 4, 128])
                    w.AT = bft("AT", g, [128, 2, 4, 128])
                    w.XY0 = bft("XY0", g, [128, 2, 2, 128])
                    w.XY = [bft("XYa", g, [128, 2, 3, 128]), bft("XYb", g, [128, 2, 3, 128])]
                    w.Yf = bft("Yf", g, [128, 2, 128])
                    w.RP = bft("RP", g, [128, 128])
                    w.MpT = bft("MpT", g, [128, 64])
                    W.append(w)
                if hp == 0:
                    Hs = [fw.sb("H%d@%d" % (i, si), [128, 64], BF16, st) for i in range(2)]

                def head(d, tau, g):
                    w = W[g]
                    bA, bB = 2 * g, 2 * g + 1
                    PA, PBb = PB[bA], PB[bB]
                    dsl = slice(d * 64, (d + 1) * 64)
                    sl = slice(tau * 128, (tau + 1) * 128)
                    rT, kTt, vT, kkT = RT_[:, sl], KT_[:, sl], VT_[:, sl], KKT[:, sl]
                    op("pe", lambda e: e.matmul(pbank(bA)[:, 0:128], lhsT=wlw[dsl, hp * 128:(hp + 1) * 128], rhs=TWT[dsl, sl], start=True, stop=True),
                       reads=(wlw, TWT), writes=(PA,))
                    op("pe", lambda e: e.matmul(pbank(bA)[:, 128:256], lhsT=wla[dsl, hp * 128:(hp + 1) * 128], rhs=ALT[dsl, sl], start=True, stop=True),
                       reads=(wla, ALT), writes=(PA,))
                    yield
                    op("act", lambda e: e.activation(out=w.sgw[:], in_=pbank(bA)[:, 0:128], func=AF.Tanh, bias=C.hw0_c[:, d, hpc], scale=0.5),
                       reads=(PA, C.hw0_c), writes=(w.sgw,))
                    op("act", lambda e: e.activation(out=w.a_[:], in_=pbank(bA)[:, 128:256], func=AF.Tanh, bias=C.ha0_c[:, d, hpc], scale=0.5),
                       reads=(PA, C.ha0_c), writes=(w.a_,))
                    yield
                    op("pool", lambda e: e.tensor_scalar(out=w.kdir[:], in0=w.a_[:], scalar1=C.ka_c[:, hpc], scalar2=C.tmka[:, hpc], op0=ALU.mult, op1=ALU.add),
                       reads=(w.a_, C.ka_c, C.tmka), writes=(w.kdir,))
                    yield
                    op("pool", lambda e: e.tensor_tensor(out=w.kdir[:], in0=kTt, in1=w.kdir[:], op=ALU.mult), reads=(KT_, w.kdir), writes=(w.kdir,))
                    yield
                    op("dve", lambda e: e.tensor_tensor_scan(out=w.cum[:], data0=w.sgw[:], data1=onesS[:], initial=0.0, op0=ALU.add, op1=ALU.add),
                       reads=(onesS, w.sgw), writes=(w.cum,))
                    yield
                    cu = w.cum
                    if d == 1:
                        op("dve", lambda e: e.tensor_scalar(out=w.c2[:], in0=w.cum[:], scalar1=-1.0, scalar2=w.cum[:, 127:128], op0=ALU.mult, op1=ALU.add),
                           reads=(w.cum,), writes=(w.c2,))
                        yield
                        op("dve", lambda e: e.scalar_tensor_tensor(out=w.c2[:], in0=w.c2[:], scalar=1.0, in1=w.sgw[:], op0=ALU.add, op1=ALU.add),
                           reads=(w.c2, w.sgw), writes=(w.c2,))
                        yield
                        cu = w.c2
                    op("dve", lambda e: e.scalar_tensor_tensor(out=w.cex[:], in0=cu[:], scalar=-1.0, in1=w.sgw[:], op0=ALU.add, op1=ALU.subtract),
                       reads=(cu, w.sgw), writes=(w.cex,))
                    yield
                    HC = 0.5 * CD
                    op("act", lambda e: e.activation(out=w.e_in[:], in_=cu[:], func=AF.Exp, scale=-HC), reads=(cu,), writes=(w.e_in,))
                    op("act", lambda e: e.activation(out=w.e_ex[:], in_=w.cex[:], func=AF.Exp, scale=-HC), reads=(w.cex,), writes=(w.e_ex,))
                    op("act", lambda e: e.activation(out=w.e_ng[:], in_=cu[:], func=AF.Exp, scale=HC, bias=C.lnhalf[:, 0:1]), reads=(cu, C.lnhalf), writes=(w.e_ng,))
                    yield
                    op("dve", lambda e: e.scalar_tensor_tensor(out=w.AR[:, 0, :], in0=kkT, scalar=-1.0, in1=w.e_ex[:], op0=ALU.mult, op1=ALU.mult),
                       reads=(KKT, w.e_ex), writes=(w.AR,))
                    yield
                    op("pool", lambda e: e.tensor_tensor(out=w.AR[:, 1, :], in0=rT, in1=w.e_in[:], op=ALU.mult), reads=(RT_, w.e_in, w.AR), writes=(w.AR,))
                    yield
                    op("dve", lambda e: e.scalar_tensor_tensor(out=w.tmpb[:], in0=w.a_[:], scalar=1.0, in1=kkT, op0=ALU.add, op1=ALU.mult),
                       reads=(KKT, w.a_), writes=(w.tmpb,))
                    yield
                    op("pool", lambda e: e.tensor_tensor(out=w.BK[:, 0, :], in0=w.tmpb[:], in1=w.e_ng[:], op=ALU.mult), reads=(w.tmpb, w.e_ng), writes=(w.BK,))
                    op("pool", lambda e: e.tensor_tensor(out=w.BK[:, 1, :], in0=w.kdir[:], in1=w.e_ng[:], op=ALU.mult), reads=(w.kdir, w.e_ng, w.BK), writes=(w.BK,))
                    yield
                    for j, (srcb, srcap) in enumerate(((w.AR, w.AR[:, 0, :]), (w.BK, w.BK[:, 0, :]), (w.BK, w.BK[:, 1, :]), (VT_, vT))):
                        op("pe", lambda e: e.transpose(out=pbankh(bB)[:, j * 128:(j + 1) * 128], in_=srcap, identity=ident[:]),
                           reads=(srcb, ident), writes=(PBb,))
                    yield
                    op("act", lambda e: e.activation(out=w.TM[:], in_=pbankh(bB)[:, 0:512].rearrange("p (a b) -> p a b", a=4), func=AF.Copy),
                       reads=(PBb,), writes=(w.TM,))
                    yield
                    for hh in (0, 1):
                        s = slice(hh * 64, (hh + 1) * 64)
                        op("pe", lambda e: e.matmul(pbank(bA + hh)[:, 0:128], lhsT=w.AR[s, 0, :], rhs=w.BK[s, 0, :], start=True, stop=True),
                           reads=(w.BK, w.AR), writes=(PB[bA + hh],))
                    yield
                    op("dve", lambda e: e.tensor_tensor(out=w.XY0[:, :, 0, :], in0=DBf[g][:, :].rearrange("p (h c) -> p h c", h=2)[:, :, 0:128],
                                                        in1=C.MASKL[d][:], op=ALU.mult), reads=(PA, PBb, C.MASKL[d]), writes=(w.XY0,))
                    yield
                    for hh in (0, 1):
                        s = slice(hh * 64, (hh + 1) * 64)
                        bj = bA + hh
                        arf = w.AR[s, :, :].rearrange("p a b -> p (a b)")
                        op("pe", lambda e: e.matmul(pbank(bj)[:, 0:256], lhsT=w.BK[s, 0, :], rhs=arf, start=True, stop=True),
                           reads=(w.BK, w.AR), writes=(PB[bj],))
                        op("pe", lambda e: e.matmul(pbank(bj)[:, 256:512], lhsT=w.BK[s, 1, :], rhs=arf, start=True, stop=True),
                           reads=(w.BK, w.AR), writes=(PB[bj],))
                    yield
                    op("dve", lambda e: e.tensor_tensor(out=w.AT[:], in0=DBf[g][:, :].rearrange("p (h a b) -> p h a b", h=2, a=4),
                                                        in1=C.MASK8[d][:], op=ALU.mult), reads=(PA, PBb, C.MASK8[d]), writes=(w.AT,))
                    yield
                    for hh in (0, 1):
                        s = slice(hh * 64, (hh + 1) * 64)
                        op("pe", lambda e: e.matmul(pbank(bA)[:, hh * 64:(hh + 1) * 64], lhsT=w.AT[:, hh, 2, :], rhs=w.TM[:, 3, s], start=True, stop=True),
                           reads=(w.AT, w.TM), writes=(PA,))
                    yield
                    op("act", lambda e: e.activation(out=w.XY0[:, :, 1, 64:128], in_=pbank(bA)[:, 0:128].rearrange("p (a b) -> p a b", a=2), func=AF.Copy),
                       reads=(PA, w.XY0), writes=(w.XY0,))
                    op("pool", lambda e: e.tensor_copy(out=w.XY0[:, :, 1, 0:64], in_=w.TM[:, 0, :].rearrange("p (a b) -> p a b", a=2)),
                       reads=(w.TM, w.XY0), writes=(w.XY0,))
                    yield
                    XN = [w.XY0[:, hh, 0, :] for hh in (0, 1)]
                    YK = [w.XY0[:, hh, 1, :] for hh in (0, 1)]
                    XNY = [w.XY0[:, hh, :, :].rearrange("p a b -> p (a b)") for hh in (0, 1)]
                    XT = [w.AT[:, hh, 0, :] for hh in (0, 1)]
                    srcb = (w.XY0, w.AT)
                    for k in range(7):
                        last = (k == 6)
                        for hh in (0, 1):
                            bj = bA + hh
                            if not last:
                                op("pe", lambda e: e.matmul(pbank(bj)[:, 256:384], lhsT=XN[hh], rhs=XT[hh], start=True, stop=True),
                                   reads=srcb, writes=(PB[bj],))
                                op("pe", lambda e: e.matmul(pbank(bj)[:, 0:256], lhsT=XT[hh], rhs=XNY[hh], start=True, stop=False),
                                   reads=srcb, writes=(PB[bj],))
                            else:
                                op("pe", lambda e: e.matmul(pbank(bj)[:, 128:256], lhsT=XT[hh], rhs=YK[hh], start=True, stop=False),
                                   reads=srcb, writes=(PB[bj],))
                            op("pe", lambda e: e.matmul(pbank(bj)[:, 128:256], lhsT=ident[:], rhs=YK[hh], start=False, stop=True),
                               reads=srcb + (ident,), writes=(PB[bj],))
                        yield
                        eng = "act" if k % 2 == 0 else "dve"
                        if not last:
                            nxt = w.XY[k % 2]
                            src = DBf[g][:, :].rearrange("p (h c) -> p h c", h=2)[:, :, 0:384].rearrange("p h (a b) -> p h a b", a=3)
                            if eng == "act":
                                op("act", lambda e: e.activation(out=nxt[:], in_=src, func=AF.Copy), reads=(PA, PBb), writes=(nxt,))
                            else:
                                op("dve", lambda e: e.tensor_copy(out=nxt[:], in_=src), reads=(PA, PBb), writes=(nxt,))
                            XN = [nxt[:, hh, 0, :] for hh in (0, 1)]
                            YK = [nxt[:, hh, 1, :] for hh in (0, 1)]
                            XNY = [nxt[:, hh, 0:2, :].rearrange("p a b -> p (a b)") for hh in (0, 1)]
                            XT = [nxt[:, hh, 2, :] for hh in (0, 1)]
                            srcb = (nxt,)
                        else:
                            src = DBf[g][:, :].rearrange("p (h c) -> p h c", h=2)[:, :, 128:256]
                            op("act", lambda e: e.activation(out=w.Yf[:], in_=src, func=AF.Copy), reads=(PA, PBb), writes=(w.Yf,))
                        yield
                    Yf = w.Yf
                    for hh in (0, 1):
                        s = slice(hh * 64, (hh + 1) * 64)
                        op("pe", lambda e: e.matmul(pbank(bA)[s, 384:512], lhsT=Yf[:, hh, 0:64], rhs=w.AT[:, hh, 1, :], start=True, stop=True),
                           reads=(Yf, w.AT), writes=(PA,))
                        op("pe", lambda e: e.matmul(pbank(bB)[s, 384:448], lhsT=Yf[:, hh, 0:64], rhs=w.TM[:, 1, s], start=True, stop=True),
                           reads=(Yf, w.TM), writes=(PBb,))
                    yield
                    op("dve", lambda e: e.tensor_tensor(out=w.RP[:], in0=pbank(bA)[:, 384:512], in1=w.AR[:, 1, :], op=ALU.add),
                       reads=(PA, w.AR), writes=(w.RP,))
                    op("dve", lambda e: e.tensor_tensor(out=w.MpT[:], in0=pbank(bB)[:, 384:448], in1=I2[:], op=ALU.add), reads=(PBb, I2), writes=(w.MpT,))
                    yield

                def tail(d, tau, g, Hc, Hn):
                    w = W[g]
                    bA, bB = 2 * g, 2 * g + 1
                    PA, PBb = PB[bA], PB[bB]
                    Yf = w.Yf
                    sl = slice(tau * 128, (tau + 1) * 128)
                    for hh in (0, 1):
                        s = slice(hh * 64, (hh + 1) * 64)
                        op("pe", lambda e: e.matmul(pbank(bA)[s, 0:128], lhsT=Yf[:, hh, 64:128], rhs=w.AT[:, hh, 1, :], start=True, stop=False),
                           reads=(Yf, w.AT), writes=(PA,))
                        op("pe", lambda e: e.matmul(pbank(bA)[s, 0:128], lhsT=w.TM[:, 3, s], rhs=w.AT[:, hh, 3, :], start=False, stop=False),
                           reads=(w.TM, w.AT), writes=(PA,))
                        op("pe", lambda e: e.matmul(pbank(bA)[s, 0:128], lhsT=Hc[s, :], rhs=w.RP[s, :], start=False, stop=True),
                           reads=(Hc, w.RP), writes=(PA,))
                    for hh in (0, 1):
                        s = slice(hh * 64, (hh + 1) * 64)
                        op("pe", lambda e: e.matmul(pbank(bB)[s, 0:64], lhsT=w.TM[:, 1, s], rhs=Yf[:, hh, 64:128], start=True, stop=False),
                           reads=(Yf, w.TM), writes=(PBb,))
                        op("pe", lambda e: e.matmul(pbank(bB)[s, 0:64], lhsT=w.TM[:, 2, s], rhs=w.TM[:, 3, s], start=False, stop=False),
                           reads=(w.TM,), writes=(PBb,))
                        op("pe", lambda e: e.matmul(pbank(bB)[s, 0:64], lhsT=w.MpT[s, :], rhs=Hc[s, :], start=False, stop=True),
                           reads=(w.MpT, Hc), writes=(PBb,))
                    WC = w.e_in[:, 127:128] if d == 0 else w.e_in[:, 0:1]
                    op("act", lambda e: e.activation(out=Hn[:], in_=pbank(bB)[:, 0:64], func=AF.Copy, scale=WC), reads=(PBb, w.e_in), writes=(Hn,))
                    dst = OFT if d == 0 else OBT
                    op("dve", lambda e: e.tensor_copy(out=dst[:, sl], in_=pbank(bA)[:, 0:128]), reads=(PA,), writes=(dst,))

                for d in (((0, 1) if 'RW_HEAD' not in os.environ else (0,)) if os.environ.get('RW_STOP') not in ('c1', 'c1nokk') else ()):
                    op("pool", lambda e: e.memset(Hs[0][:], 0.0), writes=(Hs[0],))
                    order = list(range(NT)) if d == 0 else list(range(NT - 1, -1, -1))
                    step = 0
                    DELTA = int(os.environ.get('RW_DELTA', 5))
                    slots = [None] * G
                    state = ['idle'] * G
                    completed = {}
                    next_pos = 0
                    tails_done = 0
                    rnd = 0
                    while tails_done < NT:
                        for gi in range(G):
                            if state[gi] == 'idle':
                                if next_pos < NT and rnd >= gi * DELTA:
                                    slots[gi] = (next_pos, head(d, order[next_pos], gi))
                                    state[gi] = 'run'
                                    next_pos += 1
                                else:
                                    continue
                            if state[gi] == 'run':
                                pos, gen = slots[gi]
                                try:
                                    next(gen)
                                except StopIteration:
                                    completed[pos] = gi
                                    state[gi] = 'wait'
                        while tails_done in completed:
                            gi = completed.pop(tails_done)
                            tail(d, order[tails_done], gi, Hs[step % 2], Hs[(step + 1) % 2])
                            step += 1
                            tails_done += 1
                            state[gi] = 'idle'
                        rnd += 1
            if True:
                PW = min(512, S) if S <= 2048 else 256
                if hp == 0:
                    if PW == CB:
                        o_, cen, sq2 = ctmp, kkr, nrm
                        var, rk_ = [fw.sb("%s@%d" % (n, si), [128, PW], F32, st) for n in ("pvar", "prk")]
                    else:
                        o_, cen, sq2, var, rk_ = [fw.sb("%s@%d" % (n, si), [128, PW], F32, st) for n in ("po_", "pcen", "psq2", "pvar", "prk")]
                for pb_ in range(S // PW):
                    ps = slice(pb_ * PW, (pb_ + 1) * PW)
                    op("dve", lambda e: e.tensor_tensor(out=o_[:], in0=OFT[:, ps], in1=OBT[:, ps], op=ALU.add), reads=(OFT, OBT), writes=(o_,))
                    op("pe", lambda e: e.matmul(pbank(0)[:, 0:PW], lhsT=BOf[:], rhs=o_[:], start=True, stop=True), reads=(BOf, o_), writes=(PB[0],))
                    op("dve", lambda e: e.scalar_tensor_tensor(out=cen[:], in0=pbank(0)[:, 0:PW], scalar=-1.0 / 64, in1=o_[:], op0=ALU.mult, op1=ALU.add),
                       reads=(PB[0], o_), writes=(cen,))
                    op("pool", lambda e: e.tensor_tensor(out=sq2[:], in0=cen[:], in1=cen[:], op=ALU.mult), reads=(cen,), writes=(sq2,))
                    op("pe", lambda e: e.matmul(pbank(1)[:, 0:PW], lhsT=BOf[:], rhs=sq2[:], start=True, stop=True), reads=(BOf, sq2), writes=(PB[1],))
                    op("dve", lambda e: e.tensor_scalar(out=var[:], in0=pbank(1)[:, 0:PW], scalar1=1.0 / 64, scalar2=GN_EPS, op0=ALU.mult, op1=ALU.add),
                       reads=(PB[1],), writes=(var,))
                    op("act", lambda e: e.activation(out=var[:], in_=var[:], func=AF.Sqrt), reads=(var,), writes=(var,))
                    op("dve", lambda e: e.reciprocal(out=var[:], in_=var[:]), reads=(var,), writes=(var,))
                    op("dve", lambda e: e.tensor_tensor(out=cen[:], in0=cen[:], in1=var[:], op=ALU.mult), reads=(cen, var), writes=(cen,))
                    op("dve", lambda e: e.tensor_scalar(out=cen[:], in0=cen[:], scalar1=C.lg_c[:, hpc], scalar2=C.lb_c[:, hpc], op0=ALU.mult, op1=ALU.add),
                       reads=(cen, C.lg_c, C.lb_c), writes=(cen,))
                    op("dve", lambda e: e.scalar_tensor_tensor(out=rk_[:], in0=RT_[:, ps], scalar=C.rk_c[:, hpc], in1=KT_[:, ps], op0=ALU.mult, op1=ALU.mult),
                       reads=(RT_, KT_, C.rk_c), writes=(rk_,))
                    op("pe", lambda e: e.matmul(pbank(2)[:, 0:PW], lhsT=BOf[:], rhs=rk_[:], start=True, stop=True), reads=(BOf, rk_), writes=(PB[2],))
                    op("dve", lambda e: e.tensor_tensor(out=sq2[:], in0=pbank(2)[:, 0:PW], in1=VT_[:, ps], op=ALU.mult), reads=(PB[2], VT_), writes=(sq2,))
                    op("pool", lambda e: e.tensor_tensor(out=cen[:], in0=cen[:], in1=sq2[:], op=ALU.add), reads=(cen, sq2), writes=(cen,))
                    for kc in range(8):
                        op("pe", lambda e: e.matmul(pbank(3)[:, 0:PW], lhsT=Whp[:, 3, kc, :], rhs=xnT[:, kc, 1 + pb_ * PW:1 + (pb_ + 1) * PW],
                                                    start=(kc == 0), stop=(kc == 7)), reads=(Whp, xnT), writes=(PB[3],))
                    op("act", lambda e: e.activation(out=var[:], in_=pbank(3)[:, 0:PW], func=AF.Silu), reads=(PB[3],), writes=(var,))
                    op("dve", lambda e: e.tensor_tensor(out=OFT[:, ps], in0=cen[:], in1=var[:], op=ALU.mult), reads=(cen, var, OFT), writes=(OFT,))
            fw.dma("sp", C.scrR[:, hp, 0:S], OFT[:], _stsem(fw, OFT), reads=(OFT,), writes=(C.dR,))
    with contextlib.ExitStack() as st:
        Wor = load_w(C, st, "Wor@%d" % si, C.P["w_o_rwkv"].rearrange("(c p) n -> p c n", p=128), [128, 8, 1024])
        Wgmb = load_w(C, st, "Wgmb@%d" % si, C.wview(O_GMB, 1024), [128, 8, 1024])
        branch_tail(C, st, "r", None, Wor, Wgmb, C.scrR, C.dR, C.scrR, C.dR, None)


def make_rope(C, st0):
    fw, op, S, si = C.fw, C.fw.op, C.S, C.si
    cosT = fw.sb("cosT@%d" % si, [128, S], BF16, st0)
    sinT = fw.sb("sinT@%d" % si, [128, S], BF16, st0)
    with contextlib.ExitStack() as st:
        pi_ = fw.sb("pi_@%d" % si, [128, 1], I32, st)
        pj_ = fw.sb("pj_@%d" % si, [128, 1], I32, st)
        pf_ = fw.sb("pf_@%d" % si, [128, 2], F32, st)
        invf = fw.sb("invf@%d" % si, [128, 1], F32, st)
        posi = fw.sb("posi@%d" % si, [128, S], I32, st)
        ang = fw.sb("ang@%d" % si, [128, S], F32, st)
        kf = fw.sb("kf@%d" % si, [128, S], F32, st)
        tm = fw.sb("tm@%d" % si, [128, S], F32, st)
        op("pool", lambda e: e.iota(pi_[:], pattern=[[0, 1]], base=0, channel_multiplier=1), writes=(pi_,))
        op("dve", lambda e: e.tensor_scalar(out=pj_[:], in0=pi_[:], scalar1=7, scalar2=None, op0=ALU.bitwise_and),
           reads=(pi_,), writes=(pj_,))
        op("dve", lambda e: e.tensor_copy(out=pf_[:, 0:1], in_=pj_[:]), reads=(pj_,), writes=(pf_,))
        op("dve", lambda e: e.tensor_scalar(out=pj_[:], in0=pi_[:], scalar1=63, scalar2=None, op0=ALU.bitwise_and),
           reads=(pi_,), writes=(pj_,))
        op("dve", lambda e: e.tensor_copy(out=pf_[:, 1:2], in_=pj_[:]), reads=(pj_,), writes=(pf_,))
        op("act", lambda e: e.activation(out=invf[:], in_=pf_[:, 0:1], func=AF.Exp, scale=-math.log(ROPE_THETA) / 8.0),
           reads=(pf_,), writes=(invf,))
        op("dve", lambda e: e.tensor_scalar(out=pf_[:, 1:2], in0=pf_[:, 1:2], scalar1=16.0, scalar2=None, op0=ALU.is_lt),
           reads=(pf_,), writes=(pf_,))
        op("dve", lambda e: e.tensor_tensor(out=invf[:], in0=invf[:], in1=pf_[:, 1:2], op=ALU.mult),
           reads=(invf, pf_), writes=(invf,))
        op("pool", lambda e: e.iota(posi[:], pattern=[[1, S]], base=0, channel_multiplier=0), writes=(posi,))
        op("dve", lambda e: e.tensor_copy(out=ang[:], in_=posi[:]), reads=(posi,), writes=(ang,))
        op("dve", lambda e: e.tensor_scalar(out=ang[:], in0=ang[:], scalar1=invf[:, 0:1], scalar2=None, op0=ALU.mult),
           reads=(ang, invf), writes=(ang,))

        def wrap_sin(dst, shift):
            op("dve", lambda e: e.tensor_scalar(out=tm[:], in0=ang[:], scalar1=shift, scalar2=None, op0=ALU.add),
               reads=(ang,), writes=(tm,))
            op("dve", lambda e: e.tensor_scalar(out=kf[:], in0=tm[:], scalar1=1.0 / TWO_PI, scalar2=None, op0=ALU.mult),
               reads=(tm,), writes=(kf,))
            op("dve", lambda e: e.tensor_copy(out=posi[:], in_=kf[:]), reads=(kf,), writes=(posi,))
            op("dve", lambda e: e.tensor_copy(out=kf[:], in_=posi[:]), reads=(posi,), writes=(kf,))
            op("dve", lambda e: e.scalar_tensor_tensor(out=tm[:], in0=kf[:], scalar=-CW1, in1=tm[:], op0=ALU.mult, op1=ALU.add),
               reads=(kf, tm), writes=(tm,))
            op("dve", lambda e: e.scalar_tensor_tensor(out=tm[:], in0=kf[:], scalar=-CW2, in1=tm[:], op0=ALU.mult, op1=ALU.add),
               reads=(kf, tm), writes=(tm,))
            op("dve", lambda e: e.tensor_scalar(out=kf[:], in0=tm[:], scalar1=math.pi, scalar2=-TWO_PI, op0=ALU.is_gt, op1=ALU.mult),
               reads=(tm,), writes=(kf,))
            op("dve", lambda e: e.tensor_tensor(out=tm[:], in0=tm[:], in1=kf[:], op=ALU.add), reads=(tm, kf), writes=(tm,))
            op("dve", lambda e: e.tensor_scalar(out=kf[:], in0=tm[:], scalar1=-math.pi, scalar2=TWO_PI, op0=ALU.is_lt, op1=ALU.mult),
               reads=(tm,), writes=(kf,))
            op("dve", lambda e: e.tensor_tensor(out=tm[:], in0=tm[:], in1=kf[:], op=ALU.add), reads=(tm, kf), writes=(tm,))
            op("dve", lambda e: e.tensor_scalar(out=tm[:], in0=tm[:], scalar1=3.14159, scalar2=-3.14159, op0=ALU.min, op1=ALU.max),
               reads=(tm,), writes=(tm,))
            op("act", lambda e: e.activation(out=dst[:], in_=tm[:], func=AF.Sin), reads=(tm,), writes=(dst,))
        wrap_sin(sinT, 0.0)
        wrap_sin(cosT, math.pi / 2)

    return cosT, sinT


SEQ_LENS = [2048, 2048, 2048, 2048, 4096]
_NC_CACHE = {}


def kernel(**inputs):
    xp = np.asarray(inputs["x_prompt"], dtype=np.float32)
    xs = np.asarray(inputs["x_sample"], dtype=np.float32)
    n = 8
    if "nc" not in _NC_CACHE:
        _NC_CACHE["nc"] = build(SEQ_LENS)
    nc = _NC_CACHE["nc"]
    shared = {}
    for nme, shp in PARAMS:
        shared[nme] = np.ascontiguousarray(np.asarray(inputs[nme], dtype=np.float32).reshape(shp))
    in_maps = []
    for c in range(n):
        xc = np.concatenate([xp[4 * c:4 * c + 4].reshape(-1, D), xs[c].reshape(-1, D)], axis=0)
        m = {"x": np.ascontiguousarray(xc)}
        m.update(shared)
        in_maps.append(m)
    res = run_bass_kernel_spmd(nc, in_maps, core_ids=list(range(n)))
    yp = np.empty_like(xp)
    ys = np.empty_like(xs)
    for c in range(n):
        yc = np.asarray(res.results[c]["y"])
        yp[4 * c:4 * c + 4] = yc[0:8192].reshape(4, 2048, D)
        ys[c] = yc[8192:12288].reshape(4096, D)
    return (yp, ys)
```

```python
import contextlib
import re
import numpy as np
import concourse.bass as bass
import concourse.mybir as mybir

F32 = mybir.dt.float32
BF16 = mybir.dt.bfloat16
I32 = mybir.dt.int32
AF = mybir.ActivationFunctionType
ALU = mybir.AluOpType
AX = mybir.AxisListType

SEM_ROT = 30000


class Buf:
    __slots__ = ("t", "name", "last_w", "readers", "ld_sem", "st_sem")

    def __init__(self, t, name):
        self.t = t
        self.name = name
        self.last_w = None
        self.readers = {}
        self.ld_sem = None
        self.st_sem = None

    def __getitem__(self, k):
        return self.t[k]


class FW:
    def __init__(self, nc):
        self.nc = nc
        self.stack = contextlib.ExitStack()
        self.eng = {"pe": nc.tensor, "act": nc.scalar, "dve": nc.vector,
                    "pool": nc.gpsimd, "sp": nc.sync}
        self.sems = {}
        self.cur = {}
        self.cnt = {}
        self.waited = {e: {} for e in self.eng}
        self.dma_total = {}
        self.dma_roles = {}
        self.nsem = 0
        self.n_ops = 0
        self.n_waits = 0
        for e in self.eng:
            self._new_eng_sem(e)

    def _alloc_sem(self, name):
        h = self.stack.enter_context(self.nc.semaphore(name))
        key = name
        self.sems[key] = h
        self.nsem += 1
        return key

    def _new_eng_sem(self, e):
        key = self._alloc_sem("s_%s_%d" % (e, self.nsem))
        self.cur[e] = key
        self.cnt[key] = 0

    def dma_sem(self, name):
        role = re.sub(r"@\d+", "", name)
        if role in self.dma_roles:
            return self.dma_roles[role]
        key = self._alloc_sem("d_%s_%d" % (role, self.nsem))
        self.dma_total[key] = 0
        self.dma_roles[role] = key
        return key

    def snapshot(self):
        snap = {}
        for k in self.sems:
            v = self.dma_total[k] if k in self.dma_total else self.cnt.get(k, 0)
            if v > 0:
                snap[k] = v
        return snap

    def sb(self, name, shape, dtype, stack=None):
        self.n_alloc = getattr(self, "n_alloc", 0) + 1
        t = (stack or self.stack).enter_context(self.nc.sbuf_tensor("%s_u%d" % (name.replace("@", "_"), self.n_alloc), list(shape), dtype))
        b = Buf(t, name)
        b.readers = self.snapshot()
        return b

    def ps(self, name, shape, dtype, stack=None):
        t = (stack or self.stack).enter_context(self.nc.psum_tensor(name, list(shape), dtype))
        return Buf(t, name)

    def view(self, buf_or_ap, name):
        return Buf(buf_or_ap, name)

    def _collect(self, e, reads, writes):
        deps = {}

        def add(tok):
            if tok is None:
                return
            k, v = tok
            if k in self.dma_total:
                v = self.dma_total[k]
            if deps.get(k, 0) < v:
                deps[k] = v

        for b in reads:
            add(b.last_w)
        for b in writes:
            add(b.last_w)
            for k, v in b.readers.items():
                add((k, v))
        return deps

    def _emit_waits(self, e, deps):
        eng = self.eng[e]
        w = self.waited[e]
        for k, v in deps.items():
            if e == "pe" and k.startswith("s_pe_"):
                continue
            if w.get(k, 0) >= v:
                continue
            eng.wait_ge(self.sems[k], v)
            w[k] = v
            self.n_waits += 1

    def _mark(self, tok, reads, writes):
        k, v = tok
        for b in writes:
            b.last_w = tok
            b.readers = {}
        for b in reads:
            if b.readers.get(k, 0) < v:
                b.readers[k] = v

    def op(self, e, fn, reads=(), writes=()):
        deps = self._collect(e, reads, writes)
        self._emit_waits(e, deps)
        ins = fn(self.eng[e])
        key = self.cur[e]
        if self.cnt[key] >= SEM_ROT:
            self._new_eng_sem(e)
            key = self.cur[e]
        self.cnt[key] += 1
        ins.then_inc(self.sems[key], 1)
        self._mark((key, self.cnt[key]), reads, writes)
        self.n_ops += 1
        return ins

    def dma(self, q, out, in_, sem, reads=(), writes=(), **kw):
        deps = self._collect(q, reads, writes)
        self._emit_waits(q, deps)
        ins = self.eng[q].dma_start(out=out, in_=in_, **kw)
        self.dma_total[sem] += 16
        ins.then_inc(self.sems[sem], 16)
        self._mark((sem, self.dma_total[sem]), reads, writes)
        self.n_ops += 1
        return ins

    def load(self, q, buf, out_ap, in_ap, **kw):
        if buf.ld_sem is None:
            buf.ld_sem = self.dma_sem("l" + buf.name)
        return self.dma(q, out_ap, in_ap, buf.ld_sem, reads=(), writes=(buf,), **kw)

    def store(self, q, buf, out_ap, in_ap, **kw):
        if buf.st_sem is None:
            buf.st_sem = self.dma_sem("s" + buf.name)
        return self.dma(q, out_ap, in_ap, buf.st_sem, reads=(buf,), writes=(), **kw)

    def final_wait(self, e="sp"):
        eng = self.eng[e]
        for k, h in self.sems.items():
            v = self.dma_total[k] if k in self.dma_total else self.cnt.get(k, 0)
            if v > 0 and self.waited[e].get(k, 0) < v:
                eng.wait_ge(h, v)
                self.waited[e][k] = v


import math
from types import SimpleNamespace
import contextlib
import numpy as np
import concourse.bass as bass
import concourse.mybir as mybir
from concourse.bass_utils import run_bass_kernel_spmd

D = 1024
PIN = 10496
O_Q, O_K, O_V, O_GA, O_R, O_RK, O_RV, O_GR, O_WL, O_AL, O_GMA, O_GMB = (
    0, 1024, 2048, 3072, 4096, 5120, 6144, 7168, 8192, 8320, 8448, 9472)
ROPE_THETA = 500000.0
TWO_PI = 2.0 * math.pi
CW1 = 6.28125
CW2 = TWO_PI - CW1

PARAMS = [("norm_g", [1, 1024]), ("w_in", [1024, PIN]), ("conv_rkv", [3, 3072]),
          ("lam_q1", [1, 64]), ("lam_k1", [1, 64]), ("lam_q2", [1, 64]), ("lam_k2", [1, 64]),
          ("attn_subln_g", [1, 128]), ("w_lora_up", [128, 1024]), ("w0", [2, 1024]),
          ("a_lora_up", [128, 1024]), ("a0", [2, 1024]), ("k_k", [1, 1024]), ("k_a", [1, 1024]),
          ("r_k", [1, 1024]), ("ln_x_g", [1, 1024]), ("ln_x_b", [1, 1024]),
          ("w_o_attn", [1024, 1024]), ("w_o_rwkv", [1024, 1024]), ("w_out", [1024, 1024]),
          ("final_g", [1, 1024])]


def build(seq_lens, do_attn=True, do_rwkv=True):
    nc = bass.Bass("TRN2", target_bir_lowering=False)
    TOT = sum(seq_lens)
    SMAX = max(seq_lens)
    x = nc.dram_tensor("x", [TOT, D], F32, kind="ExternalInput").ap()
    y = nc.dram_tensor("y", [TOT, D], F32, kind="ExternalOutput").ap()
    P = {n: nc.dram_tensor(n, s, F32, kind="ExternalInput").ap() for n, s in PARAMS}
    w_in = P["w_in"]
    scrA = nc.dram_tensor("scrA", [128, 8, SMAX], BF16, kind="Internal").ap()
    scrM = nc.dram_tensor("scrM", [128, 8, SMAX], BF16, kind="Internal").ap()
    scrR = nc.dram_tensor("scrR", [128, 8, SMAX], BF16, kind="Internal").ap()
    fw = FW(nc)
    op = fw.op
    ncd = nc.allow_non_contiguous_dma(reason="small param layouts")
    ncd.__enter__()

    def wview(off, ncols):
        return w_in[:, off:off + ncols].rearrange("(c p) n -> p c n", p=128)

    with fw.stack:
        dA, dM, dR = fw.view(scrA, "scrA"), fw.view(scrM, "scrM"), fw.view(scrR, "scrR")
        DBt = [fw.stack.enter_context(nc.psum_tensor("db%d" % i, [128, 1024], F32)) for i in range(4)]
        DBf = [t[:] for t in DBt]
        DBh = [t[:].bitcast(BF16) for t in DBt]
        PB = [Buf(None, "bank%d" % j) for j in range(8)]

        def bk(i, both=True, half=0):
            return (PB[2 * i], PB[2 * i + 1]) if both else (PB[2 * i + half],)

        psem = fw.dma_sem("params")
        psem2 = fw.dma_sem("paramsq")

        def pload(name, shape, src, q="sp", dtype=F32):
            b = fw.sb(name, shape, dtype)
            b.ld_sem = psem if q == "sp" else psem2
            fw.load(q, b, b[:], src)
            return b

        def colparam(name, nm):
            return pload(name, [128, 8], P[nm].rearrange("o (c p) -> p (o c)", p=128))

        gcol = colparam("gcol", "norm_g")
        kk_c = colparam("kk_c", "k_k")
        ka_c = colparam("ka_c", "k_a")
        rk_c = colparam("rk_c", "r_k")
        lg_c = colparam("lg_c", "ln_x_g")
        lb_c = colparam("lb_c", "ln_x_b")
        w0_c = pload("w0_c", [128, 2, 8], P["w0"].rearrange("d (c p) -> p d c", p=128))
        a0_c = pload("a0_c", [128, 2, 8], P["a0"].rearrange("d (c p) -> p d c", p=128))
        cv_c = pload("cv_c", [128, 3, 24], P["conv_rkv"].rearrange("i (c p) -> p i c", p=128))
        fgb = pload("fgb", [128, 1024], P["final_g"][0, :].partition_broadcast(128))
        sgb = pload("sgb", [128, 128], P["attn_subln_g"][0, :].partition_broadcast(128))
        lq = [pload("lam%d" % i, [128, 64], P[n][0, :].partition_broadcast(128))
              for i, n in enumerate(["lam_q1", "lam_k1", "lam_q2", "lam_k2"])]
        wlw = pload("wlw", [128, 1024], P["w_lora_up"], q="pool", dtype=BF16)
        wla = pload("wla", [128, 1024], P["a_lora_up"], q="pool", dtype=BF16)

        ident = fw.sb("ident", [128, 128], BF16)
        op("pool", lambda e: e.memset(ident[:], 1.0), writes=(ident,))
        op("pool", lambda e: e.affine_select(out=ident[:], in_=ident[:], pattern=[[-1, 128]], compare_op=ALU.is_equal,
                                             fill=0.0, base=0, channel_multiplier=1), reads=(ident,), writes=(ident,))
        grep = fw.sb("grep", [128, 8, 128], F32)
        op("dve", lambda e: e.memset(grep[:], 1.0), writes=(grep,))
        for c in range(8):
            op("dve", lambda e: e.tensor_scalar(out=grep[:, c, :], in0=grep[:, c, :], scalar1=gcol[:, c:c + 1],
                                                scalar2=None, op0=ALU.mult), reads=(grep, gcol), writes=(grep,))
        op("dve", lambda e: e.tensor_scalar(out=sgb[:], in0=sgb[:], scalar1=0.8, scalar2=None, op0=ALU.mult),
           reads=(sgb,), writes=(sgb,))
        hw0_c = fw.sb("hw0_c", [128, 2, 8], F32)
        ha0_c = fw.sb("ha0_c", [128, 2, 8], F32)
        op("dve", lambda e: e.tensor_scalar(out=hw0_c[:], in0=w0_c[:], scalar1=0.5, scalar2=None, op0=ALU.mult), reads=(w0_c,), writes=(hw0_c,))
        op("dve", lambda e: e.tensor_scalar(out=ha0_c[:], in0=a0_c[:], scalar1=0.5, scalar2=None, op0=ALU.mult), reads=(a0_c,), writes=(ha0_c,))
        tmka = fw.sb("tmka", [128, 8], F32)
        op("dve", lambda e: e.tensor_scalar(out=tmka[:], in0=ka_c[:], scalar1=-1.0, scalar2=2.0, op0=ALU.mult, op1=ALU.add),
           reads=(ka_c,), writes=(tmka,))
        eps5 = fw.sb("eps5", [128, 1], F32)
        op("dve", lambda e: e.memset(eps5[:], 1e-5), writes=(eps5,))
        lnhalf = fw.sb("lnhalf", [128, 1], F32)
        op("dve", lambda e: e.memset(lnhalf[:], math.log(0.5)), writes=(lnhalf,))
        omka = fw.sb("omka", [128, 8], F32)
        op("dve", lambda e: e.tensor_scalar(out=omka[:], in0=ka_c[:], scalar1=-1.0, scalar2=1.0, op0=ALU.mult, op1=ALU.add),
           reads=(ka_c,), writes=(omka,))
        lt = fw.sb("lt", [128, 64], F32)
        ls = fw.sb("ls", [128, 2], F32)
        nlam = fw.sb("nlam", [128, 1], F32)
        for i in range(2):
            op("dve", lambda e: e.tensor_tensor(out=lt[:], in0=lq[2 * i][:], in1=lq[2 * i + 1][:], op=ALU.mult),
               reads=(lq[2 * i], lq[2 * i + 1]), writes=(lt,))
            op("dve", lambda e: e.reduce_sum(out=ls[:, i:i + 1], in_=lt[:], axis=AX.X), reads=(lt,), writes=(ls,))
        op("act", lambda e: e.activation(out=ls[:], in_=ls[:], func=AF.Exp), reads=(ls,), writes=(ls,))
        op("dve", lambda e: e.tensor_tensor(out=nlam[:], in0=ls[:, 1:2], in1=ls[:, 0:1], op=ALU.subtract),
           reads=(ls,), writes=(nlam,))
        op("dve", lambda e: e.tensor_scalar(out=nlam[:], in0=nlam[:], scalar1=-0.2, scalar2=None, op0=ALU.add),
           reads=(nlam,), writes=(nlam,))

        def mkmask(name, mult_f, mult_p, base, cmp):
            m = fw.sb(name, [128, 128], BF16)
            op("pool", lambda e: e.memset(m[:], 1.0), writes=(m,))
            op("pool", lambda e: e.affine_select(out=m[:], in_=m[:], pattern=[[mult_f, 128]], compare_op=cmp,
                                                 fill=0.0, base=base, channel_multiplier=mult_p), reads=(m,), writes=(m,))
            return m
        Us = mkmask("Us", 1, -1, 0, ALU.is_gt)
        Ui = mkmask("Ui", 1, -1, 0, ALU.is_ge)
        Ls = mkmask("Ls", -1, 1, 0, ALU.is_gt)
        Li = mkmask("Li", -1, 1, 0, ALU.is_ge)
        MASK4 = []
        MASK8 = []
        MASKL = []
        for d_ in range(2):
            m4 = fw.sb("m4_%d" % d_, [128, 4, 128], BF16)
            s_, i_ = (Us, Ui) if d_ == 0 else (Ls, Li)
            for j, src in enumerate([s_, i_, s_, i_]):
                op("pool", lambda e: e.tensor_copy(out=m4[:, j, :], in_=src[:]), reads=(src,), writes=(m4,))
            MASK4.append(m4)
            m8 = fw.sb("m8_%d" % d_, [128, 2, 4, 128], BF16)
            for h_ in range(2):
                for j, src in enumerate([s_, i_, s_, i_]):
                    op("pool", lambda e: e.tensor_copy(out=m8[:, h_, j, :], in_=src[:]), reads=(src,), writes=(m8,))
            MASK8.append(m8)
            ml = fw.sb("ml_%d" % d_, [128, 2, 128], BF16)
            src = Ls if d_ == 0 else Us
            for j in range(2):
                op("pool", lambda e: e.tensor_copy(out=ml[:, j, :], in_=src[:]), reads=(src,), writes=(ml,))
            MASKL.append(ml)
        BOh = fw.sb("BOh", [128, 128], BF16)
        BOf = fw.sb("BOf", [128, 128], F32)
        for t_ in (BOh, BOf):
            op("pool", lambda e: e.memset(t_[:], 0.0), writes=(t_,))
            op("pool", lambda e: e.memset(t_[0:64, 0:64], 1.0), reads=(t_,), writes=(t_,))
            op("pool", lambda e: e.memset(t_[64:128, 64:128], 1.0), reads=(t_,), writes=(t_,))
        I2 = fw.sb("I2", [128, 64], F32)
        op("pool", lambda e: e.memset(I2[:], 1.0), writes=(I2,))
        op("pool", lambda e: e.affine_select(out=I2[0:64, :], in_=I2[0:64, :], pattern=[[-1, 64]], compare_op=ALU.is_equal,
                                             fill=0.0, base=0, channel_multiplier=1), reads=(I2,), writes=(I2,))
        op("pool", lambda e: e.affine_select(out=I2[64:128, :], in_=I2[64:128, :], pattern=[[-1, 64]], compare_op=ALU.is_equal,
                                             fill=0.0, base=0, channel_multiplier=1), reads=(I2,), writes=(I2,))
        onesS = fw.sb("onesS", [128, 128], F32)
        op("pool", lambda e: e.memset(onesS[:], 1.0), writes=(onesS,))

        def proj_fm(dst_fn, W, wslot_fn, xnT, S, evac):
            pass

        tok0 = 0
        for si, S in enumerate(seq_lens):
            NT = S // 128
            NB = S // 512 if S >= 512 else 1
            BW = min(512, S)
            with contextlib.ExitStack() as sq:
                xnT = fw.sb("xnT@%d" % si, [128, 8, S + 2], BF16, sq)
                op("pool", lambda e: e.memset(xnT[:, :, 0:1], 0.0), writes=(xnT,))
                op("pool", lambda e: e.memset(xnT[:, :, S + 1:S + 2], 0.0), writes=(xnT,))
                with contextlib.ExitStack() as sa:
                    xt = [fw.sb("xt%d@%d" % (i, si), [128, 1024], F32, sa) for i in range(2)]
                    junk = fw.sb("junkA@%d" % si, [128, 1024], F32, sa)
                    xs = [fw.sb("xs%d@%d" % (i, si), [128, 1024], BF16, sa) for i in range(2)]
                    ssA = [fw.sb("ssA%d@%d" % (i, si), [128, 1], F32, sa) for i in range(2)]
                    for tt in range(NT):
                        b = tt % 2
                        fw.load("sp", xt[b], xt[b][:], x[tok0 + tt * 128: tok0 + (tt + 1) * 128, :])
                        op("act", lambda e: e.activation(out=junk[:], in_=xt[b][:], func=AF.Square, accum_out=ssA[b][:]),
                           reads=(xt[b],), writes=(junk, ssA[b]))
                        op("dve", lambda e: e.tensor_scalar(out=ssA[b][:], in0=ssA[b][:], scalar1=1.0 / 1024, scalar2=1e-6,
                                                            op0=ALU.mult, op1=ALU.add), reads=(ssA[b],), writes=(ssA[b],))
                        op("act", lambda e: e.activation(out=ssA[b][:], in_=ssA[b][:], func=AF.Sqrt), reads=(ssA[b],), writes=(ssA[b],))
                        op("dve", lambda e: e.reciprocal(out=ssA[b][:], in_=ssA[b][:]), reads=(ssA[b],), writes=(ssA[b],))
                        op("act", lambda e: e.activation(out=xs[b][:], in_=xt[b][:], func=AF.Copy, scale=ssA[b][:, 0:1]),
                           reads=(xt[b], ssA[b]), writes=(xs[b],))
                        db = tt % 2
                        for c in range(8):
                            op("pe", lambda e: e.transpose(out=DBh[db][:, c * 128:(c + 1) * 128], in_=xs[b][:, c * 128:(c + 1) * 128],
                                                           identity=ident[:]), reads=(xs[b], ident), writes=bk(db, False, 0))
                        op("dve", lambda e: e.tensor_tensor(out=xnT[:, :, 1 + tt * 128: 1 + (tt + 1) * 128],
                                                            in0=DBh[db][:, 0:1024].rearrange("p (c t) -> p c t", c=8),
                                                            in1=grep[:], op=ALU.mult),
                           reads=bk(db, False, 0) + (grep,), writes=(xnT,))

                def xblk(kc, tb):
                    return xnT[:, kc, 1 + tb * BW: 1 + (tb + 1) * BW]

                C = SimpleNamespace(**locals())
                if do_attn:
                    stage_attn(C)
                if do_rwkv:
                    stage_rwkv(C)
                stage_out(C)
            tok0 += S
        fw.final_wait("sp")
    ncd.__exit__(None, None, None)
    return nc


def load_w(C, st, name, src_ap, shape):
    b = C.fw.sb(name, shape, BF16, st)
    C.fw.load("pool", b, b[:], src_ap)
    return b


def stage_out(C):
    fw, op, S, BW, NB, si = C.fw, C.fw.op, C.S, C.BW, C.NB, C.si
    DBf, bk = C.DBf, C.bk
    with contextlib.ExitStack() as st:
        Wout = load_w(C, st, "Wout@%d" % si, C.P["w_out"].rearrange("(c p) n -> p c n", p=128), [128, 8, 1024])
        ma = [fw.sb("ma%d@%d" % (i, si), [128, 8, BW], BF16, st) for i in range(2)]
        mr = [fw.sb("mr%d@%d" % (i, si), [128, 8, BW], BF16, st) for i in range(2)]
        xt = [fw.sb("xo%d@%d" % (i, si), [128, 1024], F32, st) for i in range(2)]
        zt = [fw.sb("zt%d@%d" % (i, si), [128, 1024], F32, st) for i in range(2)]
        junk = fw.sb("junkO@%d" % si, [128, 1024], F32, st)
        ss = [fw.sb("ssO%d@%d" % (i, si), [128, 1], F32, st) for i in range(2)]
        cnt = 0
        for tb in range(NB):
            b = tb % 2
            m = None
            if C.do_attn:
                fw.dma("sp", ma[b][:], C.scrM[:, :, tb * BW:(tb + 1) * BW], _ldsem(fw, ma[b]), reads=(C.dM,), writes=(ma[b],))
                m = ma[b]
            if C.do_rwkv:
                fw.dma("sp", mr[b][:], C.scrR[:, :, tb * BW:(tb + 1) * BW], _ldsem(fw, mr[b]), reads=(C.dR,), writes=(mr[b],))
                if m is None:
                    m = mr[b]
                else:
                    op("pool", lambda e: e.tensor_tensor(out=ma[b][:], in0=ma[b][:], in1=mr[b][:], op=ALU.add),
                       reads=(ma[b], mr[b]), writes=(ma[b],))
            for t4 in range(BW // 128):
                tt = tb * (BW // 128) + t4
                xb = cnt % 2
                db = 2 + cnt % 2
                cnt += 1
                fw.load("sp", xt[xb], xt[xb][:], C.x[C.tok0 + tt * 128: C.tok0 + (tt + 1) * 128, :])
                if m is not None:
                    for half in range(2):
                        for dc in range(8):
                            op("pe", lambda e: e.matmul(DBf[db][:, half * 512:(half + 1) * 512], lhsT=m[:, dc, t4 * 128:(t4 + 1) * 128],
                                                        rhs=Wout[:, dc, half * 512:(half + 1) * 512], start=(dc == 0), stop=(dc == 7)),
                               reads=(m, Wout), writes=bk(db, False, half))
                    op("dve", lambda e: e.tensor_tensor(out=zt[xb][:], in0=DBf[db][:], in1=xt[xb][:], op=ALU.add),
                       reads=bk(db) + (xt[xb],), writes=(zt[xb],))
                else:
                    op("dve", lambda e: e.tensor_copy(out=zt[xb][:], in_=xt[xb][:]), reads=(xt[xb],), writes=(zt[xb],))
                op("act", lambda e: e.activation(out=junk[:], in_=zt[xb][:], func=AF.Square, accum_out=ss[xb][:]),
                   reads=(zt[xb],), writes=(junk, ss[xb]))
                op("dve", lambda e: e.tensor_scalar(out=ss[xb][:], in0=ss[xb][:], scalar1=1.0 / 1024, scalar2=1e-6,
                                                    op0=ALU.mult, op1=ALU.add), reads=(ss[xb],), writes=(ss[xb],))
                op("act", lambda e: e.activation(out=ss[xb][:], in_=ss[xb][:], func=AF.Sqrt), reads=(ss[xb],), writes=(ss[xb],))
                op("dve", lambda e: e.reciprocal(out=ss[xb][:], in_=ss[xb][:]), reads=(ss[xb],), writes=(ss[xb],))
                op("dve", lambda e: e.scalar_tensor_tensor(out=zt[xb][:], in0=zt[xb][:], scalar=ss[xb][:, 0:1], in1=C.fgb[:],
                                                           op0=ALU.mult, op1=ALU.mult), reads=(zt[xb], ss[xb], C.fgb), writes=(zt[xb],))
                fw.store("sp", zt[xb], C.y[C.tok0 + tt * 128: C.tok0 + (tt + 1) * 128, :], zt[xb][:])


def _ldsem(fw, b):
    if b.ld_sem is None:
        b.ld_sem = fw.dma_sem("l" + b.name)
    return b.ld_sem


def _stsem(fw, b):
    if b.st_sem is None:
        b.st_sem = fw.dma_sem("s" + b.name)
    return b.st_sem


def stage_attn(C):
    fw, op, S, BW, NB, NT, si = C.fw, C.fw.op, C.S, C.BW, C.NB, C.NT, C.si
    DBf, DBh, bk, xnT, xblk = C.DBf, C.DBh, C.bk, C.xnT, C.xblk
    ident = C.ident
    nq = BW // 128
    with contextlib.ExitStack() as st:
        cosT, sinT = make_rope(C, st)
        Wh = fw.sb("Wh@%d" % si, [128, 5, 8, 128], BF16, st)
        op("pool", lambda e: e.memset(Wh[:, 3:5, :, :], 0.0), writes=(Wh,))
        qT = fw.sb("qT@%d" % si, [128, S], BF16, st)
        kT = fw.sb("kT@%d" % si, [128, S], BF16, st)
        Vh = fw.sb("Vh@%d" % si, [128, NT, 129], BF16, st)
        op("pool", lambda e: e.memset(Vh[:, :, 128:129], 1.0), writes=(Vh,))
        E = [fw.sb("E%d@%d" % (i, si), [128, 2, BW], BF16, st) for i in range(2)]
        t1s = [fw.sb("t1%d@%d" % (i, si), [128, BW], F32, st) for i in range(2)]
        t2s = [fw.sb("t2%d@%d" % (i, si), [128, BW], F32, st) for i in range(2)]
        rsq = [fw.sb("rs%d@%d" % (i, si), [128, 2], F32, st) for i in range(4)]
        o1q = [fw.sb("o1%d@%d" % (i, si), [128, 128], F32, st) for i in range(4)]
        ssqq = [fw.sb("ssq%d@%d" % (i, si), [128, 1], F32, st) for i in range(4)]
        onq = [fw.sb("on%d@%d" % (i, si), [128, 128], BF16, st) for i in range(4)]
        junk = fw.sb("junkB@%d" % si, [128, 128], F32, st)
        ssq = fw.sb("ssq@%d" % si, [128, 1], F32, st)
        on = fw.sb("on@%d" % si, [128, 128], BF16, st)
        ogT = [fw.sb("ogT%d@%d" % (i, si), [128, BW], BF16, st) for i in range(2)]
        for h in range(8):
            for s_, off in enumerate((O_Q, O_K, O_V)):
                fw.load("pool", Wh, Wh[:, s_, :, :], C.wview(off + h * 128, 128))
            for s_ in (0, 1):
                for m in (0, 1):
                    b0 = m * 64
                    op("pool", lambda e: e.tensor_scalar(out=Wh[:, 3 + s_, :, b0:b0 + 8], in0=Wh[:, s_, :, b0 + 8:b0 + 16],
                                                         scalar1=-1.0, scalar2=None, op0=ALU.mult), reads=(Wh,), writes=(Wh,))
                    op("pool", lambda e: e.tensor_copy(out=Wh[:, 3 + s_, :, b0 + 8:b0 + 16], in_=Wh[:, s_, :, b0:b0 + 8]),
                       reads=(Wh,), writes=(Wh,))
            for tb in range(NB):
                for s_, dst in ((0, qT), (1, kT)):
                    db = (2 * tb + s_) % 4
                    t1, t2 = t1s[s_], t2s[s_]
                    for j, slot in enumerate((s_, 3 + s_)):
                        for kc in range(8):
                            op("pe", lambda e: e.matmul(DBf[db][:, j * 512:j * 512 + BW], lhsT=Wh[:, slot, kc, :], rhs=xblk(kc, tb),
                                                        start=(kc == 0), stop=(kc == 7)), reads=(Wh, xnT), writes=bk(db, False, j))
                    op("dve", lambda e: e.tensor_tensor(out=t1[:], in0=DBf[db][:, 0:BW], in1=cosT[:, tb * BW:(tb + 1) * BW], op=ALU.mult),
                       reads=bk(db, False, 0) + (cosT,), writes=(t1,))
                    op("dve", lambda e: e.tensor_tensor(out=t2[:], in0=DBf[db][:, 512:512 + BW], in1=sinT[:, tb * BW:(tb + 1) * BW], op=ALU.mult),
                       reads=bk(db, False, 1) + (sinT,), writes=(t2,))
                    op("pool", lambda e: e.tensor_tensor(out=dst[:, tb * BW:(tb + 1) * BW], in0=t1[:], in1=t2[:], op=ALU.add),
                       reads=(t1, t2), writes=(dst,))
                vdb = (2 * tb + 2) % 4
                for t4 in range(nq):
                    tt = tb * nq + t4
                    for kc in range(8):
                        op("pe", lambda e: e.matmul(DBf[vdb][:, 512 + t4 * 128:512 + (t4 + 1) * 128], lhsT=xnT[:, kc, 1 + tt * 128:1 + (tt + 1) * 128],
                                                    rhs=Wh[:, 2, kc, :], start=(kc == 0), stop=(kc == 7)),
                           reads=(Wh, xnT), writes=bk(vdb, False, 1))
                op("act", lambda e: e.activation(out=Vh[:, tb * nq:(tb + 1) * nq, 0:128],
                                                 in_=DBf[vdb][:, 512:512 + BW].rearrange("p (a b) -> p a b", b=128), func=AF.Copy),
                   reads=bk(vdb, False, 1), writes=(Vh,))
            for qc in range(NB):
                def scores(kb):
                    sb_ = kb % 2
                    for m in (0, 1):
                        op("pe", lambda e: e.matmul(DBf[sb_][:, m * 512:m * 512 + BW], lhsT=kT[m * 64:(m + 1) * 64, kb * 128:(kb + 1) * 128],
                                                    rhs=qT[m * 64:(m + 1) * 64, qc * BW:(qc + 1) * BW], start=True, stop=True),
                           reads=(kT, qT), writes=bk(sb_, False, m))
                scores(0)
                for kb in range(NT):
                    sb_ = kb % 2
                    if kb + 1 < NT:
                        scores(kb + 1)
                    op("act", lambda e: e.activation(out=E[sb_][:], in_=DBf[sb_][:, :].rearrange("p (m q) -> p m q", m=2)[:, :, 0:BW],
                                                     func=AF.Exp, scale=0.125), reads=bk(sb_), writes=(E[sb_],))
                    for qs in range(nq):
                        dba, hf = 2 + qs // 2, qs % 2
                        for m in (0, 1):
                            off = hf * 512 + m * 129
                            op("pe", lambda e: e.matmul(DBf[dba][:, off:off + 129], lhsT=E[sb_][:, m, qs * 128:(qs + 1) * 128],
                                                        rhs=Vh[:, kb, :], start=(kb == 0 and m == 0), stop=(kb == NT - 1),
                                                        skip_group_check=True),
                               reads=(E[sb_], Vh), writes=bk(dba, False, hf))
                accs = []
                for qs in range(nq):
                    dba, hf = 2 + qs // 2, qs % 2
                    accs.append((DBf[dba][:, hf * 512:hf * 512 + 258].rearrange("p (m c) -> p m c", m=2), bk(dba, False, hf)))
                for qs in range(nq):
                    acc, pbk = accs[qs]
                    op("dve", lambda e: e.reciprocal(out=rsq[qs][:], in_=acc[:, :, 128]), reads=pbk, writes=(rsq[qs],))
                for qs in range(nq):
                    op("dve", lambda e: e.tensor_tensor(out=rsq[qs][:, 1:2], in0=rsq[qs][:, 1:2], in1=C.nlam[:, 0:1], op=ALU.mult),
                       reads=(rsq[qs], C.nlam), writes=(rsq[qs],))
                for qs in range(nq):
                    acc, pbk = accs[qs]
                    op("dve", lambda e: e.tensor_scalar(out=o1q[qs][:], in0=acc[:, 0, 0:128], scalar1=rsq[qs][:, 0:1], scalar2=None, op0=ALU.mult),
                       reads=pbk + (rsq[qs],), writes=(o1q[qs],))
                for qs in range(nq):
                    acc, pbk = accs[qs]
                    op("dve", lambda e: e.scalar_tensor_tensor(out=o1q[qs][:], in0=acc[:, 1, 0:128], scalar=rsq[qs][:, 1:2], in1=o1q[qs][:],
                                                               op0=ALU.mult, op1=ALU.add), reads=pbk + (rsq[qs], o1q[qs]), writes=(o1q[qs],))
                for qs in range(nq):
                    op("act", lambda e: e.activation(out=junk[:], in_=o1q[qs][:], func=AF.Square, accum_out=ssqq[qs][:]),
                       reads=(o1q[qs],), writes=(junk, ssqq[qs]))
                for qs in range(nq):
                    op("act", lambda e: e.activation(out=ssqq[qs][:], in_=ssqq[qs][:], func=AF.Ln, scale=1.0 / 128, bias=C.eps5[:, 0:1]),
                       reads=(ssqq[qs], C.eps5), writes=(ssqq[qs],))
                for qs in range(nq):
                    op("act", lambda e: e.activation(out=ssqq[qs][:], in_=ssqq[qs][:], func=AF.Exp, scale=-0.5), reads=(ssqq[qs],), writes=(ssqq[qs],))
                for qs in range(nq):
                    op("dve", lambda e: e.scalar_tensor_tensor(out=onq[qs][:], in0=o1q[qs][:], scalar=ssqq[qs][:, 0:1], in1=C.sgb[:],
                                                               op0=ALU.mult, op1=ALU.mult), reads=(o1q[qs], ssqq[qs], C.sgb), writes=(onq[qs],))
                for qs in range(nq):
                    op("pe", lambda e: e.transpose(out=DBh[0][:, qs * 128:(qs + 1) * 128], in_=onq[qs][:], identity=ident[:]),
                       reads=(onq[qs], ident), writes=bk(0, False, 0))
                og = ogT[qc % 2]
                op("act", lambda e: e.activation(out=og[:], in_=DBh[0][:, 0:BW], func=AF.Copy), reads=bk(0, False, 0), writes=(og,))
                fw.dma("sp", C.scrA[:, h, qc * BW:(qc + 1) * BW], og[:], _stsem(fw, og), reads=(og,), writes=(C.dA,))
    with contextlib.ExitStack() as st:
        Wg = load_w(C, st, "Wg@%d" % si, C.wview(O_GA, 1024), [128, 8, 1024])
        Woa = load_w(C, st, "Woa@%d" % si, C.P["w_o_attn"].rearrange("(c p) n -> p c n", p=128), [128, 8, 1024])
        Wgm = load_w(C, st, "Wgm@%d" % si, C.wview(O_GMA, 1024), [128, 8, 1024])
        branch_tail(C, st, "a", Wg, Woa, Wgm, C.scrA, C.dA, C.scrM, C.dM, AF.Silu)


def branch_tail(C, st, tag, Wg, Wo, Wgm, src, dsrc, dst, ddst, gate_func):
    fw, op, S, BW, NB, si = C.fw, C.fw.op, C.S, C.BW, C.NB, C.si
    DBf, bk, xnT, xblk = C.DBf, C.bk, C.xnT, C.xblk
    og = [fw.sb("og%s%d@%d" % (tag, i, si), [128, 8, BW], BF16, st) for i in range(2)]
    OG = fw.sb("OG%s@%d" % (tag, si), [128, 8, BW], BF16, st)
    sg = [fw.sb("sg%s%d@%d" % (tag, i, si), [128, BW], F32, st) for i in range(2)]
    mab = [fw.sb("mab%s%d@%d" % (tag, i, si), [128, 8, BW], BF16, st) for i in range(2)]
    for tb in range(NB):
        b = tb % 2
        fw.dma("sp", og[b][:], src[:, :, tb * BW:(tb + 1) * BW], _ldsem(fw, og[b]), reads=(dsrc,), writes=(og[b],))
        if Wg is not None:
            for dc in range(8):
                db = dc % 2
                for kc in range(8):
                    op("pe", lambda e: e.matmul(DBf[db][:, 0:BW], lhsT=Wg[:, kc, dc * 128:(dc + 1) * 128], rhs=xblk(kc, tb),
                                                start=(kc == 0), stop=(kc == 7)), reads=(Wg, xnT), writes=bk(db, False, 0))
                op("act", lambda e: e.activation(out=sg[db][:], in_=DBf[db][:, 0:BW], func=gate_func), reads=bk(db, False, 0), writes=(sg[db],))
                op("dve", lambda e: e.tensor_tensor(out=OG[:, dc, :], in0=og[b][:, dc, :], in1=sg[db][:], op=ALU.mult),
                   reads=(og[b], sg[db]), writes=(OG,))
            G = OG
        else:
            G = og[b]
        for dc in range(8):
            db = 2 + dc % 2
            for hh in range(8):
                op("pe", lambda e: e.matmul(DBf[db][:, 0:BW], lhsT=Wo[:, hh, dc * 128:(dc + 1) * 128], rhs=G[:, hh, :],
                                            start=(hh == 0), stop=(hh == 7)), reads=(Wo, G), writes=bk(db, False, 0))
            for kc in range(8):
                op("pe", lambda e: e.matmul(DBf[db][:, 512:512 + BW], lhsT=Wgm[:, kc, dc * 128:(dc + 1) * 128], rhs=xblk(kc, tb),
                                            start=(kc == 0), stop=(kc == 7)), reads=(Wgm, xnT), writes=bk(db, False, 1))
            sgi = dc % 2
            op("act", lambda e: e.activation(out=sg[sgi][:], in_=DBf[db][:, 512:512 + BW], func=AF.Sigmoid),
               reads=bk(db, False, 1), writes=(sg[sgi],))
            op("dve", lambda e: e.tensor_tensor(out=mab[b][:, dc, :], in0=DBf[db][:, 0:BW], in1=sg[sgi][:], op=ALU.mult),
               reads=bk(db, False, 0) + (sg[sgi],), writes=(mab[b],))
        fw.dma("sp", dst[:, :, tb * BW:(tb + 1) * BW], mab[b][:], _stsem(fw, mab[b]), reads=(mab[b],), writes=(ddst,))


def stage_rwkv(C):
    fw, op, S, BW, NB, NT, si = C.fw, C.fw.op, C.S, C.BW, C.NB, C.NT, C.si
    DBf, DBh, bk, xnT, xblk, PB = C.DBf, C.DBh, C.bk, C.xnT, C.xblk, C.PB
    ident, BOh, BOf, I2, onesS = C.ident, C.BOh, C.BOf, C.I2, C.onesS
    wlw, wla = C.wlw, C.wla
    GN_EPS = 64e-5
    CD = math.exp(-0.5)
    import os
    G = int(os.environ.get('RW_G', 4 if S <= 2048 else 3))
    G = min(G, NT)

    def pbank(j):
        return DBf[j // 2][:, (j % 2) * 512:(j % 2) * 512 + 512]

    def pbankh(j):
        return DBh[j // 2][:, (j % 2) * 1024:(j % 2) * 1024 + 1024]

    with contextlib.ExitStack() as st:
        TWT = fw.sb("TWT@%d" % si, [128, S], BF16, st)
        ALT = fw.sb("ALT@%d" % si, [128, S], BF16, st)
        with contextlib.ExitStack() as st0:
            Wlow = load_w(C, st0, "Wlow@%d" % si, C.wview(O_WL, 256), [128, 8, 256])
            for tb in range(NB):
                for j, (dst, func) in enumerate(((TWT, AF.Tanh), (ALT, AF.Copy))):
                    for kc in range(8):
                        op("pe", lambda e: e.matmul(pbank(j)[:, 0:BW], lhsT=Wlow[:, kc, j * 128:(j + 1) * 128], rhs=xblk(kc, tb),
                                                    start=(kc == 0), stop=(kc == 7)), reads=(Wlow, xnT), writes=(PB[j],))
                    op("act", lambda e: e.activation(out=dst[:, tb * BW:(tb + 1) * BW], in_=pbank(j)[:, 0:BW], func=func),
                       reads=(PB[j],), writes=(dst,))
        Whp = fw.sb("Whp@%d" % si, [128, 4, 8, 128], BF16, st)
        RT_ = fw.sb("RT_@%d" % si, [128, S], BF16, st)
        KT_ = fw.sb("KT_@%d" % si, [128, S], BF16, st)
        VT_ = fw.sb("VT_@%d" % si, [128, S], BF16, st)
        KKT = fw.sb("KKT@%d" % si, [128, S], BF16, st)
        OFT = fw.sb("OFT@%d" % si, [128, S], BF16, st)
        OBT = fw.sb("OBT@%d" % si, [128, S], BF16, st)
        CB = min(256, S)

        for hp in range(8):
            hpc = slice(hp, hp + 1)
            for s_, off in enumerate((O_R, O_RK, O_RV, O_GR)):
                fw.load("pool", Whp, Whp[:, s_, :, :], C.wview(off + hp * 128, 128))
            if True:
                if hp == 0:
                    ctmp = fw.sb("ctmp@%d" % si, [128, CB], F32, st)
                    kkr = fw.sb("kkrw@%d" % si, [128, CB], F32, st)
                    nrm = fw.sb("nrmw@%d" % si, [128, CB], F32, st)
                    sqw = fw.sb("sqw@%d" % si, [128, CB], BF16, st)
                for cb in range(S // CB):
                    cs = slice(cb * CB, (cb + 1) * CB)
                    for s_, dst in enumerate((RT_, KT_, VT_)):
                        j = (3 * cb + s_) % 4
                        for kc in range(8):
                            op("pe", lambda e: e.matmul(pbank(j)[:, 0:CB + 2], lhsT=Whp[:, s_, kc, :], rhs=xnT[:, kc, cb * CB:cb * CB + CB + 2],
                                                        start=(kc == 0), stop=(kc == 7)), reads=(Whp, xnT), writes=(PB[j],))
                        cw = [C.cv_c[:, i, s_ * 8 + hp:s_ * 8 + hp + 1] for i in range(3)]
                        op("dve", lambda e: e.tensor_scalar(out=ctmp[:], in0=pbank(j)[:, 0:CB], scalar1=cw[0], scalar2=None, op0=ALU.mult),
                           reads=(PB[j], C.cv_c), writes=(ctmp,))
                        op("dve", lambda e: e.scalar_tensor_tensor(out=ctmp[:], in0=pbank(j)[:, 1:CB + 1], scalar=cw[1], in1=ctmp[:],
                                                                   op0=ALU.mult, op1=ALU.add), reads=(PB[j], C.cv_c, ctmp), writes=(ctmp,))
                        op("dve", lambda e: e.scalar_tensor_tensor(out=dst[:, cs], in0=pbank(j)[:, 2:CB + 2], scalar=cw[2],
                                                                   in1=ctmp[:], op0=ALU.mult, op1=ALU.add),
                           reads=(PB[j], C.cv_c, ctmp), writes=(dst,))
                    j = 4 + cb % 2
                    if os.environ.get('RW_STOP') == 'c1nokk':
                        continue
                    op("dve", lambda e: e.tensor_scalar(out=kkr[:], in0=KT_[:, cs], scalar1=C.kk_c[:, hpc], scalar2=None, op0=ALU.mult),
                       reads=(KT_, C.kk_c), writes=(kkr,))
                    op("pool", lambda e: e.tensor_tensor(out=sqw[:], in0=kkr[:], in1=kkr[:], op=ALU.mult), reads=(kkr,), writes=(sqw,))
                    op("pe", lambda e: e.matmul(pbank(j)[:, 0:CB], lhsT=BOh[:], rhs=sqw[:], start=True, stop=True), reads=(BOh, sqw), writes=(PB[j],))
                    op("dve", lambda e: e.tensor_scalar(out=nrm[:], in0=pbank(j)[:, 0:CB], scalar1=1e-12, scalar2=None, op0=ALU.add),
                       reads=(PB[j],), writes=(nrm,))
                    op("act", lambda e: e.activation(out=nrm[:], in_=nrm[:], func=AF.Sqrt), reads=(nrm,), writes=(nrm,))
                    op("dve", lambda e: e.reciprocal(out=nrm[:], in_=nrm[:]), reads=(nrm,), writes=(nrm,))
                    op("dve", lambda e: e.tensor_tensor(out=KKT[:, cs], in0=kkr[:], in1=nrm[:], op=ALU.mult), reads=(kkr, nrm), writes=(KKT,))
            if True:
                def f32t(n, g):
                    return fw.sb("%s%d@%d" % (n, g, si), [128, 128], F32, st)

                def bft(n, g, shape):
                    return fw.sb("%s%d@%d" % (n, g, si), shape, BF16, st)
                if hp == 0:
                    W = []
                for g in (range(G) if hp == 0 else ()):
                    w = SimpleNamespace()
                    for n in ("sgw", "a_", "cum", "c2", "cex", "e_in", "e_ex", "e_ng", "kdir", "tmpb"):
                        setattr(w, n, f32t(n, g))
                    w.AR = bft("AR", g, [128, 2, 128])
                    w.BK = bft("BK", g, [128, 2, 128])
                    w.TM = bft("TM", g, [128, 4, 128])
                    w.AT = bft("AT", g, [128, 2, 4, 128])
                    w.XY0 = bft("XY0", g, [128, 2, 2, 128])
                    w.XY = [bft("XYa", g, [128, 2, 3, 128]), bft("XYb", g, [128, 2, 3, 128])]
                    w.Yf = bft("Yf", g, [128, 2, 128])
                    w.RP = bft("RP", g, [128, 128])
                    w.MpT = bft("MpT", g, [128, 64])
                    W.append(w)
                if hp == 0:
                    Hs = [fw.sb("H%d@%d" % (i, si), [128, 64], BF16, st) for i in range(2)]

                def head(d, tau, g):
                    w = W[g]
                    bA, bB = 2 * g, 2 * g + 1
                    PA, PBb = PB[bA], PB[bB]
                    dsl = slice(d * 64, (d + 1) * 64)
                    sl = slice(tau * 128, (tau + 1) * 128)
                    rT, kTt, vT, kkT = RT_[:, sl], KT_[:, sl], VT_[:, sl], KKT[:, sl]
                    op("pe", lambda e: e.matmul(pbank(bA)[:, 0:128], lhsT=wlw[dsl, hp * 128:(hp + 1) * 128], rhs=TWT[dsl, sl], start=True, stop=True),
                       reads=(wlw, TWT), writes=(PA,))
                    op("pe", lambda e: e.matmul(pbank(bA)[:, 128:256], lhsT=wla[dsl, hp * 128:(hp + 1) * 128], rhs=ALT[dsl, sl], start=True, stop=True),
                       reads=(wla, ALT), writes=(PA,))
                    yield
                    op("act", lambda e: e.activation(out=w.sgw[:], in_=pbank(bA)[:, 0:128], func=AF.Tanh, bias=C.hw0_c[:, d, hpc], scale=0.5),
                       reads=(PA, C.hw0_c), writes=(w.sgw,))
                    op("act", lambda e: e.activation(out=w.a_[:], in_=pbank(bA)[:, 128:256], func=AF.Tanh, bias=C.ha0_c[:, d, hpc], scale=0.5),
                       reads=(PA, C.ha0_c), writes=(w.a_,))
                    yield
                    op("pool", lambda e: e.tensor_scalar(out=w.kdir[:], in0=w.a_[:], scalar1=C.ka_c[:, hpc], scalar2=C.tmka[:, hpc], op0=ALU.mult, op1=ALU.add),
                       reads=(w.a_, C.ka_c, C.tmka), writes=(w.kdir,))
                    yield
                    op("pool", lambda e: e.tensor_tensor(out=w.kdir[:], in0=kTt, in1=w.kdir[:], op=ALU.mult), reads=(KT_, w.kdir), writes=(w.kdir,))
                    yield
                    op("dve", lambda e: e.tensor_tensor_scan(out=w.cum[:], data0=w.sgw[:], data1=onesS[:], initial=0.0, op0=ALU.add, op1=ALU.add),
                       reads=(onesS, w.sgw), writes=(w.cum,))
                    yield
                    cu = w.cum
                    if d == 1:
                        op("dve", lambda e: e.tensor_scalar(out=w.c2[:], in0=w.cum[:], scalar1=-1.0, scalar2=w.cum[:, 127:128], op0=ALU.mult, op1=ALU.add),
                           reads=(w.cum,), writes=(w.c2,))
                        yield
                        op("dve", lambda e: e.scalar_tensor_tensor(out=w.c2[:], in0=w.c2[:], scalar=1.0, in1=w.sgw[:], op0=ALU.add, op1=ALU.add),
                           reads=(w.c2, w.sgw), writes=(w.c2,))
                        yield
                        cu = w.c2
                    op("dve", lambda e: e.scalar_tensor_tensor(out=w.cex[:], in0=cu[:], scalar=-1.0, in1=w.sgw[:], op0=ALU.add, op1=ALU.subtract),
                       reads=(cu, w.sgw), writes=(w.cex,))
                    yield
                    HC = 0.5 * CD
                    op("act", lambda e: e.activation(out=w.e_in[:], in_=cu[:], func=AF.Exp, scale=-HC), reads=(cu,), writes=(w.e_in,))
                    op("act", lambda e: e.activation(out=w.e_ex[:], in_=w.cex[:], func=AF.Exp, scale=-HC), reads=(w.cex,), writes=(w.e_ex,))
                    op("act", lambda e: e.activation(out=w.e_ng[:], in_=cu[:], func=AF.Exp, scale=HC, bias=C.lnhalf[:, 0:1]), reads=(cu, C.lnhalf), writes=(w.e_ng,))
                    yield
                    op("dve", lambda e: e.scalar_tensor_tensor(out=w.AR[:, 0, :], in0=kkT, scalar=-1.0, in1=w.e_ex[:], op0=ALU.mult, op1=ALU.mult),
                       reads=(KKT, w.e_ex), writes=(w.AR,))
                    yield
                    op("pool", lambda e: e.tensor_tensor(out=w.AR[:, 1, :], in0=rT, in1=w.e_in[:], op=ALU.mult), reads=(RT_, w.e_in, w.AR), writes=(w.AR,))
                    yield
                    op("dve", lambda e: e.scalar_tensor_tensor(out=w.tmpb[:], in0=w.a_[:], scalar=1.0, in1=kkT, op0=ALU.add, op1=ALU.mult),
                       reads=(KKT, w.a_), writes=(w.tmpb,))
                    yield
                    op("pool", lambda e: e.tensor_tensor(out=w.BK[:, 0, :], in0=w.tmpb[:], in1=w.e_ng[:], op=ALU.mult), reads=(w.tmpb, w.e_ng), writes=(w.BK,))
                    op("pool", lambda e: e.tensor_tensor(out=w.BK[:, 1, :], in0=w.kdir[:], in1=w.e_ng[:], op=ALU.mult), reads=(w.kdir, w.e_ng, w.BK), writes=(w.BK,))
                    yield
                    for j, (srcb, srcap) in enumerate(((w.AR, w.AR[:, 0, :]), (w.BK, w.BK[:, 0, :]), (w.BK, w.BK[:, 1, :]), (VT_, vT))):
                        op("pe", lambda e: e.transpose(out=pbankh(bB)[:, j * 128:(j + 1) * 128], in_=srcap, identity=ident[:]),
                           reads=(srcb, ident), writes=(PBb,))
                    yield
                    op("act", lambda e: e.activation(out=w.TM[:], in_=pbankh(bB)[:, 0:512].rearrange("p (a b) -> p a b", a=4), func=AF.Copy),
                       reads=(PBb,), writes=(w.TM,))
                    yield
                    for hh in (0, 1):
                        s = slice(hh * 64, (hh + 1) * 64)
                        op("pe", lambda e: e.matmul(pbank(bA + hh)[:, 0:128], lhsT=w.AR[s, 0, :], rhs=w.BK[s, 0, :], start=True, stop=True),
                           reads=(w.BK, w.AR), writes=(PB[bA + hh],))
                    yield
                    op("dve", lambda e: e.tensor_tensor(out=w.XY0[:, :, 0, :], in0=DBf[g][:, :].rearrange("p (h c) -> p h c", h=2)[:, :, 0:128],
                                                        in1=C.MASKL[d][:], op=ALU.mult), reads=(PA, PBb, C.MASKL[d]), writes=(w.XY0,))
                    yield
                    for hh in (0, 1):
                        s = slice(hh * 64, (hh + 1) * 64)
                        bj = bA + hh
                        arf = w.AR[s, :, :].rearrange("p a b -> p (a b)")
                        op("pe", lambda e: e.matmul(pbank(bj)[:, 0:256], lhsT=w.BK[s, 0, :], rhs=arf, start=True, stop=True),
                           reads=(w.BK, w.AR), writes=(PB[bj],))
                        op("pe", lambda e: e.matmul(pbank(bj)[:, 256:512], lhsT=w.BK[s, 1, :], rhs=arf, start=True, stop=True),
                           reads=(w.BK, w.AR), writes=(PB[bj],))
                    yield
                    op("dve", lambda e: e.tensor_tensor(out=w.AT[:], in0=DBf[g][:, :].rearrange("p (h a b) -> p h a b", h=2, a=4),
                                                        in1=C.MASK8[d][:], op=ALU.mult), reads=(PA, PBb, C.MASK8[d]), writes=(w.AT,))
                    yield
                    for hh in (0, 1):
                        s = slice(hh * 64, (hh + 1) * 64)
                        op("pe", lambda e: e.matmul(pbank(bA)[:, hh * 64:(hh + 1) * 64], lhsT=w.AT[:, hh, 2, :], rhs=w.TM[:, 3, s], start=True, stop=True),
                           reads=(w.AT, w.TM), writes=(PA,))
                    yield
                    op("act", lambda e: e.activation(out=w.XY0[:, :, 1, 64:128], in_=pbank(bA)[:, 0:128].rearrange("p (a b) -> p a b", a=2), func=AF.Copy),
                       reads=(PA, w.XY0), writes=(w.XY0,))
                    op("pool", lambda e: e.tensor_copy(out=w.XY0[:, :, 1, 0:64], in_=w.TM[:, 0, :].rearrange("p (a b) -> p a b", a=2)),
                       reads=(w.TM, w.XY0), writes=(w.XY0,))
                    yield
                    XN = [w.XY0[:, hh, 0, :] for hh in (0, 1)]
                    YK = [w.XY0[:, hh, 1, :] for hh in (0, 1)]
                    XNY = [w.XY0[:, hh, :, :].rearrange("p a b -> p (a b)") for hh in (0, 1)]
                    XT = [w.AT[:, hh, 0, :] for hh in (0, 1)]
                    srcb = (w.XY0, w.AT)
                    for k in range(7):
                        last = (k == 6)
                        for hh in (0, 1):
                            bj = bA + hh
                            if not last:
                                op("pe", lambda e: e.matmul(pbank(bj)[:, 256:384], lhsT=XN[hh], rhs=XT[hh], start=True, stop=True),
                                   reads=srcb, writes=(PB[bj],))
                                op("pe", lambda e: e.matmul(pbank(bj)[:, 0:256], lhsT=XT[hh], rhs=XNY[hh], start=True, stop=False),
                                   reads=srcb, writes=(PB[bj],))
                            else:
                                op("pe", lambda e: e.matmul(pbank(bj)[:, 128:256], lhsT=XT[hh], rhs=YK[hh], start=True, stop=False),
                                   reads=srcb, writes=(PB[bj],))
                            op("pe", lambda e: e.matmul(pbank(bj)[:, 128:256], lhsT=ident[:], rhs=YK[hh], start=False, stop=True),
                               reads=srcb + (ident,), writes=(PB[bj],))
                        yield
                        eng = "act" if k % 2 == 0 else "dve"
                        if not last:
                            nxt = w.XY[k % 2]
                            src = DBf[g][:, :].rearrange("p (h c) -> p h c", h=2)[:, :, 0:384].rearrange("p h (a b) -> p h a b", a=3)
                            if eng == "act":
                                op("act", lambda e: e.activation(out=nxt[:], in_=src, func=AF.Copy), reads=(PA, PBb), writes=(nxt,))
                            else:
                                op("dve", lambda e: e.tensor_copy(out=nxt[:], in_=src), reads=(PA, PBb), writes=(nxt,))
                            XN = [nxt[:, hh, 0, :] for hh in (0, 1)]
                            YK = [nxt[:, hh, 1, :] for hh in (0, 1)]
                            XNY = [nxt[:, hh, 0:2, :].rearrange("p a b -> p (a b)") for hh in (0, 1)]
                            XT = [nxt[:, hh, 2, :] for hh in (0, 1)]
                            srcb = (nxt,)
                        else:
                            src = DBf[g][:, :].rearrange("p (h c) -> p h c", h=2)[:, :, 128:256]
                            op("act", lambda e: e.activation(out=w.Yf[:], in_=src, func=AF.Copy), reads=(PA, PBb), writes=(w.Yf,))
                        yield
                    Yf = w.Yf
                    for hh in (0, 1):
                        s = slice(hh * 64, (hh + 1) * 64)
                        op("pe", lambda e: e.matmul(pbank(bA)[s, 384:512], lhsT=Yf[:, hh, 0:64], rhs=w.AT[:, hh, 1, :], start=True, stop=True),
                           reads=(Yf, w.AT), writes=(PA,))
                        op("pe", lambda e: e.matmul(pbank(bB)[s, 384:448], lhsT=Yf[:, hh, 0:64], rhs=w.TM[:, 1, s], start=True, stop=True),
                           reads=(Yf, w.TM), writes=(PBb,))
                    yield
                    op("dve", lambda e: e.tensor_tensor(out=w.RP[:], in0=pbank(bA)[:, 384:512], in1=w.AR[:, 1, :], op=ALU.add),
                       reads=(PA, w.AR), writes=(w.RP,))
                    op("dve", lambda e: e.tensor_tensor(out=w.MpT[:], in0=pbank(bB)[:, 384:448], in1=I2[:], op=ALU.add), reads=(PBb, I2), writes=(w.MpT,))
                    yield

                def tail(d, tau, g, Hc, Hn):
                    w = W[g]
                    bA, bB = 2 * g, 2 * g + 1
                    PA, PBb = PB[bA], PB[bB]
                    Yf = w.Yf
                    sl = slice(tau * 128, (tau + 1) * 128)
                    for hh in (0, 1):
                        s = slice(hh * 64, (hh + 1) * 64)
                        op("pe", lambda e: e.matmul(pbank(bA)[s, 0:128], lhsT=Yf[:, hh, 64:128], rhs=w.AT[:, hh, 1, :], start=True, stop=False),
                           reads=(Yf, w.AT), writes=(PA,))
                        op("pe", lambda e: e.matmul(pbank(bA)[s, 0:128], lhsT=w.TM[:, 3, s], rhs=w.AT[:, hh, 3, :], start=False, stop=False),
                           reads=(w.TM, w.AT), writes=(PA,))
                        op("pe", lambda e: e.matmul(pbank(bA)[s, 0:128], lhsT=Hc[s, :], rhs=w.RP[s, :], start=False, stop=True),
                           reads=(Hc, w.RP), writes=(PA,))
                    for hh in (0, 1):
                        s = slice(hh * 64, (hh + 1) * 64)
                        op("pe", lambda e: e.matmul(pbank(bB)[s, 0:64], lhsT=w.TM[:, 1, s], rhs=Yf[:, hh, 64:128], start=True, stop=False),
                           reads=(Yf, w.TM), writes=(PBb,))
                        op("pe", lambda e: e.matmul(pbank(bB)[s, 0:64], lhsT=w.TM[:, 2, s], rhs=w.TM[:, 3, s], start=False, stop=False),
                           reads=(w.TM,), writes=(PBb,))
                        op("pe", lambda e: e.matmul(pbank(bB)[s, 0:64], lhsT=w.MpT[s, :], rhs=Hc[s, :], start=False, stop=True),
                           reads=(w.MpT, Hc), writes=(PBb,))
                    WC = w.e_in[:, 127:128] if d == 0 else w.e_in[:, 0:1]
                    op("act", lambda e: e.activation(out=Hn[:], in_=pbank(bB)[:, 0:64], func=AF.Copy, scale=WC), reads=(PBb, w.e_in), writes=(Hn,))
                    dst = OFT if d == 0 else OBT
                    op("dve", lambda e: e.tensor_copy(out=dst[:, sl], in_=pbank(bA)[:, 0:128]), reads=(PA,), writes=(dst,))

                for d in (((0, 1) if 'RW_HEAD' not in os.environ else (0,)) if os.environ.get('RW_STOP') not in ('c1', 'c1nokk') else ()):
                    op("pool", lambda e: e.memset(Hs[0][:], 0.0), writes=(Hs[0],))
                    order = list(range(NT)) if d == 0 else list(range(NT - 1, -1, -1))
                    step = 0
                    DELTA = int(os.environ.get('RW_DELTA', 3))
                    slots = [None] * G
                    state = ['idle'] * G
                    completed = {}
                    next_pos = 0
                    tails_done = 0
                    rnd = 0
                    while tails_done < NT:
                        for gi in range(G):
                            if state[gi] == 'idle':
                                if next_pos < NT and rnd >= gi * DELTA:
                                    slots[gi] = (next_pos, head(d, order[next_pos], gi))
                                    state[gi] = 'run'
                                    next_pos += 1
                                else:
                                    continue
                            if state[gi] == 'run':
                                pos, gen = slots[gi]
                                try:
                                    next(gen)
                                except StopIteration:
                                    completed[pos] = gi
                                    state[gi] = 'wait'
                        while tails_done in completed:
                            gi = completed.pop(tails_done)
                            tail(d, order[tails_done], gi, Hs[step % 2], Hs[(step + 1) % 2])
                            step += 1
                            tails_done += 1
                            state[gi] = 'idle'
                        rnd += 1
            if True:
                PW = min(512, S) if S <= 2048 else 256
                if hp == 0:
                    if PW == CB:
                        o_, cen, sq2 = ctmp, kkr, nrm
                        var, rk_ = [fw.sb("%s@%d" % (n, si), [128, PW], F32, st) for n in ("pvar", "prk")]
                    else:
                        o_, cen, sq2, var, rk_ = [fw.sb("%s@%d" % (n, si), [128, PW], F32, st) for n in ("po_", "pcen", "psq2", "pvar", "prk")]
                for pb_ in range(S // PW):
                    ps = slice(pb_ * PW, (pb_ + 1) * PW)
                    op("dve", lambda e: e.tensor_tensor(out=o_[:], in0=OFT[:, ps], in1=OBT[:, ps], op=ALU.add), reads=(OFT, OBT), writes=(o_,))
                    op("pe", lambda e: e.matmul(pbank(0)[:, 0:PW], lhsT=BOf[:], rhs=o_[:], start=True, stop=True), reads=(BOf, o_), writes=(PB[0],))
                    op("dve", lambda e: e.scalar_tensor_tensor(out=cen[:], in0=pbank(0)[:, 0:PW], scalar=-1.0 / 64, in1=o_[:], op0=ALU.mult, op1=ALU.add),
                       reads=(PB[0], o_), writes=(cen,))
                    op("pool", lambda e: e.tensor_tensor(out=sq2[:], in0=cen[:], in1=cen[:], op=ALU.mult), reads=(cen,), writes=(sq2,))
                    op("pe", lambda e: e.matmul(pbank(1)[:, 0:PW], lhsT=BOf[:], rhs=sq2[:], start=True, stop=True), reads=(BOf, sq2), writes=(PB[1],))
                    op("dve", lambda e: e.tensor_scalar(out=var[:], in0=pbank(1)[:, 0:PW], scalar1=1.0 / 64, scalar2=GN_EPS, op0=ALU.mult, op1=ALU.add),
                       reads=(PB[1],), writes=(var,))
                    op("act", lambda e: e.activation(out=var[:], in_=var[:], func=AF.Sqrt), reads=(var,), writes=(var,))
                    op("dve", lambda e: e.reciprocal(out=var[:], in_=var[:]), reads=(var,), writes=(var,))
                    op("dve", lambda e: e.tensor_tensor(out=cen[:], in0=cen[:], in1=var[:], op=ALU.mult), reads=(cen, var), writes=(cen,))
                    op("dve", lambda e: e.tensor_scalar(out=cen[:], in0=cen[:], scalar1=C.lg_c[:, hpc], scalar2=C.lb_c[:, hpc], op0=ALU.mult, op1=ALU.add),
                       reads=(cen, C.lg_c, C.lb_c), writes=(cen,))
                    op("dve", lambda e: e.scalar_tensor_tensor(out=rk_[:], in0=RT_[:, ps], scalar=C.rk_c[:, hpc], in1=KT_[:, ps], op0=ALU.mult, op1=ALU.mult),
                       reads=(RT_, KT_, C.rk_c), writes=(rk_,))
                    op("pe", lambda e: e.matmul(pbank(2)[:, 0:PW], lhsT=BOf[:], rhs=rk_[:], start=True, stop=True), reads=(BOf, rk_), writes=(PB[2],))
                    op("dve", lambda e: e.tensor_tensor(out=sq2[:], in0=pbank(2)[:, 0:PW], in1=VT_[:, ps], op=ALU.mult), reads=(PB[2], VT_), writes=(sq2,))
                    op("pool", lambda e: e.tensor_tensor(out=cen[:], in0=cen[:], in1=sq2[:], op=ALU.add), reads=(cen, sq2), writes=(cen,))
                    for kc in range(8):
                        op("pe", lambda e: e.matmul(pbank(3)[:, 0:PW], lhsT=Whp[:, 3, kc, :], rhs=xnT[:, kc, 1 + pb_ * PW:1 + (pb_ + 1) * PW],
                                                    start=(kc == 0), stop=(kc == 7)), reads=(Whp, xnT), writes=(PB[3],))
                    op("act", lambda e: e.activation(out=var[:], in_=pbank(3)[:, 0:PW], func=AF.Silu), reads=(PB[3],), writes=(var,))
                    op("dve", lambda e: e.tensor_tensor(out=OFT[:, ps], in0=cen[:], in1=var[:], op=ALU.mult), reads=(cen, var, OFT), writes=(OFT,))
            fw.dma("sp", C.scrR[:, hp, 0:S], OFT[:], _stsem(fw, OFT), reads=(OFT,), writes=(C.dR,))
    with contextlib.ExitStack() as st:
        Wor = load_w(C, st, "Wor@%d" % si, C.P["w_o_rwkv"].rearrange("(c p) n -> p c n", p=128), [128, 8, 1024])
        Wgmb = load_w(C, st, "Wgmb@%d" % si, C.wview(O_GMB, 1024), [128, 8, 1024])
        branch_tail(C, st, "r", None, Wor, Wgmb, C.scrR, C.dR, C.scrR, C.dR, None)


def make_rope(C, st0):
    fw, op, S, si = C.fw, C.fw.op, C.S, C.si
    cosT = fw.sb("cosT@%d" % si, [128, S], BF16, st0)
    sinT = fw.sb("sinT@%d" % si, [128, S], BF16, st0)
    with contextlib.ExitStack() as st:
        pi_ = fw.sb("pi_@%d" % si, [128, 1], I32, st)
        pj_ = fw.sb("pj_@%d" % si, [128, 1], I32, st)
        pf_ = fw.sb("pf_@%d" % si, [128, 2], F32, st)
        invf = fw.sb("invf@%d" % si, [128, 1], F32, st)
        posi = fw.sb("posi@%d" % si, [128, S], I32, st)
        ang = fw.sb("ang@%d" % si, [128, S], F32, st)
        kf = fw.sb("kf@%d" % si, [128, S], F32, st)
        tm = fw.sb("tm@%d" % si, [128, S], F32, st)
        op("pool", lambda e: e.iota(pi_[:], pattern=[[0, 1]], base=0, channel_multiplier=1), writes=(pi_,))
        op("dve", lambda e: e.tensor_scalar(out=pj_[:], in0=pi_[:], scalar1=7, scalar2=None, op0=ALU.bitwise_and),
           reads=(pi_,), writes=(pj_,))
        op("dve", lambda e: e.tensor_copy(out=pf_[:, 0:1], in_=pj_[:]), reads=(pj_,), writes=(pf_,))
        op("dve", lambda e: e.tensor_scalar(out=pj_[:], in0=pi_[:], scalar1=63, scalar2=None, op0=ALU.bitwise_and),
           reads=(pi_,), writes=(pj_,))
        op("dve", lambda e: e.tensor_copy(out=pf_[:, 1:2], in_=pj_[:]), reads=(pj_,), writes=(pf_,))
        op("act", lambda e: e.activation(out=invf[:], in_=pf_[:, 0:1], func=AF.Exp, scale=-math.log(ROPE_THETA) / 8.0),
           reads=(pf_,), writes=(invf,))
        op("dve", lambda e: e.tensor_scalar(out=pf_[:, 1:2], in0=pf_[:, 1:2], scalar1=16.0, scalar2=None, op0=ALU.is_lt),
           reads=(pf_,), writes=(pf_,))
        op("dve", lambda e: e.tensor_tensor(out=invf[:], in0=invf[:], in1=pf_[:, 1:2], op=ALU.mult),
           reads=(invf, pf_), writes=(invf,))
        op("pool", lambda e: e.iota(posi[:], pattern=[[1, S]], base=0, channel_multiplier=0), writes=(posi,))
        op("dve", lambda e: e.tensor_copy(out=ang[:], in_=posi[:]), reads=(posi,), writes=(ang,))
        op("dve", lambda e: e.tensor_scalar(out=ang[:], in0=ang[:], scalar1=invf[:, 0:1], scalar2=None, op0=ALU.mult),
           reads=(ang, invf), writes=(ang,))

        def wrap_sin(dst, shift):
            op("dve", lambda e: e.tensor_scalar(out=tm[:], in0=ang[:], scalar1=shift, scalar2=None, op0=ALU.add),
               reads=(ang,), writes=(tm,))
            op("dve", lambda e: e.tensor_scalar(out=kf[:], in0=tm[:], scalar1=1.0 / TWO_PI, scalar2=None, op0=ALU.mult),
               reads=(tm,), writes=(kf,))
            op("dve", lambda e: e.tensor_copy(out=posi[:], in_=kf[:]), reads=(kf,), writes=(posi,))
            op("dve", lambda e: e.tensor_copy(out=kf[:], in_=posi[:]), reads=(posi,), writes=(kf,))
            op("dve", lambda e: e.scalar_tensor_tensor(out=tm[:], in0=kf[:], scalar=-CW1, in1=tm[:], op0=ALU.mult, op1=ALU.add),
               reads=(kf, tm), writes=(tm,))
            op("dve", lambda e: e.scalar_tensor_tensor(out=tm[:], in0=kf[:], scalar=-CW2, in1=tm[:], op0=ALU.mult, op1=ALU.add),
               reads=(kf, tm), writes=(tm,))
            op("dve", lambda e: e.tensor_scalar(out=kf[:], in0=tm[:], scalar1=math.pi, scalar2=-TWO_PI, op0=ALU.is_gt, op1=ALU.mult),
               reads=(tm,), writes=(kf,))
            op("dve", lambda e: e.tensor_tensor(out=tm[:], in0=tm[:], in1=kf[:], op=ALU.add), reads=(tm, kf), writes=(tm,))
            op("dve", lambda e: e.tensor_scalar(out=kf[:], in0=tm[:], scalar1=-math.pi, scalar2=TWO_PI, op0=ALU.is_lt, op1=ALU.mult),
               reads=(tm,), writes=(kf,))
            op("dve", lambda e: e.tensor_tensor(out=tm[:], in0=tm[:], in1=kf[:], op=ALU.add), reads=(tm, kf), writes=(tm,))
            op("dve", lambda e: e.tensor_scalar(out=tm[:], in0=tm[:], scalar1=3.14159, scalar2=-3.14159, op0=ALU.min, op1=ALU.max),
               reads=(tm,), writes=(tm,))
            op("act", lambda e: e.activation(out=dst[:], in_=tm[:], func=AF.Sin), reads=(tm,), writes=(dst,))
        wrap_sin(sinT, 0.0)
        wrap_sin(cosT, math.pi / 2)

    return cosT, sinT


SEQ_LENS = [2048, 2048, 2048, 2048, 4096]
_NC_CACHE = {}


def kernel(**inputs):
    xp = np.asarray(inputs["x_prompt"], dtype=np.float32)
    xs = np.asarray(inputs["x_sample"], dtype=np.float32)
    n = 8
    if "nc" not in _NC_CACHE:
        _NC_CACHE["nc"] = build(SEQ_LENS)
    nc = _NC_CACHE["nc"]
    shared = {}
    for nme, shp in PARAMS:
        shared[nme] = np.ascontiguousarray(np.asarray(inputs[nme], dtype=np.float32).reshape(shp))
    in_maps = []
    for c in range(n):
        xc = np.concatenate([xp[4 * c:4 * c + 4].reshape(-1, D), xs[c].reshape(-1, D)], axis=0)
        m = {"x": np.ascontiguousarray(xc)}
        m.update(shared)
        in_maps.append(m)
    res = run_bass_kernel_spmd(nc, in_maps, core_ids=list(range(n)))
    yp = np.empty_like(xp)
    ys = np.empty_like(xs)
    for c in range(n):
        yc = np.asarray(res.results[c]["y"])
        yp[4 * c:4 * c + 4] = yc[0:8192].reshape(4, 2048, D)
        ys[c] = yc[8192:12288].reshape(4096, D)
    return (yp, ys)
```

```python
import contextlib
import re
import numpy as np
import concourse.bass as bass
import concourse.mybir as mybir

F32 = mybir.dt.float32
BF16 = mybir.dt.bfloat16
I32 = mybir.dt.int32
AF = mybir.ActivationFunctionType
ALU = mybir.AluOpType
AX = mybir.AxisListType

SEM_ROT = 30000


class Buf:
    __slots__ = ("t", "name", "last_w", "readers", "ld_sem", "st_sem")

    def __init__(self, t, name):
        self.t = t
        self.name = name
        self.last_w = None
        self.readers = {}
        self.ld_sem = None
        self.st_sem = None

    def __getitem__(self, k):
        return self.t[k]


class FW:
    def __init__(self, nc):
        self.nc = nc
        self.stack = contextlib.ExitStack()
        self.eng = {"pe": nc.tensor, "act": nc.scalar, "dve": nc.vector,
                    "pool": nc.gpsimd, "sp": nc.sync}
        self.sems = {}
        self.cur = {}
        self.cnt = {}
        self.waited = {e: {} for e in self.eng}
        self.dma_total = {}
        self.dma_roles = {}
        self.nsem = 0
        self.n_ops = 0
        self.n_waits = 0
        for e in self.eng:
            self._new_eng_sem(e)

    def _alloc_sem(self, name):
        h = self.stack.enter_context(self.nc.semaphore(name))
        key = name
        self.sems[key] = h
        self.nsem += 1
        return key

    def _new_eng_sem(self, e):
        key = self._alloc_sem("s_%s_%d" % (e, self.nsem))
        self.cur[e] = key
        self.cnt[key] = 0

    def dma_sem(self, name):
        role = re.sub(r"@\d+", "", name)
        if role in self.dma_roles:
            return self.dma_roles[role]
        key = self._alloc_sem("d_%s_%d" % (role, self.nsem))
        self.dma_total[key] = 0
        self.dma_roles[role] = key
        return key

    def snapshot(self):
        snap = {}
        for k in self.sems:
            v = self.dma_total[k] if k in self.dma_total else self.cnt.get(k, 0)
            if v > 0:
                snap[k] = v
        return snap

    def sb(self, name, shape, dtype, stack=None):
        self.n_alloc = getattr(self, "n_alloc", 0) + 1
        t = (stack or self.stack).enter_context(self.nc.sbuf_tensor("%s_u%d" % (name.replace("@", "_"), self.n_alloc), list(shape), dtype))
        b = Buf(t, name)
        b.readers = self.snapshot()
        return b

    def ps(self, name, shape, dtype, stack=None):
        t = (stack or self.stack).enter_context(self.nc.psum_tensor(name, list(shape), dtype))
        return Buf(t, name)

    def view(self, buf_or_ap, name):
        return Buf(buf_or_ap, name)

    def _collect(self, e, reads, writes):
        deps = {}

        def add(tok):
            if tok is None:
                return
            k, v = tok
            if k in self.dma_total:
                v = self.dma_total[k]
            if deps.get(k, 0) < v:
                deps[k] = v

        for b in reads:
            add(b.last_w)
        for b in writes:
            add(b.last_w)
            for k, v in b.readers.items():
                add((k, v))
        return deps

    def _emit_waits(self, e, deps):
        eng = self.eng[e]
        w = self.waited[e]
        for k, v in deps.items():
            if e == "pe" and k.startswith("s_pe_"):
                continue
            if w.get(k, 0) >= v:
                continue
            eng.wait_ge(self.sems[k], v)
            w[k] = v
            self.n_waits += 1

    def _mark(self, tok, reads, writes):
        k, v = tok
        for b in writes:
            b.last_w = tok
            b.readers = {}
        for b in reads:
            if b.readers.get(k, 0) < v:
                b.readers[k] = v

    def op(self, e, fn, reads=(), writes=()):
        deps = self._collect(e, reads, writes)
        self._emit_waits(e, deps)
        ins = fn(self.eng[e])
        key = self.cur[e]
        if self.cnt[key] >= SEM_ROT:
            self._new_eng_sem(e)
            key = self.cur[e]
        self.cnt[key] += 1
        ins.then_inc(self.sems[key], 1)
        self._mark((key, self.cnt[key]), reads, writes)
        self.n_ops += 1
        return ins

    def dma(self, q, out, in_, sem, reads=(), writes=(), **kw):
        deps = self._collect(q, reads, writes)
        self._emit_waits(q, deps)
        ins = self.eng[q].dma_start(out=out, in_=in_, **kw)
        self.dma_total[sem] += 16
        ins.then_inc(self.sems[sem], 16)
        self._mark((sem, self.dma_total[sem]), reads, writes)
        self.n_ops += 1
        return ins

    def load(self, q, buf, out_ap, in_ap, **kw):
        if buf.ld_sem is None:
            buf.ld_sem = self.dma_sem("l" + buf.name)
        return self.dma(q, out_ap, in_ap, buf.ld_sem, reads=(), writes=(buf,), **kw)

    def store(self, q, buf, out_ap, in_ap, **kw):
        if buf.st_sem is None:
            buf.st_sem = self.dma_sem("s" + buf.name)
        return self.dma(q, out_ap, in_ap, buf.st_sem, reads=(buf,), writes=(), **kw)

    def final_wait(self, e="sp"):
        eng = self.eng[e]
        for k, h in self.sems.items():
            v = self.dma_total[k] if k in self.dma_total else self.cnt.get(k, 0)
            if v > 0 and self.waited[e].get(k, 0) < v:
                eng.wait_ge(h, v)
                self.waited[e][k] = v


import math
from types import SimpleNamespace
import contextlib
import numpy as np
import concourse.bass as bass
import concourse.mybir as mybir
from concourse.bass_utils import run_bass_kernel_spmd

D = 1024
PIN = 10496
O_Q, O_K, O_V, O_GA, O_R, O_RK, O_RV, O_GR, O_WL, O_AL, O_GMA, O_GMB = (
    0, 1024, 2048, 3072, 4096, 5120, 6144, 7168, 8192, 8320, 8448, 9472)
ROPE_THETA = 500000.0
TWO_PI = 2.0 * math.pi
CW1 = 6.28125
CW2 = TWO_PI - CW1

PARAMS = [("norm_g", [1, 1024]), ("w_in", [1024, PIN]), ("conv_rkv", [3, 3072]),
          ("lam_q1", [1, 64]), ("lam_k1", [1, 64]), ("lam_q2", [1, 64]), ("lam_k2", [1, 64]),
          ("attn_subln_g", [1, 128]), ("w_lora_up", [128, 1024]), ("w0", [2, 1024]),
          ("a_lora_up", [128, 1024]), ("a0", [2, 1024]), ("k_k", [1, 1024]), ("k_a", [1, 1024]),
          ("r_k", [1, 1024]), ("ln_x_g", [1, 1024]), ("ln_x_b", [1, 1024]),
          ("w_o_attn", [1024, 1024]), ("w_o_rwkv", [1024, 1024]), ("w_out", [1024, 1024]),
          ("final_g", [1, 1024])]


def build(seq_lens, do_attn=True, do_rwkv=True):
    nc = bass.Bass("TRN2", target_bir_lowering=False)
    TOT = sum(seq_lens)
    SMAX = max(seq_lens)
    x = nc.dram_tensor("x", [TOT, D], F32, kind="ExternalInput").ap()
    y = nc.dram_tensor("y", [TOT, D], F32, kind="ExternalOutput").ap()
    P = {n: nc.dram_tensor(n, s, F32, kind="ExternalInput").ap() for n, s in PARAMS}
    w_in = P["w_in"]
    scrA = nc.dram_tensor("scrA", [128, 8, SMAX], BF16, kind="Internal").ap()
    scrM = nc.dram_tensor("scrM", [128, 8, SMAX], BF16, kind="Internal").ap()
    scrR = nc.dram_tensor("scrR", [128, 8, SMAX], BF16, kind="Internal").ap()
    fw = FW(nc)
    op = fw.op
    ncd = nc.allow_non_contiguous_dma(reason="small param layouts")
    ncd.__enter__()

    def wview(off, ncols):
        return w_in[:, off:off + ncols].rearrange("(c p) n -> p c n", p=128)

    with fw.stack:
        dA, dM, dR = fw.view(scrA, "scrA"), fw.view(scrM, "scrM"), fw.view(scrR, "scrR")
        DBt = [fw.stack.enter_context(nc.psum_tensor("db%d" % i, [128, 1024], F32)) for i in range(4)]
        DBf = [t[:] for t in DBt]
        DBh = [t[:].bitcast(BF16) for t in DBt]
        PB = [Buf(None, "bank%d" % j) for j in range(8)]

        def bk(i, both=True, half=0):
            return (PB[2 * i], PB[2 * i + 1]) if both else (PB[2 * i + half],)

        psem = fw.dma_sem("params")
        psem2 = fw.dma_sem("paramsq")

        def pload(name, shape, src, q="sp", dtype=F32):
            b = fw.sb(name, shape, dtype)
            b.ld_sem = psem if q == "sp" else psem2
            fw.load(q, b, b[:], src)
            return b

        def colparam(name, nm):
            return pload(name, [128, 8], P[nm].rearrange("o (c p) -> p (o c)", p=128))

        gcol = colparam("gcol", "norm_g")
        kk_c = colparam("kk_c", "k_k")
        ka_c = colparam("ka_c", "k_a")
        rk_c = colparam("rk_c", "r_k")
        lg_c = colparam("lg_c", "ln_x_g")
        lb_c = colparam("lb_c", "ln_x_b")
        w0_c = pload("w0_c", [128, 2, 8], P["w0"].rearrange("d (c p) -> p d c", p=128))
        a0_c = pload("a0_c", [128, 2, 8], P["a0"].rearrange("d (c p) -> p d c", p=128))
        cv_c = pload("cv_c", [128, 3, 24], P["conv_rkv"].rearrange("i (c p) -> p i c", p=128))
        fgb = pload("fgb", [128, 1024], P["final_g"][0, :].partition_broadcast(128))
        sgb = pload("sgb", [128, 128], P["attn_subln_g"][0, :].partition_broadcast(128))
        lq = [pload("lam%d" % i, [128, 64], P[n][0, :].partition_broadcast(128))
              for i, n in enumerate(["lam_q1", "lam_k1", "lam_q2", "lam_k2"])]
        wlw = pload("wlw", [128, 1024], P["w_lora_up"], q="pool", dtype=BF16)
        wla = pload("wla", [128, 1024], P["a_lora_up"], q="pool", dtype=BF16)

        ident = fw.sb("ident", [128, 128], BF16)
        op("pool", lambda e: e.memset(ident[:], 1.0), writes=(ident,))
        op("pool", lambda e: e.affine_select(out=ident[:], in_=ident[:], pattern=[[-1, 128]], compare_op=ALU.is_equal,
                                             fill=0.0, base=0, channel_multiplier=1), reads=(ident,), writes=(ident,))
        grep = fw.sb("grep", [128, 8, 128], F32)
        op("dve", lambda e: e.memset(grep[:], 1.0), writes=(grep,))
        for c in range(8):
            op("dve", lambda e: e.tensor_scalar(out=grep[:, c, :], in0=grep[:, c, :], scalar1=gcol[:, c:c + 1],
                                                scalar2=None, op0=ALU.mult), reads=(grep, gcol), writes=(grep,))
        op("dve", lambda e: e.tensor_scalar(out=sgb[:], in0=sgb[:], scalar1=0.8, scalar2=None, op0=ALU.mult),
           reads=(sgb,), writes=(sgb,))
        hw0_c = fw.sb("hw0_c", [128, 2, 8], F32)
        ha0_c = fw.sb("ha0_c", [128, 2, 8], F32)
        op("dve", lambda e: e.tensor_scalar(out=hw0_c[:], in0=w0_c[:], scalar1=0.5, scalar2=None, op0=ALU.mult), reads=(w0_c,), writes=(hw0_c,))
        op("dve", lambda e: e.tensor_scalar(out=ha0_c[:], in0=a0_c[:], scalar1=0.5, scalar2=None, op0=ALU.mult), reads=(a0_c,), writes=(ha0_c,))
        tmka = fw.sb("tmka", [128, 8], F32)
        op("dve", lambda e: e.tensor_scalar(out=tmka[:], in0=ka_c[:], scalar1=-1.0, scalar2=2.0, op0=ALU.mult, op1=ALU.add),
           reads=(ka_c,), writes=(tmka,))
        eps5 = fw.sb("eps5", [128, 1], F32)
        op("dve", lambda e: e.memset(eps5[:], 1e-5), writes=(eps5,))
        lnhalf = fw.sb("lnhalf", [128, 1], F32)
        op("dve", lambda e: e.memset(lnhalf[:], math.log(0.5)), writes=(lnhalf,))
        omka = fw.sb("omka", [128, 8], F32)
        op("dve", lambda e: e.tensor_scalar(out=omka[:], in0=ka_c[:], scalar1=-1.0, scalar2=1.0, op0=ALU.mult, op1=ALU.add),
           reads=(ka_c,), writes=(omka,))
        lt = fw.sb("lt", [128, 64], F32)
        ls = fw.sb("ls", [128, 2], F32)
        nlam = fw.sb("nlam", [128, 1], F32)
        for i in range(2):
            op("dve", lambda e: e.tensor_tensor(out=lt[:], in0=lq[2 * i][:], in1=lq[2 * i + 1][:], op=ALU.mult),
               reads=(lq[2 * i], lq[2 * i + 1]), writes=(lt,))
            op("dve", lambda e: e.reduce_sum(out=ls[:, i:i + 1], in_=lt[:], axis=AX.X), reads=(lt,), writes=(ls,))
        op("act", lambda e: e.activation(out=ls[:], in_=ls[:], func=AF.Exp), reads=(ls,), writes=(ls,))
        op("dve", lambda e: e.tensor_tensor(out=nlam[:], in0=ls[:, 1:2], in1=ls[:, 0:1], op=ALU.subtract),
           reads=(ls,), writes=(nlam,))
        op("dve", lambda e: e.tensor_scalar(out=nlam[:], in0=nlam[:], scalar1=-0.2, scalar2=None, op0=ALU.add),
           reads=(nlam,), writes=(nlam,))

        def mkmask(name, mult_f, mult_p, base, cmp):
            m = fw.sb(name, [128, 128], BF16)
            op("pool", lambda e: e.memset(m[:], 1.0), writes=(m,))
            op("pool", lambda e: e.affine_select(out=m[:], in_=m[:], pattern=[[mult_f, 128]], compare_op=cmp,
                                                 fill=0.0, base=base, channel_multiplier=mult_p), reads=(m,), writes=(m,))
            return m
        Us = mkmask("Us", 1, -1, 0, ALU.is_gt)
        Ui = mkmask("Ui", 1, -1, 0, ALU.is_ge)
        Ls = mkmask("Ls", -1, 1, 0, ALU.is_gt)
        Li = mkmask("Li", -1, 1, 0, ALU.is_ge)
        MASK4 = []
        MASK8 = []
        MASKL = []
        for d_ in range(2):
            m4 = fw.sb("m4_%d" % d_, [128, 4, 128], BF16)
            s_, i_ = (Us, Ui) if d_ == 0 else (Ls, Li)
            for j, src in enumerate([s_, i_, s_, i_]):
                op("pool", lambda e: e.tensor_copy(out=m4[:, j, :], in_=src[:]), reads=(src,), writes=(m4,))
            MASK4.append(m4)
            m8 = fw.sb("m8_%d" % d_, [128, 2, 4, 128], BF16)
            for h_ in range(2):
                for j, src in enumerate([s_, i_, s_, i_]):
                    op("pool", lambda e: e.tensor_copy(out=m8[:, h_, j, :], in_=src[:]), reads=(src,), writes=(m8,))
            MASK8.append(m8)
            ml = fw.sb("ml_%d" % d_, [128, 2, 128], BF16)
            src = Ls if d_ == 0 else Us
            for j in range(2):
                op("pool", lambda e: e.tensor_copy(out=ml[:, j, :], in_=src[:]), reads=(src,), writes=(ml,))
            MASKL.append(ml)
        BOh = fw.sb("BOh", [128, 128], BF16)
        BOf = fw.sb("BOf", [128, 128], F32)
        for t_ in (BOh, BOf):
            op("pool", lambda e: e.memset(t_[:], 0.0), writes=(t_,))
            op("pool", lambda e: e.memset(t_[0:64, 0:64], 1.0), reads=(t_,), writes=(t_,))
            op("pool", lambda e: e.memset(t_[64:128, 64:128], 1.0), reads=(t_,), writes=(t_,))
        I2 = fw.sb("I2", [128, 64], F32)
        op("pool", lambda e: e.memset(I2[:], 1.0), writes=(I2,))
        op("pool", lambda e: e.affine_select(out=I2[0:64, :], in_=I2[0:64, :], pattern=[[-1, 64]], compare_op=ALU.is_equal,
                                             fill=0.0, base=0, channel_multiplier=1), reads=(I2,), writes=(I2,))
        op("pool", lambda e: e.affine_select(out=I2[64:128, :], in_=I2[64:128, :], pattern=[[-1, 64]], compare_op=ALU.is_equal,
                                             fill=0.0, base=0, channel_multiplier=1), reads=(I2,), writes=(I2,))
        onesS = fw.sb("onesS", [128, 128], F32)
        op("pool", lambda e: e.memset(onesS[:], 1.0), writes=(onesS,))

        def proj_fm(dst_fn, W, wslot_fn, xnT, S, evac):
            pass

        tok0 = 0
        for si, S in enumerate(seq_lens):
            NT = S // 128
            NB = S // 512 if S >= 512 else 1
            BW = min(512, S)
            with contextlib.ExitStack() as sq:
                xnT = fw.sb("xnT@%d" % si, [128, 8, S + 2], BF16, sq)
                op("pool", lambda e: e.memset(xnT[:, :, 0:1], 0.0), writes=(xnT,))
                op("pool", lambda e: e.memset(xnT[:, :, S + 1:S + 2], 0.0), writes=(xnT,))
                with contextlib.ExitStack() as sa:
                    xt = [fw.sb("xt%d@%d" % (i, si), [128, 1024], F32, sa) for i in range(2)]
                    junk = fw.sb("junkA@%d" % si, [128, 1024], F32, sa)
                    xs = [fw.sb("xs%d@%d" % (i, si), [128, 1024], BF16, sa) for i in range(2)]
                    ssA = [fw.sb("ssA%d@%d" % (i, si), [128, 1], F32, sa) for i in range(2)]
                    for tt in range(NT):
                        b = tt % 2
                        fw.load("sp", xt[b], xt[b][:], x[tok0 + tt * 128: tok0 + (tt + 1) * 128, :])
                        op("act", lambda e: e.activation(out=junk[:], in_=xt[b][:], func=AF.Square, accum_out=ssA[b][:]),
                           reads=(xt[b],), writes=(junk, ssA[b]))
                        op("dve", lambda e: e.tensor_scalar(out=ssA[b][:], in0=ssA[b][:], scalar1=1.0 / 1024, scalar2=1e-6,
                                                            op0=ALU.mult, op1=ALU.add), reads=(ssA[b],), writes=(ssA[b],))
                        op("act", lambda e: e.activation(out=ssA[b][:], in_=ssA[b][:], func=AF.Sqrt), reads=(ssA[b],), writes=(ssA[b],))
                        op("dve", lambda e: e.reciprocal(out=ssA[b][:], in_=ssA[b][:]), reads=(ssA[b],), writes=(ssA[b],))
                        op("act", lambda e: e.activation(out=xs[b][:], in_=xt[b][:], func=AF.Copy, scale=ssA[b][:, 0:1]),
                           reads=(xt[b], ssA[b]), writes=(xs[b],))
                        db = tt % 2
                        for c in range(8):
                            op("pe", lambda e: e.transpose(out=DBh[db][:, c * 128:(c + 1) * 128], in_=xs[b][:, c * 128:(c + 1) * 128],
                                                           identity=ident[:]), reads=(xs[b], ident), writes=bk(db, False, 0))
                        op("dve", lambda e: e.tensor_tensor(out=xnT[:, :, 1 + tt * 128: 1 + (tt + 1) * 128],
                                                            in0=DBh[db][:, 0:1024].rearrange("p (c t) -> p c t", c=8),
                                                            in1=grep[:], op=ALU.mult),
                           reads=bk(db, False, 0) + (grep,), writes=(xnT,))

                def xblk(kc, tb):
                    return xnT[:, kc, 1 + tb * BW: 1 + (tb + 1) * BW]

                C = SimpleNamespace(**locals())
                if do_attn:
                    stage_attn(C)
                if do_rwkv:
                    stage_rwkv(C)
                stage_out(C)
            tok0 += S
        fw.final_wait("sp")
    ncd.__exit__(None, None, None)
    return nc


def load_w(C, st, name, src_ap, shape):
    b = C.fw.sb(name, shape, BF16, st)
    C.fw.load("pool", b, b[:], src_ap)
    return b


def stage_out(C):
    fw, op, S, BW, NB, si = C.fw, C.fw.op, C.S, C.BW, C.NB, C.si
    DBf, bk = C.DBf, C.bk
    with contextlib.ExitStack() as st:
        Wout = load_w(C, st, "Wout@%d" % si, C.P["w_out"].rearrange("(c p) n -> p c n", p=128), [128, 8, 1024])
        ma = [fw.sb("ma%d@%d" % (i, si), [128, 8, BW], BF16, st) for i in range(2)]
        mr = [fw.sb("mr%d@%d" % (i, si), [128, 8, BW], BF16, st) for i in range(2)]
        xt = [fw.sb("xo%d@%d" % (i, si), [128, 1024], F32, st) for i in range(2)]
        zt = [fw.sb("zt%d@%d" % (i, si), [128, 1024], F32, st) for i in range(2)]
        junk = fw.sb("junkO@%d" % si, [128, 1024], F32, st)
        ss = [fw.sb("ssO%d@%d" % (i, si), [128, 1], F32, st) for i in range(2)]
        cnt = 0
        for tb in range(NB):
            b = tb % 2
            m = None
            if C.do_attn:
                fw.dma("sp", ma[b][:], C.scrM[:, :, tb * BW:(tb + 1) * BW], _ldsem(fw, ma[b]), reads=(C.dM,), writes=(ma[b],))
                m = ma[b]
            if C.do_rwkv:
                fw.dma("sp", mr[b][:], C.scrR[:, :, tb * BW:(tb + 1) * BW], _ldsem(fw, mr[b]), reads=(C.dR,), writes=(mr[b],))
                if m is None:
                    m = mr[b]
                else:
                    op("pool", lambda e: e.tensor_tensor(out=ma[b][:], in0=ma[b][:], in1=mr[b][:], op=ALU.add),
                       reads=(ma[b], mr[b]), writes=(ma[b],))
            for t4 in range(BW // 128):
                tt = tb * (BW // 128) + t4
                xb = cnt % 2
                db = 2 + cnt % 2
                cnt += 1
                fw.load("sp", xt[xb], xt[xb][:], C.x[C.tok0 + tt * 128: C.tok0 + (tt + 1) * 128, :])
                if m is not None:
                    for half in range(2):
                        for dc in range(8):
                            op("pe", lambda e: e.matmul(DBf[db][:, half * 512:(half + 1) * 512], lhsT=m[:, dc, t4 * 128:(t4 + 1) * 128],
                                                        rhs=Wout[:, dc, half * 512:(half + 1) * 512], start=(dc == 0), stop=(dc == 7)),
                               reads=(m, Wout), writes=bk(db, False, half))
                    op("dve", lambda e: e.tensor_tensor(out=zt[xb][:], in0=DBf[db][:], in1=xt[xb][:], op=ALU.add),
                       reads=bk(db) + (xt[xb],), writes=(zt[xb],))
                else:
                    op("dve", lambda e: e.tensor_copy(out=zt[xb][:], in_=xt[xb][:]), reads=(xt[xb],), writes=(zt[xb],))
                op("act", lambda e: e.activation(out=junk[:], in_=zt[xb][:], func=AF.Square, accum_out=ss[xb][:]),
                   reads=(zt[xb],), writes=(junk, ss[xb]))
                op("dve", lambda e: e.tensor_scalar(out=ss[xb][:], in0=ss[xb][:], scalar1=1.0 / 1024, scalar2=1e-6,
                                                    op0=ALU.mult, op1=ALU.add), reads=(ss[xb],), writes=(ss[xb],))
                op("act", lambda e: e.activation(out=ss[xb][:], in_=ss[xb][:], func=AF.Sqrt), reads=(ss[xb],), writes=(ss[xb],))
                op("dve", lambda e: e.reciprocal(out=ss[xb][:], in_=ss[xb][:]), reads=(ss[xb],), writes=(ss[xb],))
                op("dve", lambda e: e.scalar_tensor_tensor(out=zt[xb][:], in0=zt[xb][:], scalar=ss[xb][:, 0:1], in1=C.fgb[:],
                                                           op0=ALU.mult, op1=ALU.mult), reads=(zt[xb], ss[xb], C.fgb), writes=(zt[xb],))
                fw.store("act", zt[xb], C.y[C.tok0 + tt * 128: C.tok0 + (tt + 1) * 128, :], zt[xb][:])


def _ldsem(fw, b):
    if b.ld_sem is None:
        b.ld_sem = fw.dma_sem("l" + b.name)
    return b.ld_sem


def _stsem(fw, b):
    if b.st_sem is None:
        b.st_sem = fw.dma_sem("s" + b.name)
    return b.st_sem


def stage_attn(C):
    fw, op, S, BW, NB, NT, si = C.fw, C.fw.op, C.S, C.BW, C.NB, C.NT, C.si
    DBf, DBh, bk, xnT, xblk = C.DBf, C.DBh, C.bk, C.xnT, C.xblk
    ident = C.ident
    nq = BW // 128
    with contextlib.ExitStack() as st:
        cosT, sinT = make_rope(C, st)
        Wh = fw.sb("Wh@%d" % si, [128, 5, 8, 128], BF16, st)
        op("pool", lambda e: e.memset(Wh[:, 3:5, :, :], 0.0), writes=(Wh,))
        qT = fw.sb("qT@%d" % si, [128, S], BF16, st)
        kT = fw.sb("kT@%d" % si, [128, S], BF16, st)
        Vh = fw.sb("Vh@%d" % si, [128, NT, 129], BF16, st)
        op("pool", lambda e: e.memset(Vh[:, :, 128:129], 1.0), writes=(Vh,))
        E = [fw.sb("E%d@%d" % (i, si), [128, 2, BW], BF16, st) for i in range(2)]
        t1s = [fw.sb("t1%d@%d" % (i, si), [128, BW], F32, st) for i in range(2)]
        t2s = [fw.sb("t2%d@%d" % (i, si), [128, BW], F32, st) for i in range(2)]
        rsq = [fw.sb("rs%d@%d" % (i, si), [128, 2], F32, st) for i in range(4)]
        o1q = [fw.sb("o1%d@%d" % (i, si), [128, 128], F32, st) for i in range(4)]
        ssqq = [fw.sb("ssq%d@%d" % (i, si), [128, 1], F32, st) for i in range(4)]
        onq = [fw.sb("on%d@%d" % (i, si), [128, 128], BF16, st) for i in range(4)]
        junk = fw.sb("junkB@%d" % si, [128, 128], F32, st)
        ssq = fw.sb("ssq@%d" % si, [128, 1], F32, st)
        on = fw.sb("on@%d" % si, [128, 128], BF16, st)
        ogT = [fw.sb("ogT%d@%d" % (i, si), [128, BW], BF16, st) for i in range(2)]
        for h in range(8):
            for s_, off in enumerate((O_Q, O_K, O_V)):
                fw.load("pool", Wh, Wh[:, s_, :, :], C.wview(off + h * 128, 128))
            for s_ in (0, 1):
                for m in (0, 1):
                    b0 = m * 64
                    op("pool", lambda e: e.tensor_scalar(out=Wh[:, 3 + s_, :, b0:b0 + 8], in0=Wh[:, s_, :, b0 + 8:b0 + 16],
                                                         scalar1=-1.0, scalar2=None, op0=ALU.mult), reads=(Wh,), writes=(Wh,))
                    op("pool", lambda e: e.tensor_copy(out=Wh[:, 3 + s_, :, b0 + 8:b0 + 16], in_=Wh[:, s_, :, b0:b0 + 8]),
                       reads=(Wh,), writes=(Wh,))
            for tb in range(NB):
                for s_, dst in ((0, qT), (1, kT)):
                    db = (2 * tb + s_) % 4
                    t1, t2 = t1s[s_], t2s[s_]
                    for j, slot in enumerate((s_, 3 + s_)):
                        for kc in range(8):
                            op("pe", lambda e: e.matmul(DBf[db][:, j * 512:j * 512 + BW], lhsT=Wh[:, slot, kc, :], rhs=xblk(kc, tb),
                                                        start=(kc == 0), stop=(kc == 7)), reads=(Wh, xnT), writes=bk(db, False, j))
                    op("dve", lambda e: e.tensor_tensor(out=t1[:], in0=DBf[db][:, 0:BW], in1=cosT[:, tb * BW:(tb + 1) * BW], op=ALU.mult),
                       reads=bk(db, False, 0) + (cosT,), writes=(t1,))
                    op("dve", lambda e: e.tensor_tensor(out=t2[:], in0=DBf[db][:, 512:512 + BW], in1=sinT[:, tb * BW:(tb + 1) * BW], op=ALU.mult),
                       reads=bk(db, False, 1) + (sinT,), writes=(t2,))
                    op("pool", lambda e: e.tensor_tensor(out=dst[:, tb * BW:(tb + 1) * BW], in0=t1[:], in1=t2[:], op=ALU.add),
                       reads=(t1, t2), writes=(dst,))
                vdb = (2 * tb + 2) % 4
                for t4 in range(nq):
                    tt = tb * nq + t4
                    for kc in range(8):
                        op("pe", lambda e: e.matmul(DBf[vdb][:, 512 + t4 * 128:512 + (t4 + 1) * 128], lhsT=xnT[:, kc, 1 + tt * 128:1 + (tt + 1) * 128],
                                                    rhs=Wh[:, 2, kc, :], start=(kc == 0), stop=(kc == 7)),
                           reads=(Wh, xnT), writes=bk(vdb, False, 1))
                op("act", lambda e: e.activation(out=Vh[:, tb * nq:(tb + 1) * nq, 0:128],
                                                 in_=DBf[vdb][:, 512:512 + BW].rearrange("p (a b) -> p a b", b=128), func=AF.Copy),
                   reads=bk(vdb, False, 1), writes=(Vh,))
            for qc in range(NB):
                def scores(kb):
                    sb_ = kb % 2
                    for m in (0, 1):
                        op("pe", lambda e: e.matmul(DBf[sb_][:, m * 512:m * 512 + BW], lhsT=kT[m * 64:(m + 1) * 64, kb * 128:(kb + 1) * 128],
                                                    rhs=qT[m * 64:(m + 1) * 64, qc * BW:(qc + 1) * BW], start=True, stop=True),
                           reads=(kT, qT), writes=bk(sb_, False, m))
                scores(0)
                for kb in range(NT):
                    sb_ = kb % 2
                    if kb + 1 < NT:
                        scores(kb + 1)
                    op("act", lambda e: e.activation(out=E[sb_][:], in_=DBf[sb_][:, :].rearrange("p (m q) -> p m q", m=2)[:, :, 0:BW],
                                                     func=AF.Exp, scale=0.125), reads=bk(sb_), writes=(E[sb_],))
                    for qs in range(nq):
                        dba, hf = 2 + qs // 2, qs % 2
                        for m in (0, 1):
                            off = hf * 512 + m * 129
                            op("pe", lambda e: e.matmul(DBf[dba][:, off:off + 129], lhsT=E[sb_][:, m, qs * 128:(qs + 1) * 128],
                                                        rhs=Vh[:, kb, :], start=(kb == 0 and m == 0), stop=(kb == NT - 1),
                                                        skip_group_check=True),
                               reads=(E[sb_], Vh), writes=bk(dba, False, hf))
                accs = []
                for qs in range(nq):
                    dba, hf = 2 + qs // 2, qs % 2
                    accs.append((DBf[dba][:, hf * 512:hf * 512 + 258].rearrange("p (m c) -> p m c", m=2), bk(dba, False, hf)))
                for qs in range(nq):
                    acc, pbk = accs[qs]
                    op("dve", lambda e: e.reciprocal(out=rsq[qs][:], in_=acc[:, :, 128]), reads=pbk, writes=(rsq[qs],))
                for qs in range(nq):
                    op("dve", lambda e: e.tensor_tensor(out=rsq[qs][:, 1:2], in0=rsq[qs][:, 1:2], in1=C.nlam[:, 0:1], op=ALU.mult),
                       reads=(rsq[qs], C.nlam), writes=(rsq[qs],))
                for qs in range(nq):
                    acc, pbk = accs[qs]
                    op("dve", lambda e: e.tensor_scalar(out=o1q[qs][:], in0=acc[:, 0, 0:128], scalar1=rsq[qs][:, 0:1], scalar2=None, op0=ALU.mult),
                       reads=pbk + (rsq[qs],), writes=(o1q[qs],))
                for qs in range(nq):
                    acc, pbk = accs[qs]
                    op("dve", lambda e: e.scalar_tensor_tensor(out=o1q[qs][:], in0=acc[:, 1, 0:128], scalar=rsq[qs][:, 1:2], in1=o1q[qs][:],
                                                               op0=ALU.mult, op1=ALU.add), reads=pbk + (rsq[qs], o1q[qs]), writes=(o1q[qs],))
                for qs in range(nq):
                    op("act", lambda e: e.activation(out=junk[:], in_=o1q[qs][:], func=AF.Square, accum_out=ssqq[qs][:]),
                       reads=(o1q[qs],), writes=(junk, ssqq[qs]))
                for qs in range(nq):
                    op("act", lambda e: e.activation(out=ssqq[qs][:], in_=ssqq[qs][:], func=AF.Ln, scale=1.0 / 128, bias=C.eps5[:, 0:1]),
                       reads=(ssqq[qs], C.eps5), writes=(ssqq[qs],))
                for qs in range(nq):
                    op("act", lambda e: e.activation(out=ssqq[qs][:], in_=ssqq[qs][:], func=AF.Exp, scale=-0.5), reads=(ssqq[qs],), writes=(ssqq[qs],))
                for qs in range(nq):
                    op("dve", lambda e: e.scalar_tensor_tensor(out=onq[qs][:], in0=o1q[qs][:], scalar=ssqq[qs][:, 0:1], in1=C.sgb[:],
                                                               op0=ALU.mult, op1=ALU.mult), reads=(o1q[qs], ssqq[qs], C.sgb), writes=(onq[qs],))
                for qs in range(nq):
                    op("pe", lambda e: e.transpose(out=DBh[0][:, qs * 128:(qs + 1) * 128], in_=onq[qs][:], identity=ident[:]),
                       reads=(onq[qs], ident), writes=bk(0, False, 0))
                og = ogT[qc % 2]
                op("act", lambda e: e.activation(out=og[:], in_=DBh[0][:, 0:BW], func=AF.Copy), reads=bk(0, False, 0), writes=(og,))
                fw.dma("sp", C.scrA[:, h, qc * BW:(qc + 1) * BW], og[:], _stsem(fw, og), reads=(og,), writes=(C.dA,))
    with contextlib.ExitStack() as st:
        Wg = load_w(C, st, "Wg@%d" % si, C.wview(O_GA, 1024), [128, 8, 1024])
        Woa = load_w(C, st, "Woa@%d" % si, C.P["w_o_attn"].rearrange("(c p) n -> p c n", p=128), [128, 8, 1024])
        Wgm = load_w(C, st, "Wgm@%d" % si, C.wview(O_GMA, 1024), [128, 8, 1024])
        branch_tail(C, st, "a", Wg, Woa, Wgm, C.scrA, C.dA, C.scrM, C.dM, AF.Silu)


def branch_tail(C, st, tag, Wg, Wo, Wgm, src, dsrc, dst, ddst, gate_func):
    fw, op, S, BW, NB, si = C.fw, C.fw.op, C.S, C.BW, C.NB, C.si
    DBf, bk, xnT, xblk = C.DBf, C.bk, C.xnT, C.xblk
    og = [fw.sb("og%s%d@%d" % (tag, i, si), [128, 8, BW], BF16, st) for i in range(2)]
    OG = fw.sb("OG%s@%d" % (tag, si), [128, 8, BW], BF16, st)
    sg = [fw.sb("sg%s%d@%d" % (tag, i, si), [128, BW], F32, st) for i in range(2)]
    mab = [fw.sb("mab%s%d@%d" % (tag, i, si), [128, 8, BW], BF16, st) for i in range(2)]
    for tb in range(NB):
        b = tb % 2
        fw.dma("sp", og[b][:], src[:, :, tb * BW:(tb + 1) * BW], _ldsem(fw, og[b]), reads=(dsrc,), writes=(og[b],))
        if Wg is not None:
            for dc in range(8):
                db = dc % 2
                for kc in range(8):
                    op("pe", lambda e: e.matmul(DBf[db][:, 0:BW], lhsT=Wg[:, kc, dc * 128:(dc + 1) * 128], rhs=xblk(kc, tb),
                                                start=(kc == 0), stop=(kc == 7)), reads=(Wg, xnT), writes=bk(db, False, 0))
                op("act", lambda e: e.activation(out=sg[db][:], in_=DBf[db][:, 0:BW], func=gate_func), reads=bk(db, False, 0), writes=(sg[db],))
                op("dve", lambda e: e.tensor_tensor(out=OG[:, dc, :], in0=og[b][:, dc, :], in1=sg[db][:], op=ALU.mult),
                   reads=(og[b], sg[db]), writes=(OG,))
            G = OG
        else:
            G = og[b]
        for dc in range(8):
            db = 2 + dc % 2
            for hh in range(8):
                op("pe", lambda e: e.matmul(DBf[db][:, 0:BW], lhsT=Wo[:, hh, dc * 128:(dc + 1) * 128], rhs=G[:, hh, :],
                                            start=(hh == 0), stop=(hh == 7)), reads=(Wo, G), writes=bk(db, False, 0))
            for kc in range(8):
                op("pe", lambda e: e.matmul(DBf[db][:, 512:512 + BW], lhsT=Wgm[:, kc, dc * 128:(dc + 1) * 128], rhs=xblk(kc, tb),
                                            start=(kc == 0), stop=(kc == 7)), reads=(Wgm, xnT), writes=bk(db, False, 1))
            sgi = dc % 2
            op("act", lambda e: e.activation(out=sg[sgi][:], in_=DBf[db][:, 512:512 + BW], func=AF.Sigmoid),
               reads=bk(db, False, 1), writes=(sg[sgi],))
            op("dve", lambda e: e.tensor_tensor(out=mab[b][:, dc, :], in0=DBf[db][:, 0:BW], in1=sg[sgi][:], op=ALU.mult),
               reads=bk(db, False, 0) + (sg[sgi],), writes=(mab[b],))
        fw.dma("act", dst[:, :, tb * BW:(tb + 1) * BW], mab[b][:], _stsem(fw, mab[b]), reads=(mab[b],), writes=(ddst,))


def stage_rwkv(C):
    fw, op, S, BW, NB, NT, si = C.fw, C.fw.op, C.S, C.BW, C.NB, C.NT, C.si
    DBf, DBh, bk, xnT, xblk, PB = C.DBf, C.DBh, C.bk, C.xnT, C.xblk, C.PB
    ident, BOh, BOf, I2, onesS = C.ident, C.BOh, C.BOf, C.I2, C.onesS
    wlw, wla = C.wlw, C.wla
    GN_EPS = 64e-5
    CD = math.exp(-0.5)
    import os
    G = int(os.environ.get('RW_G', 4 if S <= 2048 else 3))
    G = min(G, NT)

    def pbank(j):
        return DBf[j // 2][:, (j % 2) * 512:(j % 2) * 512 + 512]

    def pbankh(j):
        return DBh[j // 2][:, (j % 2) * 1024:(j % 2) * 1024 + 1024]

    with contextlib.ExitStack() as st:
        TWT = fw.sb("TWT@%d" % si, [128, S], BF16, st)
        ALT = fw.sb("ALT@%d" % si, [128, S], BF16, st)
        with contextlib.ExitStack() as st0:
            Wlow = load_w(C, st0, "Wlow@%d" % si, C.wview(O_WL, 256), [128, 8, 256])
            for tb in range(NB):
                for j, (dst, func) in enumerate(((TWT, AF.Tanh), (ALT, AF.Copy))):
                    for kc in range(8):
                        op("pe", lambda e: e.matmul(pbank(j)[:, 0:BW], lhsT=Wlow[:, kc, j * 128:(j + 1) * 128], rhs=xblk(kc, tb),
                                                    start=(kc == 0), stop=(kc == 7)), reads=(Wlow, xnT), writes=(PB[j],))
                    op("act", lambda e: e.activation(out=dst[:, tb * BW:(tb + 1) * BW], in_=pbank(j)[:, 0:BW], func=func),
                       reads=(PB[j],), writes=(dst,))
        Whp = fw.sb("Whp@%d" % si, [128, 4, 8, 128], BF16, st)
        RT_ = fw.sb("RT_@%d" % si, [128, S], BF16, st)
        KT_ = fw.sb("KT_@%d" % si, [128, S], BF16, st)
        VT_ = fw.sb("VT_@%d" % si, [128, S], BF16, st)
        KKT = fw.sb("KKT@%d" % si, [128, S], BF16, st)
        OFT = fw.sb("OFT@%d" % si, [128, S], BF16, st)
        OBT = fw.sb("OBT@%d" % si, [128, S], BF16, st)
        CB = min(256, S)

        for hp in range(8):
            hpc = slice(hp, hp + 1)
            for s_, off in enumerate((O_R, O_RK, O_RV, O_GR)):
                fw.load("pool", Whp, Whp[:, s_, :, :], C.wview(off + hp * 128, 128))
            if True:
                if hp == 0:
                    ctmp = fw.sb("ctmp@%d" % si, [128, CB], F32, st)
                    kkr = fw.sb("kkrw@%d" % si, [128, CB], F32, st)
                    nrm = fw.sb("nrmw@%d" % si, [128, CB], F32, st)
                    sqw = fw.sb("sqw@%d" % si, [128, CB], BF16, st)
                for cb in range(S // CB):
                    cs = slice(cb * CB, (cb + 1) * CB)
                    for s_, dst in enumerate((RT_, KT_, VT_)):
                        j = (3 * cb + s_) % 4
                        for kc in range(8):
                            op("pe", lambda e: e.matmul(pbank(j)[:, 0:CB + 2], lhsT=Whp[:, s_, kc, :], rhs=xnT[:, kc, cb * CB:cb * CB + CB + 2],
                                                        start=(kc == 0), stop=(kc == 7)), reads=(Whp, xnT), writes=(PB[j],))
                        cw = [C.cv_c[:, i, s_ * 8 + hp:s_ * 8 + hp + 1] for i in range(3)]
                        op("dve", lambda e: e.tensor_scalar(out=ctmp[:], in0=pbank(j)[:, 0:CB], scalar1=cw[0], scalar2=None, op0=ALU.mult),
                           reads=(PB[j], C.cv_c), writes=(ctmp,))
                        op("dve", lambda e: e.scalar_tensor_tensor(out=ctmp[:], in0=pbank(j)[:, 1:CB + 1], scalar=cw[1], in1=ctmp[:],
                                                                   op0=ALU.mult, op1=ALU.add), reads=(PB[j], C.cv_c, ctmp), writes=(ctmp,))
                        op("dve", lambda e: e.scalar_tensor_tensor(out=dst[:, cs], in0=pbank(j)[:, 2:CB + 2], scalar=cw[2],
                                                                   in1=ctmp[:], op0=ALU.mult, op1=ALU.add),
                           reads=(PB[j], C.cv_c, ctmp), writes=(dst,))
                    j = 4 + cb % 2
                    if os.environ.get('RW_STOP') == 'c1nokk':
                        continue
                    op("dve", lambda e: e.tensor_scalar(out=kkr[:], in0=KT_[:, cs], scalar1=C.kk_c[:, hpc], scalar2=None, op0=ALU.mult),
                       reads=(KT_, C.kk_c), writes=(kkr,))
                    op("pool", lambda e: e.tensor_tensor(out=sqw[:], in0=kkr[:], in1=kkr[:], op=ALU.mult), reads=(kkr,), writes=(sqw,))
                    op("pe", lambda e: e.matmul(pbank(j)[:, 0:CB], lhsT=BOh[:], rhs=sqw[:], start=True, stop=True), reads=(BOh, sqw), writes=(PB[j],))
                    op("dve", lambda e: e.tensor_scalar(out=nrm[:], in0=pbank(j)[:, 0:CB], scalar1=1e-12, scalar2=None, op0=ALU.add),
                       reads=(PB[j],), writes=(nrm,))
                    op("act", lambda e: e.activation(out=nrm[:], in_=nrm[:], func=AF.Sqrt), reads=(nrm,), writes=(nrm,))
                    op("dve", lambda e: e.reciprocal(out=nrm[:], in_=nrm[:]), reads=(nrm,), writes=(nrm,))
                    op("dve", lambda e: e.tensor_tensor(out=KKT[:, cs], in0=kkr[:], in1=nrm[:], op=ALU.mult), reads=(kkr, nrm), writes=(KKT,))
            if True:
                def f32t(n, g):
                    return fw.sb("%s%d@%d" % (n, g, si), [128, 128], F32, st)

                def bft(n, g, shape):
                    return fw.sb("%s%d@%d" % (n, g, si), shape, BF16, st)
                if hp == 0:
                    W = []
                for g in (range(G) if hp == 0 else ()):
                    w = SimpleNamespace()
                    for n in ("sgw", "a_", "cum", "c2", "cex", "e_in", "e_ex", "e_ng", "kdir", "tmpb"):
                        setattr(w, n, f32t(n, g))
                    w.AR = bft("AR", g, [128, 2, 128])
                    w.BK = bft("BK", g, [128, 2, 128])
                    w.TM = bft("TM", g, [128, 4, 128])
                    w.AT = bft("AT", g, [128, 2, 4, 128])
                    w.XY0 = bft("XY0", g, [128, 2, 2, 128])
                    w.XY = [bft("XYa", g, [128, 2, 3, 128]), bft("XYb", g, [128, 2, 3, 128])]
                    w.Yf = bft("Yf", g, [128, 2, 128])
                    w.RP = bft("RP", g, [128, 128])
                    w.MpT = bft("MpT", g, [128, 64])
                    W.append(w)
                if hp == 0:
                    Hs = [fw.sb("H%d@%d" % (i, si), [128, 64], BF16, st) for i in range(2)]

                def head(d, tau, g):
                    w = W[g]
                    bA, bB = 2 * g, 2 * g + 1
                    PA, PBb = PB[bA], PB[bB]
                    dsl = slice(d * 64, (d + 1) * 64)
                    sl = slice(tau * 128, (tau + 1) * 128)
                    rT, kTt, vT, kkT = RT_[:, sl], KT_[:, sl], VT_[:, sl], KKT[:, sl]
                    op("pe", lambda e: e.matmul(pbank(bA)[:, 0:128], lhsT=wlw[dsl, hp * 128:(hp + 1) * 128], rhs=TWT[dsl, sl], start=True, stop=True),
                       reads=(wlw, TWT), writes=(PA,))
                    op("pe", lambda e: e.matmul(pbank(bA)[:, 128:256], lhsT=wla[dsl, hp * 128:(hp + 1) * 128], rhs=ALT[dsl, sl], start=True, stop=True),
                       reads=(wla, ALT), writes=(PA,))
                    yield
                    op("act", lambda e: e.activation(out=w.sgw[:], in_=pbank(bA)[:, 0:128], func=AF.Tanh, bias=C.hw0_c[:, d, hpc], scale=0.5),
                       reads=(PA, C.hw0_c), writes=(w.sgw,))
                    op("act", lambda e: e.activation(out=w.a_[:], in_=pbank(bA)[:, 128:256], func=AF.Tanh, bias=C.ha0_c[:, d, hpc], scale=0.5),
                       reads=(PA, C.ha0_c), writes=(w.a_,))
                    yield
                    op("pool", lambda e: e.tensor_scalar(out=w.kdir[:], in0=w.a_[:], scalar1=C.ka_c[:, hpc], scalar2=C.tmka[:, hpc], op0=ALU.mult, op1=ALU.add),
                       reads=(w.a_, C.ka_c, C.tmka), writes=(w.kdir,))
                    yield
                    op("pool", lambda e: e.tensor_tensor(out=w.kdir[:], in0=kTt, in1=w.kdir[:], op=ALU.mult), reads=(KT_, w.kdir), writes=(w.kdir,))
                    yield
                    op("dve", lambda e: e.tensor_tensor_scan(out=w.cum[:], data0=w.sgw[:], data1=onesS[:], initial=0.0, op0=ALU.add, op1=ALU.add),
                       reads=(onesS, w.sgw), writes=(w.cum,))
                    yield
                    cu = w.cum
                    if d == 1:
                        op("dve", lambda e: e.tensor_scalar(out=w.c2[:], in0=w.cum[:], scalar1=-1.0, scalar2=w.cum[:, 127:128], op0=ALU.mult, op1=ALU.add),
                           reads=(w.cum,), writes=(w.c2,))
                        yield
                        op("dve", lambda e: e.scalar_tensor_tensor(out=w.c2[:], in0=w.c2[:], scalar=1.0, in1=w.sgw[:], op0=ALU.add, op1=ALU.add),
                           reads=(w.c2, w.sgw), writes=(w.c2,))
                        yield
                        cu = w.c2
                    op("dve", lambda e: e.scalar_tensor_tensor(out=w.cex[:], in0=cu[:], scalar=-1.0, in1=w.sgw[:], op0=ALU.add, op1=ALU.subtract),
                       reads=(cu, w.sgw), writes=(w.cex,))
                    yield
                    HC = 0.5 * CD
                    op("act", lambda e: e.activation(out=w.e_in[:], in_=cu[:], func=AF.Exp, scale=-HC), reads=(cu,), writes=(w.e_in,))
                    op("act", lambda e: e.activation(out=w.e_ex[:], in_=w.cex[:], func=AF.Exp, scale=-HC), reads=(w.cex,), writes=(w.e_ex,))
                    op("act", lambda e: e.activation(out=w.e_ng[:], in_=cu[:], func=AF.Exp, scale=HC, bias=C.lnhalf[:, 0:1]), reads=(cu, C.lnhalf), writes=(w.e_ng,))
                    yield
                    op("dve", lambda e: e.scalar_tensor_tensor(out=w.AR[:, 0, :], in0=kkT, scalar=-1.0, in1=w.e_ex[:], op0=ALU.mult, op1=ALU.mult),
                       reads=(KKT, w.e_ex), writes=(w.AR,))
                    yield
                    op("pool", lambda e: e.tensor_tensor(out=w.AR[:, 1, :], in0=rT, in1=w.e_in[:], op=ALU.mult), reads=(RT_, w.e_in, w.AR), writes=(w.AR,))
                    yield
                    op("dve", lambda e: e.scalar_tensor_tensor(out=w.tmpb[:], in0=w.a_[:], scalar=1.0, in1=kkT, op0=ALU.add, op1=ALU.mult),
                       reads=(KKT, w.a_), writes=(w.tmpb,))
                    yield
                    op("pool", lambda e: e.tensor_tensor(out=w.BK[:, 0, :], in0=w.tmpb[:], in1=w.e_ng[:], op=ALU.mult), reads=(w.tmpb, w.e_ng), writes=(w.BK,))
                    op("pool", lambda e: e.tensor_tensor(out=w.BK[:, 1, :], in0=w.kdir[:], in1=w.e_ng[:], op=ALU.mult), reads=(w.kdir, w.e_ng, w.BK), writes=(w.BK,))
                    yield
                    for j, (srcb, srcap) in enumerate(((w.AR, w.AR[:, 0, :]), (w.BK, w.BK[:, 0, :]), (w.BK, w.BK[:, 1, :]), (VT_, vT))):
                        op("pe", lambda e: e.transpose(out=pbankh(bB)[:, j * 128:(j + 1) * 128], in_=srcap, identity=ident[:]),
                           reads=(srcb, ident), writes=(PBb,))
                    yield
                    op("act", lambda e: e.activation(out=w.TM[:], in_=pbankh(bB)[:, 0:512].rearrange("p (a b) -> p a b", a=4), func=AF.Copy),
                       reads=(PBb,), writes=(w.TM,))
                    yield
                    for hh in (0, 1):
                        s = slice(hh * 64, (hh + 1) * 64)
                        op("pe", lambda e: e.matmul(pbank(bA + hh)[:, 0:128], lhsT=w.AR[s, 0, :], rhs=w.BK[s, 0, :], start=True, stop=True),
                           reads=(w.BK, w.AR), writes=(PB[bA + hh],))
                    yield
                    op("dve", lambda e: e.tensor_tensor(out=w.XY0[:, :, 0, :], in0=DBf[g][:, :].rearrange("p (h c) -> p h c", h=2)[:, :, 0:128],
                                                        in1=C.MASKL[d][:], op=ALU.mult), reads=(PA, PBb, C.MASKL[d]), writes=(w.XY0,))
                    yield
                    for hh in (0, 1):
                        s = slice(hh * 64, (hh + 1) * 64)
                        bj = bA + hh
                        arf = w.AR[s, :, :].rearrange("p a b -> p (a b)")
                        op("pe", lambda e: e.matmul(pbank(bj)[:, 0:256], lhsT=w.BK[s, 0, :], rhs=arf, start=True, stop=True),
                           reads=(w.BK, w.AR), writes=(PB[bj],))
                        op("pe", lambda e: e.matmul(pbank(bj)[:, 256:512], lhsT=w.BK[s, 1, :], rhs=arf, start=True, stop=True),
                           reads=(w.BK, w.AR), writes=(PB[bj],))
                    yield
                    op("dve", lambda e: e.tensor_tensor(out=w.AT[:], in0=DBf[g][:, :].rearrange("p (h a b) -> p h a b", h=2, a=4),
                                                        in1=C.MASK8[d][:], op=ALU.mult), reads=(PA, PBb, C.MASK8[d]), writes=(w.AT,))
                    yield
                    for hh in (0, 1):
                        s = slice(hh * 64, (hh + 1) * 64)
                        op("pe", lambda e: e.matmul(pbank(bA)[:, hh * 64:(hh + 1) * 64], lhsT=w.AT[:, hh, 2, :], rhs=w.TM[:, 3, s], start=True, stop=True),
                           reads=(w.AT, w.TM), writes=(PA,))
                    yield
                    op("act", lambda e: e.activation(out=w.XY0[:, :, 1, 64:128], in_=pbank(bA)[:, 0:128].rearrange("p (a b) -> p a b", a=2), func=AF.Copy),
                       reads=(PA, w.XY0), writes=(w.XY0,))
                    op("pool", lambda e: e.tensor_copy(out=w.XY0[:, :, 1, 0:64], in_=w.TM[:, 0, :].rearrange("p (a b) -> p a b", a=2)),
                       reads=(w.TM, w.XY0), writes=(w.XY0,))
                    yield
                    XN = [w.XY0[:, hh, 0, :] for hh in (0, 1)]
                    YK = [w.XY0[:, hh, 1, :] for hh in (0, 1)]
                    XNY = [w.XY0[:, hh, :, :].rearrange("p a b -> p (a b)") for hh in (0, 1)]
                    XT = [w.AT[:, hh, 0, :] for hh in (0, 1)]
                    srcb = (w.XY0, w.AT)
                    for k in range(7):
                        last = (k == 6)
                        for hh in (0, 1):
                            bj = bA + hh
                            if not last:
                                op("pe", lambda e: e.matmul(pbank(bj)[:, 256:384], lhsT=XN[hh], rhs=XT[hh], start=True, stop=True),
                                   reads=srcb, writes=(PB[bj],))
                                op("pe", lambda e: e.matmul(pbank(bj)[:, 0:256], lhsT=XT[hh], rhs=XNY[hh], start=True, stop=False),
                                   reads=srcb, writes=(PB[bj],))
                            else:
                                op("pe", lambda e: e.matmul(pbank(bj)[:, 128:256], lhsT=XT[hh], rhs=YK[hh], start=True, stop=False),
                                   reads=srcb, writes=(PB[bj],))
                            op("pe", lambda e: e.matmul(pbank(bj)[:, 128:256], lhsT=ident[:], rhs=YK[hh], start=False, stop=True),
                               reads=srcb + (ident,), writes=(PB[bj],))
                        yield
                        eng = "act" if k % 2 == 0 else "dve"
                        if not last:
                            nxt = w.XY[k % 2]
                            src = DBf[g][:, :].rearrange("p (h c) -> p h c", h=2)[:, :, 0:384].rearrange("p h (a b) -> p h a b", a=3)
                            if eng == "act":
                                op("act", lambda e: e.activation(out=nxt[:], in_=src, func=AF.Copy), reads=(PA, PBb), writes=(nxt,))
                            else:
                                op("dve", lambda e: e.tensor_copy(out=nxt[:], in_=src), reads=(PA, PBb), writes=(nxt,))
                            XN = [nxt[:, hh, 0, :] for hh in (0, 1)]
                            YK = [nxt[:, hh, 1, :] for hh in (0, 1)]
                            XNY = [nxt[:, hh, 0:2, :].rearrange("p a b -> p (a b)") for hh in (0, 1)]
                            XT = [nxt[:, hh, 2, :] for hh in (0, 1)]
                            srcb = (nxt,)
                        else:
                            src = DBf[g][:, :].rearrange("p (h c) -> p h c", h=2)[:, :, 128:256]
                            op("act", lambda e: e.activation(out=w.Yf[:], in_=src, func=AF.Copy), reads=(PA, PBb), writes=(w.Yf,))
                        yield
                    Yf = w.Yf
                    for hh in (0, 1):
                        s = slice(hh * 64, (hh + 1) * 64)
                        op("pe", lambda e: e.matmul(pbank(bA)[s, 384:512], lhsT=Yf[:, hh, 0:64], rhs=w.AT[:, hh, 1, :], start=True, stop=True),
                           reads=(Yf, w.AT), writes=(PA,))
                        op("pe", lambda e: e.matmul(pbank(bB)[s, 384:448], lhsT=Yf[:, hh, 0:64], rhs=w.TM[:, 1, s], start=True, stop=True),
                           reads=(Yf, w.TM), writes=(PBb,))
                    yield
                    op("dve", lambda e: e.tensor_tensor(out=w.RP[:], in0=pbank(bA)[:, 384:512], in1=w.AR[:, 1, :], op=ALU.add),
                       reads=(PA, w.AR), writes=(w.RP,))
                    op("dve", lambda e: e.tensor_tensor(out=w.MpT[:], in0=pbank(bB)[:, 384:448], in1=I2[:], op=ALU.add), reads=(PBb, I2), writes=(w.MpT,))
                    yield

                def tail(d, tau, g, Hc, Hn):
                    w = W[g]
                    bA, bB = 2 * g, 2 * g + 1
                    PA, PBb = PB[bA], PB[bB]
                    Yf = w.Yf
                    sl = slice(tau * 128, (tau + 1) * 128)
                    for hh in (0, 1):
                        s = slice(hh * 64, (hh + 1) * 64)
                        op("pe", lambda e: e.matmul(pbank(bA)[s, 0:128], lhsT=Yf[:, hh, 64:128], rhs=w.AT[:, hh, 1, :], start=True, stop=False),
                           reads=(Yf, w.AT), writes=(PA,))
                        op("pe", lambda e: e.matmul(pbank(bA)[s, 0:128], lhsT=w.TM[:, 3, s], rhs=w.AT[:, hh, 3, :], start=False, stop=False),
                           reads=(w.TM, w.AT), writes=(PA,))
                        op("pe", lambda e: e.matmul(pbank(bA)[s, 0:128], lhsT=Hc[s, :], rhs=w.RP[s, :], start=False, stop=True),
                           reads=(Hc, w.RP), writes=(PA,))
                    for hh in (0, 1):
                        s = slice(hh * 64, (hh + 1) * 64)
                        op("pe", lambda e: e.matmul(pbank(bB)[s, 0:64], lhsT=w.TM[:, 1, s], rhs=Yf[:, hh, 64:128], start=True, stop=False),
                           reads=(Yf, w.TM), writes=(PBb,))
                        op("pe", lambda e: e.matmul(pbank(bB)[s, 0:64], lhsT=w.TM[:, 2, s], rhs=w.TM[:, 3, s], start=False, stop=False),
                           reads=(w.TM,), writes=(PBb,))
                        op("pe", lambda e: e.matmul(pbank(bB)[s, 0:64], lhsT=w.MpT[s, :], rhs=Hc[s, :], start=False, stop=True),
                           reads=(w.MpT, Hc), writes=(PBb,))
                    WC = w.e_in[:, 127:128] if d == 0 else w.e_in[:, 0:1]
                    op("act", lambda e: e.activation(out=Hn[:], in_=pbank(bB)[:, 0:64], func=AF.Copy, scale=WC), reads=(PBb, w.e_in), writes=(Hn,))
                    dst = OFT if d == 0 else OBT
                    op("dve", lambda e: e.tensor_copy(out=dst[:, sl], in_=pbank(bA)[:, 0:128]), reads=(PA,), writes=(dst,))

                for d in (((0, 1) if 'RW_HEAD' not in os.environ else (0,)) if os.environ.get('RW_STOP') not in ('c1', 'c1nokk') else ()):
                    op("pool", lambda e: e.memset(Hs[0][:], 0.0), writes=(Hs[0],))
                    order = list(range(NT)) if d == 0 else list(range(NT - 1, -1, -1))
                    step = 0
                    DELTA = int(os.environ.get('RW_DELTA', 3))
                    slots = [None] * G
                    state = ['idle'] * G
                    completed = {}
                    next_pos = 0
                    tails_done = 0
                    rnd = 0
                    while tails_done < NT:
                        for gi in range(G):
                            if state[gi] == 'idle':
                                if next_pos < NT and rnd >= gi * DELTA:
                                    slots[gi] = (next_pos, head(d, order[next_pos], gi))
                                    state[gi] = 'run'
                                    next_pos += 1
                                else:
                                    continue
                            if state[gi] == 'run':
                                pos, gen = slots[gi]
                                try:
                                    next(gen)
                                except StopIteration:
                                    completed[pos] = gi
                                    state[gi] = 'wait'
                        while tails_done in completed:
                            gi = completed.pop(tails_done)
                            tail(d, order[tails_done], gi, Hs[step % 2], Hs[(step + 1) % 2])
                            step += 1
                            tails_done += 1
                            state[gi] = 'idle'
                        rnd += 1
            if True:
                PW = min(512, S) if S <= 2048 else 256
                if hp == 0:
                    if PW == CB:
                        o_, cen, sq2 = ctmp, kkr, nrm
                        var, rk_ = [fw.sb("%s@%d" % (n, si), [128, PW], F32, st) for n in ("pvar", "prk")]
                    else:
                        o_, cen, sq2, var, rk_ = [fw.sb("%s@%d" % (n, si), [128, PW], F32, st) for n in ("po_", "pcen", "psq2", "pvar", "prk")]
                for pb_ in range(S // PW):
                    ps = slice(pb_ * PW, (pb_ + 1) * PW)
                    op("dve", lambda e: e.tensor_tensor(out=o_[:], in0=OFT[:, ps], in1=OBT[:, ps], op=ALU.add), reads=(OFT, OBT), writes=(o_,))
                    op("pe", lambda e: e.matmul(pbank(0)[:, 0:PW], lhsT=BOf[:], rhs=o_[:], start=True, stop=True), reads=(BOf, o_), writes=(PB[0],))
                    op("dve", lambda e: e.scalar_tensor_tensor(out=cen[:], in0=pbank(0)[:, 0:PW], scalar=-1.0 / 64, in1=o_[:], op0=ALU.mult, op1=ALU.add),
                       reads=(PB[0], o_), writes=(cen,))
                    op("pool", lambda e: e.tensor_tensor(out=sq2[:], in0=cen[:], in1=cen[:], op=ALU.mult), reads=(cen,), writes=(sq2,))
                    op("pe", lambda e: e.matmul(pbank(1)[:, 0:PW], lhsT=BOf[:], rhs=sq2[:], start=True, stop=True), reads=(BOf, sq2), writes=(PB[1],))
                    op("dve", lambda e: e.tensor_scalar(out=var[:], in0=pbank(1)[:, 0:PW], scalar1=1.0 / 64, scalar2=GN_EPS, op0=ALU.mult, op1=ALU.add),
                       reads=(PB[1],), writes=(var,))
                    op("act", lambda e: e.activation(out=var[:], in_=var[:], func=AF.Sqrt), reads=(var,), writes=(var,))
                    op("dve", lambda e: e.reciprocal(out=var[:], in_=var[:]), reads=(var,), writes=(var,))
                    op("dve", lambda e: e.tensor_tensor(out=cen[:], in0=cen[:], in1=var[:], op=ALU.mult), reads=(cen, var), writes=(cen,))
                    op("dve", lambda e: e.tensor_scalar(out=cen[:], in0=cen[:], scalar1=C.lg_c[:, hpc], scalar2=C.lb_c[:, hpc], op0=ALU.mult, op1=ALU.add),
                       reads=(cen, C.lg_c, C.lb_c), writes=(cen,))
                    op("dve", lambda e: e.scalar_tensor_tensor(out=rk_[:], in0=RT_[:, ps], scalar=C.rk_c[:, hpc], in1=KT_[:, ps], op0=ALU.mult, op1=ALU.mult),
                       reads=(RT_, KT_, C.rk_c), writes=(rk_,))
                    op("pe", lambda e: e.matmul(pbank(2)[:, 0:PW], lhsT=BOf[:], rhs=rk_[:], start=True, stop=True), reads=(BOf, rk_), writes=(PB[2],))
                    op("dve", lambda e: e.tensor_tensor(out=sq2[:], in0=pbank(2)[:, 0:PW], in1=VT_[:, ps], op=ALU.mult), reads=(PB[2], VT_), writes=(sq2,))
                    op("pool", lambda e: e.tensor_tensor(out=cen[:], in0=cen[:], in1=sq2[:], op=ALU.add), reads=(cen, sq2), writes=(cen,))
                    for kc in range(8):
                        op("pe", lambda e: e.matmul(pbank(3)[:, 0:PW], lhsT=Whp[:, 3, kc, :], rhs=xnT[:, kc, 1 + pb_ * PW:1 + (pb_ + 1) * PW],
                                                    start=(kc == 0), stop=(kc == 7)), reads=(Whp, xnT), writes=(PB[3],))
                    op("act", lambda e: e.activation(out=var[:], in_=pbank(3)[:, 0:PW], func=AF.Silu), reads=(PB[3],), writes=(var,))
                    op("dve", lambda e: e.tensor_tensor(out=OFT[:, ps], in0=cen[:], in1=var[:], op=ALU.mult), reads=(cen, var, OFT), writes=(OFT,))
            fw.dma("sp", C.scrR[:, hp, 0:S], OFT[:], _stsem(fw, OFT), reads=(OFT,), writes=(C.dR,))
    with contextlib.ExitStack() as st:
        Wor = load_w(C, st, "Wor@%d" % si, C.P["w_o_rwkv"].rearrange("(c p) n -> p c n", p=128), [128, 8, 1024])
        Wgmb = load_w(C, st, "Wgmb@%d" % si, C.wview(O_GMB, 1024), [128, 8, 1024])
        branch_tail(C, st, "r", None, Wor, Wgmb, C.scrR, C.dR, C.scrR, C.dR, None)


def make_rope(C, st0):
    fw, op, S, si = C.fw, C.fw.op, C.S, C.si
    cosT = fw.sb("cosT@%d" % si, [128, S], BF16, st0)
    sinT = fw.sb("sinT@%d" % si, [128, S], BF16, st0)
    with contextlib.ExitStack() as st:
        pi_ = fw.sb("pi_@%d" % si, [128, 1], I32, st)
        pj_ = fw.sb("pj_@%d" % si, [128, 1], I32, st)
        pf_ = fw.sb("pf_@%d" % si, [128, 2], F32, st)
        invf = fw.sb("invf@%d" % si, [128, 1], F32, st)
        posi = fw.sb("posi@%d" % si, [128, S], I32, st)
        ang = fw.sb("ang@%d" % si, [128, S], F32, st)
        kf = fw.sb("kf@%d" % si, [128, S], F32, st)
        tm = fw.sb("tm@%d" % si, [128, S], F32, st)
        op("pool", lambda e: e.iota(pi_[:], pattern=[[0, 1]], base=0, channel_multiplier=1), writes=(pi_,))
        op("dve", lambda e: e.tensor_scalar(out=pj_[:], in0=pi_[:], scalar1=7, scalar2=None, op0=ALU.bitwise_and),
           reads=(pi_,), writes=(pj_,))
        op("dve", lambda e: e.tensor_copy(out=pf_[:, 0:1], in_=pj_[:]), reads=(pj_,), writes=(pf_,))
        op("dve", lambda e: e.tensor_scalar(out=pj_[:], in0=pi_[:], scalar1=63, scalar2=None, op0=ALU.bitwise_and),
           reads=(pi_,), writes=(pj_,))
        op("dve", lambda e: e.tensor_copy(out=pf_[:, 1:2], in_=pj_[:]), reads=(pj_,), writes=(pf_,))
        op("act", lambda e: e.activation(out=invf[:], in_=pf_[:, 0:1], func=AF.Exp, scale=-math.log(ROPE_THETA) / 8.0),
           reads=(pf_,), writes=(invf,))
        op("dve", lambda e: e.tensor_scalar(out=pf_[:, 1:2], in0=pf_[:, 1:2], scalar1=16.0, scalar2=None, op0=ALU.is_lt),
           reads=(pf_,), writes=(pf_,))
        op("dve", lambda e: e.tensor_tensor(out=invf[:], in0=invf[:], in1=pf_[:, 1:2], op=ALU.mult),
           reads=(invf, pf_), writes=(invf,))
        op("pool", lambda e: e.iota(posi[:], pattern=[[1, S]], base=0, channel_multiplier=0), writes=(posi,))
        op("dve", lambda e: e.tensor_copy(out=ang[:], in_=posi[:]), reads=(posi,), writes=(ang,))
        op("dve", lambda e: e.tensor_scalar(out=ang[:], in0=ang[:], scalar1=invf[:, 0:1], scalar2=None, op0=ALU.mult),
           reads=(ang, invf), writes=(ang,))

        def wrap_sin(dst, shift):
            op("dve", lambda e: e.tensor_scalar(out=tm[:], in0=ang[:], scalar1=shift, scalar2=None, op0=ALU.add),
               reads=(ang,), writes=(tm,))
            op("dve", lambda e: e.tensor_scalar(out=kf[:], in0=tm[:], scalar1=1.0 / TWO_PI, scalar2=None, op0=ALU.mult),
               reads=(tm,), writes=(kf,))
            op("dve", lambda e: e.tensor_copy(out=posi[:], in_=kf[:]), reads=(kf,), writes=(posi,))
            op("dve", lambda e: e.tensor_copy(out=kf[:], in_=posi[:]), reads=(posi,), writes=(kf,))
            op("dve", lambda e: e.scalar_tensor_tensor(out=tm[:], in0=kf[:], scalar=-CW1, in1=tm[:], op0=ALU.mult, op1=ALU.add),
               reads=(kf, tm), writes=(tm,))
            op("dve", lambda e: e.scalar_tensor_tensor(out=tm[:], in0=kf[:], scalar=-CW2, in1=tm[:], op0=ALU.mult, op1=ALU.add),
               reads=(kf, tm), writes=(tm,))
            op("dve", lambda e: e.tensor_scalar(out=kf[:], in0=tm[:], scalar1=math.pi, scalar2=-TWO_PI, op0=ALU.is_gt, op1=ALU.mult),
               reads=(tm,), writes=(kf,))
            op("dve", lambda e: e.tensor_tensor(out=tm[:], in0=tm[:], in1=kf[:], op=ALU.add), reads=(tm, kf), writes=(tm,))
            op("dve", lambda e: e.tensor_scalar(out=kf[:], in0=tm[:], scalar1=-math.pi, scalar2=TWO_PI, op0=ALU.is_lt, op1=ALU.mult),
               reads=(tm,), writes=(kf,))
            op("dve", lambda e: e.tensor_tensor(out=tm[:], in0=tm[:], in1=kf[:], op=ALU.add), reads=(tm, kf), writes=(tm,))
            op("dve", lambda e: e.tensor_scalar(out=tm[:], in0=tm[:], scalar1=3.14159, scalar2=-3.14159, op0=ALU.min, op1=ALU.max),
               reads=(tm,), writes=(tm,))
            op("act", lambda e: e.activation(out=dst[:], in_=tm[:], func=AF.Sin), reads=(tm,), writes=(dst,))
        wrap_sin(sinT, 0.0)
        wrap_sin(cosT, math.pi / 2)

    return cosT, sinT


SEQ_LENS = [2048, 2048, 2048, 2048, 4096]
_NC_CACHE = {}


def kernel(**inputs):
    xp = np.asarray(inputs["x_prompt"], dtype=np.float32)
    xs = np.asarray(inputs["x_sample"], dtype=np.float32)
    n = 8
    if "nc" not in _NC_CACHE:
        _NC_CACHE["nc"] = build(SEQ_LENS)
    nc = _NC_CACHE["nc"]
    shared = {}
    for nme, shp in PARAMS:
        shared[nme] = np.ascontiguousarray(np.asarray(inputs[nme], dtype=np.float32).reshape(shp))
    in_maps = []
    for c in range(n):
        xc = np.concatenate([xp[4 * c:4 * c + 4].reshape(-1, D), xs[c].reshape(-1, D)], axis=0)
        m = {"x": np.ascontiguousarray(xc)}
        m.update(shared)
        in_maps.append(m)
    res = run_bass_kernel_spmd(nc, in_maps, core_ids=list(range(n)))
    yp = np.empty_like(xp)
    ys = np.empty_like(xs)
    for c in range(n):
        yc = np.asarray(res.results[c]["y"])
        yp[4 * c:4 * c + 4] = yc[0:8192].reshape(4, 2048, D)
        ys[c] = yc[8192:12288].reshape(4096, D)
    return (yp, ys)
```

```python
import contextlib
import re
import numpy as np
import concourse.bass as bass
import concourse.mybir as mybir

F32 = mybir.dt.float32
BF16 = mybir.dt.bfloat16
I32 = mybir.dt.int32
AF = mybir.ActivationFunctionType
ALU = mybir.AluOpType
AX = mybir.AxisListType

SEM_ROT = 30000


class Buf:
    __slots__ = ("t", "name", "last_w", "readers", "ld_sem", "st_sem")

    def __init__(self, t, name):
        self.t = t
        self.name = name
        self.last_w = None
        self.readers = {}
        self.ld_sem = None
        self.st_sem = None

    def __getitem__(self, k):
        return self.t[k]


class FW:
    def __init__(self, nc):
        self.nc = nc
        self.stack = contextlib.ExitStack()
        self.eng = {"pe": nc.tensor, "act": nc.scalar, "dve": nc.vector,
                    "pool": nc.gpsimd, "sp": nc.sync}
        self.sems = {}
        self.cur = {}
        self.cnt = {}
        self.waited = {e: {} for e in self.eng}
        self.dma_total = {}
        self.dma_roles = {}
        self.nsem = 0
        self.n_ops = 0
        self.n_waits = 0
        for e in self.eng:
            self._new_eng_sem(e)

    def _alloc_sem(self, name):
        h = self.stack.enter_context(self.nc.semaphore(name))
        key = name
        self.sems[key] = h
        self.nsem += 1
        return key

    def _new_eng_sem(self, e):
        key = self._alloc_sem("s_%s_%d" % (e, self.nsem))
        self.cur[e] = key
        self.cnt[key] = 0

    def dma_sem(self, name):
        role = re.sub(r"@\d+", "", name)
        if role in self.dma_roles:
            return self.dma_roles[role]
        key = self._alloc_sem("d_%s_%d" % (role, self.nsem))
        self.dma_total[key] = 0
        self.dma_roles[role] = key
        return key

    def snapshot(self):
        snap = {}
        for k in self.sems:
            v = self.dma_total[k] if k in self.dma_total else self.cnt.get(k, 0)
            if v > 0:
                snap[k] = v
        return snap

    def sb(self, name, shape, dtype, stack=None):
        self.n_alloc = getattr(self, "n_alloc", 0) + 1
        t = (stack or self.stack).enter_context(self.nc.sbuf_tensor("%s_u%d" % (name.replace("@", "_"), self.n_alloc), list(shape), dtype))
        b = Buf(t, name)
        b.readers = self.snapshot()
        return b

    def ps(self, name, shape, dtype, stack=None):
        t = (stack or self.stack).enter_context(self.nc.psum_tensor(name, list(shape), dtype))
        return Buf(t, name)

    def view(self, buf_or_ap, name):
        return Buf(buf_or_ap, name)

    def _collect(self, e, reads, writes):
        deps = {}

        def add(tok):
            if tok is None:
                return
            k, v = tok
            if k in self.dma_total:
                v = self.dma_total[k]
            if deps.get(k, 0) < v:
                deps[k] = v

        for b in reads:
            add(b.last_w)
        for b in writes:
            add(b.last_w)
            for k, v in b.readers.items():
                add((k, v))
        return deps

    def _emit_waits(self, e, deps):
        eng = self.eng[e]
        w = self.waited[e]
        for k, v in deps.items():
            if e == "pe" and k.startswith("s_pe_"):
                continue
            if w.get(k, 0) >= v:
                continue
            eng.wait_ge(self.sems[k], v)
            w[k] = v
            self.n_waits += 1

    def _mark(self, tok, reads, writes):
        k, v = tok
        for b in writes:
            b.last_w = tok
            b.readers = {}
        for b in reads:
            if b.readers.get(k, 0) < v:
                b.readers[k] = v

    def op(self, e, fn, reads=(), writes=()):
        deps = self._collect(e, reads, writes)
        self._emit_waits(e, deps)
        ins = fn(self.eng[e])
        key = self.cur[e]
        if self.cnt[key] >= SEM_ROT:
            self._new_eng_sem(e)
            key = self.cur[e]
        self.cnt[key] += 1
        ins.then_inc(self.sems[key], 1)
        self._mark((key, self.cnt[key]), reads, writes)
        self.n_ops += 1
        return ins

    def dma(self, q, out, in_, sem, reads=(), writes=(), **kw):
        deps = self._collect(q, reads, writes)
        self._emit_waits(q, deps)
        ins = self.eng[q].dma_start(out=out, in_=in_, **kw)
        self.dma_total[sem] += 16
        ins.then_inc(self.sems[sem], 16)
        self._mark((sem, self.dma_total[sem]), reads, writes)
        self.n_ops += 1
        return ins

    def load(self, q, buf, out_ap, in_ap, **kw):
        if buf.ld_sem is None:
            buf.ld_sem = self.dma_sem("l" + buf.name)
        return self.dma(q, out_ap, in_ap, buf.ld_sem, reads=(), writes=(buf,), **kw)

    def store(self, q, buf, out_ap, in_ap, **kw):
        if buf.st_sem is None:
            buf.st_sem = self.dma_sem("s" + buf.name)
        return self.dma(q, out_ap, in_ap, buf.st_sem, reads=(buf,), writes=(), **kw)

    def final_wait(self, e="sp"):
        eng = self.eng[e]
        for k, h in self.sems.items():
            v = self.dma_total[k] if k in self.dma_total else self.cnt.get(k, 0)
            if v > 0 and self.waited[e].get(k, 0) < v:
                eng.wait_ge(h, v)
                self.waited[e][k] = v


import math
from types import SimpleNamespace
import contextlib
import numpy as np
import concourse.bass as bass
import concourse.mybir as mybir
from concourse.bass_utils import run_bass_kernel_spmd

D = 1024
PIN = 10496
O_Q, O_K, O_V, O_GA, O_R, O_RK, O_RV, O_GR, O_WL, O_AL, O_GMA, O_GMB = (
    0, 1024, 2048, 3072, 4096, 5120, 6144, 7168, 8192, 8320, 8448, 9472)
ROPE_THETA = 500000.0
TWO_PI = 2.0 * math.pi
CW1 = 6.28125
CW2 = TWO_PI - CW1

PARAMS = [("norm_g", [1, 1024]), ("w_in", [1024, PIN]), ("conv_rkv", [3, 3072]),
          ("lam_q1", [1, 64]), ("lam_k1", [1, 64]), ("lam_q2", [1, 64]), ("lam_k2", [1, 64]),
          ("attn_subln_g", [1, 128]), ("w_lora_up", [128, 1024]), ("w0", [2, 1024]),
          ("a_lora_up", [128, 1024]), ("a0", [2, 1024]), ("k_k", [1, 1024]), ("k_a", [1, 1024]),
          ("r_k", [1, 1024]), ("ln_x_g", [1, 1024]), ("ln_x_b", [1, 1024]),
          ("w_o_attn", [1024, 1024]), ("w_o_rwkv", [1024, 1024]), ("w_out", [1024, 1024]),
          ("final_g", [1, 1024])]


def build(seq_lens, do_attn=True, do_rwkv=True):
    nc = bass.Bass("TRN2", target_bir_lowering=False)
    TOT = sum(seq_lens)
    SMAX = max(seq_lens)
    x = nc.dram_tensor("x", [TOT, D], F32, kind="ExternalInput").ap()
    y = nc.dram_tensor("y", [TOT, D], F32, kind="ExternalOutput").ap()
    P = {n: nc.dram_tensor(n, s, F32, kind="ExternalInput").ap() for n, s in PARAMS}
    w_in = P["w_in"]
    scrA = nc.dram_tensor("scrA", [128, 8, SMAX], BF16, kind="Internal").ap()
    scrM = nc.dram_tensor("scrM", [128, 8, SMAX], BF16, kind="Internal").ap()
    scrR = nc.dram_tensor("scrR", [128, 8, SMAX], BF16, kind="Internal").ap()
    fw = FW(nc)
    op = fw.op
    ncd = nc.allow_non_contiguous_dma(reason="small param layouts")
    ncd.__enter__()

    def wview(off, ncols):
        return w_in[:, off:off + ncols].rearrange("(c p) n -> p c n", p=128)

    with fw.stack:
        dA, dM, dR = fw.view(scrA, "scrA"), fw.view(scrM, "scrM"), fw.view(scrR, "scrR")
        DBt = [fw.stack.enter_context(nc.psum_tensor("db%d" % i, [128, 1024], F32)) for i in range(4)]
        DBf = [t[:] for t in DBt]
        DBh = [t[:].bitcast(BF16) for t in DBt]
        PB = [Buf(None, "bank%d" % j) for j in range(8)]

        def bk(i, both=True, half=0):
            return (PB[2 * i], PB[2 * i + 1]) if both else (PB[2 * i + half],)

        psem = fw.dma_sem("params")
        psem2 = fw.dma_sem("paramsq")

        def pload(name, shape, src, q="sp", dtype=F32):
            b = fw.sb(name, shape, dtype)
            b.ld_sem = psem if q == "sp" else psem2
            fw.load(q, b, b[:], src)
            return b

        def colparam(name, nm):
            return pload(name, [128, 8], P[nm].rearrange("o (c p) -> p (o c)", p=128))

        gcol = colparam("gcol", "norm_g")
        kk_c = colparam("kk_c", "k_k")
        ka_c = colparam("ka_c", "k_a")
        rk_c = colparam("rk_c", "r_k")
        lg_c = colparam("lg_c", "ln_x_g")
        lb_c = colparam("lb_c", "ln_x_b")
        w0_c = pload("w0_c", [128, 2, 8], P["w0"].rearrange("d (c p) -> p d c", p=128))
        a0_c = pload("a0_c", [128, 2, 8], P["a0"].rearrange("d (c p) -> p d c", p=128))
        cv_c = pload("cv_c", [128, 3, 24], P["conv_rkv"].rearrange("i (c p) -> p i c", p=128))
        fgb = pload("fgb", [128, 1024], P["final_g"][0, :].partition_broadcast(128))
        sgb = pload("sgb", [128, 128], P["attn_subln_g"][0, :].partition_broadcast(128))
        lq = [pload("lam%d" % i, [128, 64], P[n][0, :].partition_broadcast(128))
              for i, n in enumerate(["lam_q1", "lam_k1", "lam_q2", "lam_k2"])]
        wlw = pload("wlw", [128, 1024], P["w_lora_up"], q="pool", dtype=BF16)
        wla = pload("wla", [128, 1024], P["a_lora_up"], q="pool", dtype=BF16)

        ident = fw.sb("ident", [128, 128], BF16)
        op("pool", lambda e: e.memset(ident[:], 1.0), writes=(ident,))
        op("pool", lambda e: e.affine_select(out=ident[:], in_=ident[:], pattern=[[-1, 128]], compare_op=ALU.is_equal,
                                             fill=0.0, base=0, channel_multiplier=1), reads=(ident,), writes=(ident,))
        grep = fw.sb("grep", [128, 8, 128], F32)
        op("dve", lambda e: e.memset(grep[:], 1.0), writes=(grep,))
        for c in range(8):
            op("dve", lambda e: e.tensor_scalar(out=grep[:, c, :], in0=grep[:, c, :], scalar1=gcol[:, c:c + 1],
                                                scalar2=None, op0=ALU.mult), reads=(grep, gcol), writes=(grep,))
        op("dve", lambda e: e.tensor_scalar(out=sgb[:], in0=sgb[:], scalar1=0.8, scalar2=None, op0=ALU.mult),
           reads=(sgb,), writes=(sgb,))
        hw0_c = fw.sb("hw0_c", [128, 2, 8], F32)
        ha0_c = fw.sb("ha0_c", [128, 2, 8], F32)
        op("dve", lambda e: e.tensor_scalar(out=hw0_c[:], in0=w0_c[:], scalar1=0.5, scalar2=None, op0=ALU.mult), reads=(w0_c,), writes=(hw0_c,))
        op("dve", lambda e: e.tensor_scalar(out=ha0_c[:], in0=a0_c[:], scalar1=0.5, scalar2=None, op0=ALU.mult), reads=(a0_c,), writes=(ha0_c,))
        tmka = fw.sb("tmka", [128, 8], F32)
        op("dve", lambda e: e.tensor_scalar(out=tmka[:], in0=ka_c[:], scalar1=-1.0, scalar2=2.0, op0=ALU.mult, op1=ALU.add),
           reads=(ka_c,), writes=(tmka,))
        eps5 = fw.sb("eps5", [128, 1], F32)
        op("dve", lambda e: e.memset(eps5[:], 1e-5), writes=(eps5,))
        lnhalf = fw.sb("lnhalf", [128, 1], F32)
        op("dve", lambda e: e.memset(lnhalf[:], math.log(0.5)), writes=(lnhalf,))
        omka = fw.sb("omka", [128, 8], F32)
        op("dve", lambda e: e.tensor_scalar(out=omka[:], in0=ka_c[:], scalar1=-1.0, scalar2=1.0, op0=ALU.mult, op1=ALU.add),
           reads=(ka_c,), writes=(omka,))
        lt = fw.sb("lt", [128, 64], F32)
        ls = fw.sb("ls", [128, 2], F32)
        nlam = fw.sb("nlam", [128, 1], F32)
        for i in range(2):
            op("dve", lambda e: e.tensor_tensor(out=lt[:], in0=lq[2 * i][:], in1=lq[2 * i + 1][:], op=ALU.mult),
               reads=(lq[2 * i], lq[2 * i + 1]), writes=(lt,))
            op("dve", lambda e: e.reduce_sum(out=ls[:, i:i + 1], in_=lt[:], axis=AX.X), reads=(lt,), writes=(ls,))
        op("act", lambda e: e.activation(out=ls[:], in_=ls[:], func=AF.Exp), reads=(ls,), writes=(ls,))
        op("dve", lambda e: e.tensor_tensor(out=nlam[:], in0=ls[:, 1:2], in1=ls[:, 0:1], op=ALU.subtract),
           reads=(ls,), writes=(nlam,))
        op("dve", lambda e: e.tensor_scalar(out=nlam[:], in0=nlam[:], scalar1=-0.2, scalar2=None, op0=ALU.add),
           reads=(nlam,), writes=(nlam,))

        def mkmask(name, mult_f, mult_p, base, cmp):
            m = fw.sb(name, [128, 128], BF16)
            op("pool", lambda e: e.memset(m[:], 1.0), writes=(m,))
            op("pool", lambda e: e.affine_select(out=m[:], in_=m[:], pattern=[[mult_f, 128]], compare_op=cmp,
                                                 fill=0.0, base=base, channel_multiplier=mult_p), reads=(m,), writes=(m,))
            return m
        Us = mkmask("Us", 1, -1, 0, ALU.is_gt)
        Ui = mkmask("Ui", 1, -1, 0, ALU.is_ge)
        Ls = mkmask("Ls", -1, 1, 0, ALU.is_gt)
        Li = mkmask("Li", -1, 1, 0, ALU.is_ge)
        MASK4 = []
        MASK8 = []
        MASKL = []
        for d_ in range(2):
            m4 = fw.sb("m4_%d" % d_, [128, 4, 128], BF16)
            s_, i_ = (Us, Ui) if d_ == 0 else (Ls, Li)
            for j, src in enumerate([s_, i_, s_, i_]):
                op("pool", lambda e: e.tensor_copy(out=m4[:, j, :], in_=src[:]), reads=(src,), writes=(m4,))
            MASK4.append(m4)
            m8 = fw.sb("m8_%d" % d_, [128, 2, 4, 128], BF16)
            for h_ in range(2):
                for j, src in enumerate([s_, i_, s_, i_]):
                    op("pool", lambda e: e.tensor_copy(out=m8[:, h_, j, :], in_=src[:]), reads=(src,), writes=(m8,))
            MASK8.append(m8)
            ml = fw.sb("ml_%d" % d_, [128, 2, 128], BF16)
            src = Ls if d_ == 0 else Us
            for j in range(2):
                op("pool", lambda e: e.tensor_copy(out=ml[:, j, :], in_=src[:]), reads=(src,), writes=(ml,))
            MASKL.append(ml)
        BOh = fw.sb("BOh", [128, 128], BF16)
        BOf = fw.sb("BOf", [128, 128], F32)
        for t_ in (BOh, BOf):
            op("pool", lambda e: e.memset(t_[:], 0.0), writes=(t_,))
            op("pool", lambda e: e.memset(t_[0:64, 0:64], 1.0), reads=(t_,), writes=(t_,))
            op("pool", lambda e: e.memset(t_[64:128, 64:128], 1.0), reads=(t_,), writes=(t_,))
        I2 = fw.sb("I2", [128, 64], F32)
        op("pool", lambda e: e.memset(I2[:], 1.0), writes=(I2,))
        op("pool", lambda e: e.affine_select(out=I2[0:64, :], in_=I2[0:64, :], pattern=[[-1, 64]], compare_op=ALU.is_equal,
                                             fill=0.0, base=0, channel_multiplier=1), reads=(I2,), writes=(I2,))
        op("pool", lambda e: e.affine_select(out=I2[64:128, :], in_=I2[64:128, :], pattern=[[-1, 64]], compare_op=ALU.is_equal,
                                             fill=0.0, base=0, channel_multiplier=1), reads=(I2,), writes=(I2,))
        onesS = fw.sb("onesS", [128, 128], F32)
        op("pool", lambda e: e.memset(onesS[:], 1.0), writes=(onesS,))

        def proj_fm(dst_fn, W, wslot_fn, xnT, S, evac):
            pass

        tok0 = 0
        for si, S in enumerate(seq_lens):
            NT = S // 128
            NB = S // 512 if S >= 512 else 1
            BW = min(512, S)
            with contextlib.ExitStack() as sq:
                xnT = fw.sb("xnT@%d" % si, [128, 8, S + 2], BF16, sq)
                op("pool", lambda e: e.memset(xnT[:, :, 0:1], 0.0), writes=(xnT,))
                op("pool", lambda e: e.memset(xnT[:, :, S + 1:S + 2], 0.0), writes=(xnT,))
                with contextlib.ExitStack() as sa:
                    xt = [fw.sb("xt%d@%d" % (i, si), [128, 1024], F32, sa) for i in range(2)]
                    junk = fw.sb("junkA@%d" % si, [128, 1024], F32, sa)
                    xs = [fw.sb("xs%d@%d" % (i, si), [128, 1024], BF16, sa) for i in range(2)]
                    ssA = [fw.sb("ssA%d@%d" % (i, si), [128, 1], F32, sa) for i in range(2)]
                    for tt in range(NT):
                        b = tt % 2
                        fw.load("sp", xt[b], xt[b][:], x[tok0 + tt * 128: tok0 + (tt + 1) * 128, :])
                        op("act", lambda e: e.activation(out=junk[:], in_=xt[b][:], func=AF.Square, accum_out=ssA[b][:]),
                           reads=(xt[b],), writes=(junk, ssA[b]))
                        op("dve", lambda e: e.tensor_scalar(out=ssA[b][:], in0=ssA[b][:], scalar1=1.0 / 1024, scalar2=1e-6,
                                                            op0=ALU.mult, op1=ALU.add), reads=(ssA[b],), writes=(ssA[b],))
                        op("act", lambda e: e.activation(out=ssA[b][:], in_=ssA[b][:], func=AF.Sqrt), reads=(ssA[b],), writes=(ssA[b],))
                        op("dve", lambda e: e.reciprocal(out=ssA[b][:], in_=ssA[b][:]), reads=(ssA[b],), writes=(ssA[b],))
                        op("act", lambda e: e.activation(out=xs[b][:], in_=xt[b][:], func=AF.Copy, scale=ssA[b][:, 0:1]),
                           reads=(xt[b], ssA[b]), writes=(xs[b],))
                        db = tt % 2
                        for c in range(8):
                            op("pe", lambda e: e.transpose(out=DBh[db][:, c * 128:(c + 1) * 128], in_=xs[b][:, c * 128:(c + 1) * 128],
                                                           identity=ident[:]), reads=(xs[b], ident), writes=bk(db, False, 0))
                        op("dve", lambda e: e.tensor_tensor(out=xnT[:, :, 1 + tt * 128: 1 + (tt + 1) * 128],
                                                            in0=DBh[db][:, 0:1024].rearrange("p (c t) -> p c t", c=8),
                                                            in1=grep[:], op=ALU.mult),
                           reads=bk(db, False, 0) + (grep,), writes=(xnT,))

                def xblk(kc, tb):
                    return xnT[:, kc, 1 + tb * BW: 1 + (tb + 1) * BW]

                C = SimpleNamespace(**locals())
                if do_attn:
                    stage_attn(C)
                if do_rwkv:
                    stage_rwkv(C)
                stage_out(C)
            tok0 += S
        fw.final_wait("sp")
    ncd.__exit__(None, None, None)
    return nc


def load_w(C, st, name, src_ap, shape):
    b = C.fw.sb(name, shape, BF16, st)
    C.fw.load("pool", b, b[:], src_ap)
    return b


def stage_out(C):
    fw, op, S, BW, NB, si = C.fw, C.fw.op, C.S, C.BW, C.NB, C.si
    DBf, bk = C.DBf, C.bk
    with contextlib.ExitStack() as st:
        Wout = load_w(C, st, "Wout@%d" % si, C.P["w_out"].rearrange("(c p) n -> p c n", p=128), [128, 8, 1024])
        ma = [fw.sb("ma%d@%d" % (i, si), [128, 8, BW], BF16, st) for i in range(2)]
        mr = [fw.sb("mr%d@%d" % (i, si), [128, 8, BW], BF16, st) for i in range(2)]
        xt = [fw.sb("xo%d@%d" % (i, si), [128, 1024], F32, st) for i in range(2)]
        zt = [fw.sb("zt%d@%d" % (i, si), [128, 1024], F32, st) for i in range(2)]
        junk = fw.sb("junkO@%d" % si, [128, 1024], F32, st)
        ss = [fw.sb("ssO%d@%d" % (i, si), [128, 1], F32, st) for i in range(2)]
        cnt = 0
        for tb in range(NB):
            b = tb % 2
            m = None
            if C.do_attn:
                fw.dma("sp", ma[b][:], C.scrM[:, :, tb * BW:(tb + 1) * BW], _ldsem(fw, ma[b]), reads=(C.dM,), writes=(ma[b],))
                m = ma[b]
            if C.do_rwkv:
                fw.dma("sp", mr[b][:], C.scrR[:, :, tb * BW:(tb + 1) * BW], _ldsem(fw, mr[b]), reads=(C.dR,), writes=(mr[b],))
                if m is None:
                    m = mr[b]
                else:
                    op("pool", lambda e: e.tensor_tensor(out=ma[b][:], in0=ma[b][:], in1=mr[b][:], op=ALU.add),
                       reads=(ma[b], mr[b]), writes=(ma[b],))
            for t4 in range(BW // 128):
                tt = tb * (BW // 128) + t4
                xb = cnt % 2
                db = 2 + cnt % 2
                cnt += 1
                fw.load("sp", xt[xb], xt[xb][:], C.x[C.tok0 + tt * 128: C.tok0 + (tt + 1) * 128, :])
                if m is not None:
                    for half in range(2):
                        for dc in range(8):
                            op("pe", lambda e: e.matmul(DBf[db][:, half * 512:(half + 1) * 512], lhsT=m[:, dc, t4 * 128:(t4 + 1) * 128],
                                                        rhs=Wout[:, dc, half * 512:(half + 1) * 512], start=(dc == 0), stop=(dc == 7)),
                               reads=(m, Wout), writes=bk(db, False, half))
                    op("dve", lambda e: e.tensor_tensor(out=zt[xb][:], in0=DBf[db][:], in1=xt[xb][:], op=ALU.add),
                       reads=bk(db) + (xt[xb],), writes=(zt[xb],))
                else:
                    op("dve", lambda e: e.tensor_copy(out=zt[xb][:], in_=xt[xb][:]), reads=(xt[xb],), writes=(zt[xb],))
                op("act", lambda e: e.activation(out=junk[:], in_=zt[xb][:], func=AF.Square, accum_out=ss[xb][:]),
                   reads=(zt[xb],), writes=(junk, ss[xb]))
                op("dve", lambda e: e.tensor_scalar(out=ss[xb][:], in0=ss[xb][:], scalar1=1.0 / 1024, scalar2=1e-6,
                                                    op0=ALU.mult, op1=ALU.add), reads=(ss[xb],), writes=(ss[xb],))
                op("act", lambda e: e.activation(out=ss[xb][:], in_=ss[xb][:], func=AF.Sqrt), reads=(ss[xb],), writes=(ss[xb],))
                op("dve", lambda e: e.reciprocal(out=ss[xb][:], in_=ss[xb][:]), reads=(ss[xb],), writes=(ss[xb],))
                op("dve", lambda e: e.scalar_tensor_tensor(out=zt[xb][:], in0=zt[xb][:], scalar=ss[xb][:, 0:1], in1=C.fgb[:],
                                                           op0=ALU.mult, op1=ALU.mult), reads=(zt[xb], ss[xb], C.fgb), writes=(zt[xb],))
                fw.store("act", zt[xb], C.y[C.tok0 + tt * 128: C.tok0 + (tt + 1) * 128, :], zt[xb][:])


def _ldsem(fw, b):
    if b.ld_sem is None:
        b.ld_sem = fw.dma_sem("l" + b.name)
    return b.ld_sem


def _stsem(fw, b):
    if b.st_sem is None:
        b.st_sem = fw.dma_sem("s" + b.name)
    return b.st_sem


def stage_attn(C):
    fw, op, S, BW, NB, NT, si = C.fw, C.fw.op, C.S, C.BW, C.NB, C.NT, C.si
    DBf, DBh, bk, xnT, xblk = C.DBf, C.DBh, C.bk, C.xnT, C.xblk
    ident = C.ident
    nq = BW // 128
    with contextlib.ExitStack() as st:
        cosT, sinT = make_rope(C, st)
        Whs = [fw.sb("Wh%d@%d" % (i, si), [128, 5, 8, 128], BF16, st) for i in range(2)]
        for Wh_ in Whs:
            op("pool", lambda e: e.memset(Wh_[:, 3:5, :, :], 0.0), writes=(Wh_,))

        def load_head_w(h):
            Wh_ = Whs[h % 2]
            for s_, off in enumerate((O_Q, O_K, O_V)):
                fw.load("pool", Wh_, Wh_[:, s_, :, :], C.wview(off + h * 128, 128))
            for s_ in (0, 1):
                for m in (0, 1):
                    b0 = m * 64
                    op("pool", lambda e: e.tensor_scalar(out=Wh_[:, 3 + s_, :, b0:b0 + 8], in0=Wh_[:, s_, :, b0 + 8:b0 + 16],
                                                         scalar1=-1.0, scalar2=None, op0=ALU.mult), reads=(Wh_,), writes=(Wh_,))
                    op("pool", lambda e: e.tensor_copy(out=Wh_[:, 3 + s_, :, b0 + 8:b0 + 16], in_=Wh_[:, s_, :, b0:b0 + 8]),
                       reads=(Wh_,), writes=(Wh_,))
        load_head_w(0)
        qT = fw.sb("qT@%d" % si, [128, S], BF16, st)
        kT = fw.sb("kT@%d" % si, [128, S], BF16, st)
        Vh = fw.sb("Vh@%d" % si, [128, NT, 129], BF16, st)
        op("pool", lambda e: e.memset(Vh[:, :, 128:129], 1.0), writes=(Vh,))
        E = [fw.sb("E%d@%d" % (i, si), [128, 2, BW], BF16, st) for i in range(2)]
        t1s = [fw.sb("t1%d@%d" % (i, si), [128, BW], F32, st) for i in range(2)]
        t2s = [fw.sb("t2%d@%d" % (i, si), [128, BW], F32, st) for i in range(2)]
        rsq = [fw.sb("rs%d@%d" % (i, si), [128, 2], F32, st) for i in range(4)]
        o1q = [fw.sb("o1%d@%d" % (i, si), [128, 128], F32, st) for i in range(4)]
        ssqq = [fw.sb("ssq%d@%d" % (i, si), [128, 1], F32, st) for i in range(4)]
        onq = [fw.sb("on%d@%d" % (i, si), [128, 128], BF16, st) for i in range(4)]
        junk = fw.sb("junkB@%d" % si, [128, 128], F32, st)
        ssq = fw.sb("ssq@%d" % si, [128, 1], F32, st)
        on = fw.sb("on@%d" % si, [128, 128], BF16, st)
        ogT = [fw.sb("ogT%d@%d" % (i, si), [128, BW], BF16, st) for i in range(2)]
        for h in range(8):
            Wh = Whs[h % 2]
            for tb in range(NB):
                for s_, dst in ((0, qT), (1, kT)):
                    db = (2 * tb + s_) % 4
                    t1, t2 = t1s[s_], t2s[s_]
                    for j, slot in enumerate((s_, 3 + s_)):
                        for kc in range(8):
                            op("pe", lambda e: e.matmul(DBf[db][:, j * 512:j * 512 + BW], lhsT=Wh[:, slot, kc, :], rhs=xblk(kc, tb),
                                                        start=(kc == 0), stop=(kc == 7)), reads=(Wh, xnT), writes=bk(db, False, j))
                    op("dve", lambda e: e.tensor_tensor(out=t1[:], in0=DBf[db][:, 0:BW], in1=cosT[:, tb * BW:(tb + 1) * BW], op=ALU.mult),
                       reads=bk(db, False, 0) + (cosT,), writes=(t1,))
                    op("dve", lambda e: e.tensor_tensor(out=t2[:], in0=DBf[db][:, 512:512 + BW], in1=sinT[:, tb * BW:(tb + 1) * BW], op=ALU.mult),
                       reads=bk(db, False, 1) + (sinT,), writes=(t2,))
                    op("pool", lambda e: e.tensor_tensor(out=dst[:, tb * BW:(tb + 1) * BW], in0=t1[:], in1=t2[:], op=ALU.add),
                       reads=(t1, t2), writes=(dst,))
                vdb = (2 * tb + 2) % 4
                for t4 in range(nq):
                    tt = tb * nq + t4
                    for kc in range(8):
                        op("pe", lambda e: e.matmul(DBf[vdb][:, 512 + t4 * 128:512 + (t4 + 1) * 128], lhsT=xnT[:, kc, 1 + tt * 128:1 + (tt + 1) * 128],
                                                    rhs=Wh[:, 2, kc, :], start=(kc == 0), stop=(kc == 7)),
                           reads=(Wh, xnT), writes=bk(vdb, False, 1))
                op("act", lambda e: e.activation(out=Vh[:, tb * nq:(tb + 1) * nq, 0:128],
                                                 in_=DBf[vdb][:, 512:512 + BW].rearrange("p (a b) -> p a b", b=128), func=AF.Copy),
                   reads=bk(vdb, False, 1), writes=(Vh,))
            if h + 1 < 8:
                load_head_w(h + 1)
            for qc in range(NB):
                def scores(kb):
                    sb_ = kb % 2
                    for m in (0, 1):
                        op("pe", lambda e: e.matmul(DBf[sb_][:, m * 512:m * 512 + BW], lhsT=kT[m * 64:(m + 1) * 64, kb * 128:(kb + 1) * 128],
                                                    rhs=qT[m * 64:(m + 1) * 64, qc * BW:(qc + 1) * BW], start=True, stop=True),
                           reads=(kT, qT), writes=bk(sb_, False, m))
                scores(0)
                for kb in range(NT):
                    sb_ = kb % 2
                    if kb + 1 < NT:
                        scores(kb + 1)
                    op("act", lambda e: e.activation(out=E[sb_][:], in_=DBf[sb_][:, :].rearrange("p (m q) -> p m q", m=2)[:, :, 0:BW],
                                                     func=AF.Exp, scale=0.125), reads=bk(sb_), writes=(E[sb_],))
                    for qs in range(nq):
                        dba, hf = 2 + qs // 2, qs % 2
                        for m in (0, 1):
                            off = hf * 512 + m * 129
                            op("pe", lambda e: e.matmul(DBf[dba][:, off:off + 129], lhsT=E[sb_][:, m, qs * 128:(qs + 1) * 128],
                                                        rhs=Vh[:, kb, :], start=(kb == 0 and m == 0), stop=(kb == NT - 1),
                                                        skip_group_check=True),
                               reads=(E[sb_], Vh), writes=bk(dba, False, hf))
                accs = []
                for qs in range(nq):
                    dba, hf = 2 + qs // 2, qs % 2
                    accs.append((DBf[dba][:, hf * 512:hf * 512 + 258].rearrange("p (m c) -> p m c", m=2), bk(dba, False, hf)))
                for qs in range(nq):
                    acc, pbk = accs[qs]
                    op("dve", lambda e: e.reciprocal(out=rsq[qs][:], in_=acc[:, :, 128]), reads=pbk, writes=(rsq[qs],))
                for qs in range(nq):
                    op("dve", lambda e: e.tensor_tensor(out=rsq[qs][:, 1:2], in0=rsq[qs][:, 1:2], in1=C.nlam[:, 0:1], op=ALU.mult),
                       reads=(rsq[qs], C.nlam), writes=(rsq[qs],))
                for qs in range(nq):
                    acc, pbk = accs[qs]
                    op("dve", lambda e: e.tensor_scalar(out=o1q[qs][:], in0=acc[:, 0, 0:128], scalar1=rsq[qs][:, 0:1], scalar2=None, op0=ALU.mult),
                       reads=pbk + (rsq[qs],), writes=(o1q[qs],))
                for qs in range(nq):
                    acc, pbk = accs[qs]
                    op("dve", lambda e: e.scalar_tensor_tensor(out=o1q[qs][:], in0=acc[:, 1, 0:128], scalar=rsq[qs][:, 1:2], in1=o1q[qs][:],
                                                               op0=ALU.mult, op1=ALU.add), reads=pbk + (rsq[qs], o1q[qs]), writes=(o1q[qs],))
                for qs in range(nq):
                    op("act", lambda e: e.activation(out=junk[:], in_=o1q[qs][:], func=AF.Square, accum_out=ssqq[qs][:]),
                       reads=(o1q[qs],), writes=(junk, ssqq[qs]))
                for qs in range(nq):
                    op("act", lambda e: e.activation(out=ssqq[qs][:], in_=ssqq[qs][:], func=AF.Ln, scale=1.0 / 128, bias=C.eps5[:, 0:1]),
                       reads=(ssqq[qs], C.eps5), writes=(ssqq[qs],))
                for qs in range(nq):
                    op("act", lambda e: e.activation(out=ssqq[qs][:], in_=ssqq[qs][:], func=AF.Exp, scale=-0.5), reads=(ssqq[qs],), writes=(ssqq[qs],))
                for qs in range(nq):
                    op("dve", lambda e: e.scalar_tensor_tensor(out=onq[qs][:], in0=o1q[qs][:], scalar=ssqq[qs][:, 0:1], in1=C.sgb[:],
                                                               op0=ALU.mult, op1=ALU.mult), reads=(o1q[qs], ssqq[qs], C.sgb), writes=(onq[qs],))
                for qs in range(nq):
                    op("pe", lambda e: e.transpose(out=DBh[0][:, qs * 128:(qs + 1) * 128], in_=onq[qs][:], identity=ident[:]),
                       reads=(onq[qs], ident), writes=bk(0, False, 0))
                og = ogT[qc % 2]
                op("act", lambda e: e.activation(out=og[:], in_=DBh[0][:, 0:BW], func=AF.Copy), reads=bk(0, False, 0), writes=(og,))
                fw.dma("sp", C.scrA[:, h, qc * BW:(qc + 1) * BW], og[:], _stsem(fw, og), reads=(og,), writes=(C.dA,))
    with contextlib.ExitStack() as st:
        Wg = load_w(C, st, "Wg@%d" % si, C.wview(O_GA, 1024), [128, 8, 1024])
        Woa = load_w(C, st, "Woa@%d" % si, C.P["w_o_attn"].rearrange("(c p) n -> p c n", p=128), [128, 8, 1024])
        Wgm = load_w(C, st, "Wgm@%d" % si, C.wview(O_GMA, 1024), [128, 8, 1024])
        branch_tail(C, st, "a", Wg, Woa, Wgm, C.scrA, C.dA, C.scrM, C.dM, AF.Silu)


def branch_tail(C, st, tag, Wg, Wo, Wgm, src, dsrc, dst, ddst, gate_func):
    fw, op, S, BW, NB, si = C.fw, C.fw.op, C.S, C.BW, C.NB, C.si
    DBf, bk, xnT, xblk = C.DBf, C.bk, C.xnT, C.xblk
    og = [fw.sb("og%s%d@%d" % (tag, i, si), [128, 8, BW], BF16, st) for i in range(2)]
    OG = fw.sb("OG%s@%d" % (tag, si), [128, 8, BW], BF16, st)
    sg = [fw.sb("sg%s%d@%d" % (tag, i, si), [128, BW], F32, st) for i in range(2)]
    mab = [fw.sb("mab%s%d@%d" % (tag, i, si), [128, 8, BW], BF16, st) for i in range(2)]
    for tb in range(NB):
        b = tb % 2
        fw.dma("sp", og[b][:], src[:, :, tb * BW:(tb + 1) * BW], _ldsem(fw, og[b]), reads=(dsrc,), writes=(og[b],))
        if Wg is not None:
            for dc in range(8):
                db = dc % 2
                for kc in range(8):
                    op("pe", lambda e: e.matmul(DBf[db][:, 0:BW], lhsT=Wg[:, kc, dc * 128:(dc + 1) * 128], rhs=xblk(kc, tb),
                                                start=(kc == 0), stop=(kc == 7)), reads=(Wg, xnT), writes=bk(db, False, 0))
                op("act", lambda e: e.activation(out=sg[db][:], in_=DBf[db][:, 0:BW], func=gate_func), reads=bk(db, False, 0), writes=(sg[db],))
                op("dve", lambda e: e.tensor_tensor(out=OG[:, dc, :], in0=og[b][:, dc, :], in1=sg[db][:], op=ALU.mult),
                   reads=(og[b], sg[db]), writes=(OG,))
            G = OG
        else:
            G = og[b]
        for dc in range(8):
            db = 2 + dc % 2
            for hh in range(8):
                op("pe", lambda e: e.matmul(DBf[db][:, 0:BW], lhsT=Wo[:, hh, dc * 128:(dc + 1) * 128], rhs=G[:, hh, :],
                                            start=(hh == 0), stop=(hh == 7)), reads=(Wo, G), writes=bk(db, False, 0))
            for kc in range(8):
                op("pe", lambda e: e.matmul(DBf[db][:, 512:512 + BW], lhsT=Wgm[:, kc, dc * 128:(dc + 1) * 128], rhs=xblk(kc, tb),
                                            start=(kc == 0), stop=(kc == 7)), reads=(Wgm, xnT), writes=bk(db, False, 1))
            sgi = dc % 2
            op("act", lambda e: e.activation(out=sg[sgi][:], in_=DBf[db][:, 512:512 + BW], func=AF.Sigmoid),
               reads=bk(db, False, 1), writes=(sg[sgi],))
            op("dve", lambda e: e.tensor_tensor(out=mab[b][:, dc, :], in0=DBf[db][:, 0:BW], in1=sg[sgi][:], op=ALU.mult),
               reads=bk(db, False, 0) + (sg[sgi],), writes=(mab[b],))
        fw.dma("act", dst[:, :, tb * BW:(tb + 1) * BW], mab[b][:], _stsem(fw, mab[b]), reads=(mab[b],), writes=(ddst,))


def stage_rwkv(C):
    fw, op, S, BW, NB, NT, si = C.fw, C.fw.op, C.S, C.BW, C.NB, C.NT, C.si
    DBf, DBh, bk, xnT, xblk, PB = C.DBf, C.DBh, C.bk, C.xnT, C.xblk, C.PB
    ident, BOh, BOf, I2, onesS = C.ident, C.BOh, C.BOf, C.I2, C.onesS
    wlw, wla = C.wlw, C.wla
    GN_EPS = 64e-5
    CD = math.exp(-0.5)
    import os
    G = int(os.environ.get('RW_G', 4 if S <= 2048 else 3))
    G = min(G, NT)

    def pbank(j):
        return DBf[j // 2][:, (j % 2) * 512:(j % 2) * 512 + 512]

    def pbankh(j):
        return DBh[j // 2][:, (j % 2) * 1024:(j % 2) * 1024 + 1024]

    with contextlib.ExitStack() as st:
        TWT = fw.sb("TWT@%d" % si, [128, S], BF16, st)
        ALT = fw.sb("ALT@%d" % si, [128, S], BF16, st)
        with contextlib.ExitStack() as st0:
            Wlow = load_w(C, st0, "Wlow@%d" % si, C.wview(O_WL, 256), [128, 8, 256])
            for tb in range(NB):
                for j, (dst, func) in enumerate(((TWT, AF.Tanh), (ALT, AF.Copy))):
                    for kc in range(8):
                        op("pe", lambda e: e.matmul(pbank(j)[:, 0:BW], lhsT=Wlow[:, kc, j * 128:(j + 1) * 128], rhs=xblk(kc, tb),
                                                    start=(kc == 0), stop=(kc == 7)), reads=(Wlow, xnT), writes=(PB[j],))
                    op("act", lambda e: e.activation(out=dst[:, tb * BW:(tb + 1) * BW], in_=pbank(j)[:, 0:BW], func=func),
                       reads=(PB[j],), writes=(dst,))
        nWhp = 2 if S <= 2048 else 1
        Whps = [fw.sb("Whp%d@%d" % (i, si), [128, 4, 8, 128], BF16, st) for i in range(nWhp)]

        def load_hp_w(hp_):
            Whp_ = Whps[hp_ % nWhp]
            for s_, off in enumerate((O_R, O_RK, O_RV, O_GR)):
                fw.load("pool", Whp_, Whp_[:, s_, :, :], C.wview(off + hp_ * 128, 128))
        load_hp_w(0)
        RT_ = fw.sb("RT_@%d" % si, [128, S], BF16, st)
        KT_ = fw.sb("KT_@%d" % si, [128, S], BF16, st)
        VT_ = fw.sb("VT_@%d" % si, [128, S], BF16, st)
        KKT = fw.sb("KKT@%d" % si, [128, S], BF16, st)
        OFT = fw.sb("OFT@%d" % si, [128, S], BF16, st)
        OBT = fw.sb("OBT@%d" % si, [128, S], BF16, st)
        CB = min(256, S)

        for hp in range(8):
            hpc = slice(hp, hp + 1)
            Whp = Whps[hp % nWhp]
            if nWhp == 1 and hp > 0:
                load_hp_w(hp)
            if True:
                if hp == 0:
                    ctmp = fw.sb("ctmp@%d" % si, [128, CB], F32, st)
                    kkr = fw.sb("kkrw@%d" % si, [128, CB], F32, st)
                    nrm = fw.sb("nrmw@%d" % si, [128, CB], F32, st)
                    sqw = fw.sb("sqw@%d" % si, [128, CB], BF16, st)
                for cb in range(S // CB):
                    cs = slice(cb * CB, (cb + 1) * CB)
                    for s_, dst in enumerate((RT_, KT_, VT_)):
                        j = (3 * cb + s_) % 4
                        for kc in range(8):
                            op("pe", lambda e: e.matmul(pbank(j)[:, 0:CB + 2], lhsT=Whp[:, s_, kc, :], rhs=xnT[:, kc, cb * CB:cb * CB + CB + 2],
                                                        start=(kc == 0), stop=(kc == 7)), reads=(Whp, xnT), writes=(PB[j],))
                        cw = [C.cv_c[:, i, s_ * 8 + hp:s_ * 8 + hp + 1] for i in range(3)]
                        op("dve", lambda e: e.tensor_scalar(out=ctmp[:], in0=pbank(j)[:, 0:CB], scalar1=cw[0], scalar2=None, op0=ALU.mult),
                           reads=(PB[j], C.cv_c), writes=(ctmp,))
                        op("dve", lambda e: e.scalar_tensor_tensor(out=ctmp[:], in0=pbank(j)[:, 1:CB + 1], scalar=cw[1], in1=ctmp[:],
                                                                   op0=ALU.mult, op1=ALU.add), reads=(PB[j], C.cv_c, ctmp), writes=(ctmp,))
                        op("dve", lambda e: e.scalar_tensor_tensor(out=dst[:, cs], in0=pbank(j)[:, 2:CB + 2], scalar=cw[2],
                                                                   in1=ctmp[:], op0=ALU.mult, op1=ALU.add),
                           reads=(PB[j], C.cv_c, ctmp), writes=(dst,))
                    j = 4 + cb % 2
                    if os.environ.get('RW_STOP') == 'c1nokk':
                        continue
                    op("dve", lambda e: e.tensor_scalar(out=kkr[:], in0=KT_[:, cs], scalar1=C.kk_c[:, hpc], scalar2=None, op0=ALU.mult),
                       reads=(KT_, C.kk_c), writes=(kkr,))
                    op("pool", lambda e: e.tensor_tensor(out=sqw[:], in0=kkr[:], in1=kkr[:], op=ALU.mult), reads=(kkr,), writes=(sqw,))
                    op("pe", lambda e: e.matmul(pbank(j)[:, 0:CB], lhsT=BOh[:], rhs=sqw[:], start=True, stop=True), reads=(BOh, sqw), writes=(PB[j],))
                    op("dve", lambda e: e.tensor_scalar(out=nrm[:], in0=pbank(j)[:, 0:CB], scalar1=1e-12, scalar2=None, op0=ALU.add),
                       reads=(PB[j],), writes=(nrm,))
                    op("act", lambda e: e.activation(out=nrm[:], in_=nrm[:], func=AF.Sqrt), reads=(nrm,), writes=(nrm,))
                    op("dve", lambda e: e.reciprocal(out=nrm[:], in_=nrm[:]), reads=(nrm,), writes=(nrm,))
                    op("dve", lambda e: e.tensor_tensor(out=KKT[:, cs], in0=kkr[:], in1=nrm[:], op=ALU.mult), reads=(kkr, nrm), writes=(KKT,))
            if nWhp == 2 and hp + 1 < 8:
                load_hp_w(hp + 1)
            if True:
                def f32t(n, g):
                    return fw.sb("%s%d@%d" % (n, g, si), [128, 128], F32, st)

                def bft(n, g, shape):
                    return fw.sb("%s%d@%d" % (n, g, si), shape, BF16, st)
                if hp == 0:
                    W = []
                for g in (range(G) if hp == 0 else ()):
                    w = SimpleNamespace()
                    for n in ("sgw", "a_", "cum", "c2", "cex", "e_in", "e_ex", "e_ng", "kdir", "tmpb"):
                        setattr(w, n, f32t(n, g))
                    w.AR = bft("AR", g, [128, 2, 128])
                    w.BK = bft("BK", g, [128, 2, 128])
                    w.TM = bft("TM", g, [128, 4, 128])
                    w.AT = bft("AT", g, [128, 2, 4, 128])
                    w.XY0 = bft("XY0", g, [128, 2, 2, 128])
                    w.XY = [bft("XYa", g, [128, 2, 3, 128]), bft("XYb", g, [128, 2, 3, 128])]
                    w.Yf = bft("Yf", g, [128, 2, 128])
                    w.RP = bft("RP", g, [128, 128])
                    w.MpT = bft("MpT", g, [128, 64])
                    W.append(w)
                if hp == 0:
                    Hs = [fw.sb("H%d@%d" % (i, si), [128, 64], BF16, st) for i in range(2)]

                def head(d, tau, g):
                    w = W[g]
                    bA, bB = 2 * g, 2 * g + 1
                    PA, PBb = PB[bA], PB[bB]
                    dsl = slice(d * 64, (d + 1) * 64)
                    sl = slice(tau * 128, (tau + 1) * 128)
                    rT, kTt, vT, kkT = RT_[:, sl], KT_[:, sl], VT_[:, sl], KKT[:, sl]
                    op("pe", lambda e: e.matmul(pbank(bA)[:, 0:128], lhsT=wlw[dsl, hp * 128:(hp + 1) * 128], rhs=TWT[dsl, sl], start=True, stop=True),
                       reads=(wlw, TWT), writes=(PA,))
                    op("pe", lambda e: e.matmul(pbank(bA)[:, 128:256], lhsT=wla[dsl, hp * 128:(hp + 1) * 128], rhs=ALT[dsl, sl], start=True, stop=True),
                       reads=(wla, ALT), writes=(PA,))
                    yield
                    op("act", lambda e: e.activation(out=w.sgw[:], in_=pbank(bA)[:, 0:128], func=AF.Tanh, bias=C.hw0_c[:, d, hpc], scale=0.5),
                       reads=(PA, C.hw0_c), writes=(w.sgw,))
                    op("act", lambda e: e.activation(out=w.a_[:], in_=pbank(bA)[:, 128:256], func=AF.Tanh, bias=C.ha0_c[:, d, hpc], scale=0.5),
                       reads=(PA, C.ha0_c), writes=(w.a_,))
                    yield
                    op("pool", lambda e: e.tensor_scalar(out=w.kdir[:], in0=w.a_[:], scalar1=C.ka_c[:, hpc], scalar2=C.tmka[:, hpc], op0=ALU.mult, op1=ALU.add),
                       reads=(w.a_, C.ka_c, C.tmka), writes=(w.kdir,))
                    yield
                    op("pool", lambda e: e.tensor_tensor(out=w.kdir[:], in0=kTt, in1=w.kdir[:], op=ALU.mult), reads=(KT_, w.kdir), writes=(w.kdir,))
                    yield
                    op("dve", lambda e: e.tensor_tensor_scan(out=w.cum[:], data0=w.sgw[:], data1=onesS[:], initial=0.0, op0=ALU.add, op1=ALU.add),
                       reads=(onesS, w.sgw), writes=(w.cum,))
                    yield
                    cu = w.cum
                    if d == 1:
                        op("dve", lambda e: e.tensor_scalar(out=w.c2[:], in0=w.cum[:], scalar1=-1.0, scalar2=w.cum[:, 127:128], op0=ALU.mult, op1=ALU.add),
                           reads=(w.cum,), writes=(w.c2,))
                        yield
                        op("dve", lambda e: e.scalar_tensor_tensor(out=w.c2[:], in0=w.c2[:], scalar=1.0, in1=w.sgw[:], op0=ALU.add, op1=ALU.add),
                           reads=(w.c2, w.sgw), writes=(w.c2,))
                        yield
                        cu = w.c2
                    op("dve", lambda e: e.scalar_tensor_tensor(out=w.cex[:], in0=cu[:], scalar=-1.0, in1=w.sgw[:], op0=ALU.add, op1=ALU.subtract),
                       reads=(cu, w.sgw), writes=(w.cex,))
                    yield
                    HC = 0.5 * CD
                    op("act", lambda e: e.activation(out=w.e_in[:], in_=cu[:], func=AF.Exp, scale=-HC), reads=(cu,), writes=(w.e_in,))
                    op("act", lambda e: e.activation(out=w.e_ex[:], in_=w.cex[:], func=AF.Exp, scale=-HC), reads=(w.cex,), writes=(w.e_ex,))
                    op("act", lambda e: e.activation(out=w.e_ng[:], in_=cu[:], func=AF.Exp, scale=HC, bias=C.lnhalf[:, 0:1]), reads=(cu, C.lnhalf), writes=(w.e_ng,))
                    yield
                    op("dve", lambda e: e.scalar_tensor_tensor(out=w.AR[:, 0, :], in0=kkT, scalar=-1.0, in1=w.e_ex[:], op0=ALU.mult, op1=ALU.mult),
                       reads=(KKT, w.e_ex), writes=(w.AR,))
                    yield
                    op("pool", lambda e: e.tensor_tensor(out=w.AR[:, 1, :], in0=rT, in1=w.e_in[:], op=ALU.mult), reads=(RT_, w.e_in, w.AR), writes=(w.AR,))
                    yield
                    op("dve", lambda e: e.scalar_tensor_tensor(out=w.tmpb[:], in0=w.a_[:], scalar=1.0, in1=kkT, op0=ALU.add, op1=ALU.mult),
                       reads=(KKT, w.a_), writes=(w.tmpb,))
                    yield
                    op("pool", lambda e: e.tensor_tensor(out=w.BK[:, 0, :], in0=w.tmpb[:], in1=w.e_ng[:], op=ALU.mult), reads=(w.tmpb, w.e_ng), writes=(w.BK,))
                    op("pool", lambda e: e.tensor_tensor(out=w.BK[:, 1, :], in0=w.kdir[:], in1=w.e_ng[:], op=ALU.mult), reads=(w.kdir, w.e_ng, w.BK), writes=(w.BK,))
                    yield
                    for j, (srcb, srcap) in enumerate(((w.AR, w.AR[:, 0, :]), (w.BK, w.BK[:, 0, :]), (w.BK, w.BK[:, 1, :]), (VT_, vT))):
                        op("pe", lambda e: e.transpose(out=pbankh(bB)[:, j * 128:(j + 1) * 128], in_=srcap, identity=ident[:]),
                           reads=(srcb, ident), writes=(PBb,))
                    yield
                    op("act", lambda e: e.activation(out=w.TM[:], in_=pbankh(bB)[:, 0:512].rearrange("p (a b) -> p a b", a=4), func=AF.Copy),
                       reads=(PBb,), writes=(w.TM,))
                    yield
                    for hh in (0, 1):
                        s = slice(hh * 64, (hh + 1) * 64)
                        op("pe", lambda e: e.matmul(pbank(bA + hh)[:, 0:128], lhsT=w.AR[s, 0, :], rhs=w.BK[s, 0, :], start=True, stop=True),
                           reads=(w.BK, w.AR), writes=(PB[bA + hh],))
                    yield
                    op("dve", lambda e: e.tensor_tensor(out=w.XY0[:, :, 0, :], in0=DBf[g][:, :].rearrange("p (h c) -> p h c", h=2)[:, :, 0:128],
                                                        in1=C.MASKL[d][:], op=ALU.mult), reads=(PA, PBb, C.MASKL[d]), writes=(w.XY0,))
                    yield
                    for hh in (0, 1):
                        s = slice(hh * 64, (hh + 1) * 64)
                        bj = bA + hh
                        arf = w.AR[s, :, :].rearrange("p a b -> p (a b)")
                        op("pe", lambda e: e.matmul(pbank(bj)[:, 0:256], lhsT=w.BK[s, 0, :], rhs=arf, start=True, stop=True),
                           reads=(w.BK, w.AR), writes=(PB[bj],))
                        op("pe", lambda e: e.matmul(pbank(bj)[:, 256:512], lhsT=w.BK[s, 1, :], rhs=arf, start=True, stop=True),
                           reads=(w.BK, w.AR), writes=(PB[bj],))
                    yield
                    op("dve", lambda e: e.tensor_tensor(out=w.AT[:], in0=DBf[g][:, :].rearrange("p (h a b) -> p h a b", h=2, a=4),
                                                        in1=C.MASK8[d][:], op=ALU.mult), reads=(PA, PBb, C.MASK8[d]), writes=(w.AT,))
                    yield
                    for hh in (0, 1):
                        s = slice(hh * 64, (hh + 1) * 64)
                        op("pe", lambda e: e.matmul(pbank(bA)[:, hh * 64:(hh + 1) * 64], lhsT=w.AT[:, hh, 2, :], rhs=w.TM[:, 3, s], start=True, stop=True),
                           reads=(w.AT, w.TM), writes=(PA,))
                    yield
                    op("act", lambda e: e.activation(out=w.XY0[:, :, 1, 64:128], in_=pbank(bA)[:, 0:128].rearrange("p (a b) -> p a b", a=2), func=AF.Copy),
                       reads=(PA, w.XY0), writes=(w.XY0,))
                    op("pool", lambda e: e.tensor_copy(out=w.XY0[:, :, 1, 0:64], in_=w.TM[:, 0, :].rearrange("p (a b) -> p a b", a=2)),
                       reads=(w.TM, w.XY0), writes=(w.XY0,))
                    yield
                    XN = [w.XY0[:, hh, 0, :] for hh in (0, 1)]
                    YK = [w.XY0[:, hh, 1, :] for hh in (0, 1)]
                    XNY = [w.XY0[:, hh, :, :].rearrange("p a b -> p (a b)") for hh in (0, 1)]
                    XT = [w.AT[:, hh, 0, :] for hh in (0, 1)]
                    srcb = (w.XY0, w.AT)
                    for k in range(7):
                        last = (k == 6)
                        for hh in (0, 1):
                            bj = bA + hh
                            if not last:
                                op("pe", lambda e: e.matmul(pbank(bj)[:, 256:384], lhsT=XN[hh], rhs=XT[hh], start=True, stop=True),
                                   reads=srcb, writes=(PB[bj],))
                                op("pe", lambda e: e.matmul(pbank(bj)[:, 0:256], lhsT=XT[hh], rhs=XNY[hh], start=True, stop=False),
                                   reads=srcb, writes=(PB[bj],))
                            else:
                                op("pe", lambda e: e.matmul(pbank(bj)[:, 128:256], lhsT=XT[hh], rhs=YK[hh], start=True, stop=False),
                                   reads=srcb, writes=(PB[bj],))
                            op("pe", lambda e: e.matmul(pbank(bj)[:, 128:256], lhsT=ident[:], rhs=YK[hh], start=False, stop=True),
                               reads=srcb + (ident,), writes=(PB[bj],))
                        yield
                        eng = "act" if k % 2 == 0 else "dve"
                        if not last:
                            nxt = w.XY[k % 2]
                            src = DBf[g][:, :].rearrange("p (h c) -> p h c", h=2)[:, :, 0:384].rearrange("p h (a b) -> p h a b", a=3)
                            if eng == "act":
                                op("act", lambda e: e.activation(out=nxt[:], in_=src, func=AF.Copy), reads=(PA, PBb), writes=(nxt,))
                            else:
                                op("dve", lambda e: e.tensor_copy(out=nxt[:], in_=src), reads=(PA, PBb), writes=(nxt,))
                            XN = [nxt[:, hh, 0, :] for hh in (0, 1)]
                            YK = [nxt[:, hh, 1, :] for hh in (0, 1)]
                            XNY = [nxt[:, hh, 0:2, :].rearrange("p a b -> p (a b)") for hh in (0, 1)]
                            XT = [nxt[:, hh, 2, :] for hh in (0, 1)]
                            srcb = (nxt,)
                        else:
                            src = DBf[g][:, :].rearrange("p (h c) -> p h c", h=2)[:, :, 128:256]
                            op("act", lambda e: e.activation(out=w.Yf[:], in_=src, func=AF.Copy), reads=(PA, PBb), writes=(w.Yf,))
                        yield
                    Yf = w.Yf
                    for hh in (0, 1):
                        s = slice(hh * 64, (hh + 1) * 64)
                        op("pe", lambda e: e.matmul(pbank(bA)[s, 384:512], lhsT=Yf[:, hh, 0:64], rhs=w.AT[:, hh, 1, :], start=True, stop=True),
                           reads=(Yf, w.AT), writes=(PA,))
                        op("pe", lambda e: e.matmul(pbank(bB)[s, 384:448], lhsT=Yf[:, hh, 0:64], rhs=w.TM[:, 1, s], start=True, stop=True),
                           reads=(Yf, w.TM), writes=(PBb,))
                    yield
                    op("dve", lambda e: e.tensor_tensor(out=w.RP[:], in0=pbank(bA)[:, 384:512], in1=w.AR[:, 1, :], op=ALU.add),
                       reads=(PA, w.AR), writes=(w.RP,))
                    op("dve", lambda e: e.tensor_tensor(out=w.MpT[:], in0=pbank(bB)[:, 384:448], in1=I2[:], op=ALU.add), reads=(PBb, I2), writes=(w.MpT,))
                    yield

                def tail(d, tau, g, Hc, Hn):
                    w = W[g]
                    bA, bB = 2 * g, 2 * g + 1
                    PA, PBb = PB[bA], PB[bB]
                    Yf = w.Yf
                    sl = slice(tau * 128, (tau + 1) * 128)
                    for hh in (0, 1):
                        s = slice(hh * 64, (hh + 1) * 64)
                        op("pe", lambda e: e.matmul(pbank(bA)[s, 0:128], lhsT=Yf[:, hh, 64:128], rhs=w.AT[:, hh, 1, :], start=True, stop=False),
                           reads=(Yf, w.AT), writes=(PA,))
                        op("pe", lambda e: e.matmul(pbank(bA)[s, 0:128], lhsT=w.TM[:, 3, s], rhs=w.AT[:, hh, 3, :], start=False, stop=False),
                           reads=(w.TM, w.AT), writes=(PA,))
                        op("pe", lambda e: e.matmul(pbank(bA)[s, 0:128], lhsT=Hc[s, :], rhs=w.RP[s, :], start=False, stop=True),
                           reads=(Hc, w.RP), writes=(PA,))
                    for hh in (0, 1):
                        s = slice(hh * 64, (hh + 1) * 64)
                        op("pe", lambda e: e.matmul(pbank(bB)[s, 0:64], lhsT=w.TM[:, 1, s], rhs=Yf[:, hh, 64:128], start=True, stop=False),
                           reads=(Yf, w.TM), writes=(PBb,))
                        op("pe", lambda e: e.matmul(pbank(bB)[s, 0:64], lhsT=w.TM[:, 2, s], rhs=w.TM[:, 3, s], start=False, stop=False),
                           reads=(w.TM,), writes=(PBb,))
                        op("pe", lambda e: e.matmul(pbank(bB)[s, 0:64], lhsT=w.MpT[s, :], rhs=Hc[s, :], start=False, stop=True),
                           reads=(w.MpT, Hc), writes=(PBb,))
                    WC = w.e_in[:, 127:128] if d == 0 else w.e_in[:, 0:1]
                    op("act", lambda e: e.activation(out=Hn[:], in_=pbank(bB)[:, 0:64], func=AF.Copy, scale=WC), reads=(PBb, w.e_in), writes=(Hn,))
                    dst = OFT if d == 0 else OBT
                    op("dve", lambda e: e.tensor_copy(out=dst[:, sl], in_=pbank(bA)[:, 0:128]), reads=(PA,), writes=(dst,))

                for d in (((0, 1) if 'RW_HEAD' not in os.environ else (0,)) if os.environ.get('RW_STOP') not in ('c1', 'c1nokk') else ()):
                    op("pool", lambda e: e.memset(Hs[0][:], 0.0), writes=(Hs[0],))
                    order = list(range(NT)) if d == 0 else list(range(NT - 1, -1, -1))
                    step = 0
                    DELTA = int(os.environ.get('RW_DELTA', 3))
                    slots = [None] * G
                    state = ['idle'] * G
                    completed = {}
                    next_pos = 0
                    tails_done = 0
                    rnd = 0
                    while tails_done < NT:
                        for gi in range(G):
                            if state[gi] == 'idle':
                                if next_pos < NT and rnd >= gi * DELTA:
                                    slots[gi] = (next_pos, head(d, order[next_pos], gi))
                                    state[gi] = 'run'
                                    next_pos += 1
                                else:
                                    continue
                            if state[gi] == 'run':
                                pos, gen = slots[gi]
                                try:
                                    next(gen)
                                except StopIteration:
                                    completed[pos] = gi
                                    state[gi] = 'wait'
                        while tails_done in completed:
                            gi = completed.pop(tails_done)
                            tail(d, order[tails_done], gi, Hs[step % 2], Hs[(step + 1) % 2])
                            step += 1
                            tails_done += 1
                            state[gi] = 'idle'
                        rnd += 1
            if True:
                PW = min(512, S) if S <= 2048 else 256
                if hp == 0:
                    if PW == CB:
                        o_, cen, sq2 = ctmp, kkr, nrm
                        var, rk_ = [fw.sb("%s@%d" % (n, si), [128, PW], F32, st) for n in ("pvar", "prk")]
                    else:
                        o_, cen, sq2, var, rk_ = [fw.sb("%s@%d" % (n, si), [128, PW], F32, st) for n in ("po_", "pcen", "psq2", "pvar", "prk")]
                for pb_ in range(S // PW):
                    ps = slice(pb_ * PW, (pb_ + 1) * PW)
                    op("dve", lambda e: e.tensor_tensor(out=o_[:], in0=OFT[:, ps], in1=OBT[:, ps], op=ALU.add), reads=(OFT, OBT), writes=(o_,))
                    op("pe", lambda e: e.matmul(pbank(0)[:, 0:PW], lhsT=BOf[:], rhs=o_[:], start=True, stop=True), reads=(BOf, o_), writes=(PB[0],))
                    op("dve", lambda e: e.scalar_tensor_tensor(out=cen[:], in0=pbank(0)[:, 0:PW], scalar=-1.0 / 64, in1=o_[:], op0=ALU.mult, op1=ALU.add),
                       reads=(PB[0], o_), writes=(cen,))
                    op("pool", lambda e: e.tensor_tensor(out=sq2[:], in0=cen[:], in1=cen[:], op=ALU.mult), reads=(cen,), writes=(sq2,))
                    op("pe", lambda e: e.matmul(pbank(1)[:, 0:PW], lhsT=BOf[:], rhs=sq2[:], start=True, stop=True), reads=(BOf, sq2), writes=(PB[1],))
                    op("dve", lambda e: e.tensor_scalar(out=var[:], in0=pbank(1)[:, 0:PW], scalar1=1.0 / 64, scalar2=GN_EPS, op0=ALU.mult, op1=ALU.add),
                       reads=(PB[1],), writes=(var,))
                    op("act", lambda e: e.activation(out=var[:], in_=var[:], func=AF.Sqrt), reads=(var,), writes=(var,))
                    op("dve", lambda e: e.reciprocal(out=var[:], in_=var[:]), reads=(var,), writes=(var,))
                    op("dve", lambda e: e.tensor_tensor(out=cen[:], in0=cen[:], in1=var[:], op=ALU.mult), reads=(cen, var), writes=(cen,))
                    op("dve", lambda e: e.tensor_scalar(out=cen[:], in0=cen[:], scalar1=C.lg_c[:, hpc], scalar2=C.lb_c[:, hpc], op0=ALU.mult, op1=ALU.add),
                       reads=(cen, C.lg_c, C.lb_c), writes=(cen,))
                    op("dve", lambda e: e.scalar_tensor_tensor(out=rk_[:], in0=RT_[:, ps], scalar=C.rk_c[:, hpc], in1=KT_[:, ps], op0=ALU.mult, op1=ALU.mult),
                       reads=(RT_, KT_, C.rk_c), writes=(rk_,))
                    op("pe", lambda e: e.matmul(pbank(2)[:, 0:PW], lhsT=BOf[:], rhs=rk_[:], start=True, stop=True), reads=(BOf, rk_), writes=(PB[2],))
                    op("dve", lambda e: e.tensor_tensor(out=sq2[:], in0=pbank(2)[:, 0:PW], in1=VT_[:, ps], op=ALU.mult), reads=(PB[2], VT_), writes=(sq2,))
                    op("pool", lambda e: e.tensor_tensor(out=cen[:], in0=cen[:], in1=sq2[:], op=ALU.add), reads=(cen, sq2), writes=(cen,))
                    for kc in range(8):
                        op("pe", lambda e: e.matmul(pbank(3)[:, 0:PW], lhsT=Whp[:, 3, kc, :], rhs=xnT[:, kc, 1 + pb_ * PW:1 + (pb_ + 1) * PW],
                                                    start=(kc == 0), stop=(kc == 7)), reads=(Whp, xnT), writes=(PB[3],))
                    op("act", lambda e: e.activation(out=var[:], in_=pbank(3)[:, 0:PW], func=AF.Silu), reads=(PB[3],), writes=(var,))
                    op("dve", lambda e: e.tensor_tensor(out=OFT[:, ps], in0=cen[:], in1=var[:], op=ALU.mult), reads=(cen, var, OFT), writes=(OFT,))
            fw.dma("sp", C.scrR[:, hp, 0:S], OFT[:], _stsem(fw, OFT), reads=(OFT,), writes=(C.dR,))
    with contextlib.ExitStack() as st:
        Wor = load_w(C, st, "Wor@%d" % si, C.P["w_o_rwkv"].rearrange("(c p) n -> p c n", p=128), [128, 8, 1024])
        Wgmb = load_w(C, st, "Wgmb@%d" % si, C.wview(O_GMB, 1024), [128, 8, 1024])
        branch_tail(C, st, "r", None, Wor, Wgmb, C.scrR, C.dR, C.scrR, C.dR, None)


def make_rope(C, st0):
    fw, op, S, si = C.fw, C.fw.op, C.S, C.si
    cosT = fw.sb("cosT@%d" % si, [128, S], BF16, st0)
    sinT = fw.sb("sinT@%d" % si, [128, S], BF16, st0)
    with contextlib.ExitStack() as st:
        pi_ = fw.sb("pi_@%d" % si, [128, 1], I32, st)
        pj_ = fw.sb("pj_@%d" % si, [128, 1], I32, st)
        pf_ = fw.sb("pf_@%d" % si, [128, 2], F32, st)
        invf = fw.sb("invf@%d" % si, [128, 1], F32, st)
        posi = fw.sb("posi@%d" % si, [128, S], I32, st)
        ang = fw.sb("ang@%d" % si, [128, S], F32, st)
        kf = fw.sb("kf@%d" % si, [128, S], F32, st)
        tm = fw.sb("tm@%d" % si, [128, S], F32, st)
        op("pool", lambda e: e.iota(pi_[:], pattern=[[0, 1]], base=0, channel_multiplier=1), writes=(pi_,))
        op("dve", lambda e: e.tensor_scalar(out=pj_[:], in0=pi_[:], scalar1=7, scalar2=None, op0=ALU.bitwise_and),
           reads=(pi_,), writes=(pj_,))
        op("dve", lambda e: e.tensor_copy(out=pf_[:, 0:1], in_=pj_[:]), reads=(pj_,), writes=(pf_,))
        op("dve", lambda e: e.tensor_scalar(out=pj_[:], in0=pi_[:], scalar1=63, scalar2=None, op0=ALU.bitwise_and),
           reads=(pi_,), writes=(pj_,))
        op("dve", lambda e: e.tensor_copy(out=pf_[:, 1:2], in_=pj_[:]), reads=(pj_,), writes=(pf_,))
        op("act", lambda e: e.activation(out=invf[:], in_=pf_[:, 0:1], func=AF.Exp, scale=-math.log(ROPE_THETA) / 8.0),
           reads=(pf_,), writes=(invf,))
        op("dve", lambda e: e.tensor_scalar(out=pf_[:, 1:2], in0=pf_[:, 1:2], scalar1=16.0, scalar2=None, op0=ALU.is_lt),
           reads=(pf_,), writes=(pf_,))
        op("dve", lambda e: e.tensor_tensor(out=invf[:], in0=invf[:], in1=pf_[:, 1:2], op=ALU.mult),
           reads=(invf, pf_), writes=(invf,))
        op("pool", lambda e: e.iota(posi[:], pattern=[[1, S]], base=0, channel_multiplier=0), writes=(posi,))
        op("dve", lambda e: e.tensor_copy(out=ang[:], in_=posi[:]), reads=(posi,), writes=(ang,))
        op("dve", lambda e: e.tensor_scalar(out=ang[:], in0=ang[:], scalar1=invf[:, 0:1], scalar2=None, op0=ALU.mult),
           reads=(ang, invf), writes=(ang,))

        def wrap_sin(dst, shift):
            op("dve", lambda e: e.tensor_scalar(out=tm[:], in0=ang[:], scalar1=shift, scalar2=None, op0=ALU.add),
               reads=(ang,), writes=(tm,))
            op("dve", lambda e: e.tensor_scalar(out=kf[:], in0=tm[:], scalar1=1.0 / TWO_PI, scalar2=None, op0=ALU.mult),
               reads=(tm,), writes=(kf,))
            op("dve", lambda e: e.tensor_copy(out=posi[:], in_=kf[:]), reads=(kf,), writes=(posi,))
            op("dve", lambda e: e.tensor_copy(out=kf[:], in_=posi[:]), reads=(posi,), writes=(kf,))
            op("dve", lambda e: e.scalar_tensor_tensor(out=tm[:], in0=kf[:], scalar=-CW1, in1=tm[:], op0=ALU.mult, op1=ALU.add),
               reads=(kf, tm), writes=(tm,))
            op("dve", lambda e: e.scalar_tensor_tensor(out=tm[:], in0=kf[:], scalar=-CW2, in1=tm[:], op0=ALU.mult, op1=ALU.add),
               reads=(kf, tm), writes=(tm,))
            op("dve", lambda e: e.tensor_scalar(out=kf[:], in0=tm[:], scalar1=math.pi, scalar2=-TWO_PI, op0=ALU.is_gt, op1=ALU.mult),
               reads=(tm,), writes=(kf,))
            op("dve", lambda e: e.tensor_tensor(out=tm[:], in0=tm[:], in1=kf[:], op=ALU.add), reads=(tm, kf), writes=(tm,))
            op("dve", lambda e: e.tensor_scalar(out=kf[:], in0=tm[:], scalar1=-math.pi, scalar2=TWO_PI, op0=ALU.is_lt, op1=ALU.mult),
               reads=(tm,), writes=(kf,))
            op("dve", lambda e: e.tensor_tensor(out=tm[:], in0=tm[:], in1=kf[:], op=ALU.add), reads=(tm, kf), writes=(tm,))
            op("dve", lambda e: e.tensor_scalar(out=tm[:], in0=tm[:], scalar1=3.14159, scalar2=-3.14159, op0=ALU.min, op1=ALU.max),
               reads=(tm,), writes=(tm,))
            op("act", lambda e: e.activation(out=dst[:], in_=tm[:], func=AF.Sin), reads=(tm,), writes=(dst,))
        wrap_sin(sinT, 0.0)
        wrap_sin(cosT, math.pi / 2)

    return cosT, sinT


SEQ_LENS = [2048, 2048, 2048, 2048, 4096]
_NC_CACHE = {}


def kernel(**inputs):
    xp = np.asarray(inputs["x_prompt"], dtype=np.float32)
    xs = np.asarray(inputs["x_sample"], dtype=np.float32)
    n = 8
    if "nc" not in _NC_CACHE:
        _NC_CACHE["nc"] = build(SEQ_LENS)
    nc = _NC_CACHE["nc"]
    shared = {}
    for nme, shp in PARAMS:
        shared[nme] = np.ascontiguousarray(np.asarray(inputs[nme], dtype=np.float32).reshape(shp))
    in_maps = []
    for c in range(n):
        xc = np.concatenate([xp[4 * c:4 * c + 4].reshape(-1, D), xs[c].reshape(-1, D)], axis=0)
        m = {"x": np.ascontiguousarray(xc)}
        m.update(shared)
        in_maps.append(m)
    res = run_bass_kernel_spmd(nc, in_maps, core_ids=list(range(n)))
    yp = np.empty_like(xp)
    ys = np.empty_like(xs)
    for c in range(n):
        yc = np.asarray(res.results[c]["y"])
        yp[4 * c:4 * c + 4] = yc[0:8192].reshape(4, 2048, D)
        ys[c] = yc[8192:12288].reshape(4096, D)
    return (yp, ys)
```

```python
import contextlib
import re
import numpy as np
import concourse.bass as bass
import concourse.mybir as mybir

F32 = mybir.dt.float32
BF16 = mybir.dt.bfloat16
I32 = mybir.dt.int32
AF = mybir.ActivationFunctionType
ALU = mybir.AluOpType
AX = mybir.AxisListType

SEM_ROT = 30000


class Buf:
    __slots__ = ("t", "name", "last_w", "readers", "ld_sem", "st_sem")

    def __init__(self, t, name):
        self.t = t
        self.name = name
        self.last_w = None
        self.readers = {}
        self.ld_sem = None
        self.st_sem = None

    def __getitem__(self, k):
        return self.t[k]


class FW:
    def __init__(self, nc):
        self.nc = nc
        self.stack = contextlib.ExitStack()
        self.eng = {"pe": nc.tensor, "act": nc.scalar, "dve": nc.vector,
                    "pool": nc.gpsimd, "sp": nc.sync}
        self.sems = {}
        self.cur = {}
        self.cnt = {}
        self.waited = {e: {} for e in self.eng}
        self.dma_total = {}
        self.dma_roles = {}
        self.nsem = 0
        self.n_ops = 0
        self.n_waits = 0
        for e in self.eng:
            self._new_eng_sem(e)

    def _alloc_sem(self, name):
        h = self.stack.enter_context(self.nc.semaphore(name))
        key = name
        self.sems[key] = h
        self.nsem += 1
        return key

    def _new_eng_sem(self, e):
        key = self._alloc_sem("s_%s_%d" % (e, self.nsem))
        self.cur[e] = key
        self.cnt[key] = 0

    def dma_sem(self, name):
        role = re.sub(r"@\d+", "", name)
        if role in self.dma_roles:
            return self.dma_roles[role]
        key = self._alloc_sem("d_%s_%d" % (role, self.nsem))
        self.dma_total[key] = 0
        self.dma_roles[role] = key
        return key

    def snapshot(self):
        snap = {}
        for k in self.sems:
            v = self.dma_total[k] if k in self.dma_total else self.cnt.get(k, 0)
            if v > 0:
                snap[k] = v
        return snap

    def sb(self, name, shape, dtype, stack=None):
        self.n_alloc = getattr(self, "n_alloc", 0) + 1
        t = (stack or self.stack).enter_context(self.nc.sbuf_tensor("%s_u%d" % (name.replace("@", "_"), self.n_alloc), list(shape), dtype))
        b = Buf(t, name)
        b.readers = self.snapshot()
        return b

    def ps(self, name, shape, dtype, stack=None):
        t = (stack or self.stack).enter_context(self.nc.psum_tensor(name, list(shape), dtype))
        return Buf(t, name)

    def view(self, buf_or_ap, name):
        return Buf(buf_or_ap, name)

    def _collect(self, e, reads, writes):
        deps = {}

        def add(tok):
            if tok is None:
                return
            k, v = tok
            if k in self.dma_total:
                v = self.dma_total[k]
            if deps.get(k, 0) < v:
                deps[k] = v

        for b in reads:
            add(b.last_w)
        for b in writes:
            add(b.last_w)
            for k, v in b.readers.items():
                add((k, v))
        return deps

    def _emit_waits(self, e, deps):
        eng = self.eng[e]
        w = self.waited[e]
        for k, v in deps.items():
            if e == "pe" and k.startswith("s_pe_"):
                continue
            if w.get(k, 0) >= v:
                continue
            eng.wait_ge(self.sems[k], v)
            w[k] = v
            self.n_waits += 1

    def _mark(self, tok, reads, writes):
        k, v = tok
        for b in writes:
            b.last_w = tok
            b.readers = {}
        for b in reads:
            if b.readers.get(k, 0) < v:
                b.readers[k] = v

    def op(self, e, fn, reads=(), writes=()):
        deps = self._collect(e, reads, writes)
        self._emit_waits(e, deps)
        ins = fn(self.eng[e])
        key = self.cur[e]
        if self.cnt[key] >= SEM_ROT:
            self._new_eng_sem(e)
            key = self.cur[e]
        self.cnt[key] += 1
        ins.then_inc(self.sems[key], 1)
        self._mark((key, self.cnt[key]), reads, writes)
        self.n_ops += 1
        return ins

    def dma(self, q, out, in_, sem, reads=(), writes=(), **kw):
        deps = self._collect(q, reads, writes)
        self._emit_waits(q, deps)
        ins = self.eng[q].dma_start(out=out, in_=in_, **kw)
        self.dma_total[sem] += 16
        ins.then_inc(self.sems[sem], 16)
        self._mark((sem, self.dma_total[sem]), reads, writes)
        self.n_ops += 1
        return ins

    def load(self, q, buf, out_ap, in_ap, **kw):
        if buf.ld_sem is None:
            buf.ld_sem = self.dma_sem("l" + buf.name)
        return self.dma(q, out_ap, in_ap, buf.ld_sem, reads=(), writes=(buf,), **kw)

    def store(self, q, buf, out_ap, in_ap, **kw):
        if buf.st_sem is None:
            buf.st_sem = self.dma_sem("s" + buf.name)
        return self.dma(q, out_ap, in_ap, buf.st_sem, reads=(buf,), writes=(), **kw)

    def final_wait(self, e="sp"):
        eng = self.eng[e]
        for k, h in self.sems.items():
            v = self.dma_total[k] if k in self.dma_total else self.cnt.get(k, 0)
            if v > 0 and self.waited[e].get(k, 0) < v:
                eng.wait_ge(h, v)
                self.waited[e][k] = v


import math
from types import SimpleNamespace
import contextlib
import numpy as np
import concourse.bass as bass
import concourse.mybir as mybir
from concourse.bass_utils import run_bass_kernel_spmd

D = 1024
PIN = 10496
O_Q, O_K, O_V, O_GA, O_R, O_RK, O_RV, O_GR, O_WL, O_AL, O_GMA, O_GMB = (
    0, 1024, 2048, 3072, 4096, 5120, 6144, 7168, 8192, 8320, 8448, 9472)
ROPE_THETA = 500000.0
TWO_PI = 2.0 * math.pi
CW1 = 6.28125
CW2 = TWO_PI - CW1

PARAMS = [("norm_g", [1, 1024]), ("w_in", [1024, PIN]), ("conv_rkv", [3, 3072]),
          ("lam_q1", [1, 64]), ("lam_k1", [1, 64]), ("lam_q2", [1, 64]), ("lam_k2", [1, 64]),
          ("attn_subln_g", [1, 128]), ("w_lora_up", [128, 1024]), ("w0", [2, 1024]),
          ("a_lora_up", [128, 1024]), ("a0", [2, 1024]), ("k_k", [1, 1024]), ("k_a", [1, 1024]),
          ("r_k", [1, 1024]), ("ln_x_g", [1, 1024]), ("ln_x_b", [1, 1024]),
          ("w_o_attn", [1024, 1024]), ("w_o_rwkv", [1024, 1024]), ("w_out", [1024, 1024]),
          ("final_g", [1, 1024])]


def build(seq_lens, do_attn=True, do_rwkv=True):
    nc = bass.Bass("TRN2", target_bir_lowering=False)
    TOT = sum(seq_lens)
    SMAX = max(seq_lens)
    x = nc.dram_tensor("x", [TOT, D], F32, kind="ExternalInput").ap()
    y = nc.dram_tensor("y", [TOT, D], F32, kind="ExternalOutput").ap()
    P = {n: nc.dram_tensor(n, s, F32, kind="ExternalInput").ap() for n, s in PARAMS}
    w_in = P["w_in"]
    scrA = nc.dram_tensor("scrA", [128, 8, SMAX], BF16, kind="Internal").ap()
    scrM = nc.dram_tensor("scrM", [128, 8, SMAX], BF16, kind="Internal").ap()
    scrR = nc.dram_tensor("scrR", [128, 8, SMAX], BF16, kind="Internal").ap()
    fw = FW(nc)
    op = fw.op
    ncd = nc.allow_non_contiguous_dma(reason="small param layouts")
    ncd.__enter__()

    def wview(off, ncols):
        return w_in[:, off:off + ncols].rearrange("(c p) n -> p c n", p=128)

    with fw.stack:
        dA, dM, dR = fw.view(scrA, "scrA"), fw.view(scrM, "scrM"), fw.view(scrR, "scrR")
        DBt = [fw.stack.enter_context(nc.psum_tensor("db%d" % i, [128, 1024], F32)) for i in range(4)]
        DBf = [t[:] for t in DBt]
        DBh = [t[:].bitcast(BF16) for t in DBt]
        PB = [Buf(None, "bank%d" % j) for j in range(8)]

        def bk(i, both=True, half=0):
            return (PB[2 * i], PB[2 * i + 1]) if both else (PB[2 * i + half],)

        psem = fw.dma_sem("params")
        psem2 = fw.dma_sem("paramsq")

        def pload(name, shape, src, q="sp", dtype=F32):
            b = fw.sb(name, shape, dtype)
            b.ld_sem = psem if q == "sp" else psem2
            fw.load(q, b, b[:], src)
            return b

        def colparam(name, nm):
            return pload(name, [128, 8], P[nm].rearrange("o (c p) -> p (o c)", p=128))

        gcol = colparam("gcol", "norm_g")
        kk_c = colparam("kk_c", "k_k")
        ka_c = colparam("ka_c", "k_a")
        rk_c = colparam("rk_c", "r_k")
        lg_c = colparam("lg_c", "ln_x_g")
        lb_c = colparam("lb_c", "ln_x_b")
        w0_c = pload("w0_c", [128, 2, 8], P["w0"].rearrange("d (c p) -> p d c", p=128))
        a0_c = pload("a0_c", [128, 2, 8], P["a0"].rearrange("d (c p) -> p d c", p=128))
        cv_c = pload("cv_c", [128, 3, 24], P["conv_rkv"].rearrange("i (c p) -> p i c", p=128))
        fgb = pload("fgb", [128, 1024], P["final_g"][0, :].partition_broadcast(128))
        sgb = pload("sgb", [128, 128], P["attn_subln_g"][0, :].partition_broadcast(128))
        lq = [pload("lam%d" % i, [128, 64], P[n][0, :].partition_broadcast(128))
              for i, n in enumerate(["lam_q1", "lam_k1", "lam_q2", "lam_k2"])]
        wlw = pload("wlw", [128, 1024], P["w_lora_up"], q="pool", dtype=BF16)
        wla = pload("wla", [128, 1024], P["a_lora_up"], q="pool", dtype=BF16)

        ident = fw.sb("ident", [128, 128], BF16)
        op("pool", lambda e: e.memset(ident[:], 1.0), writes=(ident,))
        op("pool", lambda e: e.affine_select(out=ident[:], in_=ident[:], pattern=[[-1, 128]], compare_op=ALU.is_equal,
                                             fill=0.0, base=0, channel_multiplier=1), reads=(ident,), writes=(ident,))
        grep = fw.sb("grep", [128, 8, 128], F32)
        op("dve", lambda e: e.memset(grep[:], 1.0), writes=(grep,))
        for c in range(8):
            op("dve", lambda e: e.tensor_scalar(out=grep[:, c, :], in0=grep[:, c, :], scalar1=gcol[:, c:c + 1],
                                                scalar2=None, op0=ALU.mult), reads=(grep, gcol), writes=(grep,))
        op("dve", lambda e: e.tensor_scalar(out=sgb[:], in0=sgb[:], scalar1=0.8, scalar2=None, op0=ALU.mult),
           reads=(sgb,), writes=(sgb,))
        hw0_c = fw.sb("hw0_c", [128, 2, 8], F32)
        ha0_c = fw.sb("ha0_c", [128, 2, 8], F32)
        op("dve", lambda e: e.tensor_scalar(out=hw0_c[:], in0=w0_c[:], scalar1=0.5, scalar2=None, op0=ALU.mult), reads=(w0_c,), writes=(hw0_c,))
        op("dve", lambda e: e.tensor_scalar(out=ha0_c[:], in0=a0_c[:], scalar1=0.5, scalar2=None, op0=ALU.mult), reads=(a0_c,), writes=(ha0_c,))
        tmka = fw.sb("tmka", [128, 8], F32)
        op("dve", lambda e: e.tensor_scalar(out=tmka[:], in0=ka_c[:], scalar1=-1.0, scalar2=2.0, op0=ALU.mult, op1=ALU.add),
           reads=(ka_c,), writes=(tmka,))
        eps5 = fw.sb("eps5", [128, 1], F32)
        op("dve", lambda e: e.memset(eps5[:], 1e-5), writes=(eps5,))
        lnhalf = fw.sb("lnhalf", [128, 1], F32)
        op("dve", lambda e: e.memset(lnhalf[:], math.log(0.5)), writes=(lnhalf,))
        omka = fw.sb("omka", [128, 8], F32)
        op("dve", lambda e: e.tensor_scalar(out=omka[:], in0=ka_c[:], scalar1=-1.0, scalar2=1.0, op0=ALU.mult, op1=ALU.add),
           reads=(ka_c,), writes=(omka,))
        lt = fw.sb("lt", [128, 64], F32)
        ls = fw.sb("ls", [128, 2], F32)
        nlam = fw.sb("nlam", [128, 1], F32)
        for i in range(2):
            op("dve", lambda e: e.tensor_tensor(out=lt[:], in0=lq[2 * i][:], in1=lq[2 * i + 1][:], op=ALU.mult),
               reads=(lq[2 * i], lq[2 * i + 1]), writes=(lt,))
            op("dve", lambda e: e.reduce_sum(out=ls[:, i:i + 1], in_=lt[:], axis=AX.X), reads=(lt,), writes=(ls,))
        op("act", lambda e: e.activation(out=ls[:], in_=ls[:], func=AF.Exp), reads=(ls,), writes=(ls,))
        op("dve", lambda e: e.tensor_tensor(out=nlam[:], in0=ls[:, 1:2], in1=ls[:, 0:1], op=ALU.subtract),
           reads=(ls,), writes=(nlam,))
        op("dve", lambda e: e.tensor_scalar(out=nlam[:], in0=nlam[:], scalar1=-0.2, scalar2=None, op0=ALU.add),
           reads=(nlam,), writes=(nlam,))

        def mkmask(name, mult_f, mult_p, base, cmp):
            m = fw.sb(name, [128, 128], BF16)
            op("pool", lambda e: e.memset(m[:], 1.0), writes=(m,))
            op("pool", lambda e: e.affine_select(out=m[:], in_=m[:], pattern=[[mult_f, 128]], compare_op=cmp,
                                                 fill=0.0, base=base, channel_multiplier=mult_p), reads=(m,), writes=(m,))
            return m
        Us = mkmask("Us", 1, -1, 0, ALU.is_gt)
        Ui = mkmask("Ui", 1, -1, 0, ALU.is_ge)
        Ls = mkmask("Ls", -1, 1, 0, ALU.is_gt)
        Li = mkmask("Li", -1, 1, 0, ALU.is_ge)
        MASK4 = []
        MASK8 = []
        MASKL = []
        for d_ in range(2):
            m4 = fw.sb("m4_%d" % d_, [128, 4, 128], BF16)
            s_, i_ = (Us, Ui) if d_ == 0 else (Ls, Li)
            for j, src in enumerate([s_, i_, s_, i_]):
                op("pool", lambda e: e.tensor_copy(out=m4[:, j, :], in_=src[:]), reads=(src,), writes=(m4,))
            MASK4.append(m4)
            m8 = fw.sb("m8_%d" % d_, [128, 2, 4, 128], BF16)
            for h_ in range(2):
                for j, src in enumerate([s_, i_, s_, i_]):
                    op("pool", lambda e: e.tensor_copy(out=m8[:, h_, j, :], in_=src[:]), reads=(src,), writes=(m8,))
            MASK8.append(m8)
            ml = fw.sb("ml_%d" % d_, [128, 2, 128], BF16)
            src = Ls if d_ == 0 else Us
            for j in range(2):
                op("pool", lambda e: e.tensor_copy(out=ml[:, j, :], in_=src[:]), reads=(src,), writes=(ml,))
            MASKL.append(ml)
        BOh = fw.sb("BOh", [128, 128], BF16)
        BOf = fw.sb("BOf", [128, 128], F32)
        for t_ in (BOh, BOf):
            op("pool", lambda e: e.memset(t_[:], 0.0), writes=(t_,))
            op("pool", lambda e: e.memset(t_[0:64, 0:64], 1.0), reads=(t_,), writes=(t_,))
            op("pool", lambda e: e.memset(t_[64:128, 64:128], 1.0), reads=(t_,), writes=(t_,))
        I2 = fw.sb("I2", [128, 64], F32)
        op("pool", lambda e: e.memset(I2[:], 1.0), writes=(I2,))
        op("pool", lambda e: e.affine_select(out=I2[0:64, :], in_=I2[0:64, :], pattern=[[-1, 64]], compare_op=ALU.is_equal,
                                             fill=0.0, base=0, channel_multiplier=1), reads=(I2,), writes=(I2,))
        op("pool", lambda e: e.affine_select(out=I2[64:128, :], in_=I2[64:128, :], pattern=[[-1, 64]], compare_op=ALU.is_equal,
                                             fill=0.0, base=0, channel_multiplier=1), reads=(I2,), writes=(I2,))
        onesS = fw.sb("onesS", [128, 128], F32)
        op("pool", lambda e: e.memset(onesS[:], 1.0), writes=(onesS,))

        def proj_fm(dst_fn, W, wslot_fn, xnT, S, evac):
            pass

        tok0 = 0
        for si, S in enumerate(seq_lens):
            NT = S // 128
            NB = S // 512 if S >= 512 else 1
            BW = min(512, S)
            with contextlib.ExitStack() as sq:
                xnT = fw.sb("xnT@%d" % si, [128, 8, S + 2], BF16, sq)
                op("pool", lambda e: e.memset(xnT[:, :, 0:1], 0.0), writes=(xnT,))
                op("pool", lambda e: e.memset(xnT[:, :, S + 1:S + 2], 0.0), writes=(xnT,))
                with contextlib.ExitStack() as sa:
                    xt = [fw.sb("xt%d@%d" % (i, si), [128, 1024], F32, sa) for i in range(2)]
                    junk = fw.sb("junkA@%d" % si, [128, 1024], F32, sa)
                    xs = [fw.sb("xs%d@%d" % (i, si), [128, 1024], BF16, sa) for i in range(2)]
                    ssA = [fw.sb("ssA%d@%d" % (i, si), [128, 1], F32, sa) for i in range(2)]
                    for tt in range(NT):
                        b = tt % 2
                        fw.load("sp", xt[b], xt[b][:], x[tok0 + tt * 128: tok0 + (tt + 1) * 128, :])
                        op("act", lambda e: e.activation(out=junk[:], in_=xt[b][:], func=AF.Square, accum_out=ssA[b][:]),
                           reads=(xt[b],), writes=(junk, ssA[b]))
                        op("dve", lambda e: e.tensor_scalar(out=ssA[b][:], in0=ssA[b][:], scalar1=1.0 / 1024, scalar2=1e-6,
                                                            op0=ALU.mult, op1=ALU.add), reads=(ssA[b],), writes=(ssA[b],))
                        op("act", lambda e: e.activation(out=ssA[b][:], in_=ssA[b][:], func=AF.Sqrt), reads=(ssA[b],), writes=(ssA[b],))
                        op("dve", lambda e: e.reciprocal(out=ssA[b][:], in_=ssA[b][:]), reads=(ssA[b],), writes=(ssA[b],))
                        op("act", lambda e: e.activation(out=xs[b][:], in_=xt[b][:], func=AF.Copy, scale=ssA[b][:, 0:1]),
                           reads=(xt[b], ssA[b]), writes=(xs[b],))
                        db = tt % 2
                        for c in range(8):
                            op("pe", lambda e: e.transpose(out=DBh[db][:, c * 128:(c + 1) * 128], in_=xs[b][:, c * 128:(c + 1) * 128],
                                                           identity=ident[:]), reads=(xs[b], ident), writes=bk(db, False, 0))
                        op("dve", lambda e: e.tensor_tensor(out=xnT[:, :, 1 + tt * 128: 1 + (tt + 1) * 128],
                                                            in0=DBh[db][:, 0:1024].rearrange("p (c t) -> p c t", c=8),
                                                            in1=grep[:], op=ALU.mult),
                           reads=bk(db, False, 0) + (grep,), writes=(xnT,))

                def xblk(kc, tb):
                    return xnT[:, kc, 1 + tb * BW: 1 + (tb + 1) * BW]

                C = SimpleNamespace(**locals())
                if do_attn:
                    stage_attn(C)
                if do_rwkv:
                    stage_rwkv(C)
                stage_out(C)
            tok0 += S
        fw.final_wait("sp")
    ncd.__exit__(None, None, None)
    return nc


def load_w(C, st, name, src_ap, shape):
    b = C.fw.sb(name, shape, BF16, st)
    C.fw.load("pool", b, b[:], src_ap)
    return b


def stage_out(C):
    fw, op, S, BW, NB, si = C.fw, C.fw.op, C.S, C.BW, C.NB, C.si
    DBf, bk = C.DBf, C.bk
    with contextlib.ExitStack() as st:
        Wout = load_w(C, st, "Wout@%d" % si, C.P["w_out"].rearrange("(c p) n -> p c n", p=128), [128, 8, 1024])
        ma = [fw.sb("ma%d@%d" % (i, si), [128, 8, BW], BF16, st) for i in range(2)]
        mr = [fw.sb("mr%d@%d" % (i, si), [128, 8, BW], BF16, st) for i in range(2)]
        xt = [fw.sb("xo%d@%d" % (i, si), [128, 1024], F32, st) for i in range(2)]
        zt = [fw.sb("zt%d@%d" % (i, si), [128, 1024], F32, st) for i in range(2)]
        junk = fw.sb("junkO@%d" % si, [128, 1024], F32, st)
        ss = [fw.sb("ssO%d@%d" % (i, si), [128, 1], F32, st) for i in range(2)]
        cnt = 0
        for tb in range(NB):
            b = tb % 2
            m = None
            if C.do_attn:
                fw.dma("sp", ma[b][:], C.scrM[:, :, tb * BW:(tb + 1) * BW], _ldsem(fw, ma[b]), reads=(C.dM,), writes=(ma[b],))
                m = ma[b]
            if C.do_rwkv:
                fw.dma("sp", mr[b][:], C.scrR[:, :, tb * BW:(tb + 1) * BW], _ldsem(fw, mr[b]), reads=(C.dR,), writes=(mr[b],))
                if m is None:
                    m = mr[b]
                else:
                    op("pool", lambda e: e.tensor_tensor(out=ma[b][:], in0=ma[b][:], in1=mr[b][:], op=ALU.add),
                       reads=(ma[b], mr[b]), writes=(ma[b],))
            for t4 in range(BW // 128):
                tt = tb * (BW // 128) + t4
                xb = cnt % 2
                db = 2 + cnt % 2
                cnt += 1
                fw.load("sp", xt[xb], xt[xb][:], C.x[C.tok0 + tt * 128: C.tok0 + (tt + 1) * 128, :])
                if m is not None:
                    for half in range(2):
                        for dc in range(8):
                            op("pe", lambda e: e.matmul(DBf[db][:, half * 512:(half + 1) * 512], lhsT=m[:, dc, t4 * 128:(t4 + 1) * 128],
                                                        rhs=Wout[:, dc, half * 512:(half + 1) * 512], start=(dc == 0), stop=(dc == 7)),
                               reads=(m, Wout), writes=bk(db, False, half))
                    op("dve", lambda e: e.tensor_tensor(out=zt[xb][:], in0=DBf[db][:], in1=xt[xb][:], op=ALU.add),
                       reads=bk(db) + (xt[xb],), writes=(zt[xb],))
                else:
                    op("dve", lambda e: e.tensor_copy(out=zt[xb][:], in_=xt[xb][:]), reads=(xt[xb],), writes=(zt[xb],))
                op("act", lambda e: e.activation(out=junk[:], in_=zt[xb][:], func=AF.Square, accum_out=ss[xb][:]),
                   reads=(zt[xb],), writes=(junk, ss[xb]))
                op("dve", lambda e: e.tensor_scalar(out=ss[xb][:], in0=ss[xb][:], scalar1=1.0 / 1024, scalar2=1e-6,
                                                    op0=ALU.mult, op1=ALU.add), reads=(ss[xb],), writes=(ss[xb],))
                op("act", lambda e: e.activation(out=ss[xb][:], in_=ss[xb][:], func=AF.Sqrt), reads=(ss[xb],), writes=(ss[xb],))
                op("dve", lambda e: e.reciprocal(out=ss[xb][:], in_=ss[xb][:]), reads=(ss[xb],), writes=(ss[xb],))
                op("dve", lambda e: e.scalar_tensor_tensor(out=zt[xb][:], in0=zt[xb][:], scalar=ss[xb][:, 0:1], in1=C.fgb[:],
                                                           op0=ALU.mult, op1=ALU.mult), reads=(zt[xb], ss[xb], C.fgb), writes=(zt[xb],))
                fw.store("act", zt[xb], C.y[C.tok0 + tt * 128: C.tok0 + (tt + 1) * 128, :], zt[xb][:])


def _ldsem(fw, b):
    if b.ld_sem is None:
        b.ld_sem = fw.dma_sem("l" + b.name)
    return b.ld_sem


def _stsem(fw, b):
    if b.st_sem is None:
        b.st_sem = fw.dma_sem("s" + b.name)
    return b.st_sem


def stage_attn(C):
    fw, op, S, BW, NB, NT, si = C.fw, C.fw.op, C.S, C.BW, C.NB, C.NT, C.si
    DBf, DBh, bk, xnT, xblk = C.DBf, C.DBh, C.bk, C.xnT, C.xblk
    ident = C.ident
    nq = BW // 128
    with contextlib.ExitStack() as st:
        cosT, sinT = make_rope(C, st)
        Wh = fw.sb("Wh@%d" % si, [128, 5, 8, 128], BF16, st)
        op("pool", lambda e: e.memset(Wh[:, 3:5, :, :], 0.0), writes=(Wh,))
        qT = fw.sb("qT@%d" % si, [128, S], BF16, st)
        kT = fw.sb("kT@%d" % si, [128, S], BF16, st)
        Vh = fw.sb("Vh@%d" % si, [128, NT, 129], BF16, st)
        op("pool", lambda e: e.memset(Vh[:, :, 128:129], 1.0), writes=(Vh,))
        E = [fw.sb("E%d@%d" % (i, si), [128, 2, BW], BF16, st) for i in range(2)]
        t1s = [fw.sb("t1%d@%d" % (i, si), [128, BW], F32, st) for i in range(2)]
        t2s = [fw.sb("t2%d@%d" % (i, si), [128, BW], F32, st) for i in range(2)]
        rsq = [fw.sb("rs%d@%d" % (i, si), [128, 2], F32, st) for i in range(4)]
        o1q = [fw.sb("o1%d@%d" % (i, si), [128, 128], F32, st) for i in range(4)]
        ssqq = [fw.sb("ssq%d@%d" % (i, si), [128, 1], F32, st) for i in range(4)]
        onq = [fw.sb("on%d@%d" % (i, si), [128, 128], BF16, st) for i in range(4)]
        junk = fw.sb("junkB@%d" % si, [128, 128], F32, st)
        ssq = fw.sb("ssq@%d" % si, [128, 1], F32, st)
        on = fw.sb("on@%d" % si, [128, 128], BF16, st)
        ogT = [fw.sb("ogT%d@%d" % (i, si), [128, BW], BF16, st) for i in range(2)]
        for h in range(8):
            for s_, off in enumerate((O_Q, O_K, O_V)):
                fw.load("pool", Wh, Wh[:, s_, :, :], C.wview(off + h * 128, 128))
            for s_ in (0, 1):
                for m in (0, 1):
                    b0 = m * 64
                    op("pool", lambda e: e.tensor_scalar(out=Wh[:, 3 + s_, :, b0:b0 + 8], in0=Wh[:, s_, :, b0 + 8:b0 + 16],
                                                         scalar1=-1.0, scalar2=None, op0=ALU.mult), reads=(Wh,), writes=(Wh,))
                    op("pool", lambda e: e.tensor_copy(out=Wh[:, 3 + s_, :, b0 + 8:b0 + 16], in_=Wh[:, s_, :, b0:b0 + 8]),
                       reads=(Wh,), writes=(Wh,))
            for tb in range(NB):
                for s_, dst in ((0, qT), (1, kT)):
                    db = (2 * tb + s_) % 4
                    t1, t2 = t1s[s_], t2s[s_]
                    for j, slot in enumerate((s_, 3 + s_)):
                        for kc in range(8):
                            op("pe", lambda e: e.matmul(DBf[db][:, j * 512:j * 512 + BW], lhsT=Wh[:, slot, kc, :], rhs=xblk(kc, tb),
                                                        start=(kc == 0), stop=(kc == 7)), reads=(Wh, xnT), writes=bk(db, False, j))
                    op("dve", lambda e: e.tensor_tensor(out=t1[:], in0=DBf[db][:, 0:BW], in1=cosT[:, tb * BW:(tb + 1) * BW], op=ALU.mult),
                       reads=bk(db, False, 0) + (cosT,), writes=(t1,))
                    op("dve", lambda e: e.tensor_tensor(out=t2[:], in0=DBf[db][:, 512:512 + BW], in1=sinT[:, tb * BW:(tb + 1) * BW], op=ALU.mult),
                       reads=bk(db, False, 1) + (sinT,), writes=(t2,))
                    op("pool", lambda e: e.tensor_tensor(out=dst[:, tb * BW:(tb + 1) * BW], in0=t1[:], in1=t2[:], op=ALU.add),
                       reads=(t1, t2), writes=(dst,))
                vdb = (2 * tb + 2) % 4
                for t4 in range(nq):
                    tt = tb * nq + t4
                    for kc in range(8):
                        op("pe", lambda e: e.matmul(DBf[vdb][:, 512 + t4 * 128:512 + (t4 + 1) * 128], lhsT=xnT[:, kc, 1 + tt * 128:1 + (tt + 1) * 128],
                                                    rhs=Wh[:, 2, kc, :], start=(kc == 0), stop=(kc == 7)),
                           reads=(Wh, xnT), writes=bk(vdb, False, 1))
                op("act", lambda e: e.activation(out=Vh[:, tb * nq:(tb + 1) * nq, 0:128],
                                                 in_=DBf[vdb][:, 512:512 + BW].rearrange("p (a b) -> p a b", b=128), func=AF.Copy),
                   reads=bk(vdb, False, 1), writes=(Vh,))
            for qc in range(NB):
                def scores(kb):
                    sb_ = kb % 2
                    for m in (0, 1):
                        op("pe", lambda e: e.matmul(DBf[sb_][:, m * 512:m * 512 + BW], lhsT=kT[m * 64:(m + 1) * 64, kb * 128:(kb + 1) * 128],
                                                    rhs=qT[m * 64:(m + 1) * 64, qc * BW:(qc + 1) * BW], start=True, stop=True),
                           reads=(kT, qT), writes=bk(sb_, False, m))
                scores(0)
                for kb in range(NT):
                    sb_ = kb % 2
                    if kb + 1 < NT:
                        scores(kb + 1)
                    op("act", lambda e: e.activation(out=E[sb_][:], in_=DBf[sb_][:, :].rearrange("p (m q) -> p m q", m=2)[:, :, 0:BW],
                                                     func=AF.Exp, scale=0.125), reads=bk(sb_), writes=(E[sb_],))
                    for qs in range(nq):
                        dba, hf = 2 + qs // 2, qs % 2
                        for m in (0, 1):
                            off = hf * 512 + m * 129
                            op("pe", lambda e: e.matmul(DBf[dba][:, off:off + 129], lhsT=E[sb_][:, m, qs * 128:(qs + 1) * 128],
                                                        rhs=Vh[:, kb, :], start=(kb == 0 and m == 0), stop=(kb == NT - 1),
                                                        skip_group_check=True),
                               reads=(E[sb_], Vh), writes=bk(dba, False, hf))
                accs = []
                for qs in range(nq):
                    dba, hf = 2 + qs // 2, qs % 2
                    accs.append((DBf[dba][:, hf * 512:hf * 512 + 258].rearrange("p (m c) -> p m c", m=2), bk(dba, False, hf)))
                for qs in range(nq):
                    acc, pbk = accs[qs]
                    op("dve", lambda e: e.reciprocal(out=rsq[qs][:], in_=acc[:, :, 128]), reads=pbk, writes=(rsq[qs],))
                for qs in range(nq):
                    op("dve", lambda e: e.tensor_tensor(out=rsq[qs][:, 1:2], in0=rsq[qs][:, 1:2], in1=C.nlam[:, 0:1], op=ALU.mult),
                       reads=(rsq[qs], C.nlam), writes=(rsq[qs],))
                for qs in range(nq):
                    acc, pbk = accs[qs]
                    op("dve", lambda e: e.tensor_scalar(out=o1q[qs][:], in0=acc[:, 0, 0:128], scalar1=rsq[qs][:, 0:1], scalar2=None, op0=ALU.mult),
                       reads=pbk + (rsq[qs],), writes=(o1q[qs],))
                for qs in range(nq):
                    acc, pbk = accs[qs]
                    op("dve", lambda e: e.scalar_tensor_tensor(out=o1q[qs][:], in0=acc[:, 1, 0:128], scalar=rsq[qs][:, 1:2], in1=o1q[qs][:],
                                                               op0=ALU.mult, op1=ALU.add), reads=pbk + (rsq[qs], o1q[qs]), writes=(o1q[qs],))
                for qs in range(nq):
                    op("act", lambda e: e.activation(out=junk[:], in_=o1q[qs][:], func=AF.Square, accum_out=ssqq[qs][:]),
                       reads=(o1q[qs],), writes=(junk, ssqq[qs]))
                for qs in range(nq):
                    op("act", lambda e: e.activation(out=ssqq[qs][:], in_=ssqq[qs][:], func=AF.Ln, scale=1.0 / 128, bias=C.eps5[:, 0:1]),
                       reads=(ssqq[qs], C.eps5), writes=(ssqq[qs],))
                for qs in range(nq):
                    op("act", lambda e: e.activation(out=ssqq[qs][:], in_=ssqq[qs][:], func=AF.Exp, scale=-0.5), reads=(ssqq[qs],), writes=(ssqq[qs],))
                for qs in range(nq):
                    op("dve", lambda e: e.scalar_tensor_tensor(out=onq[qs][:], in0=o1q[qs][:], scalar=ssqq[qs][:, 0:1], in1=C.sgb[:],
                                                               op0=ALU.mult, op1=ALU.mult), reads=(o1q[qs], ssqq[qs], C.sgb), writes=(onq[qs],))
                for qs in range(nq):
                    op("pe", lambda e: e.transpose(out=DBh[0][:, qs * 128:(qs + 1) * 128], in_=onq[qs][:], identity=ident[:]),
                       reads=(onq[qs], ident), writes=bk(0, False, 0))
                og = ogT[qc % 2]
                op("act", lambda e: e.activation(out=og[:], in_=DBh[0][:, 0:BW], func=AF.Copy), reads=bk(0, False, 0), writes=(og,))
                fw.dma("sp", C.scrA[:, h, qc * BW:(qc + 1) * BW], og[:], _stsem(fw, og), reads=(og,), writes=(C.dA,))
    with contextlib.ExitStack() as st:
        Wg = load_w(C, st, "Wg@%d" % si, C.wview(O_GA, 1024), [128, 8, 1024])
        Woa = load_w(C, st, "Woa@%d" % si, C.P["w_o_attn"].rearrange("(c p) n -> p c n", p=128), [128, 8, 1024])
        Wgm = load_w(C, st, "Wgm@%d" % si, C.wview(O_GMA, 1024), [128, 8, 1024])
        branch_tail(C, st, "a", Wg, Woa, Wgm, C.scrA, C.dA, C.scrM, C.dM, AF.Silu)


def branch_tail(C, st, tag, Wg, Wo, Wgm, src, dsrc, dst, ddst, gate_func):
    fw, op, S, BW, NB, si = C.fw, C.fw.op, C.S, C.BW, C.NB, C.si
    DBf, bk, xnT, xblk = C.DBf, C.bk, C.xnT, C.xblk
    og = [fw.sb("og%s%d@%d" % (tag, i, si), [128, 8, BW], BF16, st) for i in range(2)]
    OG = fw.sb("OG%s@%d" % (tag, si), [128, 8, BW], BF16, st)
    sg = [fw.sb("sg%s%d@%d" % (tag, i, si), [128, BW], F32, st) for i in range(2)]
    mab = [fw.sb("mab%s%d@%d" % (tag, i, si), [128, 8, BW], BF16, st) for i in range(2)]
    for tb in range(NB):
        b = tb % 2
        fw.dma("sp", og[b][:], src[:, :, tb * BW:(tb + 1) * BW], _ldsem(fw, og[b]), reads=(dsrc,), writes=(og[b],))
        if Wg is not None:
            for dc in range(8):
                db = dc % 2
                for kc in range(8):
                    op("pe", lambda e: e.matmul(DBf[db][:, 0:BW], lhsT=Wg[:, kc, dc * 128:(dc + 1) * 128], rhs=xblk(kc, tb),
                                                start=(kc == 0), stop=(kc == 7)), reads=(Wg, xnT), writes=bk(db, False, 0))
                op("act", lambda e: e.activation(out=sg[db][:], in_=DBf[db][:, 0:BW], func=gate_func), reads=bk(db, False, 0), writes=(sg[db],))
                op("dve", lambda e: e.tensor_tensor(out=OG[:, dc, :], in0=og[b][:, dc, :], in1=sg[db][:], op=ALU.mult),
                   reads=(og[b], sg[db]), writes=(OG,))
            G = OG
        else:
            G = og[b]
        for dc in range(8):
            db = 2 + dc % 2
            for hh in range(8):
                op("pe", lambda e: e.matmul(DBf[db][:, 0:BW], lhsT=Wo[:, hh, dc * 128:(dc + 1) * 128], rhs=G[:, hh, :],
                                            start=(hh == 0), stop=(hh == 7)), reads=(Wo, G), writes=bk(db, False, 0))
            for kc in range(8):
                op("pe", lambda e: e.matmul(DBf[db][:, 512:512 + BW], lhsT=Wgm[:, kc, dc * 128:(dc + 1) * 128], rhs=xblk(kc, tb),
                                            start=(kc == 0), stop=(kc == 7)), reads=(Wgm, xnT), writes=bk(db, False, 1))
            sgi = dc % 2
            op("act", lambda e: e.activation(out=sg[sgi][:], in_=DBf[db][:, 512:512 + BW], func=AF.Sigmoid),
               reads=bk(db, False, 1), writes=(sg[sgi],))
            op("dve", lambda e: e.tensor_tensor(out=mab[b][:, dc, :], in0=DBf[db][:, 0:BW], in1=sg[sgi][:], op=ALU.mult),
               reads=bk(db, False, 0) + (sg[sgi],), writes=(mab[b],))
        fw.dma("act", dst[:, :, tb * BW:(tb + 1) * BW], mab[b][:], _stsem(fw, mab[b]), reads=(mab[b],), writes=(ddst,))


def stage_rwkv(C):
    fw, op, S, BW, NB, NT, si = C.fw, C.fw.op, C.S, C.BW, C.NB, C.NT, C.si
    DBf, DBh, bk, xnT, xblk, PB = C.DBf, C.DBh, C.bk, C.xnT, C.xblk, C.PB
    ident, BOh, BOf, I2, onesS = C.ident, C.BOh, C.BOf, C.I2, C.onesS
    wlw, wla = C.wlw, C.wla
    GN_EPS = 64e-5
    CD = math.exp(-0.5)
    import os
    G = int(os.environ.get('RW_G', 4 if S <= 2048 else 3))
    G = min(G, NT)

    def pbank(j):
        return DBf[j // 2][:, (j % 2) * 512:(j % 2) * 512 + 512]

    def pbankh(j):
        return DBh[j // 2][:, (j % 2) * 1024:(j % 2) * 1024 + 1024]

    with contextlib.ExitStack() as st:
        TWT = fw.sb("TWT@%d" % si, [128, S], BF16, st)
        ALT = fw.sb("ALT@%d" % si, [128, S], BF16, st)
        with contextlib.ExitStack() as st0:
            Wlow = load_w(C, st0, "Wlow@%d" % si, C.wview(O_WL, 256), [128, 8, 256])
            for tb in range(NB):
                for j, (dst, func) in enumerate(((TWT, AF.Tanh), (ALT, AF.Copy))):
                    for kc in range(8):
                        op("pe", lambda e: e.matmul(pbank(j)[:, 0:BW], lhsT=Wlow[:, kc, j * 128:(j + 1) * 128], rhs=xblk(kc, tb),
                                                    start=(kc == 0), stop=(kc == 7)), reads=(Wlow, xnT), writes=(PB[j],))
                    op("act", lambda e: e.activation(out=dst[:, tb * BW:(tb + 1) * BW], in_=pbank(j)[:, 0:BW], func=func),
                       reads=(PB[j],), writes=(dst,))
        Whp = fw.sb("Whp@%d" % si, [128, 4, 8, 128], BF16, st)
        RT_ = fw.sb("RT_@%d" % si, [128, S], BF16, st)
        KT_ = fw.sb("KT_@%d" % si, [128, S], BF16, st)
        VT_ = fw.sb("VT_@%d" % si, [128, S], BF16, st)
        KKT = fw.sb("KKT@%d" % si, [128, S], BF16, st)
        OFT = fw.sb("OFT@%d" % si, [128, S], BF16, st)
        OBT = fw.sb("OBT@%d" % si, [128, S], BF16, st)
        CB = min(256, S)

        for hp in range(8):
            hpc = slice(hp, hp + 1)
            for s_, off in enumerate((O_R, O_RK, O_RV, O_GR)):
                fw.load("pool", Whp, Whp[:, s_, :, :], C.wview(off + hp * 128, 128))
            if True:
                if hp == 0:
                    if S <= 2048:
                        ctmps = [fw.sb("ctmp%d@%d" % (i, si), [128, CB], F32, st) for i in range(3)]
                    else:
                        ctmps = [fw.sb("ctmp0@%d" % si, [128, CB], F32, st)] * 3
                    ctmp = ctmps[0]
                    kkr = fw.sb("kkrw@%d" % si, [128, CB], F32, st)
                    nrm = fw.sb("nrmw@%d" % si, [128, CB], F32, st)
                    sqw = fw.sb("sqw@%d" % si, [128, CB], BF16, st)
                for cb in range(S // CB):
                    cs = slice(cb * CB, (cb + 1) * CB)
                    for s_, dst in enumerate((RT_, KT_, VT_)):
                        j = (3 * cb + s_) % 4
                        for kc in range(8):
                            op("pe", lambda e: e.matmul(pbank(j)[:, 0:CB + 2], lhsT=Whp[:, s_, kc, :], rhs=xnT[:, kc, cb * CB:cb * CB + CB + 2],
                                                        start=(kc == 0), stop=(kc == 7)), reads=(Whp, xnT), writes=(PB[j],))
                        cw = [C.cv_c[:, i, s_ * 8 + hp:s_ * 8 + hp + 1] for i in range(3)]
                        ct = ctmps[s_]
                        op("act", lambda e: e.activation(out=ct[:], in_=pbank(j)[:, 0:CB], func=AF.Copy, scale=cw[0]),
                           reads=(PB[j], C.cv_c), writes=(ct,))
                        op("dve", lambda e: e.scalar_tensor_tensor(out=ct[:], in0=pbank(j)[:, 1:CB + 1], scalar=cw[1], in1=ct[:],
                                                                   op0=ALU.mult, op1=ALU.add), reads=(PB[j], C.cv_c, ct), writes=(ct,))
                        op("dve", lambda e: e.scalar_tensor_tensor(out=dst[:, cs], in0=pbank(j)[:, 2:CB + 2], scalar=cw[2],
                                                                   in1=ct[:], op0=ALU.mult, op1=ALU.add),
                           reads=(PB[j], C.cv_c, ct), writes=(dst,))
                    j = 4 + cb % 2
                    if os.environ.get('RW_STOP') == 'c1nokk':
                        continue
                    op("dve", lambda e: e.tensor_scalar(out=kkr[:], in0=KT_[:, cs], scalar1=C.kk_c[:, hpc], scalar2=None, op0=ALU.mult),
                       reads=(KT_, C.kk_c), writes=(kkr,))
                    op("pool", lambda e: e.tensor_tensor(out=sqw[:], in0=kkr[:], in1=kkr[:], op=ALU.mult), reads=(kkr,), writes=(sqw,))
                    op("pe", lambda e: e.matmul(pbank(j)[:, 0:CB], lhsT=BOh[:], rhs=sqw[:], start=True, stop=True), reads=(BOh, sqw), writes=(PB[j],))
                    op("dve", lambda e: e.tensor_scalar(out=nrm[:], in0=pbank(j)[:, 0:CB], scalar1=1e-12, scalar2=None, op0=ALU.add),
                       reads=(PB[j],), writes=(nrm,))
                    op("act", lambda e: e.activation(out=nrm[:], in_=nrm[:], func=AF.Sqrt), reads=(nrm,), writes=(nrm,))
                    op("dve", lambda e: e.reciprocal(out=nrm[:], in_=nrm[:]), reads=(nrm,), writes=(nrm,))
                    op("dve", lambda e: e.tensor_tensor(out=KKT[:, cs], in0=kkr[:], in1=nrm[:], op=ALU.mult), reads=(kkr, nrm), writes=(KKT,))
            if True:
                def f32t(n, g):
                    return fw.sb("%s%d@%d" % (n, g, si), [128, 128], F32, st)

                def bft(n, g, shape):
                    return fw.sb("%s%d@%d" % (n, g, si), shape, BF16, st)
                if hp == 0:
                    W = []
                for g in (range(G) if hp == 0 else ()):
                    w = SimpleNamespace()
                    for n in ("sgw", "a_", "cum", "c2", "cex", "e_in", "e_ex", "e_ng", "kdir", "tmpb"):
                        setattr(w, n, f32t(n, g))
                    w.AR = bft("AR", g, [128, 2, 128])
                    w.BK = bft("BK", g, [128, 2, 128])
                    w.TM = bft("TM", g, [128, 4, 128])
                    w.AT = bft("AT", g, [128, 2, 4, 128])
                    w.XY0 = bft("XY0", g, [128, 2, 2, 128])
                    w.XY = [bft("XYa", g, [128, 2, 3, 128]), bft("XYb", g, [128, 2, 3, 128])]
                    w.Yf = bft("Yf", g, [128, 2, 128])
                    w.RP = bft("RP", g, [128, 128])
                    w.MpT = bft("MpT", g, [128, 64])
                    W.append(w)
                if hp == 0:
                    Hs = [fw.sb("H%d@%d" % (i, si), [128, 64], BF16, st) for i in range(2)]

                def head(d, tau, g):
                    w = W[g]
                    bA, bB = 2 * g, 2 * g + 1
                    PA, PBb = PB[bA], PB[bB]
                    dsl = slice(d * 64, (d + 1) * 64)
                    sl = slice(tau * 128, (tau + 1) * 128)
                    rT, kTt, vT, kkT = RT_[:, sl], KT_[:, sl], VT_[:, sl], KKT[:, sl]
                    op("pe", lambda e: e.matmul(pbank(bA)[:, 0:128], lhsT=wlw[dsl, hp * 128:(hp + 1) * 128], rhs=TWT[dsl, sl], start=True, stop=True),
                       reads=(wlw, TWT), writes=(PA,))
                    op("pe", lambda e: e.matmul(pbank(bA)[:, 128:256], lhsT=wla[dsl, hp * 128:(hp + 1) * 128], rhs=ALT[dsl, sl], start=True, stop=True),
                       reads=(wla, ALT), writes=(PA,))
                    yield
                    op("act", lambda e: e.activation(out=w.sgw[:], in_=pbank(bA)[:, 0:128], func=AF.Tanh, bias=C.hw0_c[:, d, hpc], scale=0.5),
                       reads=(PA, C.hw0_c), writes=(w.sgw,))
                    op("act", lambda e: e.activation(out=w.a_[:], in_=pbank(bA)[:, 128:256], func=AF.Tanh, bias=C.ha0_c[:, d, hpc], scale=0.5),
                       reads=(PA, C.ha0_c), writes=(w.a_,))
                    yield
                    op("pool", lambda e: e.tensor_scalar(out=w.kdir[:], in0=w.a_[:], scalar1=C.ka_c[:, hpc], scalar2=C.tmka[:, hpc], op0=ALU.mult, op1=ALU.add),
                       reads=(w.a_, C.ka_c, C.tmka), writes=(w.kdir,))
                    yield
                    op("pool", lambda e: e.tensor_tensor(out=w.kdir[:], in0=kTt, in1=w.kdir[:], op=ALU.mult), reads=(KT_, w.kdir), writes=(w.kdir,))
                    yield
                    op("dve", lambda e: e.tensor_tensor_scan(out=w.cum[:], data0=w.sgw[:], data1=onesS[:], initial=0.0, op0=ALU.add, op1=ALU.add),
                       reads=(onesS, w.sgw), writes=(w.cum,))
                    yield
                    cu = w.cum
                    if d == 1:
                        op("dve", lambda e: e.tensor_scalar(out=w.c2[:], in0=w.cum[:], scalar1=-1.0, scalar2=w.cum[:, 127:128], op0=ALU.mult, op1=ALU.add),
                           reads=(w.cum,), writes=(w.c2,))
                        yield
                        op("dve", lambda e: e.scalar_tensor_tensor(out=w.c2[:], in0=w.c2[:], scalar=1.0, in1=w.sgw[:], op0=ALU.add, op1=ALU.add),
                           reads=(w.c2, w.sgw), writes=(w.c2,))
                        yield
                        cu = w.c2
                    op("dve", lambda e: e.scalar_tensor_tensor(out=w.cex[:], in0=cu[:], scalar=-1.0, in1=w.sgw[:], op0=ALU.add, op1=ALU.subtract),
                       reads=(cu, w.sgw), writes=(w.cex,))
                    yield
                    HC = 0.5 * CD
                    op("act", lambda e: e.activation(out=w.e_in[:], in_=cu[:], func=AF.Exp, scale=-HC), reads=(cu,), writes=(w.e_in,))
                    op("act", lambda e: e.activation(out=w.e_ex[:], in_=w.cex[:], func=AF.Exp, scale=-HC), reads=(w.cex,), writes=(w.e_ex,))
                    op("act", lambda e: e.activation(out=w.e_ng[:], in_=cu[:], func=AF.Exp, scale=HC, bias=C.lnhalf[:, 0:1]), reads=(cu, C.lnhalf), writes=(w.e_ng,))
                    yield
                    op("dve", lambda e: e.scalar_tensor_tensor(out=w.AR[:, 0, :], in0=kkT, scalar=-1.0, in1=w.e_ex[:], op0=ALU.mult, op1=ALU.mult),
                       reads=(KKT, w.e_ex), writes=(w.AR,))
                    yield
                    op("pool", lambda e: e.tensor_tensor(out=w.AR[:, 1, :], in0=rT, in1=w.e_in[:], op=ALU.mult), reads=(RT_, w.e_in, w.AR), writes=(w.AR,))
                    yield
                    op("dve", lambda e: e.scalar_tensor_tensor(out=w.tmpb[:], in0=w.a_[:], scalar=1.0, in1=kkT, op0=ALU.add, op1=ALU.mult),
                       reads=(KKT, w.a_), writes=(w.tmpb,))
                    yield
                    op("pool", lambda e: e.tensor_tensor(out=w.BK[:, 0, :], in0=w.tmpb[:], in1=w.e_ng[:], op=ALU.mult), reads=(w.tmpb, w.e_ng), writes=(w.BK,))
                    op("pool", lambda e: e.tensor_tensor(out=w.BK[:, 1, :], in0=w.kdir[:], in1=w.e_ng[:], op=ALU.mult), reads=(w.kdir, w.e_ng, w.BK), writes=(w.BK,))
                    yield
                    for j, (srcb, srcap) in enumerate(((w.AR, w.AR[:, 0, :]), (w.BK, w.BK[:, 0, :]), (w.BK, w.BK[:, 1, :]), (VT_, vT))):
                        op("pe", lambda e: e.transpose(out=pbankh(bB)[:, j * 128:(j + 1) * 128], in_=srcap, identity=ident[:]),
                           reads=(srcb, ident), writes=(PBb,))
                    yield
                    op("act", lambda e: e.activation(out=w.TM[:], in_=pbankh(bB)[:, 0:512].rearrange("p (a b) -> p a b", a=4), func=AF.Copy),
                       reads=(PBb,), writes=(w.TM,))
                    yield
                    for hh in (0, 1):
                        s = slice(hh * 64, (hh + 1) * 64)
                        op("pe", lambda e: e.matmul(pbank(bA + hh)[:, 0:128], lhsT=w.AR[s, 0, :], rhs=w.BK[s, 0, :], start=True, stop=True),
                           reads=(w.BK, w.AR), writes=(PB[bA + hh],))
                    yield
                    op("dve", lambda e: e.tensor_tensor(out=w.XY0[:, :, 0, :], in0=DBf[g][:, :].rearrange("p (h c) -> p h c", h=2)[:, :, 0:128],
                                                        in1=C.MASKL[d][:], op=ALU.mult), reads=(PA, PBb, C.MASKL[d]), writes=(w.XY0,))
                    yield
                    for hh in (0, 1):
                        s = slice(hh * 64, (hh + 1) * 64)
                        bj = bA + hh
                        arf = w.AR[s, :, :].rearrange("p a b -> p (a b)")
                        op("pe", lambda e: e.matmul(pbank(bj)[:, 0:256], lhsT=w.BK[s, 0, :], rhs=arf, start=True, stop=True),
                           reads=(w.BK, w.AR), writes=(PB[bj],))
                        op("pe", lambda e: e.matmul(pbank(bj)[:, 256:512], lhsT=w.BK[s, 1, :], rhs=arf, start=True, stop=True),
                           reads=(w.BK, w.AR), writes=(PB[bj],))
                    yield
                    op("dve", lambda e: e.tensor_tensor(out=w.AT[:], in0=DBf[g][:, :].rearrange("p (h a b) -> p h a b", h=2, a=4),
                                                        in1=C.MASK8[d][:], op=ALU.mult), reads=(PA, PBb, C.MASK8[d]), writes=(w.AT,))
                    yield
                    for hh in (0, 1):
                        s = slice(hh * 64, (hh + 1) * 64)
                        op("pe", lambda e: e.matmul(pbank(bA)[:, hh * 64:(hh + 1) * 64], lhsT=w.AT[:, hh, 2, :], rhs=w.TM[:, 3, s], start=True, stop=True),
                           reads=(w.AT, w.TM), writes=(PA,))
                    yield
                    op("act", lambda e: e.activation(out=w.XY0[:, :, 1, 64:128], in_=pbank(bA)[:, 0:128].rearrange("p (a b) -> p a b", a=2), func=AF.Copy),
                       reads=(PA, w.XY0), writes=(w.XY0,))
                    op("pool", lambda e: e.tensor_copy(out=w.XY0[:, :, 1, 0:64], in_=w.TM[:, 0, :].rearrange("p (a b) -> p a b", a=2)),
                       reads=(w.TM, w.XY0), writes=(w.XY0,))
                    yield
                    XN = [w.XY0[:, hh, 0, :] for hh in (0, 1)]
                    YK = [w.XY0[:, hh, 1, :] for hh in (0, 1)]
                    XNY = [w.XY0[:, hh, :, :].rearrange("p a b -> p (a b)") for hh in (0, 1)]
                    XT = [w.AT[:, hh, 0, :] for hh in (0, 1)]
                    srcb = (w.XY0, w.AT)
                    for k in range(7):
                        last = (k == 6)
                        for hh in (0, 1):
                            bj = bA + hh
                            if not last:
                                op("pe", lambda e: e.matmul(pbank(bj)[:, 256:384], lhsT=XN[hh], rhs=XT[hh], start=True, stop=True),
                                   reads=srcb, writes=(PB[bj],))
                                op("pe", lambda e: e.matmul(pbank(bj)[:, 0:256], lhsT=XT[hh], rhs=XNY[hh], start=True, stop=False),
                                   reads=srcb, writes=(PB[bj],))
                            else:
                                op("pe", lambda e: e.matmul(pbank(bj)[:, 128:256], lhsT=XT[hh], rhs=YK[hh], start=True, stop=False),
                                   reads=srcb, writes=(PB[bj],))
                            op("pe", lambda e: e.matmul(pbank(bj)[:, 128:256], lhsT=ident[:], rhs=YK[hh], start=False, stop=True),
                               reads=srcb + (ident,), writes=(PB[bj],))
                        yield
                        eng = "act" if k % 2 == 0 else "dve"
                        if not last:
                            nxt = w.XY[k % 2]
                            src = DBf[g][:, :].rearrange("p (h c) -> p h c", h=2)[:, :, 0:384].rearrange("p h (a b) -> p h a b", a=3)
                            if eng == "act":
                                op("act", lambda e: e.activation(out=nxt[:], in_=src, func=AF.Copy), reads=(PA, PBb), writes=(nxt,))
                            else:
                                op("dve", lambda e: e.tensor_copy(out=nxt[:], in_=src), reads=(PA, PBb), writes=(nxt,))
                            XN = [nxt[:, hh, 0, :] for hh in (0, 1)]
                            YK = [nxt[:, hh, 1, :] for hh in (0, 1)]
                            XNY = [nxt[:, hh, 0:2, :].rearrange("p a b -> p (a b)") for hh in (0, 1)]
                            XT = [nxt[:, hh, 2, :] for hh in (0, 1)]
                            srcb = (nxt,)
                        else:
                            src = DBf[g][:, :].rearrange("p (h c) -> p h c", h=2)[:, :, 128:256]
                            op("act", lambda e: e.activation(out=w.Yf[:], in_=src, func=AF.Copy), reads=(PA, PBb), writes=(w.Yf,))
                        yield
                    Yf = w.Yf
                    for hh in (0, 1):
                        s = slice(hh * 64, (hh + 1) * 64)
                        op("pe", lambda e: e.matmul(pbank(bA)[s, 384:512], lhsT=Yf[:, hh, 0:64], rhs=w.AT[:, hh, 1, :], start=True, stop=True),
                           reads=(Yf, w.AT), writes=(PA,))
                        op("pe", lambda e: e.matmul(pbank(bB)[s, 384:448], lhsT=Yf[:, hh, 0:64], rhs=w.TM[:, 1, s], start=True, stop=True),
                           reads=(Yf, w.TM), writes=(PBb,))
                    yield
                    op("dve", lambda e: e.tensor_tensor(out=w.RP[:], in0=pbank(bA)[:, 384:512], in1=w.AR[:, 1, :], op=ALU.add),
                       reads=(PA, w.AR), writes=(w.RP,))
                    op("dve", lambda e: e.tensor_tensor(out=w.MpT[:], in0=pbank(bB)[:, 384:448], in1=I2[:], op=ALU.add), reads=(PBb, I2), writes=(w.MpT,))
                    yield

                def tail(d, tau, g, Hc, Hn):
                    w = W[g]
                    bA, bB = 2 * g, 2 * g + 1
                    PA, PBb = PB[bA], PB[bB]
                    Yf = w.Yf
                    sl = slice(tau * 128, (tau + 1) * 128)
                    for hh in (0, 1):
                        s = slice(hh * 64, (hh + 1) * 64)
                        op("pe", lambda e: e.matmul(pbank(bA)[s, 0:128], lhsT=Yf[:, hh, 64:128], rhs=w.AT[:, hh, 1, :], start=True, stop=False),
                           reads=(Yf, w.AT), writes=(PA,))
                        op("pe", lambda e: e.matmul(pbank(bA)[s, 0:128], lhsT=w.TM[:, 3, s], rhs=w.AT[:, hh, 3, :], start=False, stop=False),
                           reads=(w.TM, w.AT), writes=(PA,))
                        op("pe", lambda e: e.matmul(pbank(bA)[s, 0:128], lhsT=Hc[s, :], rhs=w.RP[s, :], start=False, stop=True),
                           reads=(Hc, w.RP), writes=(PA,))
                    for hh in (0, 1):
                        s = slice(hh * 64, (hh + 1) * 64)
                        op("pe", lambda e: e.matmul(pbank(bB)[s, 0:64], lhsT=w.TM[:, 1, s], rhs=Yf[:, hh, 64:128], start=True, stop=False),
                           reads=(Yf, w.TM), writes=(PBb,))
                        op("pe", lambda e: e.matmul(pbank(bB)[s, 0:64], lhsT=w.TM[:, 2, s], rhs=w.TM[:, 3, s], start=False, stop=False),
                           reads=(w.TM,), writes=(PBb,))
                        op("pe", lambda e: e.matmul(pbank(bB)[s, 0:64], lhsT=w.MpT[s, :], rhs=Hc[s, :], start=False, stop=True),
                           reads=(w.MpT, Hc), writes=(PBb,))
                    WC = w.e_in[:, 127:128] if d == 0 else w.e_in[:, 0:1]
                    op("act", lambda e: e.activation(out=Hn[:], in_=pbank(bB)[:, 0:64], func=AF.Copy, scale=WC), reads=(PBb, w.e_in), writes=(Hn,))
                    dst = OFT if d == 0 else OBT
                    op("dve", lambda e: e.tensor_copy(out=dst[:, sl], in_=pbank(bA)[:, 0:128]), reads=(PA,), writes=(dst,))

                for d in (((0, 1) if 'RW_HEAD' not in os.environ else (0,)) if os.environ.get('RW_STOP') not in ('c1', 'c1nokk') else ()):
                    op("pool", lambda e: e.memset(Hs[0][:], 0.0), writes=(Hs[0],))
                    order = list(range(NT)) if d == 0 else list(range(NT - 1, -1, -1))
                    step = 0
                    DELTA = int(os.environ.get('RW_DELTA', 3))
                    slots = [None] * G
                    state = ['idle'] * G
                    completed = {}
                    next_pos = 0
                    tails_done = 0
                    rnd = 0
                    while tails_done < NT:
                        for gi in range(G):
                            if state[gi] == 'idle':
                                if next_pos < NT and rnd >= gi * DELTA:
                                    slots[gi] = (next_pos, head(d, order[next_pos], gi))
                                    state[gi] = 'run'
                                    next_pos += 1
                                else:
                                    continue
                            if state[gi] == 'run':
                                pos, gen = slots[gi]
                                try:
                                    next(gen)
                                except StopIteration:
                                    completed[pos] = gi
                                    state[gi] = 'wait'
                        while tails_done in completed:
                            gi = completed.pop(tails_done)
                            tail(d, order[tails_done], gi, Hs[step % 2], Hs[(step + 1) % 2])
                            step += 1
                            tails_done += 1
                            state[gi] = 'idle'
                        rnd += 1
            if True:
                PW = min(512, S) if S <= 2048 else 256
                if hp == 0:
                    if PW == CB:
                        o_, cen, sq2 = ctmp, kkr, nrm
                        var, rk_ = [fw.sb("%s@%d" % (n, si), [128, PW], F32, st) for n in ("pvar", "prk")]
                    else:
                        o_, cen, sq2, var, rk_ = [fw.sb("%s@%d" % (n, si), [128, PW], F32, st) for n in ("po_", "pcen", "psq2", "pvar", "prk")]
                for pb_ in range(S // PW):
                    ps = slice(pb_ * PW, (pb_ + 1) * PW)
                    op("dve", lambda e: e.tensor_tensor(out=o_[:], in0=OFT[:, ps], in1=OBT[:, ps], op=ALU.add), reads=(OFT, OBT), writes=(o_,))
                    op("pe", lambda e: e.matmul(pbank(0)[:, 0:PW], lhsT=BOf[:], rhs=o_[:], start=True, stop=True), reads=(BOf, o_), writes=(PB[0],))
                    op("dve", lambda e: e.scalar_tensor_tensor(out=cen[:], in0=pbank(0)[:, 0:PW], scalar=-1.0 / 64, in1=o_[:], op0=ALU.mult, op1=ALU.add),
                       reads=(PB[0], o_), writes=(cen,))
                    op("pool", lambda e: e.tensor_tensor(out=sq2[:], in0=cen[:], in1=cen[:], op=ALU.mult), reads=(cen,), writes=(sq2,))
                    op("pe", lambda e: e.matmul(pbank(1)[:, 0:PW], lhsT=BOf[:], rhs=sq2[:], start=True, stop=True), reads=(BOf, sq2), writes=(PB[1],))
                    op("dve", lambda e: e.tensor_scalar(out=var[:], in0=pbank(1)[:, 0:PW], scalar1=1.0 / 64, scalar2=GN_EPS, op0=ALU.mult, op1=ALU.add),
                       reads=(PB[1],), writes=(var,))
                    op("act", lambda e: e.activation(out=var[:], in_=var[:], func=AF.Sqrt), reads=(var,), writes=(var,))
                    op("dve", lambda e: e.reciprocal(out=var[:], in_=var[:]), reads=(var,), writes=(var,))
                    op("dve", lambda e: e.tensor_tensor(out=cen[:], in0=cen[:], in1=var[:], op=ALU.mult), reads=(cen, var), writes=(cen,))
                    op("dve", lambda e: e.tensor_scalar(out=cen[:], in0=cen[:], scalar1=C.lg_c[:, hpc], scalar2=C.lb_c[:, hpc], op0=ALU.mult, op1=ALU.add),
                       reads=(cen, C.lg_c, C.lb_c), writes=(cen,))
                    op("dve", lambda e: e.scalar_tensor_tensor(out=rk_[:], in0=RT_[:, ps], scalar=C.rk_c[:, hpc], in1=KT_[:, ps], op0=ALU.mult, op1=ALU.mult),
                       reads=(RT_, KT_, C.rk_c), writes=(rk_,))
                    op("pe", lambda e: e.matmul(pbank(2)[:, 0:PW], lhsT=BOf[:], rhs=rk_[:], start=True, stop=True), reads=(BOf, rk_), writes=(PB[2],))
                    op("dve", lambda e: e.tensor_tensor(out=sq2[:], in0=pbank(2)[:, 0:PW], in1=VT_[:, ps], op=ALU.mult), reads=(PB[2], VT_), writes=(sq2,))
                    op("pool", lambda e: e.tensor_tensor(out=cen[:], in0=cen[:], in1=sq2[:], op=ALU.add), reads=(cen, sq2), writes=(cen,))
                    for kc in range(8):
                        op("pe", lambda e: e.matmul(pbank(3)[:, 0:PW], lhsT=Whp[:, 3, kc, :], rhs=xnT[:, kc, 1 + pb_ * PW:1 + (pb_ + 1) * PW],
                                                    start=(kc == 0), stop=(kc == 7)), reads=(Whp, xnT), writes=(PB[3],))
                    op("act", lambda e: e.activation(out=var[:], in_=pbank(3)[:, 0:PW], func=AF.Silu), reads=(PB[3],), writes=(var,))
                    op("dve", lambda e: e.tensor_tensor(out=OFT[:, ps], in0=cen[:], in1=var[:], op=ALU.mult), reads=(cen, var, OFT), writes=(OFT,))
            fw.dma("sp", C.scrR[:, hp, 0:S], OFT[:], _stsem(fw, OFT), reads=(OFT,), writes=(C.dR,))
    with contextlib.ExitStack() as st:
        Wor = load_w(C, st, "Wor@%d" % si, C.P["w_o_rwkv"].rearrange("(c p) n -> p c n", p=128), [128, 8, 1024])
        Wgmb = load_w(C, st, "Wgmb@%d" % si, C.wview(O_GMB, 1024), [128, 8, 1024])
        branch_tail(C, st, "r", None, Wor, Wgmb, C.scrR, C.dR, C.scrR, C.dR, None)


def make_rope(C, st0):
    fw, op, S, si = C.fw, C.fw.op, C.S, C.si
    cosT = fw.sb("cosT@%d" % si, [128, S], BF16, st0)
    sinT = fw.sb("sinT@%d" % si, [128, S], BF16, st0)
    with contextlib.ExitStack() as st:
        pi_ = fw.sb("pi_@%d" % si, [128, 1], I32, st)
        pj_ = fw.sb("pj_@%d" % si, [128, 1], I32, st)
        pf_ = fw.sb("pf_@%d" % si, [128, 2], F32, st)
        invf = fw.sb("invf@%d" % si, [128, 1], F32, st)
        posi = fw.sb("posi@%d" % si, [128, S], I32, st)
        ang = fw.sb("ang@%d" % si, [128, S], F32, st)
        kf = fw.sb("kf@%d" % si, [128, S], F32, st)
        tm = fw.sb("tm@%d" % si, [128, S], F32, st)
        op("pool", lambda e: e.iota(pi_[:], pattern=[[0, 1]], base=0, channel_multiplier=1), writes=(pi_,))
        op("dve", lambda e: e.tensor_scalar(out=pj_[:], in0=pi_[:], scalar1=7, scalar2=None, op0=ALU.bitwise_and),
           reads=(pi_,), writes=(pj_,))
        op("dve", lambda e: e.tensor_copy(out=pf_[:, 0:1], in_=pj_[:]), reads=(pj_,), writes=(pf_,))
        op("dve", lambda e: e.tensor_scalar(out=pj_[:], in0=pi_[:], scalar1=63, scalar2=None, op0=ALU.bitwise_and),
           reads=(pi_,), writes=(pj_,))
        op("dve", lambda e: e.tensor_copy(out=pf_[:, 1:2], in_=pj_[:]), reads=(pj_,), writes=(pf_,))
        op("act", lambda e: e.activation(out=invf[:], in_=pf_[:, 0:1], func=AF.Exp, scale=-math.log(ROPE_THETA) / 8.0),
           reads=(pf_,), writes=(invf,))
        op("dve", lambda e: e.tensor_scalar(out=pf_[:, 1:2], in0=pf_[:, 1:2], scalar1=16.0, scalar2=None, op0=ALU.is_lt),
           reads=(pf_,), writes=(pf_,))
        op("dve", lambda e: e.tensor_tensor(out=invf[:], in0=invf[:], in1=pf_[:, 1:2], op=ALU.mult),
           reads=(invf, pf_), writes=(invf,))
        op("pool", lambda e: e.iota(posi[:], pattern=[[1, S]], base=0, channel_multiplier=0), writes=(posi,))
        op("dve", lambda e: e.tensor_copy(out=ang[:], in_=posi[:]), reads=(posi,), writes=(ang,))
        op("dve", lambda e: e.tensor_scalar(out=ang[:], in0=ang[:], scalar1=invf[:, 0:1], scalar2=None, op0=ALU.mult),
           reads=(ang, invf), writes=(ang,))

        def wrap_sin(dst, shift):
            op("dve", lambda e: e.tensor_scalar(out=tm[:], in0=ang[:], scalar1=shift, scalar2=None, op0=ALU.add),
               reads=(ang,), writes=(tm,))
            op("dve", lambda e: e.tensor_scalar(out=kf[:], in0=tm[:], scalar1=1.0 / TWO_PI, scalar2=None, op0=ALU.mult),
               reads=(tm,), writes=(kf,))
            op("dve", lambda e: e.tensor_copy(out=posi[:], in_=kf[:]), reads=(kf,), writes=(posi,))
            op("dve", lambda e: e.tensor_copy(out=kf[:], in_=posi[:]), reads=(posi,), writes=(kf,))
            op("dve", lambda e: e.scalar_tensor_tensor(out=tm[:], in0=kf[:], scalar=-CW1, in1=tm[:], op0=ALU.mult, op1=ALU.add),
               reads=(kf, tm), writes=(tm,))
            op("dve", lambda e: e.scalar_tensor_tensor(out=tm[:], in0=kf[:], scalar=-CW2, in1=tm[:], op0=ALU.mult, op1=ALU.add),
               reads=(kf, tm), writes=(tm,))
            op("dve", lambda e: e.tensor_scalar(out=kf[:], in0=tm[:], scalar1=math.pi, scalar2=-TWO_PI, op0=ALU.is_gt, op1=ALU.mult),
               reads=(tm,), writes=(kf,))
            op("dve", lambda e: e.tensor_tensor(out=tm[:], in0=tm[:], in1=kf[:], op=ALU.add), reads=(tm, kf), writes=(tm,))
            op("dve", lambda e: e.tensor_scalar(out=kf[:], in0=tm[:], scalar1=-math.pi, scalar2=TWO_PI, op0=ALU.is_lt, op1=ALU.mult),
               reads=(tm,), writes=(kf,))
            op("dve", lambda e: e.tensor_tensor(out=tm[:], in0=tm[:], in1=kf[:], op=ALU.add), reads=(tm, kf), writes=(tm,))
            op("dve", lambda e: e.tensor_scalar(out=tm[:], in0=tm[:], scalar1=3.14159, scalar2=-3.14159, op0=ALU.min, op1=ALU.max),
               reads=(tm,), writes=(tm,))
            op("act", lambda e: e.activation(out=dst[:], in_=tm[:], func=AF.Sin), reads=(tm,), writes=(dst,))
        wrap_sin(sinT, 0.0)
        wrap_sin(cosT, math.pi / 2)

    return cosT, sinT


SEQ_LENS = [2048, 2048, 2048, 2048, 4096]
_NC_CACHE = {}


def kernel(**inputs):
    xp = np.asarray(inputs["x_prompt"], dtype=np.float32)
    xs = np.asarray(inputs["x_sample"], dtype=np.float32)
    n = 8
    if "nc" not in _NC_CACHE:
        _NC_CACHE["nc"] = build(SEQ_LENS)
    nc = _NC_CACHE["nc"]
    shared = {}
    for nme, shp in PARAMS:
        shared[nme] = np.ascontiguousarray(np.asarray(inputs[nme], dtype=np.float32).reshape(shp))
    in_maps = []
    for c in range(n):
        xc = np.concatenate([xp[4 * c:4 * c + 4].reshape(-1, D), xs[c].reshape(-1, D)], axis=0)
        m = {"x": np.ascontiguousarray(xc)}
        m.update(shared)
        in_maps.append(m)
    res = run_bass_kernel_spmd(nc, in_maps, core_ids=list(range(n)))
    yp = np.empty_like(xp)
    ys = np.empty_like(xs)
    for c in range(n):
        yc = np.asarray(res.results[c]["y"])
        yp[4 * c:4 * c + 4] = yc[0:8192].reshape(4, 2048, D)
        ys[c] = yc[8192:12288].reshape(4096, D)
    return (yp, ys)
```

```python
import contextlib
import re
import numpy as np
import concourse.bass as bass
import concourse.mybir as mybir

F32 = mybir.dt.float32
BF16 = mybir.dt.bfloat16
I32 = mybir.dt.int32
AF = mybir.ActivationFunctionType
ALU = mybir.AluOpType
AX = mybir.AxisListType

SEM_ROT = 30000


class Buf:
    __slots__ = ("t", "name", "last_w", "readers", "ld_sem", "st_sem")

    def __init__(self, t, name):
        self.t = t
        self.name = name
        self.last_w = None
        self.readers = {}
        self.ld_sem = None
        self.st_sem = None

    def __getitem__(self, k):
        return self.t[k]


class FW:
    def __init__(self, nc):
        self.nc = nc
        self.stack = contextlib.ExitStack()
        self.eng = {"pe": nc.tensor, "act": nc.scalar, "dve": nc.vector,
                    "pool": nc.gpsimd, "sp": nc.sync}
        self.sems = {}
        self.cur = {}
        self.cnt = {}
        self.waited = {e: {} for e in self.eng}
        self.dma_total = {}
        self.dma_roles = {}
        self.nsem = 0
        self.n_ops = 0
        self.n_waits = 0
        for e in self.eng:
            self._new_eng_sem(e)

    def _alloc_sem(self, name):
        h = self.stack.enter_context(self.nc.semaphore(name))
        key = name
        self.sems[key] = h
        self.nsem += 1
        return key

    def _new_eng_sem(self, e):
        key = self._alloc_sem("s_%s_%d" % (e, self.nsem))
        self.cur[e] = key
        self.cnt[key] = 0

    def dma_sem(self, name):
        role = re.sub(r"@\d+", "", name)
        if role in self.dma_roles:
            return self.dma_roles[role]
        key = self._alloc_sem("d_%s_%d" % (role, self.nsem))
        self.dma_total[key] = 0
        self.dma_roles[role] = key
        return key

    def snapshot(self):
        snap = {}
        for k in self.sems:
            v = self.dma_total[k] if k in self.dma_total else self.cnt.get(k, 0)
            if v > 0:
                snap[k] = v
        return snap

    def sb(self, name, shape, dtype, stack=None):
        self.n_alloc = getattr(self, "n_alloc", 0) + 1
        t = (stack or self.stack).enter_context(self.nc.sbuf_tensor("%s_u%d" % (name.replace("@", "_"), self.n_alloc), list(shape), dtype))
        b = Buf(t, name)
        b.readers = self.snapshot()
        return b

    def ps(self, name, shape, dtype, stack=None):
        t = (stack or self.stack).enter_context(self.nc.psum_tensor(name, list(shape), dtype))
        return Buf(t, name)

    def view(self, buf_or_ap, name):
        return Buf(buf_or_ap, name)

    def _collect(self, e, reads, writes):
        deps = {}

        def add(tok):
            if tok is None:
                return
            k, v = tok
            if k in self.dma_total:
                v = self.dma_total[k]
            if deps.get(k, 0) < v:
                deps[k] = v

        for b in reads:
            add(b.last_w)
        for b in writes:
            add(b.last_w)
            for k, v in b.readers.items():
                add((k, v))
        return deps

    def _emit_waits(self, e, deps):
        eng = self.eng[e]
        w = self.waited[e]
        for k, v in deps.items():
            if e == "pe" and k.startswith("s_pe_"):
                continue
            if w.get(k, 0) >= v:
                continue
            eng.wait_ge(self.sems[k], v)
            w[k] = v
            self.n_waits += 1

    def _mark(self, tok, reads, writes):
        k, v = tok
        for b in writes:
            b.last_w = tok
            b.readers = {}
        for b in reads:
            if b.readers.get(k, 0) < v:
                b.readers[k] = v

    def op(self, e, fn, reads=(), writes=()):
        deps = self._collect(e, reads, writes)
        self._emit_waits(e, deps)
        ins = fn(self.eng[e])
        key = self.cur[e]
        if self.cnt[key] >= SEM_ROT:
            self._new_eng_sem(e)
            key = self.cur[e]
        self.cnt[key] += 1
        ins.then_inc(self.sems[key], 1)
        self._mark((key, self.cnt[key]), reads, writes)
        self.n_ops += 1
        return ins

    def dma(self, q, out, in_, sem, reads=(), writes=(), **kw):
        deps = self._collect(q, reads, writes)
        self._emit_waits(q, deps)
        ins = self.eng[q].dma_start(out=out, in_=in_, **kw)
        self.dma_total[sem] += 16
        ins.then_inc(self.sems[sem], 16)
        self._mark((sem, self.dma_total[sem]), reads, writes)
        self.n_ops += 1
        return ins

    def load(self, q, buf, out_ap, in_ap, **kw):
        if buf.ld_sem is None:
            buf.ld_sem = self.dma_sem("l" + buf.name)
        return self.dma(q, out_ap, in_ap, buf.ld_sem, reads=(), writes=(buf,), **kw)

    def store(self, q, buf, out_ap, in_ap, **kw):
        if buf.st_sem is None:
            buf.st_sem = self.dma_sem("s" + buf.name)
        return self.dma(q, out_ap, in_ap, buf.st_sem, reads=(buf,), writes=(), **kw)

    def final_wait(self, e="sp"):
        eng = self.eng[e]
        for k, h in self.sems.items():
            v = self.dma_total[k] if k in self.dma_total else self.cnt.get(k, 0)
            if v > 0 and self.waited[e].get(k, 0) < v:
                eng.wait_ge(h, v)
                self.waited[e][k] = v


import math
from types import SimpleNamespace
import contextlib
import numpy as np
import concourse.bass as bass
import concourse.mybir as mybir
from concourse.bass_utils import run_bass_kernel_spmd

D = 1024
PIN = 10496
O_Q, O_K, O_V, O_GA, O_R, O_RK, O_RV, O_GR, O_WL, O_AL, O_GMA, O_GMB = (
    0, 1024, 2048, 3072, 4096, 5120, 6144, 7168, 8192, 8320, 8448, 9472)
ROPE_THETA = 500000.0
TWO_PI = 2.0 * math.pi
CW1 = 6.28125
CW2 = TWO_PI - CW1

PARAMS = [("norm_g", [1, 1024]), ("w_in", [1024, PIN]), ("conv_rkv", [3, 3072]),
          ("lam_q1", [1, 64]), ("lam_k1", [1, 64]), ("lam_q2", [1, 64]), ("lam_k2", [1, 64]),
          ("attn_subln_g", [1, 128]), ("w_lora_up", [128, 1024]), ("w0", [2, 1024]),
          ("a_lora_up", [128, 1024]), ("a0", [2, 1024]), ("k_k", [1, 1024]), ("k_a", [1, 1024]),
          ("r_k", [1, 1024]), ("ln_x_g", [1, 1024]), ("ln_x_b", [1, 1024]),
          ("w_o_attn", [1024, 1024]), ("w_o_rwkv", [1024, 1024]), ("w_out", [1024, 1024]),
          ("final_g", [1, 1024])]


def build(seq_lens, do_attn=True, do_rwkv=True):
    nc = bass.Bass("TRN2", target_bir_lowering=False)
    TOT = sum(seq_lens)
    SMAX = max(seq_lens)
    x = nc.dram_tensor("x", [TOT, D], F32, kind="ExternalInput").ap()
    y = nc.dram_tensor("y", [TOT, D], F32, kind="ExternalOutput").ap()
    P = {n: nc.dram_tensor(n, s, F32, kind="ExternalInput").ap() for n, s in PARAMS}
    w_in = P["w_in"]
    scrA = nc.dram_tensor("scrA", [128, 8, SMAX], BF16, kind="Internal").ap()
    scrM = nc.dram_tensor("scrM", [128, 8, SMAX], BF16, kind="Internal").ap()
    scrR = nc.dram_tensor("scrR", [128, 8, SMAX], BF16, kind="Internal").ap()
    fw = FW(nc)
    op = fw.op
    ncd = nc.allow_non_contiguous_dma(reason="small param layouts")
    ncd.__enter__()

    def wview(off, ncols):
        return w_in[:, off:off + ncols].rearrange("(c p) n -> p c n", p=128)

    with fw.stack:
        dA, dM, dR = fw.view(scrA, "scrA"), fw.view(scrM, "scrM"), fw.view(scrR, "scrR")
        DBt = [fw.stack.enter_context(nc.psum_tensor("db%d" % i, [128, 1024], F32)) for i in range(4)]
        DBf = [t[:] for t in DBt]
        DBh = [t[:].bitcast(BF16) for t in DBt]
        PB = [Buf(None, "bank%d" % j) for j in range(8)]

        def bk(i, both=True, half=0):
            return (PB[2 * i], PB[2 * i + 1]) if both else (PB[2 * i + half],)

        psem = fw.dma_sem("params")
        psem2 = fw.dma_sem("paramsq")

        def pload(name, shape, src, q="sp", dtype=F32):
            b = fw.sb(name, shape, dtype)
            b.ld_sem = psem if q == "sp" else psem2
            fw.load(q, b, b[:], src)
            return b

        def colparam(name, nm):
            return pload(name, [128, 8], P[nm].rearrange("o (c p) -> p (o c)", p=128))

        gcol = colparam("gcol", "norm_g")
        kk_c = colparam("kk_c", "k_k")
        ka_c = colparam("ka_c", "k_a")
        rk_c = colparam("rk_c", "r_k")
        lg_c = colparam("lg_c", "ln_x_g")
        lb_c = colparam("lb_c", "ln_x_b")
        w0_c = pload("w0_c", [128, 2, 8], P["w0"].rearrange("d (c p) -> p d c", p=128))
        a0_c = pload("a0_c", [128, 2, 8], P["a0"].rearrange("d (c p) -> p d c", p=128))
        cv_c = pload("cv_c", [128, 3, 24], P["conv_rkv"].rearrange("i (c p) -> p i c", p=128))
        fgb = pload("fgb", [128, 1024], P["final_g"][0, :].partition_broadcast(128))
        sgb = pload("sgb", [128, 128], P["attn_subln_g"][0, :].partition_broadcast(128))
        lq = [pload("lam%d" % i, [128, 64], P[n][0, :].partition_broadcast(128))
              for i, n in enumerate(["lam_q1", "lam_k1", "lam_q2", "lam_k2"])]
        wlw = pload("wlw", [128, 1024], P["w_lora_up"], q="pool", dtype=BF16)
        wla = pload("wla", [128, 1024], P["a_lora_up"], q="pool", dtype=BF16)

        ident = fw.sb("ident", [128, 128], BF16)
        op("pool", lambda e: e.memset(ident[:], 1.0), writes=(ident,))
        op("pool", lambda e: e.affine_select(out=ident[:], in_=ident[:], pattern=[[-1, 128]], compare_op=ALU.is_equal,
                                             fill=0.0, base=0, channel_multiplier=1), reads=(ident,), writes=(ident,))
        grep = fw.sb("grep", [128, 8, 128], F32)
        op("dve", lambda e: e.memset(grep[:], 1.0), writes=(grep,))
        for c in range(8):
            op("dve", lambda e: e.tensor_scalar(out=grep[:, c, :], in0=grep[:, c, :], scalar1=gcol[:, c:c + 1],
                                                scalar2=None, op0=ALU.mult), reads=(grep, gcol), writes=(grep,))
        op("dve", lambda e: e.tensor_scalar(out=sgb[:], in0=sgb[:], scalar1=0.8, scalar2=None, op0=ALU.mult),
           reads=(sgb,), writes=(sgb,))
        hw0_c = fw.sb("hw0_c", [128, 2, 8], F32)
        ha0_c = fw.sb("ha0_c", [128, 2, 8], F32)
        op("dve", lambda e: e.tensor_scalar(out=hw0_c[:], in0=w0_c[:], scalar1=0.5, scalar2=None, op0=ALU.mult), reads=(w0_c,), writes=(hw0_c,))
        op("dve", lambda e: e.tensor_scalar(out=ha0_c[:], in0=a0_c[:], scalar1=0.5, scalar2=None, op0=ALU.mult), reads=(a0_c,), writes=(ha0_c,))
        tmka = fw.sb("tmka", [128, 8], F32)
        op("dve", lambda e: e.tensor_scalar(out=tmka[:], in0=ka_c[:], scalar1=-1.0, scalar2=2.0, op0=ALU.mult, op1=ALU.add),
           reads=(ka_c,), writes=(tmka,))
        eps12 = fw.sb("eps12", [128, 1], F32)
        op("dve", lambda e: e.memset(eps12[:], 1e-12), writes=(eps12,))
        eps5 = fw.sb("eps5", [128, 1], F32)
        op("dve", lambda e: e.memset(eps5[:], 1e-5), writes=(eps5,))
        lnhalf = fw.sb("lnhalf", [128, 1], F32)
        op("dve", lambda e: e.memset(lnhalf[:], math.log(0.5)), writes=(lnhalf,))
        omka = fw.sb("omka", [128, 8], F32)
        op("dve", lambda e: e.tensor_scalar(out=omka[:], in0=ka_c[:], scalar1=-1.0, scalar2=1.0, op0=ALU.mult, op1=ALU.add),
           reads=(ka_c,), writes=(omka,))
        lt = fw.sb("lt", [128, 64], F32)
        ls = fw.sb("ls", [128, 2], F32)
        nlam = fw.sb("nlam", [128, 1], F32)
        for i in range(2):
            op("dve", lambda e: e.tensor_tensor(out=lt[:], in0=lq[2 * i][:], in1=lq[2 * i + 1][:], op=ALU.mult),
               reads=(lq[2 * i], lq[2 * i + 1]), writes=(lt,))
            op("dve", lambda e: e.reduce_sum(out=ls[:, i:i + 1], in_=lt[:], axis=AX.X), reads=(lt,), writes=(ls,))
        op("act", lambda e: e.activation(out=ls[:], in_=ls[:], func=AF.Exp), reads=(ls,), writes=(ls,))
        op("dve", lambda e: e.tensor_tensor(out=nlam[:], in0=ls[:, 1:2], in1=ls[:, 0:1], op=ALU.subtract),
           reads=(ls,), writes=(nlam,))
        op("dve", lambda e: e.tensor_scalar(out=nlam[:], in0=nlam[:], scalar1=-0.2, scalar2=None, op0=ALU.add),
           reads=(nlam,), writes=(nlam,))

        def mkmask(name, mult_f, mult_p, base, cmp):
            m = fw.sb(name, [128, 128], BF16)
            op("pool", lambda e: e.memset(m[:], 1.0), writes=(m,))
            op("pool", lambda e: e.affine_select(out=m[:], in_=m[:], pattern=[[mult_f, 128]], compare_op=cmp,
                                                 fill=0.0, base=base, channel_multiplier=mult_p), reads=(m,), writes=(m,))
            return m
        Us = mkmask("Us", 1, -1, 0, ALU.is_gt)
        Ui = mkmask("Ui", 1, -1, 0, ALU.is_ge)
        Ls = mkmask("Ls", -1, 1, 0, ALU.is_gt)
        Li = mkmask("Li", -1, 1, 0, ALU.is_ge)
        MASK4 = []
        MASK8 = []
        MASKL = []
        for d_ in range(2):
            m4 = fw.sb("m4_%d" % d_, [128, 4, 128], BF16)
            s_, i_ = (Us, Ui) if d_ == 0 else (Ls, Li)
            for j, src in enumerate([s_, i_, s_, i_]):
                op("pool", lambda e: e.tensor_copy(out=m4[:, j, :], in_=src[:]), reads=(src,), writes=(m4,))
            MASK4.append(m4)
            m8 = fw.sb("m8_%d" % d_, [128, 2, 4, 128], BF16)
            for h_ in range(2):
                for j, src in enumerate([s_, i_, s_, i_]):
                    op("pool", lambda e: e.tensor_copy(out=m8[:, h_, j, :], in_=src[:]), reads=(src,), writes=(m8,))
            MASK8.append(m8)
            ml = fw.sb("ml_%d" % d_, [128, 2, 128], BF16)
            src = Ls if d_ == 0 else Us
            for j in range(2):
                op("pool", lambda e: e.tensor_copy(out=ml[:, j, :], in_=src[:]), reads=(src,), writes=(ml,))
            MASKL.append(ml)
        BOh = fw.sb("BOh", [128, 128], BF16)
        BOf = fw.sb("BOf", [128, 128], F32)
        for t_ in (BOh, BOf):
            op("pool", lambda e: e.memset(t_[:], 0.0), writes=(t_,))
            op("pool", lambda e: e.memset(t_[0:64, 0:64], 1.0), reads=(t_,), writes=(t_,))
            op("pool", lambda e: e.memset(t_[64:128, 64:128], 1.0), reads=(t_,), writes=(t_,))
        I2 = fw.sb("I2", [128, 64], F32)
        op("pool", lambda e: e.memset(I2[:], 1.0), writes=(I2,))
        op("pool", lambda e: e.affine_select(out=I2[0:64, :], in_=I2[0:64, :], pattern=[[-1, 64]], compare_op=ALU.is_equal,
                                             fill=0.0, base=0, channel_multiplier=1), reads=(I2,), writes=(I2,))
        op("pool", lambda e: e.affine_select(out=I2[64:128, :], in_=I2[64:128, :], pattern=[[-1, 64]], compare_op=ALU.is_equal,
                                             fill=0.0, base=0, channel_multiplier=1), reads=(I2,), writes=(I2,))
        onesS = fw.sb("onesS", [128, 128], F32)
        op("pool", lambda e: e.memset(onesS[:], 1.0), writes=(onesS,))

        def proj_fm(dst_fn, W, wslot_fn, xnT, S, evac):
            pass

        tok0 = 0
        for si, S in enumerate(seq_lens):
            NT = S // 128
            NB = S // 512 if S >= 512 else 1
            BW = min(512, S)
            with contextlib.ExitStack() as sq:
                xnT = fw.sb("xnT@%d" % si, [128, 8, S + 2], BF16, sq)
                op("pool", lambda e: e.memset(xnT[:, :, 0:1], 0.0), writes=(xnT,))
                op("pool", lambda e: e.memset(xnT[:, :, S + 1:S + 2], 0.0), writes=(xnT,))
                with contextlib.ExitStack() as sa:
                    xt = [fw.sb("xt%d@%d" % (i, si), [128, 1024], F32, sa) for i in range(2)]
                    junk = fw.sb("junkA@%d" % si, [128, 1024], F32, sa)
                    xs = [fw.sb("xs%d@%d" % (i, si), [128, 1024], BF16, sa) for i in range(2)]
                    ssA = [fw.sb("ssA%d@%d" % (i, si), [128, 1], F32, sa) for i in range(2)]
                    for tt in range(NT):
                        b = tt % 2
                        fw.load("sp", xt[b], xt[b][:], x[tok0 + tt * 128: tok0 + (tt + 1) * 128, :])
                        op("act", lambda e: e.activation(out=junk[:], in_=xt[b][:], func=AF.Square, accum_out=ssA[b][:]),
                           reads=(xt[b],), writes=(junk, ssA[b]))
                        op("dve", lambda e: e.tensor_scalar(out=ssA[b][:], in0=ssA[b][:], scalar1=1.0 / 1024, scalar2=1e-6,
                                                            op0=ALU.mult, op1=ALU.add), reads=(ssA[b],), writes=(ssA[b],))
                        op("act", lambda e: e.activation(out=ssA[b][:], in_=ssA[b][:], func=AF.Sqrt), reads=(ssA[b],), writes=(ssA[b],))
                        op("dve", lambda e: e.reciprocal(out=ssA[b][:], in_=ssA[b][:]), reads=(ssA[b],), writes=(ssA[b],))
                        op("act", lambda e: e.activation(out=xs[b][:], in_=xt[b][:], func=AF.Copy, scale=ssA[b][:, 0:1]),
                           reads=(xt[b], ssA[b]), writes=(xs[b],))
                        db = tt % 2
                        for c in range(8):
                            op("pe", lambda e: e.transpose(out=DBh[db][:, c * 128:(c + 1) * 128], in_=xs[b][:, c * 128:(c + 1) * 128],
                                                           identity=ident[:]), reads=(xs[b], ident), writes=bk(db, False, 0))
                        op("dve", lambda e: e.tensor_tensor(out=xnT[:, :, 1 + tt * 128: 1 + (tt + 1) * 128],
                                                            in0=DBh[db][:, 0:1024].rearrange("p (c t) -> p c t", c=8),
                                                            in1=grep[:], op=ALU.mult),
                           reads=bk(db, False, 0) + (grep,), writes=(xnT,))

                def xblk(kc, tb):
                    return xnT[:, kc, 1 + tb * BW: 1 + (tb + 1) * BW]

                C = SimpleNamespace(**locals())
                if do_attn:
                    stage_attn(C)
                if do_rwkv:
                    stage_rwkv(C)
                stage_out(C)
            tok0 += S
        fw.final_wait("sp")
    ncd.__exit__(None, None, None)
    return nc


def load_w(C, st, name, src_ap, shape):
    b = C.fw.sb(name, shape, BF16, st)
    C.fw.load("pool", b, b[:], src_ap)
    return b


def stage_out(C):
    fw, op, S, BW, NB, si = C.fw, C.fw.op, C.S, C.BW, C.NB, C.si
    DBf, bk = C.DBf, C.bk
    with contextlib.ExitStack() as st:
        Wout = load_w(C, st, "Wout@%d" % si, C.P["w_out"].rearrange("(c p) n -> p c n", p=128), [128, 8, 1024])
        ma = [fw.sb("ma%d@%d" % (i, si), [128, 8, BW], BF16, st) for i in range(2)]
        mr = [fw.sb("mr%d@%d" % (i, si), [128, 8, BW], BF16, st) for i in range(2)]
        xt = [fw.sb("xo%d@%d" % (i, si), [128, 1024], F32, st) for i in range(2)]
        zt = [fw.sb("zt%d@%d" % (i, si), [128, 1024], F32, st) for i in range(2)]
        junk = fw.sb("junkO@%d" % si, [128, 1024], F32, st)
        ss = [fw.sb("ssO%d@%d" % (i, si), [128, 1], F32, st) for i in range(2)]
        cnt = 0
        for tb in range(NB):
            b = tb % 2
            m = None
            if C.do_attn:
                fw.dma("sp", ma[b][:], C.scrM[:, :, tb * BW:(tb + 1) * BW], _ldsem(fw, ma[b]), reads=(C.dM,), writes=(ma[b],))
                m = ma[b]
            if C.do_rwkv:
                fw.dma("sp", mr[b][:], C.scrR[:, :, tb * BW:(tb + 1) * BW], _ldsem(fw, mr[b]), reads=(C.dR,), writes=(mr[b],))
                if m is None:
                    m = mr[b]
                else:
                    op("pool", lambda e: e.tensor_tensor(out=ma[b][:], in0=ma[b][:], in1=mr[b][:], op=ALU.add),
                       reads=(ma[b], mr[b]), writes=(ma[b],))
            for t4 in range(BW // 128):
                tt = tb * (BW // 128) + t4
                xb = cnt % 2
                db = 2 + cnt % 2
                cnt += 1
                fw.load("sp", xt[xb], xt[xb][:], C.x[C.tok0 + tt * 128: C.tok0 + (tt + 1) * 128, :])
                if m is not None:
                    for half in range(2):
                        for dc in range(8):
                            op("pe", lambda e: e.matmul(DBf[db][:, half * 512:(half + 1) * 512], lhsT=m[:, dc, t4 * 128:(t4 + 1) * 128],
                                                        rhs=Wout[:, dc, half * 512:(half + 1) * 512], start=(dc == 0), stop=(dc == 7)),
                               reads=(m, Wout), writes=bk(db, False, half))
                    op("dve", lambda e: e.tensor_tensor(out=zt[xb][:], in0=DBf[db][:], in1=xt[xb][:], op=ALU.add),
                       reads=bk(db) + (xt[xb],), writes=(zt[xb],))
                else:
                    op("dve", lambda e: e.tensor_copy(out=zt[xb][:], in_=xt[xb][:]), reads=(xt[xb],), writes=(zt[xb],))
                op("act", lambda e: e.activation(out=junk[:], in_=zt[xb][:], func=AF.Square, accum_out=ss[xb][:]),
                   reads=(zt[xb],), writes=(junk, ss[xb]))
                op("dve", lambda e: e.tensor_scalar(out=ss[xb][:], in0=ss[xb][:], scalar1=1.0 / 1024, scalar2=1e-6,
                                                    op0=ALU.mult, op1=ALU.add), reads=(ss[xb],), writes=(ss[xb],))
                op("act", lambda e: e.activation(out=ss[xb][:], in_=ss[xb][:], func=AF.Sqrt), reads=(ss[xb],), writes=(ss[xb],))
                op("dve", lambda e: e.reciprocal(out=ss[xb][:], in_=ss[xb][:]), reads=(ss[xb],), writes=(ss[xb],))
                op("dve", lambda e: e.scalar_tensor_tensor(out=zt[xb][:], in0=zt[xb][:], scalar=ss[xb][:, 0:1], in1=C.fgb[:],
                                                           op0=ALU.mult, op1=ALU.mult), reads=(zt[xb], ss[xb], C.fgb), writes=(zt[xb],))
                fw.store("act", zt[xb], C.y[C.tok0 + tt * 128: C.tok0 + (tt + 1) * 128, :], zt[xb][:])


def _ldsem(fw, b):
    if b.ld_sem is None:
        b.ld_sem = fw.dma_sem("l" + b.name)
    return b.ld_sem


def _stsem(fw, b):
    if b.st_sem is None:
        b.st_sem = fw.dma_sem("s" + b.name)
    return b.st_sem


def stage_attn(C):
    fw, op, S, BW, NB, NT, si = C.fw, C.fw.op, C.S, C.BW, C.NB, C.NT, C.si
    DBf, DBh, bk, xnT, xblk = C.DBf, C.DBh, C.bk, C.xnT, C.xblk
    ident = C.ident
    nq = BW // 128
    with contextlib.ExitStack() as st:
        cosT, sinT = make_rope(C, st)
        Wh = fw.sb("Wh@%d" % si, [128, 5, 8, 128], BF16, st)
        op("pool", lambda e: e.memset(Wh[:, 3:5, :, :], 0.0), writes=(Wh,))
        qT = fw.sb("qT@%d" % si, [128, S], BF16, st)
        kT = fw.sb("kT@%d" % si, [128, S], BF16, st)
        Vh = fw.sb("Vh@%d" % si, [128, NT, 129], BF16, st)
        op("pool", lambda e: e.memset(Vh[:, :, 128:129], 1.0), writes=(Vh,))
        E = [fw.sb("E%d@%d" % (i, si), [128, 2, BW], BF16, st) for i in range(2)]
        t1s = [fw.sb("t1%d@%d" % (i, si), [128, BW], F32, st) for i in range(2)]
        t2s = [fw.sb("t2%d@%d" % (i, si), [128, BW], F32, st) for i in range(2)]
        rsq = [fw.sb("rs%d@%d" % (i, si), [128, 2], F32, st) for i in range(4)]
        o1q = [fw.sb("o1%d@%d" % (i, si), [128, 128], F32, st) for i in range(4)]
        ssqq = [fw.sb("ssq%d@%d" % (i, si), [128, 1], F32, st) for i in range(4)]
        onq = [fw.sb("on%d@%d" % (i, si), [128, 128], BF16, st) for i in range(4)]
        junk = fw.sb("junkB@%d" % si, [128, 128], F32, st)
        ssq = fw.sb("ssq@%d" % si, [128, 1], F32, st)
        on = fw.sb("on@%d" % si, [128, 128], BF16, st)
        ogT = [fw.sb("ogT%d@%d" % (i, si), [128, BW], BF16, st) for i in range(2)]
        for h in range(8):
            for s_, off in enumerate((O_Q, O_K, O_V)):
                fw.load("pool", Wh, Wh[:, s_, :, :], C.wview(off + h * 128, 128))
            for s_ in (0, 1):
                for m in (0, 1):
                    b0 = m * 64
                    op("pool", lambda e: e.tensor_scalar(out=Wh[:, 3 + s_, :, b0:b0 + 8], in0=Wh[:, s_, :, b0 + 8:b0 + 16],
                                                         scalar1=-1.0, scalar2=None, op0=ALU.mult), reads=(Wh,), writes=(Wh,))
                    op("pool", lambda e: e.tensor_copy(out=Wh[:, 3 + s_, :, b0 + 8:b0 + 16], in_=Wh[:, s_, :, b0:b0 + 8]),
                       reads=(Wh,), writes=(Wh,))
            for tb in range(NB):
                for s_, dst in ((0, qT), (1, kT)):
                    db = (2 * tb + s_) % 4
                    t1, t2 = t1s[s_], t2s[s_]
                    for j, slot in enumerate((s_, 3 + s_)):
                        for kc in range(8):
                            op("pe", lambda e: e.matmul(DBf[db][:, j * 512:j * 512 + BW], lhsT=Wh[:, slot, kc, :], rhs=xblk(kc, tb),
                                                        start=(kc == 0), stop=(kc == 7)), reads=(Wh, xnT), writes=bk(db, False, j))
                    op("dve", lambda e: e.tensor_tensor(out=t1[:], in0=DBf[db][:, 0:BW], in1=cosT[:, tb * BW:(tb + 1) * BW], op=ALU.mult),
                       reads=bk(db, False, 0) + (cosT,), writes=(t1,))
                    op("dve", lambda e: e.tensor_tensor(out=t2[:], in0=DBf[db][:, 512:512 + BW], in1=sinT[:, tb * BW:(tb + 1) * BW], op=ALU.mult),
                       reads=bk(db, False, 1) + (sinT,), writes=(t2,))
                    op("pool", lambda e: e.tensor_tensor(out=dst[:, tb * BW:(tb + 1) * BW], in0=t1[:], in1=t2[:], op=ALU.add),
                       reads=(t1, t2), writes=(dst,))
                vdb = (2 * tb + 2) % 4
                for t4 in range(nq):
                    tt = tb * nq + t4
                    for kc in range(8):
                        op("pe", lambda e: e.matmul(DBf[vdb][:, 512 + t4 * 128:512 + (t4 + 1) * 128], lhsT=xnT[:, kc, 1 + tt * 128:1 + (tt + 1) * 128],
                                                    rhs=Wh[:, 2, kc, :], start=(kc == 0), stop=(kc == 7)),
                           reads=(Wh, xnT), writes=bk(vdb, False, 1))
                op("act", lambda e: e.activation(out=Vh[:, tb * nq:(tb + 1) * nq, 0:128],
                                                 in_=DBf[vdb][:, 512:512 + BW].rearrange("p (a b) -> p a b", b=128), func=AF.Copy),
                   reads=bk(vdb, False, 1), writes=(Vh,))
            for qc in range(NB):
                def scores(kb):
                    sb_ = kb % 2
                    for m in (0, 1):
                        op("pe", lambda e: e.matmul(DBf[sb_][:, m * 512:m * 512 + BW], lhsT=kT[m * 64:(m + 1) * 64, kb * 128:(kb + 1) * 128],
                                                    rhs=qT[m * 64:(m + 1) * 64, qc * BW:(qc + 1) * BW], start=True, stop=True),
                           reads=(kT, qT), writes=bk(sb_, False, m))
                scores(0)
                for kb in range(NT):
                    sb_ = kb % 2
                    if kb + 1 < NT:
                        scores(kb + 1)
                    op("act", lambda e: e.activation(out=E[sb_][:], in_=DBf[sb_][:, :].rearrange("p (m q) -> p m q", m=2)[:, :, 0:BW],
                                                     func=AF.Exp, scale=0.125), reads=bk(sb_), writes=(E[sb_],))
                    for qs in range(nq):
                        dba, hf = 2 + qs // 2, qs % 2
                        for m in (0, 1):
                            off = hf * 512 + m * 129
                            op("pe", lambda e: e.matmul(DBf[dba][:, off:off + 129], lhsT=E[sb_][:, m, qs * 128:(qs + 1) * 128],
                                                        rhs=Vh[:, kb, :], start=(kb == 0 and m == 0), stop=(kb == NT - 1),
                                                        skip_group_check=True),
                               reads=(E[sb_], Vh), writes=bk(dba, False, hf))
                accs = []
                for qs in range(nq):
                    dba, hf = 2 + qs // 2, qs % 2
                    accs.append((DBf[dba][:, hf * 512:hf * 512 + 258].rearrange("p (m c) -> p m c", m=2), bk(dba, False, hf)))
                for qs in range(nq):
                    acc, pbk = accs[qs]
                    op("dve", lambda e: e.reciprocal(out=rsq[qs][:], in_=acc[:, :, 128]), reads=pbk, writes=(rsq[qs],))
                for qs in range(nq):
                    op("dve", lambda e: e.tensor_tensor(out=rsq[qs][:, 1:2], in0=rsq[qs][:, 1:2], in1=C.nlam[:, 0:1], op=ALU.mult),
                       reads=(rsq[qs], C.nlam), writes=(rsq[qs],))
                for qs in range(nq):
                    acc, pbk = accs[qs]
                    op("dve", lambda e: e.tensor_scalar(out=o1q[qs][:], in0=acc[:, 0, 0:128], scalar1=rsq[qs][:, 0:1], scalar2=None, op0=ALU.mult),
                       reads=pbk + (rsq[qs],), writes=(o1q[qs],))
                for qs in range(nq):
                    acc, pbk = accs[qs]
                    op("dve", lambda e: e.scalar_tensor_tensor(out=o1q[qs][:], in0=acc[:, 1, 0:128], scalar=rsq[qs][:, 1:2], in1=o1q[qs][:],
                                                               op0=ALU.mult, op1=ALU.add), reads=pbk + (rsq[qs], o1q[qs]), writes=(o1q[qs],))
                for qs in range(nq):
                    op("act", lambda e: e.activation(out=junk[:], in_=o1q[qs][:], func=AF.Square, accum_out=ssqq[qs][:]),
                       reads=(o1q[qs],), writes=(junk, ssqq[qs]))
                for qs in range(nq):
                    op("act", lambda e: e.activation(out=ssqq[qs][:], in_=ssqq[qs][:], func=AF.Ln, scale=1.0 / 128, bias=C.eps5[:, 0:1]),
                       reads=(ssqq[qs], C.eps5), writes=(ssqq[qs],))
                for qs in range(nq):
                    op("act", lambda e: e.activation(out=ssqq[qs][:], in_=ssqq[qs][:], func=AF.Exp, scale=-0.5), reads=(ssqq[qs],), writes=(ssqq[qs],))
                for qs in range(nq):
                    op("dve", lambda e: e.scalar_tensor_tensor(out=onq[qs][:], in0=o1q[qs][:], scalar=ssqq[qs][:, 0:1], in1=C.sgb[:],
                                                               op0=ALU.mult, op1=ALU.mult), reads=(o1q[qs], ssqq[qs], C.sgb), writes=(onq[qs],))
                for qs in range(nq):
                    op("pe", lambda e: e.transpose(out=DBh[0][:, qs * 128:(qs + 1) * 128], in_=onq[qs][:], identity=ident[:]),
                       reads=(onq[qs], ident), writes=bk(0, False, 0))
                og = ogT[qc % 2]
                op("act", lambda e: e.activation(out=og[:], in_=DBh[0][:, 0:BW], func=AF.Copy), reads=bk(0, False, 0), writes=(og,))
                fw.dma("sp", C.scrA[:, h, qc * BW:(qc + 1) * BW], og[:], _stsem(fw, og), reads=(og,), writes=(C.dA,))
    with contextlib.ExitStack() as st:
        Wg = load_w(C, st, "Wg@%d" % si, C.wview(O_GA, 1024), [128, 8, 1024])
        Woa = load_w(C, st, "Woa@%d" % si, C.P["w_o_attn"].rearrange("(c p) n -> p c n", p=128), [128, 8, 1024])
        Wgm = load_w(C, st, "Wgm@%d" % si, C.wview(O_GMA, 1024), [128, 8, 1024])
        branch_tail(C, st, "a", Wg, Woa, Wgm, C.scrA, C.dA, C.scrM, C.dM, AF.Silu)


def branch_tail(C, st, tag, Wg, Wo, Wgm, src, dsrc, dst, ddst, gate_func):
    fw, op, S, BW, NB, si = C.fw, C.fw.op, C.S, C.BW, C.NB, C.si
    DBf, bk, xnT, xblk = C.DBf, C.bk, C.xnT, C.xblk
    og = [fw.sb("og%s%d@%d" % (tag, i, si), [128, 8, BW], BF16, st) for i in range(2)]
    OG = fw.sb("OG%s@%d" % (tag, si), [128, 8, BW], BF16, st)
    sg = [fw.sb("sg%s%d@%d" % (tag, i, si), [128, BW], F32, st) for i in range(2)]
    mab = [fw.sb("mab%s%d@%d" % (tag, i, si), [128, 8, BW], BF16, st) for i in range(2)]
    for tb in range(NB):
        b = tb % 2
        fw.dma("sp", og[b][:], src[:, :, tb * BW:(tb + 1) * BW], _ldsem(fw, og[b]), reads=(dsrc,), writes=(og[b],))
        if Wg is not None:
            for dc in range(8):
                db = dc % 2
                for kc in range(8):
                    op("pe", lambda e: e.matmul(DBf[db][:, 0:BW], lhsT=Wg[:, kc, dc * 128:(dc + 1) * 128], rhs=xblk(kc, tb),
                                                start=(kc == 0), stop=(kc == 7)), reads=(Wg, xnT), writes=bk(db, False, 0))
                op("act", lambda e: e.activation(out=sg[db][:], in_=DBf[db][:, 0:BW], func=gate_func), reads=bk(db, False, 0), writes=(sg[db],))
                op("dve", lambda e: e.tensor_tensor(out=OG[:, dc, :], in0=og[b][:, dc, :], in1=sg[db][:], op=ALU.mult),
                   reads=(og[b], sg[db]), writes=(OG,))
            G = OG
        else:
            G = og[b]
        for dc in range(8):
            db = 2 + dc % 2
            for hh in range(8):
                op("pe", lambda e: e.matmul(DBf[db][:, 0:BW], lhsT=Wo[:, hh, dc * 128:(dc + 1) * 128], rhs=G[:, hh, :],
                                            start=(hh == 0), stop=(hh == 7)), reads=(Wo, G), writes=bk(db, False, 0))
            for kc in range(8):
                op("pe", lambda e: e.matmul(DBf[db][:, 512:512 + BW], lhsT=Wgm[:, kc, dc * 128:(dc + 1) * 128], rhs=xblk(kc, tb),
                                            start=(kc == 0), stop=(kc == 7)), reads=(Wgm, xnT), writes=bk(db, False, 1))
            sgi = dc % 2
            op("act", lambda e: e.activation(out=sg[sgi][:], in_=DBf[db][:, 512:512 + BW], func=AF.Sigmoid),
               reads=bk(db, False, 1), writes=(sg[sgi],))
            op("dve", lambda e: e.tensor_tensor(out=mab[b][:, dc, :], in0=DBf[db][:, 0:BW], in1=sg[sgi][:], op=ALU.mult),
               reads=bk(db, False, 0) + (sg[sgi],), writes=(mab[b],))
        fw.dma("act", dst[:, :, tb * BW:(tb + 1) * BW], mab[b][:], _stsem(fw, mab[b]), reads=(mab[b],), writes=(ddst,))


def stage_rwkv(C):
    fw, op, S, BW, NB, NT, si = C.fw, C.fw.op, C.S, C.BW, C.NB, C.NT, C.si
    DBf, DBh, bk, xnT, xblk, PB = C.DBf, C.DBh, C.bk, C.xnT, C.xblk, C.PB
    ident, BOh, BOf, I2, onesS = C.ident, C.BOh, C.BOf, C.I2, C.onesS
    wlw, wla = C.wlw, C.wla
    GN_EPS = 64e-5
    CD = math.exp(-0.5)
    import os
    G = int(os.environ.get('RW_G', 4 if S <= 2048 else 3))
    G = min(G, NT)

    def pbank(j):
        return DBf[j // 2][:, (j % 2) * 512:(j % 2) * 512 + 512]

    def pbankh(j):
        return DBh[j // 2][:, (j % 2) * 1024:(j % 2) * 1024 + 1024]

    with contextlib.ExitStack() as st:
        TWT = fw.sb("TWT@%d" % si, [128, S], BF16, st)
        ALT = fw.sb("ALT@%d" % si, [128, S], BF16, st)
        with contextlib.ExitStack() as st0:
            Wlow = load_w(C, st0, "Wlow@%d" % si, C.wview(O_WL, 256), [128, 8, 256])
            for tb in range(NB):
                for j, (dst, func) in enumerate(((TWT, AF.Tanh), (ALT, AF.Copy))):
                    for kc in range(8):
                        op("pe", lambda e: e.matmul(pbank(j)[:, 0:BW], lhsT=Wlow[:, kc, j * 128:(j + 1) * 128], rhs=xblk(kc, tb),
                                                    start=(kc == 0), stop=(kc == 7)), reads=(Wlow, xnT), writes=(PB[j],))
                    op("act", lambda e: e.activation(out=dst[:, tb * BW:(tb + 1) * BW], in_=pbank(j)[:, 0:BW], func=func),
                       reads=(PB[j],), writes=(dst,))
        Whp = fw.sb("Whp@%d" % si, [128, 4, 8, 128], BF16, st)
        RT_ = fw.sb("RT_@%d" % si, [128, S], BF16, st)
        KT_ = fw.sb("KT_@%d" % si, [128, S], BF16, st)
        VT_ = fw.sb("VT_@%d" % si, [128, S], BF16, st)
        KKT = fw.sb("KKT@%d" % si, [128, S], BF16, st)
        OFT = fw.sb("OFT@%d" % si, [128, S], BF16, st)
        OBT = fw.sb("OBT@%d" % si, [128, S], BF16, st)
        CB = min(256, S)

        for hp in range(8):
            hpc = slice(hp, hp + 1)
            for s_, off in enumerate((O_R, O_RK, O_RV, O_GR)):
                fw.load("pool", Whp, Whp[:, s_, :, :], C.wview(off + hp * 128, 128))
            if True:
                if hp == 0:
                    if S <= 2048:
                        ctmps = [fw.sb("ctmp%d@%d" % (i, si), [128, CB], F32, st) for i in range(3)]
                    else:
                        ctmps = [fw.sb("ctmp0@%d" % si, [128, CB], F32, st)] * 3
                    ctmp = ctmps[0]
                    kkr = fw.sb("kkrw@%d" % si, [128, CB], F32, st)
                    nrm = fw.sb("nrmw@%d" % si, [128, CB], F32, st)
                    sqw = fw.sb("sqw@%d" % si, [128, CB], BF16, st)
                for cb in range(S // CB):
                    cs = slice(cb * CB, (cb + 1) * CB)
                    for s_, dst in enumerate((RT_, KT_, VT_)):
                        j = (3 * cb + s_) % 4
                        for kc in range(8):
                            op("pe", lambda e: e.matmul(pbank(j)[:, 0:CB + 2], lhsT=Whp[:, s_, kc, :], rhs=xnT[:, kc, cb * CB:cb * CB + CB + 2],
                                                        start=(kc == 0), stop=(kc == 7)), reads=(Whp, xnT), writes=(PB[j],))
                        cw = [C.cv_c[:, i, s_ * 8 + hp:s_ * 8 + hp + 1] for i in range(3)]
                        ct = ctmps[s_]
                        op("act", lambda e: e.activation(out=ct[:], in_=pbank(j)[:, 0:CB], func=AF.Copy, scale=cw[0]),
                           reads=(PB[j], C.cv_c), writes=(ct,))
                        op("dve", lambda e: e.scalar_tensor_tensor(out=ct[:], in0=pbank(j)[:, 1:CB + 1], scalar=cw[1], in1=ct[:],
                                                                   op0=ALU.mult, op1=ALU.add), reads=(PB[j], C.cv_c, ct), writes=(ct,))
                        op("dve", lambda e: e.scalar_tensor_tensor(out=dst[:, cs], in0=pbank(j)[:, 2:CB + 2], scalar=cw[2],
                                                                   in1=ct[:], op0=ALU.mult, op1=ALU.add),
                           reads=(PB[j], C.cv_c, ct), writes=(dst,))
                    j = 4 + cb % 2
                    if os.environ.get('RW_STOP') == 'c1nokk':
                        continue
                    op("act", lambda e: e.activation(out=kkr[:], in_=KT_[:, cs], func=AF.Copy, scale=C.kk_c[:, hpc]),
                       reads=(KT_, C.kk_c), writes=(kkr,))
                    op("pool", lambda e: e.tensor_tensor(out=sqw[:], in0=kkr[:], in1=kkr[:], op=ALU.mult), reads=(kkr,), writes=(sqw,))
                    op("pe", lambda e: e.matmul(pbank(j)[:, 0:CB], lhsT=BOh[:], rhs=sqw[:], start=True, stop=True), reads=(BOh, sqw), writes=(PB[j],))
                    op("act", lambda e: e.activation(out=nrm[:], in_=pbank(j)[:, 0:CB], func=AF.Sqrt, bias=C.eps12[:, 0:1]),
                       reads=(PB[j], C.eps12), writes=(nrm,))
                    op("dve", lambda e: e.reciprocal(out=nrm[:], in_=nrm[:]), reads=(nrm,), writes=(nrm,))
                    op("dve", lambda e: e.tensor_tensor(out=KKT[:, cs], in0=kkr[:], in1=nrm[:], op=ALU.mult), reads=(kkr, nrm), writes=(KKT,))
            if True:
                def f32t(n, g):
                    return fw.sb("%s%d@%d" % (n, g, si), [128, 128], F32, st)

                def bft(n, g, shape):
                    return fw.sb("%s%d@%d" % (n, g, si), shape, BF16, st)
                if hp == 0:
                    W = []
                for g in (range(G) if hp == 0 else ()):
                    w = SimpleNamespace()
                    for n in ("sgw", "a_", "cum", "c2", "cex", "e_in", "e_ex", "e_ng", "kdir", "tmpb"):
                        setattr(w, n, f32t(n, g))
                    w.AR = bft("AR", g, [128, 2, 128])
                    w.BK = bft("BK", g, [128, 2, 128])
                    w.TM = bft("TM", g, [128, 4, 128])
                    w.AT = bft("AT", g, [128, 2, 4, 128])
                    w.XY0 = bft("XY0", g, [128, 2, 2, 128])
                    w.XY = [bft("XYa", g, [128, 2, 3, 128]), bft("XYb", g, [128, 2, 3, 128])]
                    w.Yf = bft("Yf", g, [128, 2, 128])
                    w.RP = bft("RP", g, [128, 128])
                    w.MpT = bft("MpT", g, [128, 64])
                    W.append(w)
                if hp == 0:
                    Hs = [fw.sb("H%d@%d" % (i, si), [128, 64], BF16, st) for i in range(2)]

                def head(d, tau, g):
                    w = W[g]
                    bA, bB = 2 * g, 2 * g + 1
                    PA, PBb = PB[bA], PB[bB]
                    dsl = slice(d * 64, (d + 1) * 64)
                    sl = slice(tau * 128, (tau + 1) * 128)
                    rT, kTt, vT, kkT = RT_[:, sl], KT_[:, sl], VT_[:, sl], KKT[:, sl]
                    op("pe", lambda e: e.matmul(pbank(bA)[:, 0:128], lhsT=wlw[dsl, hp * 128:(hp + 1) * 128], rhs=TWT[dsl, sl], start=True, stop=True),
                       reads=(wlw, TWT), writes=(PA,))
                    op("pe", lambda e: e.matmul(pbank(bA)[:, 128:256], lhsT=wla[dsl, hp * 128:(hp + 1) * 128], rhs=ALT[dsl, sl], start=True, stop=True),
                       reads=(wla, ALT), writes=(PA,))
                    yield
                    op("act", lambda e: e.activation(out=w.sgw[:], in_=pbank(bA)[:, 0:128], func=AF.Tanh, bias=C.hw0_c[:, d, hpc], scale=0.5),
                       reads=(PA, C.hw0_c), writes=(w.sgw,))
                    op("act", lambda e: e.activation(out=w.a_[:], in_=pbank(bA)[:, 128:256], func=AF.Tanh, bias=C.ha0_c[:, d, hpc], scale=0.5),
                       reads=(PA, C.ha0_c), writes=(w.a_,))
                    yield
                    op("pool", lambda e: e.tensor_scalar(out=w.kdir[:], in0=w.a_[:], scalar1=C.ka_c[:, hpc], scalar2=C.tmka[:, hpc], op0=ALU.mult, op1=ALU.add),
                       reads=(w.a_, C.ka_c, C.tmka), writes=(w.kdir,))
                    yield
                    op("pool", lambda e: e.tensor_tensor(out=w.kdir[:], in0=kTt, in1=w.kdir[:], op=ALU.mult), reads=(KT_, w.kdir), writes=(w.kdir,))
                    yield
                    op("dve", lambda e: e.tensor_tensor_scan(out=w.cum[:], data0=w.sgw[:], data1=onesS[:], initial=0.0, op0=ALU.add, op1=ALU.add),
                       reads=(onesS, w.sgw), writes=(w.cum,))
                    yield
                    cu = w.cum
                    if d == 1:
                        op("dve", lambda e: e.tensor_scalar(out=w.c2[:], in0=w.cum[:], scalar1=-1.0, scalar2=w.cum[:, 127:128], op0=ALU.mult, op1=ALU.add),
                           reads=(w.cum,), writes=(w.c2,))
                        yield
                        op("dve", lambda e: e.scalar_tensor_tensor(out=w.c2[:], in0=w.c2[:], scalar=1.0, in1=w.sgw[:], op0=ALU.add, op1=ALU.add),
                           reads=(w.c2, w.sgw), writes=(w.c2,))
                        yield
                        cu = w.c2
                    op("dve", lambda e: e.scalar_tensor_tensor(out=w.cex[:], in0=cu[:], scalar=-1.0, in1=w.sgw[:], op0=ALU.add, op1=ALU.subtract),
                       reads=(cu, w.sgw), writes=(w.cex,))
                    yield
                    HC = 0.5 * CD
                    op("act", lambda e: e.activation(out=w.e_in[:], in_=cu[:], func=AF.Exp, scale=-HC), reads=(cu,), writes=(w.e_in,))
                    op("act", lambda e: e.activation(out=w.e_ex[:], in_=w.cex[:], func=AF.Exp, scale=-HC), reads=(w.cex,), writes=(w.e_ex,))
                    op("act", lambda e: e.activation(out=w.e_ng[:], in_=cu[:], func=AF.Exp, scale=HC, bias=C.lnhalf[:, 0:1]), reads=(cu, C.lnhalf), writes=(w.e_ng,))
                    yield
                    op("dve", lambda e: e.scalar_tensor_tensor(out=w.AR[:, 0, :], in0=kkT, scalar=-1.0, in1=w.e_ex[:], op0=ALU.mult, op1=ALU.mult),
                       reads=(KKT, w.e_ex), writes=(w.AR,))
                    yield
                    op("pool", lambda e: e.tensor_tensor(out=w.AR[:, 1, :], in0=rT, in1=w.e_in[:], op=ALU.mult), reads=(RT_, w.e_in, w.AR), writes=(w.AR,))
                    yield
                    op("dve", lambda e: e.scalar_tensor_tensor(out=w.tmpb[:], in0=w.a_[:], scalar=1.0, in1=kkT, op0=ALU.add, op1=ALU.mult),
                       reads=(KKT, w.a_), writes=(w.tmpb,))
                    yield
                    op("pool", lambda e: e.tensor_tensor(out=w.BK[:, 0, :], in0=w.tmpb[:], in1=w.e_ng[:], op=ALU.mult), reads=(w.tmpb, w.e_ng), writes=(w.BK,))
                    op("pool", lambda e: e.tensor_tensor(out=w.BK[:, 1, :], in0=w.kdir[:], in1=w.e_ng[:], op=ALU.mult), reads=(w.kdir, w.e_ng, w.BK), writes=(w.BK,))
                    yield
                    for j, (srcb, srcap) in enumerate(((w.AR, w.AR[:, 0, :]), (w.BK, w.BK[:, 0, :]), (w.BK, w.BK[:, 1, :]), (VT_, vT))):
                        op("pe", lambda e: e.transpose(out=pbankh(bB)[:, j * 128:(j + 1) * 128], in_=srcap, identity=ident[:]),
                           reads=(srcb, ident), writes=(PBb,))
                    yield
                    op("act", lambda e: e.activation(out=w.TM[:], in_=pbankh(bB)[:, 0:512].rearrange("p (a b) -> p a b", a=4), func=AF.Copy),
                       reads=(PBb,), writes=(w.TM,))
                    yield
                    for hh in (0, 1):
                        s = slice(hh * 64, (hh + 1) * 64)
                        op("pe", lambda e: e.matmul(pbank(bA + hh)[:, 0:128], lhsT=w.AR[s, 0, :], rhs=w.BK[s, 0, :], start=True, stop=True),
                           reads=(w.BK, w.AR), writes=(PB[bA + hh],))
                    yield
                    op("dve", lambda e: e.tensor_tensor(out=w.XY0[:, :, 0, :], in0=DBf[g][:, :].rearrange("p (h c) -> p h c", h=2)[:, :, 0:128],
                                                        in1=C.MASKL[d][:], op=ALU.mult), reads=(PA, PBb, C.MASKL[d]), writes=(w.XY0,))
                    yield
                    for hh in (0, 1):
                        s = slice(hh * 64, (hh + 1) * 64)
                        bj = bA + hh
                        arf = w.AR[s, :, :].rearrange("p a b -> p (a b)")
                        op("pe", lambda e: e.matmul(pbank(bj)[:, 0:256], lhsT=w.BK[s, 0, :], rhs=arf, start=True, stop=True),
                           reads=(w.BK, w.AR), writes=(PB[bj],))
                        op("pe", lambda e: e.matmul(pbank(bj)[:, 256:512], lhsT=w.BK[s, 1, :], rhs=arf, start=True, stop=True),
                           reads=(w.BK, w.AR), writes=(PB[bj],))
                    yield
                    op("dve", lambda e: e.tensor_tensor(out=w.AT[:], in0=DBf[g][:, :].rearrange("p (h a b) -> p h a b", h=2, a=4),
                                                        in1=C.MASK8[d][:], op=ALU.mult), reads=(PA, PBb, C.MASK8[d]), writes=(w.AT,))
                    yield
                    for hh in (0, 1):
                        s = slice(hh * 64, (hh + 1) * 64)
                        op("pe", lambda e: e.matmul(pbank(bA)[:, hh * 64:(hh + 1) * 64], lhsT=w.AT[:, hh, 2, :], rhs=w.TM[:, 3, s], start=True, stop=True),
                           reads=(w.AT, w.TM), writes=(PA,))
                    yield
                    op("act", lambda e: e.activation(out=w.XY0[:, :, 1, 64:128], in_=pbank(bA)[:, 0:128].rearrange("p (a b) -> p a b", a=2), func=AF.Copy),
                       reads=(PA, w.XY0), writes=(w.XY0,))
                    op("pool", lambda e: e.tensor_copy(out=w.XY0[:, :, 1, 0:64], in_=w.TM[:, 0, :].rearrange("p (a b) -> p a b", a=2)),
                       reads=(w.TM, w.XY0), writes=(w.XY0,))
                    yield
                    XN = [w.XY0[:, hh, 0, :] for hh in (0, 1)]
                    YK = [w.XY0[:, hh, 1, :] for hh in (0, 1)]
                    XNY = [w.XY0[:, hh, :, :].rearrange("p a b -> p (a b)") for hh in (0, 1)]
                    XT = [w.AT[:, hh, 0, :] for hh in (0, 1)]
                    srcb = (w.XY0, w.AT)
                    for k in range(7):
                        last = (k == 6)
                        for hh in (0, 1):
                            bj = bA + hh
                            if not last:
                                op("pe", lambda e: e.matmul(pbank(bj)[:, 256:384], lhsT=XN[hh], rhs=XT[hh], start=True, stop=True),
                                   reads=srcb, writes=(PB[bj],))
                                op("pe", lambda e: e.matmul(pbank(bj)[:, 0:256], lhsT=XT[hh], rhs=XNY[hh], start=True, stop=False),
                                   reads=srcb, writes=(PB[bj],))
                            else:
                                op("pe", lambda e: e.matmul(pbank(bj)[:, 128:256], lhsT=XT[hh], rhs=YK[hh], start=True, stop=False),
                                   reads=srcb, writes=(PB[bj],))
                            op("pe", lambda e: e.matmul(pbank(bj)[:, 128:256], lhsT=ident[:], rhs=YK[hh], start=False, stop=True),
                               reads=srcb + (ident,), writes=(PB[bj],))
                        yield
                        eng = "act" if k % 2 == 0 else "dve"
                        if not last:
                            nxt = w.XY[k % 2]
                            src = DBf[g][:, :].rearrange("p (h c) -> p h c", h=2)[:, :, 0:384].rearrange("p h (a b) -> p h a b", a=3)
                            if eng == "act":
                                op("act", lambda e: e.activation(out=nxt[:], in_=src, func=AF.Copy), reads=(PA, PBb), writes=(nxt,))
                            else:
                                op("dve", lambda e: e.tensor_copy(out=nxt[:], in_=src), reads=(PA, PBb), writes=(nxt,))
                            XN = [nxt[:, hh, 0, :] for hh in (0, 1)]
                            YK = [nxt[:, hh, 1, :] for hh in (0, 1)]
                            XNY = [nxt[:, hh, 0:2, :].rearrange("p a b -> p (a b)") for hh in (0, 1)]
                            XT = [nxt[:, hh, 2, :] for hh in (0, 1)]
                            srcb = (nxt,)
                        else:
                            src = DBf[g][:, :].rearrange("p (h c) -> p h c", h=2)[:, :, 128:256]
                            op("act", lambda e: e.activation(out=w.Yf[:], in_=src, func=AF.Copy), reads=(PA, PBb), writes=(w.Yf,))
                        yield
                    Yf = w.Yf
                    for hh in (0, 1):
                        s = slice(hh * 64, (hh + 1) * 64)
                        op("pe", lambda e: e.matmul(pbank(bA)[s, 384:512], lhsT=Yf[:, hh, 0:64], rhs=w.AT[:, hh, 1, :], start=True, stop=True),
                           reads=(Yf, w.AT), writes=(PA,))
                        op("pe", lambda e: e.matmul(pbank(bB)[s, 384:448], lhsT=Yf[:, hh, 0:64], rhs=w.TM[:, 1, s], start=True, stop=True),
                           reads=(Yf, w.TM), writes=(PBb,))
                    yield
                    op("dve", lambda e: e.tensor_tensor(out=w.RP[:], in0=pbank(bA)[:, 384:512], in1=w.AR[:, 1, :], op=ALU.add),
                       reads=(PA, w.AR), writes=(w.RP,))
                    op("dve", lambda e: e.tensor_tensor(out=w.MpT[:], in0=pbank(bB)[:, 384:448], in1=I2[:], op=ALU.add), reads=(PBb, I2), writes=(w.MpT,))
                    yield

                def tail(d, tau, g, Hc, Hn):
                    w = W[g]
                    bA, bB = 2 * g, 2 * g + 1
                    PA, PBb = PB[bA], PB[bB]
                    Yf = w.Yf
                    sl = slice(tau * 128, (tau + 1) * 128)
                    for hh in (0, 1):
                        s = slice(hh * 64, (hh + 1) * 64)
                        op("pe", lambda e: e.matmul(pbank(bA)[s, 0:128], lhsT=Yf[:, hh, 64:128], rhs=w.AT[:, hh, 1, :], start=True, stop=False),
                           reads=(Yf, w.AT), writes=(PA,))
                        op("pe", lambda e: e.matmul(pbank(bA)[s, 0:128], lhsT=w.TM[:, 3, s], rhs=w.AT[:, hh, 3, :], start=False, stop=False),
                           reads=(w.TM, w.AT), writes=(PA,))
                        op("pe", lambda e: e.matmul(pbank(bA)[s, 0:128], lhsT=Hc[s, :], rhs=w.RP[s, :], start=False, stop=True),
                           reads=(Hc, w.RP), writes=(PA,))
                    for hh in (0, 1):
                        s = slice(hh * 64, (hh + 1) * 64)
                        op("pe", lambda e: e.matmul(pbank(bB)[s, 0:64], lhsT=w.TM[:, 1, s], rhs=Yf[:, hh, 64:128], start=True, stop=False),
                           reads=(Yf, w.TM), writes=(PBb,))
                        op("pe", lambda e: e.matmul(pbank(bB)[s, 0:64], lhsT=w.TM[:, 2, s], rhs=w.TM[:, 3, s], start=False, stop=False),
                           reads=(w.TM,), writes=(PBb,))
                        op("pe", lambda e: e.matmul(pbank(bB)[s, 0:64], lhsT=w.MpT[s, :], rhs=Hc[s, :], start=False, stop=True),
                           reads=(w.MpT, Hc), writes=(PBb,))
                    WC = w.e_in[:, 127:128] if d == 0 else w.e_in[:, 0:1]
                    op("act", lambda e: e.activation(out=Hn[:], in_=pbank(bB)[:, 0:64], func=AF.Copy, scale=WC), reads=(PBb, w.e_in), writes=(Hn,))
                    dst = OFT if d == 0 else OBT
                    op("dve", lambda e: e.tensor_copy(out=dst[:, sl], in_=pbank(bA)[:, 0:128]), reads=(PA,), writes=(dst,))

                for d in (((0, 1) if 'RW_HEAD' not in os.environ else (0,)) if os.environ.get('RW_STOP') not in ('c1', 'c1nokk') else ()):
                    op("pool", lambda e: e.memset(Hs[0][:], 0.0), writes=(Hs[0],))
                    order = list(range(NT)) if d == 0 else list(range(NT - 1, -1, -1))
                    step = 0
                    DELTA = int(os.environ.get('RW_DELTA', 3))
                    slots = [None] * G
                    state = ['idle'] * G
                    completed = {}
                    next_pos = 0
                    tails_done = 0
                    rnd = 0
                    while tails_done < NT:
                        for gi in range(G):
                            if state[gi] == 'idle':
                                if next_pos < NT and rnd >= gi * DELTA:
                                    slots[gi] = (next_pos, head(d, order[next_pos], gi))
                                    state[gi] = 'run'
                                    next_pos += 1
                                else:
                                    continue
                            if state[gi] == 'run':
                                pos, gen = slots[gi]
                                try:
                                    next(gen)
                                except StopIteration:
                                    completed[pos] = gi
                                    state[gi] = 'wait'
                        while tails_done in completed:
                            gi = completed.pop(tails_done)
                            tail(d, order[tails_done], gi, Hs[step % 2], Hs[(step + 1) % 2])
                            step += 1
                            tails_done += 1
                            state[gi] = 'idle'
                        rnd += 1
            if True:
                PW = min(512, S) if S <= 2048 else 256
                if hp == 0:
                    if PW == CB:
                        o_, cen, sq2 = ctmp, kkr, nrm
                        var, rk_ = [fw.sb("%s@%d" % (n, si), [128, PW], F32, st) for n in ("pvar", "prk")]
                    else:
                        o_, cen, sq2, var, rk_ = [fw.sb("%s@%d" % (n, si), [128, PW], F32, st) for n in ("po_", "pcen", "psq2", "pvar", "prk")]
                for pb_ in range(S // PW):
                    ps = slice(pb_ * PW, (pb_ + 1) * PW)
                    op("dve", lambda e: e.tensor_tensor(out=o_[:], in0=OFT[:, ps], in1=OBT[:, ps], op=ALU.add), reads=(OFT, OBT), writes=(o_,))
                    op("pe", lambda e: e.matmul(pbank(0)[:, 0:PW], lhsT=BOf[:], rhs=o_[:], start=True, stop=True), reads=(BOf, o_), writes=(PB[0],))
                    op("dve", lambda e: e.scalar_tensor_tensor(out=cen[:], in0=pbank(0)[:, 0:PW], scalar=-1.0 / 64, in1=o_[:], op0=ALU.mult, op1=ALU.add),
                       reads=(PB[0], o_), writes=(cen,))
                    op("pool", lambda e: e.tensor_tensor(out=sq2[:], in0=cen[:], in1=cen[:], op=ALU.mult), reads=(cen,), writes=(sq2,))
                    op("pe", lambda e: e.matmul(pbank(1)[:, 0:PW], lhsT=BOf[:], rhs=sq2[:], start=True, stop=True), reads=(BOf, sq2), writes=(PB[1],))
                    op("dve", lambda e: e.tensor_scalar(out=var[:], in0=pbank(1)[:, 0:PW], scalar1=1.0 / 64, scalar2=GN_EPS, op0=ALU.mult, op1=ALU.add),
                       reads=(PB[1],), writes=(var,))
                    op("act", lambda e: e.activation(out=var[:], in_=var[:], func=AF.Sqrt), reads=(var,), writes=(var,))
                    op("dve", lambda e: e.reciprocal(out=var[:], in_=var[:]), reads=(var,), writes=(var,))
                    op("dve", lambda e: e.tensor_tensor(out=cen[:], in0=cen[:], in1=var[:], op=ALU.mult), reads=(cen, var), writes=(cen,))
                    op("dve", lambda e: e.tensor_scalar(out=cen[:], in0=cen[:], scalar1=C.lg_c[:, hpc], scalar2=C.lb_c[:, hpc], op0=ALU.mult, op1=ALU.add),
                       reads=(cen, C.lg_c, C.lb_c), writes=(cen,))
                    op("dve", lambda e: e.scalar_tensor_tensor(out=rk_[:], in0=RT_[:, ps], scalar=C.rk_c[:, hpc], in1=KT_[:, ps], op0=ALU.mult, op1=ALU.mult),
                       reads=(RT_, KT_, C.rk_c), writes=(rk_,))
                    op("pe", lambda e: e.matmul(pbank(2)[:, 0:PW], lhsT=BOf[:], rhs=rk_[:], start=True, stop=True), reads=(BOf, rk_), writes=(PB[2],))
                    op("dve", lambda e: e.tensor_tensor(out=sq2[:], in0=pbank(2)[:, 0:PW], in1=VT_[:, ps], op=ALU.mult), reads=(PB[2], VT_), writes=(sq2,))
                    op("pool", lambda e: e.tensor_tensor(out=cen[:], in0=cen[:], in1=sq2[:], op=ALU.add), reads=(cen, sq2), writes=(cen,))
                    for kc in range(8):
                        op("pe", lambda e: e.matmul(pbank(3)[:, 0:PW], lhsT=Whp[:, 3, kc, :], rhs=xnT[:, kc, 1 + pb_ * PW:1 + (pb_ + 1) * PW],
                                                    start=(kc == 0), stop=(kc == 7)), reads=(Whp, xnT), writes=(PB[3],))
                    op("act", lambda e: e.activation(out=var[:], in_=pbank(3)[:, 0:PW], func=AF.Silu), reads=(PB[3],), writes=(var,))
                    op("dve", lambda e: e.tensor_tensor(out=OFT[:, ps], in0=cen[:], in1=var[:], op=ALU.mult), reads=(cen, var, OFT), writes=(OFT,))
            fw.dma("sp", C.scrR[:, hp, 0:S], OFT[:], _stsem(fw, OFT), reads=(OFT,), writes=(C.dR,))
    with contextlib.ExitStack() as st:
        Wor = load_w(C, st, "Wor@%d" % si, C.P["w_o_rwkv"].rearrange("(c p) n -> p c n", p=128), [128, 8, 1024])
        Wgmb = load_w(C, st, "Wgmb@%d" % si, C.wview(O_GMB, 1024), [128, 8, 1024])
        branch_tail(C, st, "r", None, Wor, Wgmb, C.scrR, C.dR, C.scrR, C.dR, None)


def make_rope(C, st0):
    fw, op, S, si = C.fw, C.fw.op, C.S, C.si
    cosT = fw.sb("cosT@%d" % si, [128, S], BF16, st0)
    sinT = fw.sb("sinT@%d" % si, [128, S], BF16, st0)
    with contextlib.ExitStack() as st:
        pi_ = fw.sb("pi_@%d" % si, [128, 1], I32, st)
        pj_ = fw.sb("pj_@%d" % si, [128, 1], I32, st)
        pf_ = fw.sb("pf_@%d" % si, [128, 2], F32, st)
        invf = fw.sb("invf@%d" % si, [128, 1], F32, st)
        posi = fw.sb("posi@%d" % si, [128, S], I32, st)
        ang = fw.sb("ang@%d" % si, [128, S], F32, st)
        kf = fw.sb("kf@%d" % si, [128, S], F32, st)
        tm = fw.sb("tm@%d" % si, [128, S], F32, st)
        op("pool", lambda e: e.iota(pi_[:], pattern=[[0, 1]], base=0, channel_multiplier=1), writes=(pi_,))
        op("dve", lambda e: e.tensor_scalar(out=pj_[:], in0=pi_[:], scalar1=7, scalar2=None, op0=ALU.bitwise_and),
           reads=(pi_,), writes=(pj_,))
        op("dve", lambda e: e.tensor_copy(out=pf_[:, 0:1], in_=pj_[:]), reads=(pj_,), writes=(pf_,))
        op("dve", lambda e: e.tensor_scalar(out=pj_[:], in0=pi_[:], scalar1=63, scalar2=None, op0=ALU.bitwise_and),
           reads=(pi_,), writes=(pj_,))
        op("dve", lambda e: e.tensor_copy(out=pf_[:, 1:2], in_=pj_[:]), reads=(pj_,), writes=(pf_,))
        op("act", lambda e: e.activation(out=invf[:], in_=pf_[:, 0:1], func=AF.Exp, scale=-math.log(ROPE_THETA) / 8.0),
           reads=(pf_,), writes=(invf,))
        op("dve", lambda e: e.tensor_scalar(out=pf_[:, 1:2], in0=pf_[:, 1:2], scalar1=16.0, scalar2=None, op0=ALU.is_lt),
           reads=(pf_,), writes=(pf_,))
        op("dve", lambda e: e.tensor_tensor(out=invf[:], in0=invf[:], in1=pf_[:, 1:2], op=ALU.mult),
           reads=(invf, pf_), writes=(invf,))
        op("pool", lambda e: e.iota(posi[:], pattern=[[1, S]], base=0, channel_multiplier=0), writes=(posi,))
        op("dve", lambda e: e.tensor_copy(out=ang[:], in_=posi[:]), reads=(posi,), writes=(ang,))
        op("dve", lambda e: e.tensor_scalar(out=ang[:], in0=ang[:], scalar1=invf[:, 0:1], scalar2=None, op0=ALU.mult),
           reads=(ang, invf), writes=(ang,))

        def wrap_sin(dst, shift):
            op("dve", lambda e: e.tensor_scalar(out=tm[:], in0=ang[:], scalar1=shift, scalar2=None, op0=ALU.add),
               reads=(ang,), writes=(tm,))
            op("dve", lambda e: e.tensor_scalar(out=kf[:], in0=tm[:], scalar1=1.0 / TWO_PI, scalar2=None, op0=ALU.mult),
               reads=(tm,), writes=(kf,))
            op("dve", lambda e: e.tensor_copy(out=posi[:], in_=kf[:]), reads=(kf,), writes=(posi,))
            op("dve", lambda e: e.tensor_copy(out=kf[:], in_=posi[:]), reads=(posi,), writes=(kf,))
            op("dve", lambda e: e.scalar_tensor_tensor(out=tm[:], in0=kf[:], scalar=-CW1, in1=tm[:], op0=ALU.mult, op1=ALU.add),
               reads=(kf, tm), writes=(tm,))
            op("dve", lambda e: e.scalar_tensor_tensor(out=tm[:], in0=kf[:], scalar=-CW2, in1=tm[:], op0=ALU.mult, op1=ALU.add),
               reads=(kf, tm), writes=(tm,))
            op("dve", lambda e: e.tensor_scalar(out=kf[:], in0=tm[:], scalar1=math.pi, scalar2=-TWO_PI, op0=ALU.is_gt, op1=ALU.mult),
               reads=(tm,), writes=(kf,))
            op("dve", lambda e: e.tensor_tensor(out=tm[:], in0=tm[:], in1=kf[:], op=ALU.add), reads=(tm, kf), writes=(tm,))
            op("dve", lambda e: e.tensor_scalar(out=kf[:], in0=tm[:], scalar1=-math.pi, scalar2=TWO_PI, op0=ALU.is_lt, op1=ALU.mult),
               reads=(tm,), writes=(kf,))
            op("dve", lambda e: e.tensor_tensor(out=tm[:], in0=tm[:], in1=kf[:], op=ALU.add), reads=(tm, kf), writes=(tm,))
            op("dve", lambda e: e.tensor_scalar(out=tm[:], in0=tm[:], scalar1=3.14159, scalar2=-3.14159, op0=ALU.min, op1=ALU.max),
               reads=(tm,), writes=(tm,))
            op("act", lambda e: e.activation(out=dst[:], in_=tm[:], func=AF.Sin), reads=(tm,), writes=(dst,))
        wrap_sin(sinT, 0.0)
        wrap_sin(cosT, math.pi / 2)

    return cosT, sinT


SEQ_LENS = [2048, 2048, 2048, 2048, 4096]
_NC_CACHE = {}


def kernel(**inputs):
    xp = np.asarray(inputs["x_prompt"], dtype=np.float32)
    xs = np.asarray(inputs["x_sample"], dtype=np.float32)
    n = 8
    if "nc" not in _NC_CACHE:
        _NC_CACHE["nc"] = build(SEQ_LENS)
    nc = _NC_CACHE["nc"]
    shared = {}
    for nme, shp in PARAMS:
        shared[nme] = np.ascontiguousarray(np.asarray(inputs[nme], dtype=np.float32).reshape(shp))
    in_maps = []
    for c in range(n):
        xc = np.concatenate([xp[4 * c:4 * c + 4].reshape(-1, D), xs[c].reshape(-1, D)], axis=0)
        m = {"x": np.ascontiguousarray(xc)}
        m.update(shared)
        in_maps.append(m)
    res = run_bass_kernel_spmd(nc, in_maps, core_ids=list(range(n)))
    yp = np.empty_like(xp)
    ys = np.empty_like(xs)
    for c in range(n):
        yc = np.asarray(res.results[c]["y"])
        yp[4 * c:4 * c + 4] = yc[0:8192].reshape(4, 2048, D)
        ys[c] = yc[8192:12288].reshape(4096, D)
    return (yp, ys)
```
